# Optimizing a Trainium2 kernel written in Bass

```python
import math
import jax, jax.numpy as jnp
from jax import lax
import numpy as np

D_MODEL = 1024
BATCH = 4
SEQ = 8192
DEPTH = 1
DEC_BATCH = 32
DEC_SEQ = 64
PAST_LEN = 2048

CHUNK = 64
SB_HEADS = 16
SB_HEAD_DIM = 64
SB_WIDTH = SB_HEADS * SB_HEAD_DIM
CONV_CH = D_MODEL
CONV_WIDTH = 31
MEM_TOKENS = 256
MEM_HEADS = 4
MEM_HEAD_DIM = D_MODEL // MEM_HEADS
MEM_WIDTH = MEM_HEADS * MEM_HEAD_DIM
N_BRANCH = 3
FFN_DIM = 2816
FFN_CONV_WIDTH = 3
Q_BLOCK = 128
NORM_EPS = 1e-6
IN_WIDTH = 3 * SB_WIDTH + 2 * CONV_CH + MEM_WIDTH + N_BRANCH * D_MODEL

kernel_name = 'stickbreak_conformer_memory_streaming_encoder'


def rms_norm(x, g):
    xf = x.astype(jnp.float32)
    y = xf * lax.rsqrt(jnp.mean(xf * xf, axis=-1, keepdims=True) + NORM_EPS)
    return (y * g.astype(jnp.float32)).astype(x.dtype)


def layer_norm(x, g, b):
    xf = x.astype(jnp.float32)
    mu = jnp.mean(xf, axis=-1, keepdims=True)
    var = jnp.mean(jnp.square(xf - mu), axis=-1, keepdims=True)
    y = (xf - mu) * lax.rsqrt(var + NORM_EPS)
    return (y * g.astype(jnp.float32) + b.astype(jnp.float32)).astype(x.dtype)


def causal_dwconv(xp, w):
    c = xp.shape[-1]
    return lax.conv_general_dilated(
        xp, w.astype(xp.dtype)[:, None, :], window_strides=(1,), padding='VALID',
        dimension_numbers=('NWC', 'WIO', 'NWC'), feature_group_count=c)


def _sb_block(qb, k, v, q_pos):
    k_pos = jnp.arange(k.shape[2], dtype=jnp.int32)
    mask = k_pos[None, :] < q_pos[:, None]
    z = jnp.einsum('bhqd,bhkd->bhqk', qb.astype(jnp.float32), k.astype(jnp.float32)) * (SB_HEAD_DIM ** -0.5)
    log_beta = jax.nn.log_sigmoid(z)
    log_1mb = jnp.where(mask, jax.nn.log_sigmoid(-z), 0.0)
    excl = lax.cumsum(log_1mb, axis=3, reverse=True) - log_1mb
    a = jnp.where(mask, jnp.exp(log_beta + excl), 0.0)
    return jnp.einsum('bhqk,bhkd->bhqd', a, v.astype(jnp.float32)).astype(qb.dtype)


def stick_breaking(q, k, v, past_len):
    b, h, t, d = q.shape
    if t <= Q_BLOCK:
        return _sb_block(q, k, v, past_len + jnp.arange(t, dtype=jnp.int32))
    nb = -(-t // Q_BLOCK)
    pad = nb * Q_BLOCK - t
    qp = jnp.pad(q, ((0, 0), (0, 0), (0, pad), (0, 0)))
    qb = qp.reshape(b, h, nb, Q_BLOCK, d).transpose(2, 0, 1, 3, 4)
    pos = (past_len + jnp.arange(nb * Q_BLOCK, dtype=jnp.int32)).reshape(nb, Q_BLOCK)
    out = lax.map(lambda a: _sb_block(a[0], k, v, a[1]), (qb, pos))
    return out.transpose(1, 2, 0, 3, 4).reshape(b, h, nb * Q_BLOCK, d)[:, :, :t]


def memory_kv(mem, g_mem, w_mem_kv):
    b, m, _ = mem.shape
    kv = rms_norm(mem, g_mem) @ w_mem_kv
    mk, mv = jnp.split(kv, 2, axis=-1)
    mk = mk.reshape(b, m, MEM_HEADS, MEM_HEAD_DIM).transpose(0, 2, 1, 3)
    mv = mv.reshape(b, m, MEM_HEADS, MEM_HEAD_DIM).transpose(0, 2, 1, 3)
    return mk, mv


def memory_attention(qm, mk, mv):
    s = jnp.einsum('bthd,bhmd->bhtm', qm.astype(jnp.float32), mk.astype(jnp.float32)) * (MEM_HEAD_DIM ** -0.5)
    p = jax.nn.softmax(s, axis=-1)
    return jnp.einsum('bhtm,bhmd->bthd', p, mv.astype(jnp.float32)).astype(qm.dtype)


def encoder_layer(x, mem_k, mem_v, sb_k_past, sb_v_past, conv_left, ffn_left, past_len,
                  g_mix_pre, g_mix_post, w_in, w_sb_o, conv_dw_w, conv_dw_b, conv_ln_g, conv_ln_b,
                  w_conv_o, w_mem_o, w_out, g_ffn_pre, g_ffn_post, w_ffn_up, ffn_dw_w, w_ffn_down):
    b, t, _ = x.shape
    h = rms_norm(x, g_mix_pre)
    proj = h @ w_in
    idx = [SB_WIDTH, 2 * SB_WIDTH, 3 * SB_WIDTH,
           3 * SB_WIDTH + CONV_CH, 3 * SB_WIDTH + 2 * CONV_CH,
           3 * SB_WIDTH + 2 * CONV_CH + MEM_WIDTH,
           3 * SB_WIDTH + 2 * CONV_CH + MEM_WIDTH + D_MODEL,
           3 * SB_WIDTH + 2 * CONV_CH + MEM_WIDTH + 2 * D_MODEL]
    q, k, v, ca, cb, qm, ga, gb, gc = jnp.split(proj, idx, axis=-1)

    def heads(z):
        return z.reshape(b, t, SB_HEADS, SB_HEAD_DIM).transpose(0, 2, 1, 3)
    q, k, v = heads(q), heads(k), heads(v)
    k_all = jnp.concatenate([sb_k_past.astype(k.dtype), k], axis=2)
    v_all = jnp.concatenate([sb_v_past.astype(v.dtype), v], axis=2)
    o_sb = stick_breaking(q, k_all, v_all, past_len)
    y_sb = o_sb.transpose(0, 2, 1, 3).reshape(b, t, SB_WIDTH) @ w_sb_o

    u = ca * jax.nn.sigmoid(cb)
    u_full = jnp.concatenate([conv_left.astype(u.dtype), u], axis=1)
    cc = causal_dwconv(u_full, conv_dw_w) + conv_dw_b
    cc = jax.nn.silu(layer_norm(cc, conv_ln_g, conv_ln_b))
    y_conv = cc @ w_conv_o

    o_mem = memory_attention(qm.reshape(b, t, MEM_HEADS, MEM_HEAD_DIM), mem_k, mem_v)
    y_mem = o_mem.reshape(b, t, MEM_WIDTH) @ w_mem_o

    merged = jax.nn.sigmoid(ga) * y_sb + jax.nn.sigmoid(gb) * y_conv + jax.nn.sigmoid(gc) * y_mem
    x = x + rms_norm(merged @ w_out, g_mix_post)

    h2 = rms_norm(x, g_ffn_pre)
    up = h2 @ w_ffn_up
    up_full = jnp.concatenate([ffn_left.astype(up.dtype), up], axis=1)
    upc = causal_dwconv(up_full, ffn_dw_w)
    fg, fv = jnp.split(upc, 2, axis=-1)
    x = x + rms_norm((jax.nn.gelu(fg) * fv) @ w_ffn_down, g_ffn_post)
    return (x, k, v, u_full[:, -(CONV_WIDTH - 1):], up_full[:, -(FFN_CONV_WIDTH - 1):])


def setup_inputs(seed: int = 0) -> dict:
    key = jax.random.key(seed)
    ks = iter(jax.random.split(key, 40))

    def nrm(shape, scale=1.0):
        return jax.random.normal(next(ks), shape, jnp.float32) * scale

    def gain(shape):
        return 1.0 + nrm(shape, 0.02)

    L = DEPTH
    return {
        'x_prompt': nrm((BATCH, SEQ, D_MODEL)),
        'x_sample': nrm((DEC_BATCH, DEC_SEQ, D_MODEL)),
        'mem_prompt': nrm((BATCH, MEM_TOKENS, D_MODEL)),
        'cache_sb_k': nrm((L, DEC_BATCH, SB_HEADS, PAST_LEN, SB_HEAD_DIM)),
        'cache_sb_v': nrm((L, DEC_BATCH, SB_HEADS, PAST_LEN, SB_HEAD_DIM)),
        'state_conv': nrm((L, DEC_BATCH, CONV_WIDTH - 1, CONV_CH), 0.5),
        'state_ffn_conv': nrm((L, DEC_BATCH, FFN_CONV_WIDTH - 1, 2 * FFN_DIM)),
        'cache_mem_k': nrm((L, DEC_BATCH, MEM_HEADS, MEM_TOKENS, MEM_HEAD_DIM)),
        'cache_mem_v': nrm((L, DEC_BATCH, MEM_HEADS, MEM_TOKENS, MEM_HEAD_DIM)),
        'g_mem': gain((L, D_MODEL)),
        'w_mem_kv': nrm((L, D_MODEL, 2 * MEM_WIDTH), D_MODEL ** -0.5),
        'g_mix_pre': gain((L, D_MODEL)),
        'g_mix_post': gain((L, D_MODEL)),
        'w_in': nrm((L, D_MODEL, IN_WIDTH), D_MODEL ** -0.5),
        'w_sb_o': nrm((L, SB_WIDTH, D_MODEL), SB_WIDTH ** -0.5),
        'conv_dw_w': nrm((L, CONV_WIDTH, CONV_CH), CONV_WIDTH ** -0.5),
        'conv_dw_b': nrm((L, CONV_CH), 0.02),
        'conv_ln_g': gain((L, CONV_CH)),
        'conv_ln_b': nrm((L, CONV_CH), 0.02),
        'w_conv_o': nrm((L, CONV_CH, D_MODEL), CONV_CH ** -0.5),
        'w_mem_o': nrm((L, MEM_WIDTH, D_MODEL), MEM_WIDTH ** -0.5),
        'w_out': nrm((L, D_MODEL, D_MODEL), D_MODEL ** -0.5),
        'g_ffn_pre': gain((L, D_MODEL)),
        'g_ffn_post': gain((L, D_MODEL)),
        'w_ffn_up': nrm((L, D_MODEL, 2 * FFN_DIM), D_MODEL ** -0.5),
        'ffn_dw_w': nrm((L, FFN_CONV_WIDTH, 2 * FFN_DIM), FFN_CONV_WIDTH ** -0.5),
        'w_ffn_down': nrm((L, FFN_DIM, D_MODEL), FFN_DIM ** -0.5),
    }


def reference(x_prompt, x_sample, mem_prompt, cache_sb_k, cache_sb_v, state_conv, state_ffn_conv,
              cache_mem_k, cache_mem_v, g_mem, w_mem_kv, g_mix_pre, g_mix_post, w_in, w_sb_o,
              conv_dw_w, conv_dw_b, conv_ln_g, conv_ln_b, w_conv_o, w_mem_o, w_out,
              g_ffn_pre, g_ffn_post, w_ffn_up, ffn_dw_w, w_ffn_down):
    yp, ys = x_prompt, x_sample
    bp = x_prompt.shape[0]
    past_len = cache_sb_k.shape[3]
    kp_l, vp_l, ks_l, vs_l, cp_l, cs_l, fp_l, fs_l, mkp_l, mvp_l = ([] for _ in range(10))
    for l in range(DEPTH):
        lw = (g_mix_pre[l], g_mix_post[l], w_in[l], w_sb_o[l], conv_dw_w[l], conv_dw_b[l],
              conv_ln_g[l], conv_ln_b[l], w_conv_o[l], w_mem_o[l], w_out[l],
              g_ffn_pre[l], g_ffn_post[l], w_ffn_up[l], ffn_dw_w[l], w_ffn_down[l])
        mk_p, mv_p = memory_kv(mem_prompt, g_mem[l], w_mem_kv[l])
        empty = jnp.zeros((bp, SB_HEADS, 0, SB_HEAD_DIM), yp.dtype)
        yp, kp, vp, cp, fp = encoder_layer(
            yp, mk_p, mv_p, empty, empty,
            jnp.zeros((bp, CONV_WIDTH - 1, CONV_CH), yp.dtype),
            jnp.zeros((bp, FFN_CONV_WIDTH - 1, 2 * FFN_DIM), yp.dtype), 0, *lw)
        ys, k_s, v_s, c_s, f_s = encoder_layer(
            ys, cache_mem_k[l], cache_mem_v[l], cache_sb_k[l], cache_sb_v[l],
            state_conv[l], state_ffn_conv[l], past_len, *lw)
        kp_l.append(kp); vp_l.append(vp); ks_l.append(k_s); vs_l.append(v_s)
        cp_l.append(cp); cs_l.append(c_s); fp_l.append(fp); fs_l.append(f_s)
        mkp_l.append(mk_p); mvp_l.append(mv_p)
    return (yp, ys, jnp.stack(kp_l), jnp.stack(vp_l), jnp.stack(ks_l), jnp.stack(vs_l),
            jnp.stack(cp_l), jnp.stack(cs_l), jnp.stack(fp_l), jnp.stack(fs_l),
            jnp.stack(mkp_l), jnp.stack(mvp_l))
```

```python
import contextlib
import numpy as np
import concourse.bass as bass
import concourse.mybir as mybir
from concourse.bass_utils import run_bass_kernel_spmd

F32 = mybir.dt.float32
BF16 = mybir.dt.bfloat16
ALU = mybir.AluOpType
AF = mybir.ActivationFunctionType

D = 1024
NCH = 8
FF = 2816
FF2 = 5632
NFC = 22
EPS = 1e-6


class _SkipPhase(Exception):
    pass


def _phase_gate(k):
    import os
    en = os.environ.get("KPH")
    if en is not None and str(k) not in en.split(","):
        raise _SkipPhase()


class Buf:
    def __init__(self, t, name):
        self.t = t
        self.name = name
        self.w = None
        self.r = {}
        self.dsem = None
        self.dcnt = 0
        self.wl = {} if t is None else None

    def __getitem__(self, k):
        return self.t[k]


class Prog:
    def __init__(self, nc, es):
        self.nc = nc
        self.es = es
        self.E = {"pe": nc.tensor, "act": nc.scalar, "dve": nc.vector, "pool": nc.gpsimd, "sp": nc.sync}
        self.sem = {e: es.enter_context(nc.semaphore("s_" + e)) for e in ("pe", "act", "dve", "pool")}
        self.cnt = {e: 0 for e in self.sem}
        self.seen = {e: {} for e in self.E}
        self.dbufs = []

    def _wait(self, e, dep, same_ok):
        if dep is None:
            return
        key, sem, val, src = dep
        if src is not None:
            val = 16 * src.dcnt
        elif key == e and not same_ok:
            return
        if self.seen[e].get(key, 0) >= val:
            return
        self.E[e].wait_ge(sem, val)
        self.seen[e][key] = val

    def _deps(self, e, reads, writes):
        for b in reads:
            self._wait(e, b.w, True)
            if b.wl:
                for d in list(b.wl.values()):
                    self._wait(e, d, True)
        for b in writes:
            self._wait(e, b.w, False)
            for d in list(b.r.values()):
                self._wait(e, d, False)

    def op(self, e, fn, reads=(), writes=(), inc=True):
        self._deps(e, reads, writes)
        ins = fn(self.E[e])
        if inc:
            self.cnt[e] += 1
            ins.then_inc(self.sem[e], 1)
            t = self.cnt[e]
        else:
            t = self.cnt[e] + 1
        dep = (e, self.sem[e], t, None)
        for b in reads:
            b.r[e] = dep
        for b in writes:
            b.w = dep
            b.r = {}
        return ins

    def dma(self, q, out_ap, in_ap, sbuf, reads=(), writes=()):
        self._deps(q, reads, writes)
        if sbuf.dsem is None:
            sbuf.dsem = self.es.enter_context(self.nc.semaphore("d_" + sbuf.name))
            self.dbufs.append(sbuf)
        sbuf.dcnt += 1
        self.E[q].dma_start(out=out_ap, in_=in_ap).then_inc(sbuf.dsem, 16)
        key = ("d", id(sbuf))
        dep = (key, sbuf.dsem, 16 * sbuf.dcnt, sbuf)
        for b in reads:
            b.r[key] = dep
        for b in writes:
            if b.wl is not None:
                b.wl[key] = dep
            else:
                b.w = dep
                b.r = {}

    def barrier(self):
        for e in self.E:
            for k in self.sem:
                if k != e and self.cnt[k] > self.seen[e].get(k, 0):
                    self.E[e].wait_ge(self.sem[k], self.cnt[k])
                    self.seen[e][k] = self.cnt[k]
            for b in self.dbufs:
                key = ("d", id(b))
                if 16 * b.dcnt > self.seen[e].get(key, 0):
                    self.E[e].wait_ge(b.dsem, 16 * b.dcnt)
                    self.seen[e][key] = 16 * b.dcnt

    def finish(self):
        for b in self.dbufs:
            self.E["sp"].wait_ge(b.dsem, 16 * b.dcnt)


def build(T, NS, PAST):
    nc = bass.Bass("TRN2", target_bir_lowering=False)
    TS = NS * 64
    NT = T // 512
    PB = PAST // 128

    def din(name, shape):
        return nc.dram_tensor(name, list(shape), F32, kind="ExternalInput").ap()

    def dout(name, shape):
        return nc.dram_tensor(name, list(shape), F32, kind="ExternalOutput").ap()

    xp = din("xp", [T, D]); xs = din("xs", [TS, D]); memp = din("memp", [256, D])
    ck = din("ck", [NS, 16, PAST, 64]); cv = din("cv", [NS, 16, PAST, 64])
    sconv = din("sconv", [NS, 30, D]); sffn = din("sffn", [NS, 2, FF2])
    cmk = din("cmk", [NS, 4, 256, 256]); cmv = din("cmv", [NS, 4, 256, 256])
    g_mem = din("g_mem", [1, D]); w_mem_kv = din("w_mem_kv", [D, 2048])
    g_mix_pre = din("g_mix_pre", [1, D]); g_mix_post = din("g_mix_post", [1, D])
    w_in = din("w_in", [D, 9216]); w_sb_o = din("w_sb_o", [D, D])
    conv_dw_w = din("conv_dw_w", [31, D]); conv_dw_b = din("conv_dw_b", [1, D])
    conv_ln_g = din("conv_ln_g", [1, D]); conv_ln_b = din("conv_ln_b", [1, D])
    w_conv_o = din("w_conv_o", [D, D]); w_mem_o = din("w_mem_o", [D, D]); w_out = din("w_out", [D, D])
    g_ffn_pre = din("g_ffn_pre", [1, D]); g_ffn_post = din("g_ffn_post", [1, D])
    w_ffn_up = din("w_ffn_up", [D, FF2]); ffn_dw_w = din("ffn_dw_w", [3, FF2]); w_ffn_down = din("w_ffn_down", [FF, D])

    yp = dout("yp", [T, D]); ys = dout("ys", [TS, D])
    kp = dout("kp", [16, T, 64]); vp = dout("vp", [16, T, 64])
    ks = dout("ks", [NS, 16, 64, 64]); vs = dout("vs", [NS, 16, 64, 64])
    cp = dout("cp", [30, D]); cs = dout("cs", [NS, 30, D])
    fp = dout("fp", [2, FF2]); fs = dout("fs", [NS, 2, FF2])
    mkp = dout("mkp", [4, 256, 256]); mvp = dout("mvp", [4, 256, 256])

    TT = T + TS
    KTs = nc.dram_tensor("KTs", [128, 8, TT], BF16).ap()
    VSs = nc.dram_tensor("VSs", [TT, D], BF16).ap()
    OSs = nc.dram_tensor("OSs", [128, 8, TT], BF16).ap()
    YCs = nc.dram_tensor("YCs", [128, 8, TT], F32).ap()
    YMs = nc.dram_tensor("YMs", [128, 8, TT], F32).ap()
    XMs = nc.dram_tensor("XMs", [TT, D], F32).ap()

    tiles = []
    for i in range(NT):
        tiles.append(dict(r0=i * 512, N=512, segs=[(0, 512, None)], idx=i, sample=False))
    tiles.append(dict(r0=T, N=TS, segs=[(s * 64, 64, s) for s in range(NS)], idx=NT, sample=True))
    ntile = len(tiles)
    trk = {n: [Buf(None, f"{n}{i}") for i in range(ntile)] for n in ("KT", "VS", "OS", "YC", "YM", "XM")}

    def xrows(tl, j):
        r = tl["r0"] + j * 128
        if tl["sample"]:
            return xs[r - T:r - T + 128, :]
        return xp[r:r + 128, :]

    es = contextlib.ExitStack()
    with es:
        P = Prog(nc, es)

        uid = [0]

        def sb(st, name, shape, dt):
            uid[0] += 1
            name = f"{name}_{uid[0]}"
            return Buf(st.enter_context(nc.sbuf_tensor(name, list(shape), dt)), name)

        PS = [Buf(es.enter_context(nc.psum_tensor(f"ps{i}", [128, 512], F32)), f"ps{i}") for i in range(8)]

        identb = sb(es, "identb", [128, 128], BF16)
        identf = sb(es, "identf", [128, 128], F32)
        negtri = sb(es, "negtri", [128, 128], BF16)
        negones = sb(es, "negones", [128, 128], BF16)
        onesb = sb(es, "onesb", [128, 128], BF16)
        onesf = sb(es, "onesf", [128, 128], F32)
        for bfr, val in ((identb, 1.0), (identf, 1.0), (negtri, -1.0), (negones, -1.0), (onesb, 1.0),
                         (onesf, 1.0 / D)):
            P.op("pool", lambda g, b=bfr, v=val: g.memset(b[:], v), writes=[bfr])
        for bfr in (identb, identf):
            P.op("pool", lambda g, b=bfr: g.affine_select(out=b[:], in_=b[:], pattern=[[-1, 128]],
                 compare_op=ALU.is_equal, fill=0.0, base=0, channel_multiplier=1), reads=[bfr], writes=[bfr])
        P.op("pool", lambda g: g.affine_select(out=negtri[:], in_=negtri[:], pattern=[[-1, 128]],
             compare_op=ALU.is_ge, fill=0.0, base=0, channel_multiplier=1), reads=[negtri], writes=[negtri])

        gbc = {}
        gsrc = {"pre": g_mix_pre, "post": g_mix_post, "fpre": g_ffn_pre, "fpost": g_ffn_post, "mem": g_mem}
        xt = [None] * 4
        xn = [None] * 2
        cm = {}

        class _HT:
            def __getitem__(self, k):
                return cm["hT"].t[k]
        hT = _HT()

        def common(ph, nsub, N, gs):
            for j in range(nsub):
                xt[j] = sb(ph, f"xt{j}", [128, D], F32)
            for j in range(2):
                xn[j] = sb(ph, f"xn{j}", [128, D], BF16)
            cm["hT"] = sb(ph, "hT", [128, 8, N], BF16)
            cm["junk"] = sb(ph, "junk", [128, D], BF16)
            cm["ssq"] = sb(ph, "ssq", [128, 4], F32)
            cm["rstd"] = sb(ph, "rstd", [128, 4], F32)
            for nm in gs:
                gbc[nm] = sb(ph, "g_" + nm, [128, D], F32)
                P.dma("sp", gbc[nm][:], gsrc[nm][0:1, :].partition_broadcast(128), gbc[nm], writes=[gbc[nm]])

        def rstd_from(ss_ap, out_ap, ssb, outb, n):
            P.op("act", lambda a: a.activation(out=out_ap, in_=ss_ap, func=AF.Ln, scale=1.0 / n, bias=EPS),
                 reads=[ssb], writes=[outb])
            P.op("act", lambda a: a.activation(out=out_ap, in_=out_ap, func=AF.Exp, scale=-0.5),
                 reads=[outb], writes=[outb])

        def load_x(tl, src_fn=None, trkb=None):
            nsub = tl["N"] // 128
            for j in range(nsub):
                src = src_fn(tl, j) if src_fn else xrows(tl, j)
                P.dma("sp", xt[j][:], src, xt[j], reads=[trkb] if trkb else [], writes=[xt[j]])

        def norm_T(tl, g):
            nsub = tl["N"] // 128
            junk, ssq, rstd, hTb = cm["junk"], cm["ssq"], cm["rstd"], cm["hT"]
            P.op("dve", lambda v: v.memset(ssq[:], 0.0), writes=[ssq])
            for j in range(nsub):
                P.op("act", lambda a, j=j: a.activation(out=junk[:], in_=xt[j][:], func=AF.Square,
                     accum_out=ssq[:, j:j + 1]), reads=[xt[j]], writes=[junk, ssq])
            rstd_from(ssq[:, 0:nsub], rstd[:, 0:nsub], ssq, rstd, D)
            for j in range(nsub):
                xb = xn[j % 2]
                P.op("dve", lambda v, j=j, xb=xb: v.scalar_tensor_tensor(out=xb[:], in0=xt[j][:],
                     scalar=rstd[:, j:j + 1], in1=g[:], op0=ALU.mult, op1=ALU.mult),
                     reads=[xt[j], rstd, g], writes=[xb])
                for half in range(2):
                    pb = PS[6 + half]
                    for cc in range(4):
                        c = half * 4 + cc
                        P.op("pe", lambda t, c=c, cc=cc, pb=pb, xb=xb: t.matmul(pb[:, cc * 128:(cc + 1) * 128],
                             lhsT=xb[:, c * 128:(c + 1) * 128], rhs=identb[:], start=True, stop=True),
                             reads=[xb, identb], writes=[pb], inc=(cc == 3))
                    eng = "act" if half == 0 else "dve"
                    if eng == "act":
                        P.op("act", lambda a, half=half, pb=pb, j=j: a.copy(
                             out=hT[:, half * 4:half * 4 + 4, j * 128:(j + 1) * 128],
                             in_=pb[:].rearrange("p (c t) -> p c t", c=4)), reads=[pb], writes=[hTb])
                    else:
                        P.op("dve", lambda v, half=half, pb=pb, j=j: v.tensor_copy(
                             out=hT[:, half * 4:half * 4 + 4, j * 128:(j + 1) * 128],
                             in_=pb[:].rearrange("p (c t) -> p c t", c=4)), reads=[pb], writes=[hTb])

        wst = [None, None]
        wcnt = [0]

        def load_w(dst, src2d, nrc, ncols, dcol0=0):
            for rc in range(nrc):
                for cb in range(0, ncols, 2048):
                    w = min(2048, ncols - cb)
                    k = wcnt[0] % 2
                    wcnt[0] += 1
                    st = wst[k]
                    P.dma("sp", st[:, 0:w], src2d[rc * 128:(rc + 1) * 128, cb:cb + w], st, writes=[st])
                    eng = "dve" if k == 0 else "pool"
                    P.op(eng, lambda v, st=st, rc=rc, cb=cb, w=w: v.tensor_copy(
                         out=dst[:, rc, dcol0 + cb:dcol0 + cb + w], in_=st[:, 0:w]), reads=[st], writes=[dst])

        def load_cols(dst, srcs, R, nchunk, stg):
            r0 = 0
            for ap, nr in srcs:
                P.dma("sp", stg[r0:r0 + nr, 0:nchunk * 128], ap, stg, writes=[stg])
                r0 += nr
            pb = PS[6]
            for c in range(nchunk):
                P.op("pe", lambda t, c=c: t.matmul(pb[:, c * R:(c + 1) * R], lhsT=stg[0:R, c * 128:(c + 1) * 128],
                     rhs=identf[0:R, 0:R], start=True, stop=True), reads=[stg, identf], writes=[pb],
                     inc=(c == nchunk - 1))
            P.op("dve", lambda v: v.tensor_copy(out=dst[:].rearrange("p c r -> p (c r)"),
                 in_=pb[:, 0:nchunk * R]), reads=[pb], writes=[dst])

        def proj_fm(W, col0, rhsT, N, pb, K=NCH):
            rb = cm["hT"] if rhsT is hT else rhsT
            for c in range(K):
                P.op("pe", lambda t, c=c: t.matmul(pb[:, 0:N], lhsT=W[:, c, col0:col0 + 128], rhs=rhsT[:, c, 0:N],
                     start=(c == 0), stop=(c == K - 1)), reads=[W, rb], writes=[pb], inc=(c == K - 1))

        def proj_tm(W, col0, lhs, t0, pb, K=NCH, M=128):
            lb = cm["hT"] if lhs is hT else lhs
            for c in range(K):
                P.op("pe", lambda t, c=c: t.matmul(pb[0:M, :], lhsT=lhs[:, c, t0:t0 + M], rhs=W[:, c, col0:col0 + 512],
                     start=(c == 0), stop=(c == K - 1)), reads=[W, lb], writes=[pb], inc=(c == K - 1))

        memst = contextlib.ExitStack()
        mkT = sb(memst, "mkT", [128, 8, 256], BF16)
        mvb = sb(memst, "mvb", [128, 2, D], BF16)
        with contextlib.suppress(_SkipPhase), contextlib.ExitStack() as ph:
            _phase_gate(0)
            Wm = sb(ph, "Wm", [128, 8, 2048], BF16)
            with contextlib.ExitStack() as ws:
                wst[0] = sb(ws, "wst0", [128, 2048], F32); wst[1] = sb(ws, "wst1", [128, 2048], F32)
                load_w(Wm, w_mem_kv, 8, 2048)
                P.barrier()
            mtok = sb(ph, "mtok", [128, 2048], F32)
            common(ph, 2, 256, ["mem"])
            mt = dict(r0=0, N=256, segs=[], sample=False)
            load_x(mt, src_fn=lambda tl, j: memp[j * 128:(j + 1) * 128, :])
            norm_T(mt, gbc["mem"])
            for j in range(2):
                for hf in range(4):
                    pb = PS[hf % 4]
                    proj_tm(Wm, hf * 512, hT, j * 128, pb)
                    P.op("act" if hf % 2 else "dve", (lambda a, hf=hf, pb=pb: a.copy(out=mtok[:, hf * 512:(hf + 1) * 512], in_=pb[:])) if hf % 2
                         else (lambda v, hf=hf, pb=pb: v.tensor_copy(out=mtok[:, hf * 512:(hf + 1) * 512], in_=pb[:])),
                         reads=[pb], writes=[mtok])
                P.op("pool", lambda g, j=j: g.tensor_copy(out=mvb[:, j, :], in_=mtok[:, 1024:2048]),
                     reads=[mtok], writes=[mvb])
                P.dma("pool", mkp[:, j * 128:(j + 1) * 128, :].rearrange("h m d -> m h d"),
                      mtok[:, 0:1024].rearrange("m (h d) -> m h d", h=4), mtok, reads=[mtok])
                P.dma("pool", mvp[:, j * 128:(j + 1) * 128, :].rearrange("h m d -> m h d"),
                      mtok[:, 1024:2048].rearrange("m (h d) -> m h d", h=4), mtok, reads=[mtok])
            for cc in range(8):
                pb = PS[cc % 4]
                proj_fm(Wm, cc * 128, hT, 256, pb)
                P.op("act", lambda a, cc=cc, pb=pb: a.copy(out=mkT[:, cc, :], in_=pb[:, 0:256]), reads=[pb], writes=[mkT])

            P.barrier()
        with contextlib.suppress(_SkipPhase), contextlib.ExitStack() as ph:
            _phase_gate(1)
            Wq = sb(ph, "Wqm", [128, 8, D], BF16)
            Wmo = sb(ph, "Wmo", [128, 8, D], BF16)
            with contextlib.ExitStack() as ws:
                wst[0] = sb(ws, "wst0", [128, 2048], F32); wst[1] = sb(ws, "wst1", [128, 2048], F32)
                load_w(Wq, w_in[:, 5120:6144], 8, D)
                load_w(Wmo, w_mem_o, 8, D)
                P.barrier()
            common(ph, 4, 512, ["pre"])
            qmT = sb(ph, "qmT", [128, 8, 512], BF16)
            pT = [sb(ph, f"pT{k}", [128, 2, 512], BF16) for k in range(2)]
            omT = sb(ph, "omT", [128, 8, 512], BF16)
            rden = [sb(ph, f"rden{k}", [128, 512], F32) for k in range(2)]
            yst = [sb(ph, f"ystm{k}", [128, 512], F32) for k in range(2)]
            smkT = sb(ph, "smkT", [128, 8, 256], BF16)
            smvb = sb(ph, "smvb", [128, 2, D], BF16)
            cmst = [sb(ph, f"cmst{k}", [128, 2, 256], F32) for k in range(2)]

            def mem_attn(kT, vB, c0, L):
                for hm in range(4):
                    pk = pT[hm % 2]
                    for mc in range(2):
                        pb = PS[mc]
                        for dc in range(2):
                            P.op("pe", lambda t, mc=mc, dc=dc, pb=pb: t.matmul(pb[:, 0:L],
                                 lhsT=kT[:, hm * 2 + dc, mc * 128:(mc + 1) * 128], rhs=qmT[:, hm * 2 + dc, c0:c0 + L],
                                 start=(dc == 0), stop=(dc == 1)), reads=[kT, qmT], writes=[pb], inc=(dc == 1))
                        P.op("act", lambda a, mc=mc, pb=pb, pk=pk: a.activation(out=pk[:, mc, 0:L], in_=pb[:, 0:L], func=AF.Exp),
                             reads=[pb], writes=[pk])
                    pd = PS[2]
                    for mc in range(2):
                        P.op("pe", lambda t, mc=mc, pk=pk: t.matmul(pd[:, 0:L], lhsT=onesb[:], rhs=pk[:, mc, 0:L],
                             start=(mc == 0), stop=(mc == 1)), reads=[onesb, pk], writes=[pd], inc=(mc == 1))
                    rd = rden[hm % 2]
                    P.op("dve", lambda v, rd=rd: v.reciprocal(out=rd[:, 0:L], in_=pd[:, 0:L]), reads=[pd], writes=[rd])
                    for dc in range(2):
                        po = PS[4 + dc]
                        for mc in range(2):
                            P.op("pe", lambda t, mc=mc, dc=dc, po=po, pk=pk: t.matmul(po[:, 0:L],
                                 lhsT=vB[:, mc, hm * 256 + dc * 128:hm * 256 + dc * 128 + 128], rhs=pk[:, mc, 0:L],
                                 start=(mc == 0), stop=(mc == 1)), reads=[vB, pk], writes=[po], inc=(mc == 1))
                        P.op("dve", lambda v, dc=dc, po=po, rd=rd: v.tensor_tensor(out=omT[:, hm * 2 + dc, c0:c0 + L],
                             in0=po[:, 0:L], in1=rd[:, 0:L], op=ALU.mult), reads=[po, rd], writes=[omT])

            for tl in tiles:
                N = tl["N"]; ti = tl["idx"]; smp = tl["sample"]
                load_x(tl)
                norm_T(tl, gbc["pre"])
                for c2 in range(8):
                    pb = PS[c2 % 4]
                    proj_fm(Wq, c2 * 128, hT, N, pb)
                    P.op("act", lambda a, c2=c2, pb=pb: a.activation(out=qmT[:, c2, 0:N], in_=pb[:, 0:N], func=AF.Copy,
                         scale=1.0 / 16.0), reads=[pb], writes=[qmT])
                if not smp:
                    mem_attn(mkT, mvb, 0, N)
                else:
                    for (c0, L, s) in tl["segs"]:
                        for hm in range(4):
                            st = cmst[hm % 2]
                            P.dma("sp", st[:, :, :], cmk[s, hm, :, :].rearrange("(j m) d -> m j d", j=2), st, writes=[st])
                            pb = PS[6 + hm % 2]
                            for dc in range(2):
                                for j in range(2):
                                    P.op("pe", lambda t, dc=dc, j=j, pb=pb, st=st: t.matmul(
                                         pb[:, dc * 256 + j * 128:dc * 256 + j * 128 + 128],
                                         lhsT=st[:, j, dc * 128:(dc + 1) * 128], rhs=identf[:], start=True, stop=True),
                                         reads=[st, identf], writes=[pb], inc=(dc == 1 and j == 1))
                            P.op("dve", lambda v, hm=hm, pb=pb: v.tensor_copy(out=smkT[:, hm * 2:hm * 2 + 2, :],
                                 in_=pb[:].rearrange("p (c m) -> p c m", c=2)), reads=[pb], writes=[smkT])
                            st2 = cmst[(hm + 1) % 2]
                            P.dma("sp", st2[:, :, :], cmv[s, hm, :, :].rearrange("(j m) d -> m j d", j=2), st2, writes=[st2])
                            P.op("pool", lambda g, hm=hm, st2=st2: g.tensor_copy(out=smvb[:, :, hm * 256:(hm + 1) * 256],
                                 in_=st2[:, :, :]), reads=[st2], writes=[smvb])
                        mem_attn(smkT, smvb, c0, L)
                for c2 in range(8):
                    pb = PS[c2 % 4]
                    proj_fm(Wmo, c2 * 128, omT, N, pb)
                    y = yst[c2 % 2]
                    P.op("act", lambda a, pb=pb, y=y: a.copy(out=y[:, 0:N], in_=pb[:, 0:N]), reads=[pb], writes=[y])
                    P.dma("pool", YMs[:, c2, tl["r0"]:tl["r0"] + N], y[:, 0:N], y, reads=[y], writes=[trk["YM"][ti]])
            P.barrier()
        P.barrier()
        memst.close()

        with contextlib.suppress(_SkipPhase), contextlib.ExitStack() as ph:
            _phase_gate(2)
            Wc = sb(ph, "Wc", [128, 8, 2048], BF16)
            Wco = sb(ph, "Wco", [128, 8, D], BF16)
            with contextlib.ExitStack() as ws:
                wst[0] = sb(ws, "wst0", [128, 2048], F32); wst[1] = sb(ws, "wst1", [128, 2048], F32)
                load_w(Wc, w_in[:, 3072:5120], 8, 2048)
                load_w(Wco, w_conv_o, 8, D)
                P.barrier()
            common(ph, 4, 512, ["pre"])
            cw = sb(ph, "cw", [128, 8, 31], F32)
            cvec = sb(ph, "cvec", [128, 8, 3], F32)
            with contextlib.ExitStack() as ws:
                stg = sb(ws, "stg", [31, D], F32)
                load_cols(cw, [(conv_dw_w[:, :], 31)], 31, 8, stg)
                load_cols(cvec, [(conv_dw_b[0:1, :], 1), (conv_ln_g[0:1, :], 1), (conv_ln_b[0:1, :], 1)], 3, 8, stg)
                P.barrier()
            uP = sb(ph, "uP", [128, 8, 542], F32)
            uS = sb(ph, "uS", [128, 8, NS, 94], F32)
            cc_ = sb(ph, "cc", [128, 8, 512], F32)
            csq = [sb(ph, f"csq{k}", [128, 512], F32) for k in range(2)]
            ccT = sb(ph, "ccT", [128, 8, 512], BF16)
            sg = [sb(ph, f"sg{k}", [128, 512], F32) for k in range(2)]
            mean = sb(ph, "mean", [128, 512], F32)
            rs = sb(ph, "rs", [128, 512], F32)
            t1 = [sb(ph, f"t1{k}", [128, 512], F32) for k in range(2)]
            yst = [sb(ph, f"yst{k}", [128, 512], F32) for k in range(2)]
            cst = sb(ph, "cst", [30, D], F32)
            sst = sb(ph, "sst", [30, D], F32)
            P.op("dve", lambda v: v.memset(uP[:, :, 0:30], 0.0), writes=[uP])
            for tl in tiles:
                N = tl["N"]; ti = tl["idx"]; smp = tl["sample"]
                load_x(tl)
                norm_T(tl, gbc["pre"])
                if smp:
                    for s in range(NS):
                        P.dma("sp", sst[:, :], sconv[s, :, :], sst, writes=[sst])
                        pb = PS[4 + s % 2]
                        for c in range(8):
                            P.op("pe", lambda t, c=c, pb=pb: t.matmul(pb[:, c * 30:(c + 1) * 30],
                                 lhsT=sst[0:30, c * 128:(c + 1) * 128], rhs=identf[0:30, 0:30], start=True, stop=True),
                                 reads=[sst, identf], writes=[pb], inc=(c == 7))
                        P.op("dve", lambda v, s=s, pb=pb: v.tensor_copy(out=uS[:, :, s, 0:30],
                             in_=pb[:, 0:240].rearrange("p (c r) -> p c r", c=8)), reads=[pb], writes=[uS])
                for c2 in range(8):
                    pa, pbb = PS[(2 * c2) % 4], PS[(2 * c2 + 1) % 4]
                    proj_fm(Wc, c2 * 128, hT, N, pa)
                    proj_fm(Wc, 1024 + c2 * 128, hT, N, pbb)
                    sgk = sg[c2 % 2]
                    P.op("act", lambda a, pbb=pbb, sgk=sgk: a.activation(out=sgk[:, 0:N], in_=pbb[:, 0:N], func=AF.Sigmoid),
                         reads=[pbb], writes=[sgk])
                    if smp:
                        P.op("dve", lambda v, c2=c2, pa=pa, sgk=sgk: v.tensor_tensor(out=uS[:, c2, :, 30:94],
                             in0=pa[:, 0:N].rearrange("p (s t) -> p s t", s=NS),
                             in1=sgk[:, 0:N].rearrange("p (s t) -> p s t", s=NS), op=ALU.mult),
                             reads=[pa, sgk], writes=[uS])
                    else:
                        P.op("dve", lambda v, c2=c2, pa=pa, sgk=sgk: v.tensor_tensor(out=uP[:, c2, 30:542],
                             in0=pa[:, 0:N], in1=sgk[:, 0:N], op=ALU.mult), reads=[pa, sgk], writes=[uP])
                for c2 in range(8):
                    eng = "dve"
                    for (c0, L, s) in tl["segs"]:
                        usrc = (lambda k, c2=c2, s=s, L=L: uS[:, c2, s, k:k + L]) if smp else (lambda k, c2=c2, L=L: uP[:, c2, k:k + L])
                        ub = uS if smp else uP
                        P.op(eng, lambda v, c2=c2, c0=c0, L=L, usrc=usrc: v.tensor_scalar(out=cc_[:, c2, c0:c0 + L], in0=usrc(0),
                             scalar1=cw[:, c2, 0:1], scalar2=cvec[:, c2, 0:1], op0=ALU.mult, op1=ALU.add),
                             reads=[ub, cw, cvec], writes=[cc_])
                        for k in range(1, 31):
                            P.op(eng, lambda v, c2=c2, c0=c0, L=L, k=k, usrc=usrc: v.scalar_tensor_tensor(
                                 out=cc_[:, c2, c0:c0 + L], in0=usrc(k), scalar=cw[:, c2, k:k + 1],
                                 in1=cc_[:, c2, c0:c0 + L], op0=ALU.mult, op1=ALU.add),
                                 reads=[ub, cw, cc_], writes=[cc_])
                pm, pq = PS[4], PS[5]
                for c2 in range(8):
                    q = csq[c2 % 2]
                    P.op("act", lambda a, c2=c2, q=q: a.activation(out=q[:, 0:N], in_=cc_[:, c2, 0:N], func=AF.Square),
                         reads=[cc_], writes=[q])
                    P.op("pe", lambda t, c2=c2: t.matmul(pm[:, 0:N], lhsT=onesf[:], rhs=cc_[:, c2, 0:N],
                         start=(c2 == 0), stop=(c2 == 7)), reads=[onesf, cc_], writes=[pm])
                    P.op("pe", lambda t, c2=c2, q=q: t.matmul(pq[:, 0:N], lhsT=onesf[:], rhs=q[:, 0:N],
                         start=(c2 == 0), stop=(c2 == 7)), reads=[onesf, q], writes=[pq])
                P.op("act", lambda a: a.copy(out=mean[:, 0:N], in_=pm[:, 0:N]), reads=[pm], writes=[mean])
                P.op("dve", lambda v: v.tensor_tensor(out=rs[:, 0:N], in0=mean[:, 0:N], in1=mean[:, 0:N], op=ALU.mult),
                     reads=[mean], writes=[rs])
                P.op("dve", lambda v: v.tensor_tensor(out=rs[:, 0:N], in0=pq[:, 0:N], in1=rs[:, 0:N], op=ALU.subtract),
                     reads=[pq, rs], writes=[rs])
                rstd_from(rs[:, 0:N], rs[:, 0:N], rs, rs, 1.0)
                for c2 in range(8):
                    tk = t1[c2 % 2]
                    P.op("dve", lambda v, c2=c2, tk=tk: v.tensor_tensor(out=tk[:, 0:N], in0=cc_[:, c2, 0:N], in1=mean[:, 0:N],
                         op=ALU.subtract), reads=[cc_, mean], writes=[tk])
                    P.op("dve", lambda v, tk=tk: v.tensor_tensor(out=tk[:, 0:N], in0=tk[:, 0:N], in1=rs[:, 0:N], op=ALU.mult),
                         reads=[tk, rs], writes=[tk])
                    P.op("act", lambda a, c2=c2, tk=tk: a.activation(out=ccT[:, c2, 0:N], in_=tk[:, 0:N], func=AF.Silu,
                         scale=cvec[:, c2, 1:2], bias=cvec[:, c2, 2:3]), reads=[tk, cvec], writes=[ccT])
                for c2 in range(8):
                    pb = PS[c2 % 4]
                    proj_fm(Wco, c2 * 128, ccT, N, pb)
                    y = yst[c2 % 2]
                    P.op("act", lambda a, pb=pb, y=y: a.copy(out=y[:, 0:N], in_=pb[:, 0:N]), reads=[pb], writes=[y])
                    P.dma("pool", YCs[:, c2, tl["r0"]:tl["r0"] + N], y[:, 0:N], y, reads=[y], writes=[trk["YC"][ti]])
                ends = [(s, 64) for (_, _, s) in tl["segs"]] if smp else ([(None, 512)] if ti == NT - 1 else [])
                for (s, L) in ends:
                    for half in range(2):
                        pb = PS[4 + half]
                        for cq in range(4):
                            c2 = half * 4 + cq
                            src = uS[:, c2, s, L:L + 30] if smp else uP[:, c2, L:L + 30]
                            P.op("pe", lambda t, cq=cq, pb=pb, src=src: t.matmul(pb[0:30, cq * 128:(cq + 1) * 128], lhsT=src,
                                 rhs=identf[:], start=True, stop=True), reads=[uS if smp else uP, identf], writes=[pb],
                                 inc=(cq == 3))
                        P.op("dve", lambda v, half=half, pb=pb: v.tensor_copy(out=cst[:, half * 512:(half + 1) * 512],
                             in_=pb[0:30, :]), reads=[pb], writes=[cst])
                    P.dma("pool", cs[s, :, :] if smp else cp[:, :], cst[:, :], cst, reads=[cst])
                if not smp:
                    P.op("dve", lambda v: v.tensor_copy(out=uP[:, :, 0:30], in_=uP[:, :, 512:542]), reads=[uP], writes=[uP])

            P.barrier()
        with contextlib.suppress(_SkipPhase), contextlib.ExitStack() as ph:
            _phase_gate(3)
            Wqkv = sb(ph, "Wqkv", [128, 8, 3072], BF16)
            with contextlib.ExitStack() as ws:
                wst[0] = sb(ws, "wst0", [128, 2048], F32); wst[1] = sb(ws, "wst1", [128, 2048], F32)
                load_w(Wqkv, w_in[:, 0:3072], 8, 3072)
                P.barrier()
            common(ph, 4, 512, ["pre"])
            masks = sb(ph, "masks", [128, 4, 512], BF16)
            P.op("pool", lambda g: g.memset(masks[:], 1.0), writes=[masks])
            for r in range(4):
                P.op("pool", lambda g, r=r: g.affine_select(out=masks[:, r, :], in_=masks[:, r, :], pattern=[[1, 512]],
                     compare_op=ALU.is_gt, fill=0.0, base=-128 * r, channel_multiplier=-1), reads=[masks], writes=[masks])
            qT = sb(ph, "qT", [128, 8, 512], BF16)
            kT = sb(ph, "kT", [128, 8, 512], BF16)
            tok = [sb(ph, f"tok{k}", [128, D], F32) for k in range(2)]
            vbf = [sb(ph, f"vbf{k}", [128, D], BF16) for k in range(2)]
            osb = sb(ph, "osb", [128, 8, 512], BF16)
            Kt = [sb(ph, f"Kt{k}", [128, 512], BF16) for k in range(2)]
            Vt = [sb(ph, f"Vt{k}", [128, 4, 128], BF16) for k in range(2)]
            Ee = [sb(ph, f"Ee{k}", [128, 512], F32) for k in range(2)]
            Sp = [sb(ph, f"Sp{k}", [128, 512], BF16) for k in range(2)]
            Ls = [[sb(ph, f"Ls{h}{k}", [128, 512], BF16) for k in range(2)] for h in range(2)]
            Aa = [sb(ph, f"Aa{k}", [128, 512], BF16) for k in range(2)]
            kc = [sb(ph, f"kc{k}", [128, 4, 128], F32) for k in range(2)]
            vc = [sb(ph, f"vc{k}", [128, 2, 4, 64], F32) for k in range(2)]
            PZ = [PS[0], PS[1]]; PC = [PS[2], PS[3]]; PO = PS[4]

            def sb_block(hh, p, Ktb, kcols, nk, Vb, vr, qcols, N, first, last, mask_ap, lsi):
                hp = slice(hh * 64, hh * 64 + 64)
                z, cb_, e, s_, a_ = PZ[hh], PC[hh], Ee[hh], Sp[hh], Aa[hh]
                lo, ln = Ls[hh][lsi % 2], Ls[hh][(lsi + 1) % 2]
                P.op("pe", lambda t: t.matmul(z[0:nk, 0:N], lhsT=Ktb[hp, kcols], rhs=qT[hp, p, qcols], start=True, stop=True),
                     reads=[Ktb, qT], writes=[z])
                P.op("act", lambda a: a.activation(out=e[0:nk, 0:N], in_=z[0:nk, 0:N], func=AF.Exp), reads=[z], writes=[e])
                P.op("act", lambda a: a.activation(out=s_[0:nk, 0:N], in_=e[0:nk, 0:N], func=AF.Ln, bias=1.0, scale=1.0),
                     reads=[e], writes=[s_])
                if mask_ap is not None:
                    P.op("dve", lambda v: v.tensor_tensor(out=s_[0:nk, 0:N], in0=s_[0:nk, 0:N], in1=mask_ap, op=ALU.mult),
                         reads=[s_, masks], writes=[s_])
                P.op("pe", lambda t: t.matmul(cb_[0:nk, 0:N], lhsT=negtri[0:nk, 0:nk], rhs=s_[0:nk, 0:N], start=True, stop=False),
                     reads=[negtri, s_], writes=[cb_], inc=False)
                if not first:
                    P.op("pe", lambda t: t.matmul(cb_[0:nk, 0:N], lhsT=negones[:, 0:nk], rhs=lo[:, 0:N], start=False, stop=False),
                         reads=[negones, lo], writes=[cb_], inc=False)
                P.op("pe", lambda t: t.matmul(cb_[0:nk, 0:N], lhsT=Ktb[hp, kcols], rhs=qT[hp, p, qcols], start=False, stop=True),
                     reads=[Ktb, qT], writes=[cb_])
                if not last:
                    if first:
                        if nk < 128:
                            P.op("pool", lambda g: g.memset(ln[:, 0:N], 0.0), writes=[ln])
                        P.op("pool", lambda g: g.tensor_copy(out=ln[0:nk, 0:N], in_=s_[0:nk, 0:N]), reads=[s_], writes=[ln])
                    else:
                        P.op("dve", lambda v: v.tensor_tensor(out=ln[:, 0:N], in0=lo[:, 0:N], in1=s_[:, 0:N], op=ALU.add),
                             reads=[lo, s_], writes=[ln])
                P.op("act", lambda a: a.activation(out=a_[0:nk, 0:N], in_=cb_[0:nk, 0:N], func=AF.Exp), reads=[cb_], writes=[a_])
                if mask_ap is not None:
                    P.op("dve", lambda v: v.tensor_tensor(out=a_[0:nk, 0:N], in0=a_[0:nk, 0:N], in1=mask_ap, op=ALU.mult),
                         reads=[a_, masks], writes=[a_])
                P.op("pe", lambda t: t.matmul(PO[hp, 0:N], lhsT=Vb[0:nk, vr, hp], rhs=a_[0:nk, 0:N], start=first, stop=last),
                     reads=[Vb, a_], writes=[PO], inc=last)

            ldc = [0]

            def load_kv(p, tj, ti):
                k = ldc[0] % 2
                ldc[0] += 1
                r0 = tiles[tj]["r0"]
                P.dma("sp", Kt[k][:, :], KTs[:, p, r0:r0 + 512], Kt[k], reads=[trk["KT"][tj]], writes=[Kt[k]])
                P.dma("sp", Vt[k][:, :, :], VSs[r0:r0 + 512, p * 128:(p + 1) * 128].rearrange("(r s) c -> s r c", r=4),
                      Vt[k], reads=[trk["VS"][tj]], writes=[Vt[k]])
                return k

            for tl in tiles:
                N = tl["N"]; ti = tl["idx"]; smp = tl["sample"]; r0 = tl["r0"]
                nsub = N // 128
                load_x(tl)
                norm_T(tl, gbc["pre"])
                for p in range(8):
                    pq_, pk_ = PS[5], PS[6]
                    proj_fm(Wqkv, p * 128, hT, N, pq_)
                    P.op("act", lambda a, p=p: a.activation(out=qT[:, p, 0:N], in_=pq_[:, 0:N], func=AF.Copy, scale=0.125),
                         reads=[pq_], writes=[qT])
                    proj_fm(Wqkv, 1024 + p * 128, hT, N, pk_)
                    P.op("dve", lambda v, p=p: v.tensor_copy(out=kT[:, p, 0:N], in_=pk_[:, 0:N]), reads=[pk_], writes=[kT])
                P.dma("pool", KTs[:, :, r0:r0 + N], kT[:, :, 0:N], kT, reads=[kT], writes=[trk["KT"][ti]])
                for j in range(nsub):
                    for which in range(2):
                        tk = tok[which]
                        for hf in range(2):
                            pb = PS[5 + hf]
                            proj_tm(Wqkv, 1024 * (1 + which) + hf * 512, hT, j * 128, pb)
                            if hf == 0:
                                P.op("act", lambda a, tk=tk, pb=pb: a.copy(out=tk[:, 0:512], in_=pb[:]), reads=[pb], writes=[tk])
                            else:
                                P.op("dve", lambda v, tk=tk, pb=pb: v.tensor_copy(out=tk[:, 512:1024], in_=pb[:]), reads=[pb], writes=[tk])
                        if not smp:
                            dst = (kp if which == 0 else vp)[:, r0 + j * 128:r0 + (j + 1) * 128, :].rearrange("h t d -> t h d")
                            P.dma("pool", dst, tk[:, :].rearrange("t (h d) -> t h d", h=16), tk, reads=[tk])
                        else:
                            for s2 in range(2):
                                s = j * 2 + s2
                                dst = (ks if which == 0 else vs)[s, :, :, :].rearrange("h t d -> t h d")
                                P.dma("pool", dst, tk[s2 * 64:(s2 + 1) * 64, :].rearrange("t (h d) -> t h d", h=16), tk, reads=[tk])
                        if which == 1:
                            vb = vbf[j % 2]
                            P.op("pool", lambda g, vb=vb, tk=tk: g.tensor_copy(out=vb[:], in_=tk[:]), reads=[tk], writes=[vb])
                            P.dma("pool", VSs[r0 + j * 128:r0 + (j + 1) * 128, :], vb[:, :], vb, reads=[vb], writes=[trk["VS"][ti]])
                import os as _os
                _ka = _os.environ.get("KA_SKIP", "")
                if not smp:
                    for p in range(8 if "p" not in _ka else 0):
                        nkt = ti + 1
                        cur = load_kv(p, ti, ti)
                        step = 0
                        nsteps = 4 * nkt
                        for jj in range(nkt):
                            tj = ti - jj
                            nxt = load_kv(p, tj - 1, ti) if jj + 1 < nkt else None
                            for r in (3, 2, 1, 0):
                                for hh in range(2):
                                    sb_block(hh, p, Kt[cur], slice(r * 128, (r + 1) * 128), 128, Vt[cur], r, slice(0, N), N,
                                             step == 0, step == nsteps - 1, masks[:, r, :] if jj == 0 else None, step)
                                step += 1
                            cur = nxt
                        P.op("act", lambda a, p=p: a.copy(out=osb[:, p, 0:N], in_=PO[:, 0:N]), reads=[PO], writes=[osb])
                else:
                    for (c0, L, s) in tl["segs"]:
                        for p in range(8 if "s" not in _ka else 0):
                            k = ldc[0] % 2
                            ldc[0] += 1
                            P.op("pool", lambda g, k=k: g.memset(Kt[k][:, 64:128], 0.0), writes=[Kt[k]])
                            P.op("pool", lambda g, k=k: g.memset(Vt[k][64:128, 0, :], 0.0), writes=[Vt[k]])
                            P.dma("sp", Kt[k][:, 0:64], KTs[:, p, r0 + c0:r0 + c0 + 64], Kt[k], reads=[trk["KT"][ti]], writes=[Kt[k]])
                            P.dma("sp", Vt[k][0:64, 0, :], VSs[r0 + c0:r0 + c0 + 64, p * 128:(p + 1) * 128], Vt[k],
                                  reads=[trk["VS"][ti]], writes=[Vt[k]])
                            nsteps = 1 + PB
                            for hh in range(2):
                                sb_block(hh, p, Kt[k], slice(0, 128), 128, Vt[k], 0, slice(c0, c0 + L), L,
                                         True, False, masks[:, 0, 0:64], 0)
                            step = 1
                            for g4 in range(PB // 4 - 1, -1, -1):
                                kk = ldc[0] % 2
                                ldc[0] += 1
                                kcb, vcb = kc[kk], vc[kk]
                                for h2 in range(2):
                                    P.dma("sp", kcb[:, :, h2 * 64:(h2 + 1) * 64], ck[s, 2 * p + h2, g4 * 512:(g4 + 1) * 512, :].rearrange(
                                          "(b k) d -> k b d", b=4), kcb, writes=[kcb])
                                    P.dma("sp", vcb[:, h2, :, :], cv[s, 2 * p + h2, g4 * 512:(g4 + 1) * 512, :].rearrange(
                                          "(b k) d -> k b d", b=4), vcb, writes=[vcb])
                                pt = PS[7]
                                for b in range(4):
                                    P.op("pe", lambda t, b=b: t.matmul(pt[:, b * 128:(b + 1) * 128],
                                         lhsT=kcb[:, b, :], rhs=identf[:], start=True, stop=True),
                                         reads=[kcb, identf], writes=[pt], inc=(b == 3))
                                P.op("dve", lambda v, kk=kk: v.tensor_copy(out=Kt[kk][:, :], in_=pt[:, :]), reads=[pt], writes=[Kt[kk]])
                                for h2 in range(2):
                                    P.op("pool", lambda g, kk=kk, h2=h2: g.tensor_copy(out=Vt[kk][:, :, h2 * 64:(h2 + 1) * 64],
                                         in_=vcb[:, h2, :, :]), reads=[vcb], writes=[Vt[kk]])
                                for b in (3, 2, 1, 0):
                                    for hh in range(2):
                                        sb_block(hh, p, Kt[kk], slice(b * 128, (b + 1) * 128), 128, Vt[kk], b, slice(c0, c0 + L), L,
                                                 False, step == nsteps - 1, None, step)
                                    step += 1
                            P.op("act", lambda a, p=p, c0=c0, L=L: a.copy(out=osb[:, p, c0:c0 + L], in_=PO[:, 0:L]), reads=[PO], writes=[osb])
                P.dma("pool", OSs[:, :, r0:r0 + N], osb[:, :, 0:N], osb, reads=[osb], writes=[trk["OS"][ti]])

            P.barrier()
        with contextlib.suppress(_SkipPhase), contextlib.ExitStack() as ph:
            _phase_gate(4)
            Wg = sb(ph, "Wg", [128, 8, 3072], BF16)
            Wso = sb(ph, "Wso", [128, 8, D], BF16)
            Wo = sb(ph, "Wo", [128, 8, D], BF16)
            with contextlib.ExitStack() as ws:
                wst[0] = sb(ws, "wst0", [128, 2048], F32); wst[1] = sb(ws, "wst1", [128, 2048], F32)
                load_w(Wg, w_in[:, 6144:9216], 8, 3072)
                load_w(Wso, w_sb_o, 8, D)
                load_w(Wo, w_out, 8, D)
                P.barrier()
            common(ph, 4, 512, ["pre", "post"])
            os_ = sb(ph, "os_", [128, 8, 512], BF16)
            ycb = [sb(ph, f"ycb{k}", [128, 512], F32) for k in range(2)]
            ymb = [sb(ph, f"ymb{k}", [128, 512], F32) for k in range(2)]
            sgg = [sb(ph, f"sgg{k}", [128, 512], F32) for k in range(3)]
            mrg = [sb(ph, f"mrg{k}", [128, 512], F32) for k in range(2)]
            mg = sb(ph, "mg", [128, 8, 512], BF16)
            mo = [sb(ph, f"mo{k}", [128, D], F32) for k in range(2)]
            ss2 = sb(ph, "ss2", [128, 4], F32)
            rs2 = sb(ph, "rs2", [128, 4], F32)
            for tl in tiles:
                N = tl["N"]; ti = tl["idx"]; r0 = tl["r0"]
                nsub = N // 128
                load_x(tl)
                norm_T(tl, gbc["pre"])
                P.dma("sp", os_[:, :, 0:N], OSs[:, :, r0:r0 + N], os_, reads=[trk["OS"][ti]], writes=[os_])
                for c2 in range(8):
                    yc, ym = ycb[c2 % 2], ymb[c2 % 2]
                    P.dma("sp", yc[:, 0:N], YCs[:, c2, r0:r0 + N], yc, reads=[trk["YC"][ti]], writes=[yc])
                    P.dma("sp", ym[:, 0:N], YMs[:, c2, r0:r0 + N], ym, reads=[trk["YM"][ti]], writes=[ym])
                    pys, pg = PS[0 + (c2 % 2) * 4], [PS[1 + (c2 % 2) * 4], PS[2 + (c2 % 2) * 4], PS[3 + (c2 % 2) * 4]]
                    proj_fm(Wso, c2 * 128, os_, N, pys)
                    for gi in range(3):
                        proj_fm(Wg, gi * 1024 + c2 * 128, hT, N, pg[gi])
                        P.op("act", lambda a, gi=gi, pg=pg: a.activation(out=sgg[gi][:, 0:N], in_=pg[gi][:, 0:N], func=AF.Sigmoid),
                             reads=[pg[gi]], writes=[sgg[gi]])
                    m = mrg[c2 % 2]
                    P.op("dve", lambda v, m=m, pys=pys: v.tensor_tensor(out=m[:, 0:N], in0=pys[:, 0:N], in1=sgg[0][:, 0:N], op=ALU.mult),
                         reads=[pys, sgg[0]], writes=[m])
                    P.op("pool", lambda g, yc=yc: g.tensor_tensor(out=sgg[1][:, 0:N], in0=sgg[1][:, 0:N], in1=yc[:, 0:N], op=ALU.mult),
                         reads=[sgg[1], yc], writes=[sgg[1]])
                    P.op("pool", lambda g, ym=ym: g.tensor_tensor(out=sgg[2][:, 0:N], in0=sgg[2][:, 0:N], in1=ym[:, 0:N], op=ALU.mult),
                         reads=[sgg[2], ym], writes=[sgg[2]])
                    P.op("dve", lambda v, m=m: v.tensor_tensor(out=m[:, 0:N], in0=m[:, 0:N], in1=sgg[1][:, 0:N], op=ALU.add),
                         reads=[m, sgg[1]], writes=[m])
                    P.op("dve", lambda v, m=m, c2=c2: v.tensor_tensor(out=mg[:, c2, 0:N], in0=m[:, 0:N], in1=sgg[2][:, 0:N], op=ALU.add),
                         reads=[m, sgg[2]], writes=[mg])
                for j in range(nsub):
                    mj = mo[j % 2]
                    for hf in range(2):
                        pb = PS[hf]
                        proj_tm(Wo, hf * 512, mg, j * 128, pb)
                        if hf == 0:
                            P.op("act", lambda a, mj=mj, pb=pb: a.copy(out=mj[:, 0:512], in_=pb[:]), reads=[pb], writes=[mj])
                        else:
                            P.op("dve", lambda v, mj=mj, pb=pb: v.tensor_copy(out=mj[:, 512:1024], in_=pb[:]), reads=[pb], writes=[mj])
                    P.op("dve", lambda v, j=j: v.memset(ss2[:, j:j + 1], 0.0), writes=[ss2])
                    P.op("act", lambda a, mj=mj, j=j: a.activation(out=cm["junk"][:], in_=mj[:], func=AF.Square, accum_out=ss2[:, j:j + 1]),
                         reads=[mj], writes=[cm["junk"], ss2])
                    rstd_from(ss2[:, j:j + 1], rs2[:, j:j + 1], ss2, rs2, D)
                    xo = mj
                    P.op("dve", lambda v, mj=mj, j=j: v.scalar_tensor_tensor(out=mj[:], in0=mj[:], scalar=rs2[:, j:j + 1],
                         in1=gbc["post"][:], op0=ALU.mult, op1=ALU.mult), reads=[mj, rs2, gbc["post"]], writes=[mj])
                    P.op("pool", lambda g, mj=mj, j=j, xo=xo: g.tensor_tensor(out=xo[:], in0=mj[:], in1=xt[j][:], op=ALU.add),
                         reads=[mj, xt[j]], writes=[xo])
                    P.dma("pool", XMs[r0 + j * 128:r0 + (j + 1) * 128, :], xo[:, :], xo, reads=[xo], writes=[trk["XM"][ti]])

            P.barrier()
        with contextlib.suppress(_SkipPhase), contextlib.ExitStack() as ph:
            _phase_gate(5)
            Wup = sb(ph, "Wup", [128, 8, FF2], BF16)
            Wdn = sb(ph, "Wdn", [128, NFC, D], BF16)
            with contextlib.ExitStack() as ws:
                wst[0] = sb(ws, "wst0", [128, 2048], F32); wst[1] = sb(ws, "wst1", [128, 2048], F32)
                load_w(Wup, w_ffn_up, 8, FF2)
                load_w(Wdn, w_ffn_down, NFC, D)
                P.barrier()
            common(ph, 2, 256, ["fpre", "fpost"])
            fw = sb(ph, "fw", [128, 44, 3], F32)
            halP = sb(ph, "halP", [128, 44, 2], F32)
            halS = sb(ph, "halS", [128, 44, NS, 2], F32)
            sfs = sb(ph, "sfs", [3, 512], F32)
            for g11 in range(11):
                P.dma("sp", sfs[0:3, :], ffn_dw_w[:, g11 * 512:(g11 + 1) * 512], sfs, writes=[sfs])
                pb = PS[6]
                for c in range(4):
                    P.op("pe", lambda t, c=c: t.matmul(pb[:, c * 3:(c + 1) * 3], lhsT=sfs[0:3, c * 128:(c + 1) * 128],
                         rhs=identf[0:3, 0:3], start=True, stop=True), reads=[sfs, identf], writes=[pb], inc=(c == 3))
                P.op("dve", lambda v, g11=g11: v.tensor_copy(out=fw[:, g11 * 4:(g11 + 1) * 4, :],
                     in_=pb[:, 0:12].rearrange("p (c r) -> p c r", c=4)), reads=[pb], writes=[fw])
            for s in range(NS):
                for g11 in range(11):
                    P.dma("sp", sfs[0:2, :], sffn[s, :, g11 * 512:(g11 + 1) * 512], sfs, writes=[sfs])
                    pb = PS[7]
                    for c in range(4):
                        P.op("pe", lambda t, c=c: t.matmul(pb[:, c * 2:(c + 1) * 2], lhsT=sfs[0:2, c * 128:(c + 1) * 128],
                             rhs=identf[0:2, 0:2], start=True, stop=True), reads=[sfs, identf], writes=[pb], inc=(c == 3))
                    P.op("dve", lambda v, s=s, g11=g11: v.tensor_copy(out=halS[:, g11 * 4:(g11 + 1) * 4, s, :],
                         in_=pb[:, 0:8].rearrange("p (c r) -> p c r", c=4)), reads=[pb], writes=[halS])
            upb = [sb(ph, f"upb{k}", [128, 4, 66], F32) for k in range(2)]
            cv_ = [sb(ph, f"cvv{k}", [128, 256], F32) for k in range(2)]
            gl = sb(ph, "gl", [128, 256], F32)
            g2 = sb(ph, "g2", [128, 256], F32)
            gT = sb(ph, "gT", [128, NFC, 256], BF16)
            dn = [sb(ph, f"dn{k}", [128, D], F32) for k in range(2)]
            fst = [sb(ph, f"fst{k}", [2, 512], F32) for k in range(2)]
            ss3 = sb(ph, "ss3", [128, 2], F32)
            rs3 = sb(ph, "rs3", [128, 2], F32)
            P.op("dve", lambda v: v.memset(halP[:], 0.0), writes=[halP])
            ftiles = []
            for i in range(T // 256):
                ftiles.append(dict(r0=i * 256, N=256, segs=[(0, 256, None)], idx=i // 2, sample=False, last=(i == T // 256 - 1)))
            ftiles.append(dict(r0=T, N=TS, segs=[(s * 64, 64, s) for s in range(NS)], idx=NT, sample=True, last=True))
            for tl in ftiles:
                N = tl["N"]; ti = tl["idx"]; r0 = tl["r0"]; smp = tl["sample"]
                nsub = N // 128
                load_x(tl, src_fn=lambda tl, j: XMs[tl["r0"] + j * 128:tl["r0"] + (j + 1) * 128, :], trkb=trk["XM"][ti])
                norm_T(tl, gbc["fpre"])
                for j in range(NFC):
                    outs = []
                    for which in range(2):
                        ch = which * NFC + j
                        pb = PS[(2 * j + which) % 4]
                        proj_fm(Wup, ch * 128, hT, N, pb)
                        ub = upb[which]
                        if smp:
                            P.op("act", lambda a, ch=ch, ub=ub: a.copy(out=ub[:, :, 0:2], in_=halS[:, ch, :, :]), reads=[halS], writes=[ub])
                            P.op("act", lambda a, pb=pb, ub=ub: a.copy(out=ub[:, :, 2:66], in_=pb[:, 0:N].rearrange("p (s t) -> p s t", s=NS)),
                                 reads=[pb], writes=[ub])
                            src = lambda k, ub=ub: ub[:, :, k:k + 64]
                            o3 = lambda t_: t_[:, 0:N].rearrange("p (s t) -> p s t", s=NS)
                        else:
                            uf = ub[:, :, :].rearrange("p a b -> p (a b)")
                            P.op("act", lambda a, ch=ch, uf=uf: a.copy(out=uf[:, 0:2], in_=halP[:, ch, :]), reads=[halP], writes=[ub])
                            P.op("act", lambda a, pb=pb, uf=uf: a.copy(out=uf[:, 2:2 + N], in_=pb[:, 0:N]), reads=[pb], writes=[ub])
                            P.op("pool", lambda g, ch=ch, uf=uf: g.tensor_copy(out=halP[:, ch, :], in_=uf[:, N:N + 2]),
                                 reads=[ub], writes=[halP])
                            src = lambda k, uf=uf: uf[:, k:k + N]
                            o3 = lambda t_: t_[:, 0:N]
                        co = cv_[which]
                        P.op("dve", lambda v, co=co, src=src, o3=o3, ch=ch: v.tensor_scalar(out=o3(co), in0=src(0), scalar1=fw[:, ch, 0:1],
                             scalar2=0.0, op0=ALU.mult, op1=ALU.add), reads=[ub, fw], writes=[co])
                        for k in (1, 2):
                            P.op("dve", lambda v, co=co, src=src, o3=o3, ch=ch, k=k: v.scalar_tensor_tensor(out=o3(co), in0=src(k),
                                 scalar=fw[:, ch, k:k + 1], in1=o3(co), op0=ALU.mult, op1=ALU.add), reads=[ub, fw, co], writes=[co])
                        outs.append(co)
                    xg, xv = outs
                    P.op("pool", lambda g, xg=xg: g.tensor_tensor(out=g2[:, 0:N], in0=xg[:, 0:N], in1=xg[:, 0:N], op=ALU.mult),
                         reads=[xg], writes=[g2])
                    P.op("pool", lambda g: g.tensor_scalar(out=g2[:, 0:N], in0=g2[:, 0:N], scalar1=0.044715, scalar2=1.0,
                         op0=ALU.mult, op1=ALU.add), reads=[g2], writes=[g2])
                    P.op("pool", lambda g, xg=xg: g.tensor_tensor(out=g2[:, 0:N], in0=g2[:, 0:N], in1=xg[:, 0:N], op=ALU.mult),
                         reads=[g2, xg], writes=[g2])
                    P.op("act", lambda a: a.activation(out=gl[:, 0:N], in_=g2[:, 0:N], func=AF.Sigmoid, scale=1.5957691216057308),
                         reads=[g2], writes=[gl])
                    P.op("dve", lambda v, xg=xg: v.tensor_tensor(out=gl[:, 0:N], in0=gl[:, 0:N], in1=xg[:, 0:N], op=ALU.mult),
                         reads=[gl, xg], writes=[gl])
                    P.op("dve", lambda v, j=j, xv=xv: v.tensor_tensor(out=gT[:, j, 0:N], in0=gl[:, 0:N], in1=xv[:, 0:N], op=ALU.mult),
                         reads=[gl, xv], writes=[gT])
                ends = [(c0 + 62, s) for (c0, L, s) in tl["segs"]] if smp else ([(254, None)] if tl["last"] else [])
                for (t0, s) in ends:
                    for cb in range(11):
                        pb = PS[4 + cb % 2]
                        fb = fst[cb % 2]
                        proj_tm(Wup, cb * 512, hT, t0, pb, M=2)
                        P.op("dve", lambda v, fb=fb, pb=pb: v.tensor_copy(out=fb[:, :], in_=pb[0:2, :]), reads=[pb], writes=[fb])
                        dst = fs[s, :, cb * 512:(cb + 1) * 512] if smp else fp[:, cb * 512:(cb + 1) * 512]
                        P.dma("pool", dst, fb[:, :], fb, reads=[fb])
                for j in range(nsub):
                    dj = dn[j % 2]
                    for hf in range(2):
                        pb = PS[6 + hf]
                        proj_tm(Wdn, hf * 512, gT, j * 128, pb, K=NFC)
                        if hf == 0:
                            P.op("act", lambda a, dj=dj, pb=pb: a.copy(out=dj[:, 0:512], in_=pb[:]), reads=[pb], writes=[dj])
                        else:
                            P.op("dve", lambda v, dj=dj, pb=pb: v.tensor_copy(out=dj[:, 512:1024], in_=pb[:]), reads=[pb], writes=[dj])
                    P.op("dve", lambda v, j=j: v.memset(ss3[:, j:j + 1], 0.0), writes=[ss3])
                    P.op("act", lambda a, dj=dj, j=j: a.activation(out=cm["junk"][:], in_=dj[:], func=AF.Square, accum_out=ss3[:, j:j + 1]),
                         reads=[dj], writes=[cm["junk"], ss3])
                    rstd_from(ss3[:, j:j + 1], rs3[:, j:j + 1], ss3, rs3, D)
                    P.op("dve", lambda v, dj=dj, j=j: v.scalar_tensor_tensor(out=dj[:], in0=dj[:], scalar=rs3[:, j:j + 1],
                         in1=gbc["fpost"][:], op0=ALU.mult, op1=ALU.mult), reads=[dj, rs3, gbc["fpost"]], writes=[dj])
                    P.op("pool", lambda g, dj=dj, j=j: g.tensor_tensor(out=dj[:], in0=dj[:], in1=xt[j][:], op=ALU.add),
                         reads=[dj, xt[j]], writes=[dj])
                    dst = ys[r0 - T + j * 128:r0 - T + (j + 1) * 128, :] if smp else yp[r0 + j * 128:r0 + (j + 1) * 128, :]
                    P.dma("pool", dst, dj[:, :], dj, reads=[dj])
            P.barrier()
        P.finish()
    return nc


_CACHE = {}


def run(T, NS, PAST, per_core):
    key = (T, NS, PAST)
    if key not in _CACHE:
        _CACHE[key] = build(T, NS, PAST)
    nc = _CACHE[key]
    res = run_bass_kernel_spmd(nc, per_core, core_ids=list(range(len(per_core))))
    return res.results


WNAMES = ["g_mem", "w_mem_kv", "g_mix_pre", "g_mix_post", "w_in", "w_sb_o", "conv_dw_w", "conv_dw_b", "conv_ln_g",
          "conv_ln_b", "w_conv_o", "w_mem_o", "w_out", "g_ffn_pre", "g_ffn_post", "w_ffn_up", "ffn_dw_w", "w_ffn_down"]


def make_maps(inp, ncores, NS):
    f = lambda a: np.ascontiguousarray(np.asarray(a, dtype=np.float32))
    B = inp["x_prompt"].shape[0]
    maps = []
    for c in range(ncores):
        b = c % B
        sl = slice(c * NS, (c + 1) * NS)
        m = {"xp": f(inp["x_prompt"][b]), "xs": f(inp["x_sample"][sl]).reshape(NS * 64, D),
             "memp": f(inp["mem_prompt"][b]), "ck": f(inp["cache_sb_k"][0, sl]), "cv": f(inp["cache_sb_v"][0, sl]),
             "sconv": f(inp["state_conv"][0, sl]), "sffn": f(inp["state_ffn_conv"][0, sl]),
             "cmk": f(inp["cache_mem_k"][0, sl]), "cmv": f(inp["cache_mem_v"][0, sl])}
        for n in WNAMES:
            w = f(inp[n][0])
            m[n] = w.reshape(1, -1) if w.ndim == 1 else w
        maps.append(m)
    return maps


def assemble(res, B, ncores):
    cat = lambda n, rng: np.stack([res[c][n] for c in rng])
    pc = range(B)
    sc = range(ncores)
    yp = cat("yp", pc); ys = np.concatenate([res[c]["ys"].reshape(-1, 64, D) for c in sc])
    kp = cat("kp", pc)[None]; vp = cat("vp", pc)[None]
    ks = np.concatenate([res[c]["ks"] for c in sc])[None]; vs = np.concatenate([res[c]["vs"] for c in sc])[None]
    cp = cat("cp", pc)[None]; cs = np.concatenate([res[c]["cs"] for c in sc])[None]
    fp = cat("fp", pc)[None]; fs = np.concatenate([res[c]["fs"] for c in sc])[None]
    mkp = cat("mkp", pc)[None]; mvp = cat("mvp", pc)[None]
    return (yp, ys, kp, vp, ks, vs, cp, cs, fp, fs, mkp, mvp)


def kernel(**inputs):
    T = inputs["x_prompt"].shape[1]
    PAST = inputs["cache_sb_k"].shape[3]
    ncores = 8
    NS = inputs["x_sample"].shape[0] // ncores
    maps = make_maps(inputs, ncores, NS)
    res = run(T, NS, PAST, maps)
    return assemble(res, inputs["x_prompt"].shape[0], ncores)
```

```python
import contextlib
import numpy as np
import concourse.bass as bass
import concourse.mybir as mybir
from concourse.bass_utils import run_bass_kernel_spmd

F32 = mybir.dt.float32
BF16 = mybir.dt.bfloat16
ALU = mybir.AluOpType
AF = mybir.ActivationFunctionType

D = 1024
NCH = 8
FF = 2816
FF2 = 5632
NFC = 22
EPS = 1e-6


class _SkipPhase(Exception):
    pass


def _phase_gate(k):
    import os
    en = os.environ.get("KPH")
    if en is not None and str(k) not in en.split(","):
        raise _SkipPhase()


class Buf:
    def __init__(self, t, name):
        self.t = t
        self.name = name
        self.w = None
        self.r = {}
        self.dsem = None
        self.dcnt = 0
        self.wl = {} if t is None else None

    def __getitem__(self, k):
        return self.t[k]


class Prog:
    def __init__(self, nc, es):
        self.nc = nc
        self.es = es
        self.E = {"pe": nc.tensor, "act": nc.scalar, "dve": nc.vector, "pool": nc.gpsimd, "sp": nc.sync}
        self.sem = {e: es.enter_context(nc.semaphore("s_" + e)) for e in ("pe", "act", "dve", "pool")}
        self.cnt = {e: 0 for e in self.sem}
        self.seen = {e: {} for e in self.E}
        self.dbufs = []

    def _wait(self, e, dep, same_ok):
        if dep is None:
            return
        key, sem, val, src = dep
        if src is not None:
            val = 16 * src.dcnt
        elif key == e and not same_ok:
            return
        if self.seen[e].get(key, 0) >= val:
            return
        self.E[e].wait_ge(sem, val)
        self.seen[e][key] = val

    def _deps(self, e, reads, writes):
        for b in reads:
            self._wait(e, b.w, True)
            if b.wl:
                for d in list(b.wl.values()):
                    self._wait(e, d, True)
        for b in writes:
            self._wait(e, b.w, False)
            for d in list(b.r.values()):
                self._wait(e, d, False)

    def op(self, e, fn, reads=(), writes=(), inc=True):
        self._deps(e, reads, writes)
        ins = fn(self.E[e])
        if inc:
            self.cnt[e] += 1
            ins.then_inc(self.sem[e], 1)
            t = self.cnt[e]
        else:
            t = self.cnt[e] + 1
        dep = (e, self.sem[e], t, None)
        for b in reads:
            b.r[e] = dep
        for b in writes:
            b.w = dep
            b.r = {}
        return ins

    def dma(self, q, out_ap, in_ap, sbuf, reads=(), writes=()):
        self._deps(q, reads, writes)
        if sbuf.dsem is None:
            sbuf.dsem = self.es.enter_context(self.nc.semaphore("d_" + sbuf.name))
            self.dbufs.append(sbuf)
        sbuf.dcnt += 1
        self.E[q].dma_start(out=out_ap, in_=in_ap).then_inc(sbuf.dsem, 16)
        key = ("d", id(sbuf))
        dep = (key, sbuf.dsem, 16 * sbuf.dcnt, sbuf)
        for b in reads:
            b.r[key] = dep
        for b in writes:
            if b.wl is not None:
                b.wl[key] = dep
            else:
                b.w = dep
                b.r = {}

    def barrier(self):
        for e in self.E:
            for k in self.sem:
                if k != e and self.cnt[k] > self.seen[e].get(k, 0):
                    self.E[e].wait_ge(self.sem[k], self.cnt[k])
                    self.seen[e][k] = self.cnt[k]
            for b in self.dbufs:
                key = ("d", id(b))
                if 16 * b.dcnt > self.seen[e].get(key, 0):
                    self.E[e].wait_ge(b.dsem, 16 * b.dcnt)
                    self.seen[e][key] = 16 * b.dcnt

    def finish(self):
        for b in self.dbufs:
            self.E["sp"].wait_ge(b.dsem, 16 * b.dcnt)


def build(T, NS, PAST):
    nc = bass.Bass("TRN2", target_bir_lowering=False)
    TS = NS * 64
    NT = T // 512
    PB = PAST // 128

    def din(name, shape):
        return nc.dram_tensor(name, list(shape), F32, kind="ExternalInput").ap()

    def dout(name, shape):
        return nc.dram_tensor(name, list(shape), F32, kind="ExternalOutput").ap()

    xp = din("xp", [T, D]); xs = din("xs", [TS, D]); memp = din("memp", [256, D])
    ck = din("ck", [NS, 16, PAST, 64]); cv = din("cv", [NS, 16, PAST, 64])
    sconv = din("sconv", [NS, 30, D]); sffn = din("sffn", [NS, 2, FF2])
    cmk = din("cmk", [NS, 4, 256, 256]); cmv = din("cmv", [NS, 4, 256, 256])
    g_mem = din("g_mem", [1, D]); w_mem_kv = din("w_mem_kv", [D, 2048])
    g_mix_pre = din("g_mix_pre", [1, D]); g_mix_post = din("g_mix_post", [1, D])
    w_in = din("w_in", [D, 9216]); w_sb_o = din("w_sb_o", [D, D])
    conv_dw_w = din("conv_dw_w", [31, D]); conv_dw_b = din("conv_dw_b", [1, D])
    conv_ln_g = din("conv_ln_g", [1, D]); conv_ln_b = din("conv_ln_b", [1, D])
    w_conv_o = din("w_conv_o", [D, D]); w_mem_o = din("w_mem_o", [D, D]); w_out = din("w_out", [D, D])
    g_ffn_pre = din("g_ffn_pre", [1, D]); g_ffn_post = din("g_ffn_post", [1, D])
    w_ffn_up = din("w_ffn_up", [D, FF2]); ffn_dw_w = din("ffn_dw_w", [3, FF2]); w_ffn_down = din("w_ffn_down", [FF, D])

    yp = dout("yp", [T, D]); ys = dout("ys", [TS, D])
    kp = dout("kp", [16, T, 64]); vp = dout("vp", [16, T, 64])
    ks = dout("ks", [NS, 16, 64, 64]); vs = dout("vs", [NS, 16, 64, 64])
    cp = dout("cp", [30, D]); cs = dout("cs", [NS, 30, D])
    fp = dout("fp", [2, FF2]); fs = dout("fs", [NS, 2, FF2])
    mkp = dout("mkp", [4, 256, 256]); mvp = dout("mvp", [4, 256, 256])

    TT = T + TS
    KTs = nc.dram_tensor("KTs", [128, 8, TT], BF16).ap()
    VSs = nc.dram_tensor("VSs", [TT, D], BF16).ap()
    OSs = nc.dram_tensor("OSs", [128, 8, TT], BF16).ap()
    YCs = nc.dram_tensor("YCs", [128, 8, TT], F32).ap()
    YMs = nc.dram_tensor("YMs", [128, 8, TT], F32).ap()
    XMs = nc.dram_tensor("XMs", [TT, D], F32).ap()

    tiles = []
    for i in range(NT):
        tiles.append(dict(r0=i * 512, N=512, segs=[(0, 512, None)], idx=i, sample=False))
    tiles.append(dict(r0=T, N=TS, segs=[(s * 64, 64, s) for s in range(NS)], idx=NT, sample=True))
    ntile = len(tiles)
    trk = {n: [Buf(None, f"{n}{i}") for i in range(ntile)] for n in ("KT", "VS", "OS", "YC", "YM", "XM")}

    def xrows(tl, j):
        r = tl["r0"] + j * 128
        if tl["sample"]:
            return xs[r - T:r - T + 128, :]
        return xp[r:r + 128, :]

    es = contextlib.ExitStack()
    with es:
        P = Prog(nc, es)

        uid = [0]

        def sb(st, name, shape, dt):
            uid[0] += 1
            name = f"{name}_{uid[0]}"
            return Buf(st.enter_context(nc.sbuf_tensor(name, list(shape), dt)), name)

        PS = [Buf(es.enter_context(nc.psum_tensor(f"ps{i}", [128, 512], F32)), f"ps{i}") for i in range(8)]

        identb = sb(es, "identb", [128, 128], BF16)
        identf = sb(es, "identf", [128, 128], F32)
        negtri = sb(es, "negtri", [128, 128], BF16)
        negones = sb(es, "negones", [128, 128], BF16)
        onesb = sb(es, "onesb", [128, 128], BF16)
        onesf = sb(es, "onesf", [128, 128], F32)
        for bfr, val in ((identb, 1.0), (identf, 1.0), (negtri, -1.0), (negones, -1.0), (onesb, 1.0),
                         (onesf, 1.0 / D)):
            P.op("pool", lambda g, b=bfr, v=val: g.memset(b[:], v), writes=[bfr])
        for bfr in (identb, identf):
            P.op("pool", lambda g, b=bfr: g.affine_select(out=b[:], in_=b[:], pattern=[[-1, 128]],
                 compare_op=ALU.is_equal, fill=0.0, base=0, channel_multiplier=1), reads=[bfr], writes=[bfr])
        P.op("pool", lambda g: g.affine_select(out=negtri[:], in_=negtri[:], pattern=[[-1, 128]],
             compare_op=ALU.is_ge, fill=0.0, base=0, channel_multiplier=1), reads=[negtri], writes=[negtri])

        gbc = {}
        gsrc = {"pre": g_mix_pre, "post": g_mix_post, "fpre": g_ffn_pre, "fpost": g_ffn_post, "mem": g_mem}
        xt = [None] * 4
        xn = [None] * 2
        cm = {}

        class _HT:
            def __getitem__(self, k):
                return cm["hT"].t[k]
        hT = _HT()

        def common(ph, nsub, N, gs):
            for j in range(nsub):
                xt[j] = sb(ph, f"xt{j}", [128, D], F32)
            for j in range(2):
                xn[j] = sb(ph, f"xn{j}", [128, D], BF16)
            cm["hT"] = sb(ph, "hT", [128, 8, N], BF16)
            cm["junk"] = sb(ph, "junk", [128, D], BF16)
            cm["ssq"] = sb(ph, "ssq", [128, 4], F32)
            cm["rstd"] = sb(ph, "rstd", [128, 4], F32)
            for nm in gs:
                gbc[nm] = sb(ph, "g_" + nm, [128, D], F32)
                P.dma("sp", gbc[nm][:], gsrc[nm][0:1, :].partition_broadcast(128), gbc[nm], writes=[gbc[nm]])

        def rstd_from(ss_ap, out_ap, ssb, outb, n):
            P.op("act", lambda a: a.activation(out=out_ap, in_=ss_ap, func=AF.Ln, scale=1.0 / n, bias=EPS),
                 reads=[ssb], writes=[outb])
            P.op("act", lambda a: a.activation(out=out_ap, in_=out_ap, func=AF.Exp, scale=-0.5),
                 reads=[outb], writes=[outb])

        def load_x(tl, src_fn=None, trkb=None):
            nsub = tl["N"] // 128
            for j in range(nsub):
                src = src_fn(tl, j) if src_fn else xrows(tl, j)
                P.dma("sp", xt[j][:], src, xt[j], reads=[trkb] if trkb else [], writes=[xt[j]])

        def norm_T(tl, g):
            nsub = tl["N"] // 128
            junk, ssq, rstd, hTb = cm["junk"], cm["ssq"], cm["rstd"], cm["hT"]
            P.op("dve", lambda v: v.memset(ssq[:], 0.0), writes=[ssq])
            for j in range(nsub):
                P.op("act", lambda a, j=j: a.activation(out=junk[:], in_=xt[j][:], func=AF.Square,
                     accum_out=ssq[:, j:j + 1]), reads=[xt[j]], writes=[junk, ssq])
            rstd_from(ssq[:, 0:nsub], rstd[:, 0:nsub], ssq, rstd, D)
            for j in range(nsub):
                xb = xn[j % 2]
                P.op("dve", lambda v, j=j, xb=xb: v.scalar_tensor_tensor(out=xb[:], in0=xt[j][:],
                     scalar=rstd[:, j:j + 1], in1=g[:], op0=ALU.mult, op1=ALU.mult),
                     reads=[xt[j], rstd, g], writes=[xb])
                for half in range(2):
                    pb = PS[6 + half]
                    for cc in range(4):
                        c = half * 4 + cc
                        P.op("pe", lambda t, c=c, cc=cc, pb=pb, xb=xb: t.matmul(pb[:, cc * 128:(cc + 1) * 128],
                             lhsT=xb[:, c * 128:(c + 1) * 128], rhs=identb[:], start=True, stop=True),
                             reads=[xb, identb], writes=[pb], inc=(cc == 3))
                    eng = "act" if half == 0 else "dve"
                    if eng == "act":
                        P.op("act", lambda a, half=half, pb=pb, j=j: a.copy(
                             out=hT[:, half * 4:half * 4 + 4, j * 128:(j + 1) * 128],
                             in_=pb[:].rearrange("p (c t) -> p c t", c=4)), reads=[pb], writes=[hTb])
                    else:
                        P.op("dve", lambda v, half=half, pb=pb, j=j: v.tensor_copy(
                             out=hT[:, half * 4:half * 4 + 4, j * 128:(j + 1) * 128],
                             in_=pb[:].rearrange("p (c t) -> p c t", c=4)), reads=[pb], writes=[hTb])

        wst = [None, None]
        wcnt = [0]

        def load_w(dst, src2d, nrc, ncols, dcol0=0):
            for rc in range(nrc):
                for cb in range(0, ncols, 2048):
                    w = min(2048, ncols - cb)
                    k = wcnt[0] % 2
                    wcnt[0] += 1
                    st = wst[k]
                    P.dma("sp", st[:, 0:w], src2d[rc * 128:(rc + 1) * 128, cb:cb + w], st, writes=[st])
                    eng = "dve" if k == 0 else "pool"
                    P.op(eng, lambda v, st=st, rc=rc, cb=cb, w=w: v.tensor_copy(
                         out=dst[:, rc, dcol0 + cb:dcol0 + cb + w], in_=st[:, 0:w]), reads=[st], writes=[dst])

        def load_cols(dst, srcs, R, nchunk, stg):
            r0 = 0
            for ap, nr in srcs:
                P.dma("sp", stg[r0:r0 + nr, 0:nchunk * 128], ap, stg, writes=[stg])
                r0 += nr
            pb = PS[6]
            for c in range(nchunk):
                P.op("pe", lambda t, c=c: t.matmul(pb[:, c * R:(c + 1) * R], lhsT=stg[0:R, c * 128:(c + 1) * 128],
                     rhs=identf[0:R, 0:R], start=True, stop=True), reads=[stg, identf], writes=[pb],
                     inc=(c == nchunk - 1))
            P.op("dve", lambda v: v.tensor_copy(out=dst[:].rearrange("p c r -> p (c r)"),
                 in_=pb[:, 0:nchunk * R]), reads=[pb], writes=[dst])

        def proj_fm(W, col0, rhsT, N, pb, K=NCH):
            rb = cm["hT"] if rhsT is hT else rhsT
            for c in range(K):
                P.op("pe", lambda t, c=c: t.matmul(pb[:, 0:N], lhsT=W[:, c, col0:col0 + 128], rhs=rhsT[:, c, 0:N],
                     start=(c == 0), stop=(c == K - 1)), reads=[W, rb], writes=[pb], inc=(c == K - 1))

        def proj_tm(W, col0, lhs, t0, pb, K=NCH, M=128):
            lb = cm["hT"] if lhs is hT else lhs
            for c in range(K):
                P.op("pe", lambda t, c=c: t.matmul(pb[0:M, :], lhsT=lhs[:, c, t0:t0 + M], rhs=W[:, c, col0:col0 + 512],
                     start=(c == 0), stop=(c == K - 1)), reads=[W, lb], writes=[pb], inc=(c == K - 1))

        memst = contextlib.ExitStack()
        mkT = sb(memst, "mkT", [128, 8, 256], BF16)
        mvb = sb(memst, "mvb", [128, 2, D], BF16)
        with contextlib.suppress(_SkipPhase), contextlib.ExitStack() as ph:
            _phase_gate(0)
            Wm = sb(ph, "Wm", [128, 8, 2048], BF16)
            with contextlib.ExitStack() as ws:
                wst[0] = sb(ws, "wst0", [128, 2048], F32); wst[1] = sb(ws, "wst1", [128, 2048], F32)
                load_w(Wm, w_mem_kv, 8, 2048)
                P.barrier()
            mtok = sb(ph, "mtok", [128, 2048], F32)
            common(ph, 2, 256, ["mem"])
            mt = dict(r0=0, N=256, segs=[], sample=False)
            load_x(mt, src_fn=lambda tl, j: memp[j * 128:(j + 1) * 128, :])
            norm_T(mt, gbc["mem"])
            for j in range(2):
                for hf in range(4):
                    pb = PS[hf % 4]
                    proj_tm(Wm, hf * 512, hT, j * 128, pb)
                    P.op("act" if hf % 2 else "dve", (lambda a, hf=hf, pb=pb: a.copy(out=mtok[:, hf * 512:(hf + 1) * 512], in_=pb[:])) if hf % 2
                         else (lambda v, hf=hf, pb=pb: v.tensor_copy(out=mtok[:, hf * 512:(hf + 1) * 512], in_=pb[:])),
                         reads=[pb], writes=[mtok])
                P.op("pool", lambda g, j=j: g.tensor_copy(out=mvb[:, j, :], in_=mtok[:, 1024:2048]),
                     reads=[mtok], writes=[mvb])
                P.dma("pool", mkp[:, j * 128:(j + 1) * 128, :].rearrange("h m d -> m h d"),
                      mtok[:, 0:1024].rearrange("m (h d) -> m h d", h=4), mtok, reads=[mtok])
                P.dma("pool", mvp[:, j * 128:(j + 1) * 128, :].rearrange("h m d -> m h d"),
                      mtok[:, 1024:2048].rearrange("m (h d) -> m h d", h=4), mtok, reads=[mtok])
            for cc in range(8):
                pb = PS[cc % 4]
                proj_fm(Wm, cc * 128, hT, 256, pb)
                P.op("act", lambda a, cc=cc, pb=pb: a.copy(out=mkT[:, cc, :], in_=pb[:, 0:256]), reads=[pb], writes=[mkT])

            P.barrier()
        with contextlib.suppress(_SkipPhase), contextlib.ExitStack() as ph:
            _phase_gate(1)
            Wq = sb(ph, "Wqm", [128, 8, D], BF16)
            Wmo = sb(ph, "Wmo", [128, 8, D], BF16)
            with contextlib.ExitStack() as ws:
                wst[0] = sb(ws, "wst0", [128, 2048], F32); wst[1] = sb(ws, "wst1", [128, 2048], F32)
                load_w(Wq, w_in[:, 5120:6144], 8, D)
                load_w(Wmo, w_mem_o, 8, D)
                P.barrier()
            common(ph, 4, 512, ["pre"])
            qmT = sb(ph, "qmT", [128, 8, 512], BF16)
            pT = [sb(ph, f"pT{k}", [128, 2, 512], BF16) for k in range(2)]
            omT = sb(ph, "omT", [128, 8, 512], BF16)
            rden = [sb(ph, f"rden{k}", [128, 512], F32) for k in range(2)]
            yst = [sb(ph, f"ystm{k}", [128, 512], F32) for k in range(2)]
            smkT = sb(ph, "smkT", [128, 8, 256], BF16)
            smvb = sb(ph, "smvb", [128, 2, D], BF16)
            cmst = [sb(ph, f"cmst{k}", [128, 2, 256], F32) for k in range(2)]

            def mem_attn(kT, vB, c0, L):
                for hm in range(4):
                    pk = pT[hm % 2]
                    for mc in range(2):
                        pb = PS[mc]
                        for dc in range(2):
                            P.op("pe", lambda t, mc=mc, dc=dc, pb=pb: t.matmul(pb[:, 0:L],
                                 lhsT=kT[:, hm * 2 + dc, mc * 128:(mc + 1) * 128], rhs=qmT[:, hm * 2 + dc, c0:c0 + L],
                                 start=(dc == 0), stop=(dc == 1)), reads=[kT, qmT], writes=[pb], inc=(dc == 1))
                        P.op("act", lambda a, mc=mc, pb=pb, pk=pk: a.activation(out=pk[:, mc, 0:L], in_=pb[:, 0:L], func=AF.Exp),
                             reads=[pb], writes=[pk])
                    pd = PS[2]
                    for mc in range(2):
                        P.op("pe", lambda t, mc=mc, pk=pk: t.matmul(pd[:, 0:L], lhsT=onesb[:], rhs=pk[:, mc, 0:L],
                             start=(mc == 0), stop=(mc == 1)), reads=[onesb, pk], writes=[pd], inc=(mc == 1))
                    rd = rden[hm % 2]
                    P.op("dve", lambda v, rd=rd: v.reciprocal(out=rd[:, 0:L], in_=pd[:, 0:L]), reads=[pd], writes=[rd])
                    for dc in range(2):
                        po = PS[4 + dc]
                        for mc in range(2):
                            P.op("pe", lambda t, mc=mc, dc=dc, po=po, pk=pk: t.matmul(po[:, 0:L],
                                 lhsT=vB[:, mc, hm * 256 + dc * 128:hm * 256 + dc * 128 + 128], rhs=pk[:, mc, 0:L],
                                 start=(mc == 0), stop=(mc == 1)), reads=[vB, pk], writes=[po], inc=(mc == 1))
                        P.op("dve", lambda v, dc=dc, po=po, rd=rd: v.tensor_tensor(out=omT[:, hm * 2 + dc, c0:c0 + L],
                             in0=po[:, 0:L], in1=rd[:, 0:L], op=ALU.mult), reads=[po, rd], writes=[omT])

            for tl in tiles:
                N = tl["N"]; ti = tl["idx"]; smp = tl["sample"]
                load_x(tl)
                norm_T(tl, gbc["pre"])
                for c2 in range(8):
                    pb = PS[c2 % 4]
                    proj_fm(Wq, c2 * 128, hT, N, pb)
                    P.op("act", lambda a, c2=c2, pb=pb: a.activation(out=qmT[:, c2, 0:N], in_=pb[:, 0:N], func=AF.Copy,
                         scale=1.0 / 16.0), reads=[pb], writes=[qmT])
                if not smp:
                    mem_attn(mkT, mvb, 0, N)
                else:
                    for (c0, L, s) in tl["segs"]:
                        for hm in range(4):
                            st = cmst[hm % 2]
                            P.dma("sp", st[:, :, :], cmk[s, hm, :, :].rearrange("(j m) d -> m j d", j=2), st, writes=[st])
                            pb = PS[6 + hm % 2]
                            for dc in range(2):
                                for j in range(2):
                                    P.op("pe", lambda t, dc=dc, j=j, pb=pb, st=st: t.matmul(
                                         pb[:, dc * 256 + j * 128:dc * 256 + j * 128 + 128],
                                         lhsT=st[:, j, dc * 128:(dc + 1) * 128], rhs=identf[:], start=True, stop=True),
                                         reads=[st, identf], writes=[pb], inc=(dc == 1 and j == 1))
                            P.op("dve", lambda v, hm=hm, pb=pb: v.tensor_copy(out=smkT[:, hm * 2:hm * 2 + 2, :],
                                 in_=pb[:].rearrange("p (c m) -> p c m", c=2)), reads=[pb], writes=[smkT])
                            st2 = cmst[(hm + 1) % 2]
                            P.dma("sp", st2[:, :, :], cmv[s, hm, :, :].rearrange("(j m) d -> m j d", j=2), st2, writes=[st2])
                            P.op("pool", lambda g, hm=hm, st2=st2: g.tensor_copy(out=smvb[:, :, hm * 256:(hm + 1) * 256],
                                 in_=st2[:, :, :]), reads=[st2], writes=[smvb])
                        mem_attn(smkT, smvb, c0, L)
                for c2 in range(8):
                    pb = PS[c2 % 4]
                    proj_fm(Wmo, c2 * 128, omT, N, pb)
                    y = yst[c2 % 2]
                    P.op("act", lambda a, pb=pb, y=y: a.copy(out=y[:, 0:N], in_=pb[:, 0:N]), reads=[pb], writes=[y])
                    P.dma("pool", YMs[:, c2, tl["r0"]:tl["r0"] + N], y[:, 0:N], y, reads=[y], writes=[trk["YM"][ti]])
            P.barrier()
        P.barrier()
        memst.close()

        with contextlib.suppress(_SkipPhase), contextlib.ExitStack() as ph:
            _phase_gate(2)
            Wc = sb(ph, "Wc", [128, 8, 2048], BF16)
            Wco = sb(ph, "Wco", [128, 8, D], BF16)
            with contextlib.ExitStack() as ws:
                wst[0] = sb(ws, "wst0", [128, 2048], F32); wst[1] = sb(ws, "wst1", [128, 2048], F32)
                load_w(Wc, w_in[:, 3072:5120], 8, 2048)
                load_w(Wco, w_conv_o, 8, D)
                P.barrier()
            common(ph, 4, 512, ["pre"])
            cw = sb(ph, "cw", [128, 8, 31], F32)
            cvec = sb(ph, "cvec", [128, 8, 3], F32)
            with contextlib.ExitStack() as ws:
                stg = sb(ws, "stg", [31, D], F32)
                load_cols(cw, [(conv_dw_w[:, :], 31)], 31, 8, stg)
                load_cols(cvec, [(conv_dw_b[0:1, :], 1), (conv_ln_g[0:1, :], 1), (conv_ln_b[0:1, :], 1)], 3, 8, stg)
                P.barrier()
            uP = sb(ph, "uP", [128, 8, 542], F32)
            uS = sb(ph, "uS", [128, 8, NS, 94], F32)
            cc_ = sb(ph, "cc", [128, 8, 512], F32)
            csq = [sb(ph, f"csq{k}", [128, 512], F32) for k in range(2)]
            ccT = sb(ph, "ccT", [128, 8, 512], BF16)
            sg = [sb(ph, f"sg{k}", [128, 512], F32) for k in range(2)]
            mean = sb(ph, "mean", [128, 512], F32)
            rs = sb(ph, "rs", [128, 512], F32)
            t1 = [sb(ph, f"t1{k}", [128, 512], F32) for k in range(2)]
            yst = [sb(ph, f"yst{k}", [128, 512], F32) for k in range(2)]
            cst = sb(ph, "cst", [30, D], F32)
            sst = sb(ph, "sst", [30, D], F32)
            P.op("dve", lambda v: v.memset(uP[:, :, 0:30], 0.0), writes=[uP])
            for tl in tiles:
                N = tl["N"]; ti = tl["idx"]; smp = tl["sample"]
                load_x(tl)
                norm_T(tl, gbc["pre"])
                if smp:
                    for s in range(NS):
                        P.dma("sp", sst[:, :], sconv[s, :, :], sst, writes=[sst])
                        pb = PS[4 + s % 2]
                        for c in range(8):
                            P.op("pe", lambda t, c=c, pb=pb: t.matmul(pb[:, c * 30:(c + 1) * 30],
                                 lhsT=sst[0:30, c * 128:(c + 1) * 128], rhs=identf[0:30, 0:30], start=True, stop=True),
                                 reads=[sst, identf], writes=[pb], inc=(c == 7))
                        P.op("dve", lambda v, s=s, pb=pb: v.tensor_copy(out=uS[:, :, s, 0:30],
                             in_=pb[:, 0:240].rearrange("p (c r) -> p c r", c=8)), reads=[pb], writes=[uS])
                for c2 in range(8):
                    pa, pbb = PS[(2 * c2) % 4], PS[(2 * c2 + 1) % 4]
                    proj_fm(Wc, c2 * 128, hT, N, pa)
                    proj_fm(Wc, 1024 + c2 * 128, hT, N, pbb)
                    sgk = sg[c2 % 2]
                    P.op("act", lambda a, pbb=pbb, sgk=sgk: a.activation(out=sgk[:, 0:N], in_=pbb[:, 0:N], func=AF.Sigmoid),
                         reads=[pbb], writes=[sgk])
                    if smp:
                        P.op("dve", lambda v, c2=c2, pa=pa, sgk=sgk: v.tensor_tensor(out=uS[:, c2, :, 30:94],
                             in0=pa[:, 0:N].rearrange("p (s t) -> p s t", s=NS),
                             in1=sgk[:, 0:N].rearrange("p (s t) -> p s t", s=NS), op=ALU.mult),
                             reads=[pa, sgk], writes=[uS])
                    else:
                        P.op("dve", lambda v, c2=c2, pa=pa, sgk=sgk: v.tensor_tensor(out=uP[:, c2, 30:542],
                             in0=pa[:, 0:N], in1=sgk[:, 0:N], op=ALU.mult), reads=[pa, sgk], writes=[uP])
                for c2 in range(8):
                    eng = "dve"
                    for (c0, L, s) in tl["segs"]:
                        usrc = (lambda k, c2=c2, s=s, L=L: uS[:, c2, s, k:k + L]) if smp else (lambda k, c2=c2, L=L: uP[:, c2, k:k + L])
                        ub = uS if smp else uP
                        P.op(eng, lambda v, c2=c2, c0=c0, L=L, usrc=usrc: v.tensor_scalar(out=cc_[:, c2, c0:c0 + L], in0=usrc(0),
                             scalar1=cw[:, c2, 0:1], scalar2=cvec[:, c2, 0:1], op0=ALU.mult, op1=ALU.add),
                             reads=[ub, cw, cvec], writes=[cc_])
                        for k in range(1, 31):
                            P.op(eng, lambda v, c2=c2, c0=c0, L=L, k=k, usrc=usrc: v.scalar_tensor_tensor(
                                 out=cc_[:, c2, c0:c0 + L], in0=usrc(k), scalar=cw[:, c2, k:k + 1],
                                 in1=cc_[:, c2, c0:c0 + L], op0=ALU.mult, op1=ALU.add),
                                 reads=[ub, cw, cc_], writes=[cc_])
                pm, pq = PS[4], PS[5]
                for c2 in range(8):
                    q = csq[c2 % 2]
                    P.op("act", lambda a, c2=c2, q=q: a.activation(out=q[:, 0:N], in_=cc_[:, c2, 0:N], func=AF.Square),
                         reads=[cc_], writes=[q])
                    P.op("pe", lambda t, c2=c2: t.matmul(pm[:, 0:N], lhsT=onesf[:], rhs=cc_[:, c2, 0:N],
                         start=(c2 == 0), stop=(c2 == 7)), reads=[onesf, cc_], writes=[pm])
                    P.op("pe", lambda t, c2=c2, q=q: t.matmul(pq[:, 0:N], lhsT=onesf[:], rhs=q[:, 0:N],
                         start=(c2 == 0), stop=(c2 == 7)), reads=[onesf, q], writes=[pq])
                P.op("act", lambda a: a.copy(out=mean[:, 0:N], in_=pm[:, 0:N]), reads=[pm], writes=[mean])
                P.op("dve", lambda v: v.tensor_tensor(out=rs[:, 0:N], in0=mean[:, 0:N], in1=mean[:, 0:N], op=ALU.mult),
                     reads=[mean], writes=[rs])
                P.op("dve", lambda v: v.tensor_tensor(out=rs[:, 0:N], in0=pq[:, 0:N], in1=rs[:, 0:N], op=ALU.subtract),
                     reads=[pq, rs], writes=[rs])
                rstd_from(rs[:, 0:N], rs[:, 0:N], rs, rs, 1.0)
                for c2 in range(8):
                    tk = t1[c2 % 2]
                    P.op("dve", lambda v, c2=c2, tk=tk: v.tensor_tensor(out=tk[:, 0:N], in0=cc_[:, c2, 0:N], in1=mean[:, 0:N],
                         op=ALU.subtract), reads=[cc_, mean], writes=[tk])
                    P.op("dve", lambda v, tk=tk: v.tensor_tensor(out=tk[:, 0:N], in0=tk[:, 0:N], in1=rs[:, 0:N], op=ALU.mult),
                         reads=[tk, rs], writes=[tk])
                    P.op("act", lambda a, c2=c2, tk=tk: a.activation(out=ccT[:, c2, 0:N], in_=tk[:, 0:N], func=AF.Silu,
                         scale=cvec[:, c2, 1:2], bias=cvec[:, c2, 2:3]), reads=[tk, cvec], writes=[ccT])
                for c2 in range(8):
                    pb = PS[c2 % 4]
                    proj_fm(Wco, c2 * 128, ccT, N, pb)
                    y = yst[c2 % 2]
                    P.op("act", lambda a, pb=pb, y=y: a.copy(out=y[:, 0:N], in_=pb[:, 0:N]), reads=[pb], writes=[y])
                    P.dma("pool", YCs[:, c2, tl["r0"]:tl["r0"] + N], y[:, 0:N], y, reads=[y], writes=[trk["YC"][ti]])
                ends = [(s, 64) for (_, _, s) in tl["segs"]] if smp else ([(None, 512)] if ti == NT - 1 else [])
                for (s, L) in ends:
                    for half in range(2):
                        pb = PS[4 + half]
                        for cq in range(4):
                            c2 = half * 4 + cq
                            src = uS[:, c2, s, L:L + 30] if smp else uP[:, c2, L:L + 30]
                            P.op("pe", lambda t, cq=cq, pb=pb, src=src: t.matmul(pb[0:30, cq * 128:(cq + 1) * 128], lhsT=src,
                                 rhs=identf[:], start=True, stop=True), reads=[uS if smp else uP, identf], writes=[pb],
                                 inc=(cq == 3))
                        P.op("dve", lambda v, half=half, pb=pb: v.tensor_copy(out=cst[:, half * 512:(half + 1) * 512],
                             in_=pb[0:30, :]), reads=[pb], writes=[cst])
                    P.dma("pool", cs[s, :, :] if smp else cp[:, :], cst[:, :], cst, reads=[cst])
                if not smp:
                    P.op("dve", lambda v: v.tensor_copy(out=uP[:, :, 0:30], in_=uP[:, :, 512:542]), reads=[uP], writes=[uP])

            P.barrier()
        with contextlib.suppress(_SkipPhase), contextlib.ExitStack() as ph:
            _phase_gate(3)
            Wqkv = sb(ph, "Wqkv", [128, 8, 3072], BF16)
            with contextlib.ExitStack() as ws:
                wst[0] = sb(ws, "wst0", [128, 2048], F32); wst[1] = sb(ws, "wst1", [128, 2048], F32)
                load_w(Wqkv, w_in[:, 0:3072], 8, 3072)
                P.barrier()
            common(ph, 4, 512, ["pre"])
            masks = sb(ph, "masks", [128, 4, 512], BF16)
            P.op("pool", lambda g: g.memset(masks[:], 1.0), writes=[masks])
            for r in range(4):
                P.op("pool", lambda g, r=r: g.affine_select(out=masks[:, r, :], in_=masks[:, r, :], pattern=[[1, 512]],
                     compare_op=ALU.is_gt, fill=0.0, base=-128 * r, channel_multiplier=-1), reads=[masks], writes=[masks])
            qT = sb(ph, "qT", [128, 8, 512], BF16)
            kT = sb(ph, "kT", [128, 8, 512], BF16)
            tok = [sb(ph, f"tok{k}", [128, D], F32) for k in range(2)]
            vbf = [sb(ph, f"vbf{k}", [128, D], BF16) for k in range(2)]
            osb = sb(ph, "osb", [128, 8, 512], BF16)
            Kt = [sb(ph, f"Kt{k}", [128, 512], BF16) for k in range(2)]
            Vt = [sb(ph, f"Vt{k}", [128, 4, 128], BF16) for k in range(2)]
            Ee = [sb(ph, f"Ee{k}", [128, 512], F32) for k in range(2)]
            Sp = [sb(ph, f"Sp{k}", [128, 512], BF16) for k in range(2)]
            Ls = [[sb(ph, f"Ls{h}{k}", [128, 512], BF16) for k in range(2)] for h in range(2)]
            Aa = [sb(ph, f"Aa{k}", [128, 512], BF16) for k in range(2)]
            kc = [sb(ph, f"kc{k}", [128, 4, 128], F32) for k in range(2)]
            vc = [sb(ph, f"vc{k}", [128, 2, 4, 64], F32) for k in range(2)]
            PZ = [PS[0], PS[1]]; PC = [PS[2], PS[3]]; PO = PS[4]

            def S1(u):
                hh, p, Ktb, kcols, qcols, N, mask_ap = u["hh"], u["p"], u["Ktb"], u["kcols"], u["qcols"], u["N"], u["mask"]
                hp = slice(hh * 64, hh * 64 + 64)
                z, e, s_ = PZ[hh], Ee[hh], Sp[hh]
                P.op("pe", lambda t: t.matmul(z[:, 0:N], lhsT=Ktb[hp, kcols], rhs=qT[hp, p, qcols], start=True, stop=True),
                     reads=[Ktb, qT], writes=[z])
                P.op("act", lambda a: a.activation(out=e[:, 0:N], in_=z[:, 0:N], func=AF.Exp), reads=[z], writes=[e])
                P.op("act", lambda a: a.activation(out=s_[:, 0:N], in_=e[:, 0:N], func=AF.Ln, bias=1.0, scale=1.0),
                     reads=[e], writes=[s_])
                if mask_ap is not None:
                    P.op("dve", lambda v: v.tensor_tensor(out=s_[:, 0:N], in0=s_[:, 0:N], in1=mask_ap, op=ALU.mult),
                         reads=[s_, masks], writes=[s_])

            def S2a(u):
                hh, p, Ktb, kcols, qcols, N = u["hh"], u["p"], u["Ktb"], u["kcols"], u["qcols"], u["N"]
                first, last, lsi = u["first"], u["last"], u["lsi"]
                hp = slice(hh * 64, hh * 64 + 64)
                cb_, s_, a_ = PC[hh], Sp[hh], Aa[hh]
                lo, ln = Ls[hh][lsi % 2], Ls[hh][(lsi + 1) % 2]
                P.op("pe", lambda t: t.matmul(cb_[:, 0:N], lhsT=negtri[:, :], rhs=s_[:, 0:N], start=True, stop=False),
                     reads=[negtri, s_], writes=[cb_], inc=False)
                if not first:
                    P.op("pe", lambda t: t.matmul(cb_[:, 0:N], lhsT=negones[:, :], rhs=lo[:, 0:N], start=False, stop=False),
                         reads=[negones, lo], writes=[cb_], inc=False)
                P.op("pe", lambda t: t.matmul(cb_[:, 0:N], lhsT=Ktb[hp, kcols], rhs=qT[hp, p, qcols], start=False, stop=True),
                     reads=[Ktb, qT], writes=[cb_])
                if not last:
                    if first:
                        P.op("pool", lambda g: g.tensor_copy(out=ln[:, 0:N], in_=s_[:, 0:N]), reads=[s_], writes=[ln])
                    else:
                        P.op("dve", lambda v: v.tensor_tensor(out=ln[:, 0:N], in0=lo[:, 0:N], in1=s_[:, 0:N], op=ALU.add),
                             reads=[lo, s_], writes=[ln])
                P.op("act", lambda a: a.activation(out=a_[:, 0:N], in_=cb_[:, 0:N], func=AF.Exp), reads=[cb_], writes=[a_])

            def S2b(u):
                hh, Vb, vr, N, mask_ap, first, last = u["hh"], u["Vb"], u["vr"], u["N"], u["mask"], u["first"], u["last"]
                hp = slice(hh * 64, hh * 64 + 64)
                a_ = Aa[hh]
                if mask_ap is not None:
                    P.op("dve", lambda v: v.tensor_tensor(out=a_[:, 0:N], in0=a_[:, 0:N], in1=mask_ap, op=ALU.mult),
                         reads=[a_, masks], writes=[a_])
                P.op("pe", lambda t: t.matmul(PO[hp, 0:N], lhsT=Vb[:, vr, hp], rhs=a_[:, 0:N], start=first, stop=last),
                     reads=[Vb, a_], writes=[PO], inc=last)

            def emit_units(units):
                n = len(units)
                for i in range(n + 2):
                    if i < n:
                        if units[i].get("pre"):
                            units[i]["pre"]()
                        S1(units[i])
                    if 0 <= i - 1 < n:
                        S2a(units[i - 1])
                    if 0 <= i - 2 < n:
                        S2b(units[i - 2])

            ldc = [0]

            def load_kv(p, tj, ti):
                k = ldc[0] % 2
                ldc[0] += 1
                r0 = tiles[tj]["r0"]
                P.dma("sp", Kt[k][:, :], KTs[:, p, r0:r0 + 512], Kt[k], reads=[trk["KT"][tj]], writes=[Kt[k]])
                P.dma("sp", Vt[k][:, :, :], VSs[r0:r0 + 512, p * 128:(p + 1) * 128].rearrange("(r s) c -> s r c", r=4),
                      Vt[k], reads=[trk["VS"][tj]], writes=[Vt[k]])
                return k

            for tl in tiles:
                N = tl["N"]; ti = tl["idx"]; smp = tl["sample"]; r0 = tl["r0"]
                nsub = N // 128
                load_x(tl)
                norm_T(tl, gbc["pre"])
                for p in range(8):
                    pq_, pk_ = PS[5], PS[6]
                    proj_fm(Wqkv, p * 128, hT, N, pq_)
                    P.op("act", lambda a, p=p: a.activation(out=qT[:, p, 0:N], in_=pq_[:, 0:N], func=AF.Copy, scale=0.125),
                         reads=[pq_], writes=[qT])
                    proj_fm(Wqkv, 1024 + p * 128, hT, N, pk_)
                    P.op("dve", lambda v, p=p: v.tensor_copy(out=kT[:, p, 0:N], in_=pk_[:, 0:N]), reads=[pk_], writes=[kT])
                P.dma("pool", KTs[:, :, r0:r0 + N], kT[:, :, 0:N], kT, reads=[kT], writes=[trk["KT"][ti]])
                for j in range(nsub):
                    for which in range(2):
                        tk = tok[which]
                        for hf in range(2):
                            pb = PS[5 + hf]
                            proj_tm(Wqkv, 1024 * (1 + which) + hf * 512, hT, j * 128, pb)
                            if hf == 0:
                                P.op("act", lambda a, tk=tk, pb=pb: a.copy(out=tk[:, 0:512], in_=pb[:]), reads=[pb], writes=[tk])
                            else:
                                P.op("dve", lambda v, tk=tk, pb=pb: v.tensor_copy(out=tk[:, 512:1024], in_=pb[:]), reads=[pb], writes=[tk])
                        if not smp:
                            dst = (kp if which == 0 else vp)[:, r0 + j * 128:r0 + (j + 1) * 128, :].rearrange("h t d -> t h d")
                            P.dma("pool", dst, tk[:, :].rearrange("t (h d) -> t h d", h=16), tk, reads=[tk])
                        else:
                            for s2 in range(2):
                                s = j * 2 + s2
                                dst = (ks if which == 0 else vs)[s, :, :, :].rearrange("h t d -> t h d")
                                P.dma("pool", dst, tk[s2 * 64:(s2 + 1) * 64, :].rearrange("t (h d) -> t h d", h=16), tk, reads=[tk])
                        if which == 1:
                            vb = vbf[j % 2]
                            P.op("pool", lambda g, vb=vb, tk=tk: g.tensor_copy(out=vb[:], in_=tk[:]), reads=[tk], writes=[vb])
                            P.dma("pool", VSs[r0 + j * 128:r0 + (j + 1) * 128, :], vb[:, :], vb, reads=[vb], writes=[trk["VS"][ti]])
                import os as _os
                _ka = _os.environ.get("KA_SKIP", "")
                if not smp:
                    for p in range(8 if "p" not in _ka else 0):
                        nkt = ti + 1
                        nsteps = 4 * nkt
                        units = []
                        bufk = {0: load_kv(p, ti, ti)}
                        step = 0
                        for jj in range(nkt):
                            tj = ti - jj
                            for r in (3, 2, 1, 0):
                                for hh in range(2):
                                    u = dict(hh=hh, p=p, jj=jj, kcols=slice(r * 128, (r + 1) * 128), vr=r, qcols=slice(0, N), N=N,
                                             first=(step == 0), last=(step == nsteps - 1),
                                             mask=(masks[:, r, :] if jj == 0 else None), lsi=step)
                                    if r == 2 and hh == 0 and jj + 1 < nkt:
                                        u["pre"] = (lambda jj=jj, tj=tj: bufk.__setitem__(jj + 1, load_kv(p, tj - 1, ti)))
                                    units.append(u)
                                step += 1

                        class _Lazy(dict):
                            pass
                        def _res(u):
                            k = bufk[u["jj"]]
                            u["Ktb"], u["Vb"] = Kt[k], Vt[k]
                        n = len(units)
                        for i in range(n + 2):
                            if i < n:
                                if units[i].get("pre"):
                                    units[i]["pre"]()
                                _res(units[i])
                                S1(units[i])
                            if 0 <= i - 1 < n:
                                S2a(units[i - 1])
                            if 0 <= i - 2 < n:
                                S2b(units[i - 2])
                        P.op("act", lambda a, p=p: a.copy(out=osb[:, p, 0:N], in_=PO[:, 0:N]), reads=[PO], writes=[osb])
                else:
                    for (c0, L, s) in tl["segs"]:
                        for p in range(8 if "s" not in _ka else 0):
                            k = ldc[0] % 2
                            ldc[0] += 1
                            nsteps = 1 + PB
                            qc = slice(c0, c0 + L)

                            def pre_first(k=k, p=p, c0=c0):
                                P.op("pool", lambda g: g.memset(Kt[k][:, 64:128], 0.0), writes=[Kt[k]])
                                P.op("pool", lambda g: g.memset(Vt[k][64:128, 0, :], 0.0), writes=[Vt[k]])
                                P.dma("sp", Kt[k][:, 0:64], KTs[:, p, r0 + c0:r0 + c0 + 64], Kt[k], reads=[trk["KT"][ti]], writes=[Kt[k]])
                                P.dma("sp", Vt[k][0:64, 0, :], VSs[r0 + c0:r0 + c0 + 64, p * 128:(p + 1) * 128], Vt[k],
                                      reads=[trk["VS"][ti]], writes=[Vt[k]])

                            def pre_group(kk, g4, p=p, s=s):
                                kcb, vcb = kc[kk], vc[kk]
                                for h2 in range(2):
                                    P.dma("sp", kcb[:, :, h2 * 64:(h2 + 1) * 64], ck[s, 2 * p + h2, g4 * 512:(g4 + 1) * 512, :].rearrange(
                                          "(b k) d -> k b d", b=4), kcb, writes=[kcb])
                                    P.dma("sp", vcb[:, h2, :, :], cv[s, 2 * p + h2, g4 * 512:(g4 + 1) * 512, :].rearrange(
                                          "(b k) d -> k b d", b=4), vcb, writes=[vcb])
                                pt = PS[7]
                                for b in range(4):
                                    P.op("pe", lambda t, b=b: t.matmul(pt[:, b * 128:(b + 1) * 128],
                                         lhsT=kcb[:, b, :], rhs=identf[:], start=True, stop=True),
                                         reads=[kcb, identf], writes=[pt], inc=(b == 3))
                                P.op("dve", lambda v: v.tensor_copy(out=Kt[kk][:, :], in_=pt[:, :]), reads=[pt], writes=[Kt[kk]])
                                for h2 in range(2):
                                    P.op("pool", lambda g, h2=h2: g.tensor_copy(out=Vt[kk][:, :, h2 * 64:(h2 + 1) * 64],
                                         in_=vcb[:, h2, :, :]), reads=[vcb], writes=[Vt[kk]])

                            units = []
                            for hh in range(2):
                                units.append(dict(hh=hh, p=p, Ktb=Kt[k], Vb=Vt[k], kcols=slice(0, 128), vr=0, qcols=qc, N=L,
                                                  first=True, last=False, mask=masks[:, 0, 0:64], lsi=0,
                                                  pre=(pre_first if hh == 0 else None)))
                            step = 1
                            glist = list(range(PB // 4 - 1, -1, -1))
                            gk = []
                            for gi, g4 in enumerate(glist):
                                kk = ldc[0] % 2
                                ldc[0] += 1
                                gk.append(kk)
                            for gi, g4 in enumerate(glist):
                                kk = gk[gi]
                                for bi, b in enumerate((3, 2, 1, 0)):
                                    for hh in range(2):
                                        u = dict(hh=hh, p=p, Ktb=Kt[kk], Vb=Vt[kk], kcols=slice(b * 128, (b + 1) * 128), vr=b, qcols=qc, N=L,
                                                 first=False, last=(step == nsteps - 1), mask=None, lsi=step)
                                        if gi == 0 and bi == 0 and hh == 0:
                                            u["pre"] = (lambda kk=kk, g4=g4: pre_group(kk, g4))
                                        if bi == 1 and hh == 0 and gi + 1 < len(glist):
                                            u["pre"] = (lambda kk=gk[gi + 1], g4=glist[gi + 1]: pre_group(kk, g4))
                                        units.append(u)
                                    step += 1
                            emit_units(units)
                            P.op("act", lambda a, p=p, c0=c0, L=L: a.copy(out=osb[:, p, c0:c0 + L], in_=PO[:, 0:L]), reads=[PO], writes=[osb])
                P.dma("pool", OSs[:, :, r0:r0 + N], osb[:, :, 0:N], osb, reads=[osb], writes=[trk["OS"][ti]])

            P.barrier()
        with contextlib.suppress(_SkipPhase), contextlib.ExitStack() as ph:
            _phase_gate(4)
            Wg = sb(ph, "Wg", [128, 8, 3072], BF16)
            Wso = sb(ph, "Wso", [128, 8, D], BF16)
            Wo = sb(ph, "Wo", [128, 8, D], BF16)
            with contextlib.ExitStack() as ws:
                wst[0] = sb(ws, "wst0", [128, 2048], F32); wst[1] = sb(ws, "wst1", [128, 2048], F32)
                load_w(Wg, w_in[:, 6144:9216], 8, 3072)
                load_w(Wso, w_sb_o, 8, D)
                load_w(Wo, w_out, 8, D)
                P.barrier()
            common(ph, 4, 512, ["pre", "post"])
            os_ = sb(ph, "os_", [128, 8, 512], BF16)
            ycb = [sb(ph, f"ycb{k}", [128, 512], F32) for k in range(2)]
            ymb = [sb(ph, f"ymb{k}", [128, 512], F32) for k in range(2)]
            sgg = [sb(ph, f"sgg{k}", [128, 512], F32) for k in range(3)]
            mrg = [sb(ph, f"mrg{k}", [128, 512], F32) for k in range(2)]
            mg = sb(ph, "mg", [128, 8, 512], BF16)
            mo = [sb(ph, f"mo{k}", [128, D], F32) for k in range(2)]
            ss2 = sb(ph, "ss2", [128, 4], F32)
            rs2 = sb(ph, "rs2", [128, 4], F32)
            for tl in tiles:
                N = tl["N"]; ti = tl["idx"]; r0 = tl["r0"]
                nsub = N // 128
                load_x(tl)
                norm_T(tl, gbc["pre"])
                P.dma("sp", os_[:, :, 0:N], OSs[:, :, r0:r0 + N], os_, reads=[trk["OS"][ti]], writes=[os_])
                for c2 in range(8):
                    yc, ym = ycb[c2 % 2], ymb[c2 % 2]
                    P.dma("sp", yc[:, 0:N], YCs[:, c2, r0:r0 + N], yc, reads=[trk["YC"][ti]], writes=[yc])
                    P.dma("sp", ym[:, 0:N], YMs[:, c2, r0:r0 + N], ym, reads=[trk["YM"][ti]], writes=[ym])
                    pys, pg = PS[0 + (c2 % 2) * 4], [PS[1 + (c2 % 2) * 4], PS[2 + (c2 % 2) * 4], PS[3 + (c2 % 2) * 4]]
                    proj_fm(Wso, c2 * 128, os_, N, pys)
                    for gi in range(3):
                        proj_fm(Wg, gi * 1024 + c2 * 128, hT, N, pg[gi])
                        P.op("act", lambda a, gi=gi, pg=pg: a.activation(out=sgg[gi][:, 0:N], in_=pg[gi][:, 0:N], func=AF.Sigmoid),
                             reads=[pg[gi]], writes=[sgg[gi]])
                    m = mrg[c2 % 2]
                    P.op("dve", lambda v, m=m, pys=pys: v.tensor_tensor(out=m[:, 0:N], in0=pys[:, 0:N], in1=sgg[0][:, 0:N], op=ALU.mult),
                         reads=[pys, sgg[0]], writes=[m])
                    P.op("pool", lambda g, yc=yc: g.tensor_tensor(out=sgg[1][:, 0:N], in0=sgg[1][:, 0:N], in1=yc[:, 0:N], op=ALU.mult),
                         reads=[sgg[1], yc], writes=[sgg[1]])
                    P.op("pool", lambda g, ym=ym: g.tensor_tensor(out=sgg[2][:, 0:N], in0=sgg[2][:, 0:N], in1=ym[:, 0:N], op=ALU.mult),
                         reads=[sgg[2], ym], writes=[sgg[2]])
                    P.op("dve", lambda v, m=m: v.tensor_tensor(out=m[:, 0:N], in0=m[:, 0:N], in1=sgg[1][:, 0:N], op=ALU.add),
                         reads=[m, sgg[1]], writes=[m])
                    P.op("dve", lambda v, m=m, c2=c2: v.tensor_tensor(out=mg[:, c2, 0:N], in0=m[:, 0:N], in1=sgg[2][:, 0:N], op=ALU.add),
                         reads=[m, sgg[2]], writes=[mg])
                for j in range(nsub):
                    mj = mo[j % 2]
                    for hf in range(2):
                        pb = PS[hf]
                        proj_tm(Wo, hf * 512, mg, j * 128, pb)
                        if hf == 0:
                            P.op("act", lambda a, mj=mj, pb=pb: a.copy(out=mj[:, 0:512], in_=pb[:]), reads=[pb], writes=[mj])
                        else:
                            P.op("dve", lambda v, mj=mj, pb=pb: v.tensor_copy(out=mj[:, 512:1024], in_=pb[:]), reads=[pb], writes=[mj])
                    P.op("dve", lambda v, j=j: v.memset(ss2[:, j:j + 1], 0.0), writes=[ss2])
                    P.op("act", lambda a, mj=mj, j=j: a.activation(out=cm["junk"][:], in_=mj[:], func=AF.Square, accum_out=ss2[:, j:j + 1]),
                         reads=[mj], writes=[cm["junk"], ss2])
                    rstd_from(ss2[:, j:j + 1], rs2[:, j:j + 1], ss2, rs2, D)
                    xo = mj
                    P.op("dve", lambda v, mj=mj, j=j: v.scalar_tensor_tensor(out=mj[:], in0=mj[:], scalar=rs2[:, j:j + 1],
                         in1=gbc["post"][:], op0=ALU.mult, op1=ALU.mult), reads=[mj, rs2, gbc["post"]], writes=[mj])
                    P.op("pool", lambda g, mj=mj, j=j, xo=xo: g.tensor_tensor(out=xo[:], in0=mj[:], in1=xt[j][:], op=ALU.add),
                         reads=[mj, xt[j]], writes=[xo])
                    P.dma("pool", XMs[r0 + j * 128:r0 + (j + 1) * 128, :], xo[:, :], xo, reads=[xo], writes=[trk["XM"][ti]])

            P.barrier()
        with contextlib.suppress(_SkipPhase), contextlib.ExitStack() as ph:
            _phase_gate(5)
            Wup = sb(ph, "Wup", [128, 8, FF2], BF16)
            Wdn = sb(ph, "Wdn", [128, NFC, D], BF16)
            with contextlib.ExitStack() as ws:
                wst[0] = sb(ws, "wst0", [128, 2048], F32); wst[1] = sb(ws, "wst1", [128, 2048], F32)
                load_w(Wup, w_ffn_up, 8, FF2)
                load_w(Wdn, w_ffn_down, NFC, D)
                P.barrier()
            common(ph, 2, 256, ["fpre", "fpost"])
            fw = sb(ph, "fw", [128, 44, 3], F32)
            halP = sb(ph, "halP", [128, 44, 2], F32)
            halS = sb(ph, "halS", [128, 44, NS, 2], F32)
            sfs = sb(ph, "sfs", [3, 512], F32)
            for g11 in range(11):
                P.dma("sp", sfs[0:3, :], ffn_dw_w[:, g11 * 512:(g11 + 1) * 512], sfs, writes=[sfs])
                pb = PS[6]
                for c in range(4):
                    P.op("pe", lambda t, c=c: t.matmul(pb[:, c * 3:(c + 1) * 3], lhsT=sfs[0:3, c * 128:(c + 1) * 128],
                         rhs=identf[0:3, 0:3], start=True, stop=True), reads=[sfs, identf], writes=[pb], inc=(c == 3))
                P.op("dve", lambda v, g11=g11: v.tensor_copy(out=fw[:, g11 * 4:(g11 + 1) * 4, :],
                     in_=pb[:, 0:12].rearrange("p (c r) -> p c r", c=4)), reads=[pb], writes=[fw])
            for s in range(NS):
                for g11 in range(11):
                    P.dma("sp", sfs[0:2, :], sffn[s, :, g11 * 512:(g11 + 1) * 512], sfs, writes=[sfs])
                    pb = PS[7]
                    for c in range(4):
                        P.op("pe", lambda t, c=c: t.matmul(pb[:, c * 2:(c + 1) * 2], lhsT=sfs[0:2, c * 128:(c + 1) * 128],
                             rhs=identf[0:2, 0:2], start=True, stop=True), reads=[sfs, identf], writes=[pb], inc=(c == 3))
                    P.op("dve", lambda v, s=s, g11=g11: v.tensor_copy(out=halS[:, g11 * 4:(g11 + 1) * 4, s, :],
                         in_=pb[:, 0:8].rearrange("p (c r) -> p c r", c=4)), reads=[pb], writes=[halS])
            upb = [sb(ph, f"upb{k}", [128, 4, 66], F32) for k in range(2)]
            cv_ = [sb(ph, f"cvv{k}", [128, 256], F32) for k in range(2)]
            gl = sb(ph, "gl", [128, 256], F32)
            g2 = sb(ph, "g2", [128, 256], F32)
            gT = sb(ph, "gT", [128, NFC, 256], BF16)
            dn = [sb(ph, f"dn{k}", [128, D], F32) for k in range(2)]
            fst = [sb(ph, f"fst{k}", [2, 512], F32) for k in range(2)]
            ss3 = sb(ph, "ss3", [128, 2], F32)
            rs3 = sb(ph, "rs3", [128, 2], F32)
            P.op("dve", lambda v: v.memset(halP[:], 0.0), writes=[halP])
            ftiles = []
            for i in range(T // 256):
                ftiles.append(dict(r0=i * 256, N=256, segs=[(0, 256, None)], idx=i // 2, sample=False, last=(i == T // 256 - 1)))
            ftiles.append(dict(r0=T, N=TS, segs=[(s * 64, 64, s) for s in range(NS)], idx=NT, sample=True, last=True))
            for tl in ftiles:
                N = tl["N"]; ti = tl["idx"]; r0 = tl["r0"]; smp = tl["sample"]
                nsub = N // 128
                load_x(tl, src_fn=lambda tl, j: XMs[tl["r0"] + j * 128:tl["r0"] + (j + 1) * 128, :], trkb=trk["XM"][ti])
                norm_T(tl, gbc["fpre"])
                for j in range(NFC):
                    outs = []
                    for which in range(2):
                        ch = which * NFC + j
                        pb = PS[(2 * j + which) % 4]
                        proj_fm(Wup, ch * 128, hT, N, pb)
                        ub = upb[which]
                        if smp:
                            P.op("act", lambda a, ch=ch, ub=ub: a.copy(out=ub[:, :, 0:2], in_=halS[:, ch, :, :]), reads=[halS], writes=[ub])
                            P.op("act", lambda a, pb=pb, ub=ub: a.copy(out=ub[:, :, 2:66], in_=pb[:, 0:N].rearrange("p (s t) -> p s t", s=NS)),
                                 reads=[pb], writes=[ub])
                            src = lambda k, ub=ub: ub[:, :, k:k + 64]
                            o3 = lambda t_: t_[:, 0:N].rearrange("p (s t) -> p s t", s=NS)
                        else:
                            uf = ub[:, :, :].rearrange("p a b -> p (a b)")
                            P.op("act", lambda a, ch=ch, uf=uf: a.copy(out=uf[:, 0:2], in_=halP[:, ch, :]), reads=[halP], writes=[ub])
                            P.op("act", lambda a, pb=pb, uf=uf: a.copy(out=uf[:, 2:2 + N], in_=pb[:, 0:N]), reads=[pb], writes=[ub])
                            P.op("pool", lambda g, ch=ch, uf=uf: g.tensor_copy(out=halP[:, ch, :], in_=uf[:, N:N + 2]),
                                 reads=[ub], writes=[halP])
                            src = lambda k, uf=uf: uf[:, k:k + N]
                            o3 = lambda t_: t_[:, 0:N]
                        co = cv_[which]
                        P.op("dve", lambda v, co=co, src=src, o3=o3, ch=ch: v.tensor_scalar(out=o3(co), in0=src(0), scalar1=fw[:, ch, 0:1],
                             scalar2=0.0, op0=ALU.mult, op1=ALU.add), reads=[ub, fw], writes=[co])
                        for k in (1, 2):
                            P.op("dve", lambda v, co=co, src=src, o3=o3, ch=ch, k=k: v.scalar_tensor_tensor(out=o3(co), in0=src(k),
                                 scalar=fw[:, ch, k:k + 1], in1=o3(co), op0=ALU.mult, op1=ALU.add), reads=[ub, fw, co], writes=[co])
                        outs.append(co)
                    xg, xv = outs
                    P.op("pool", lambda g, xg=xg: g.tensor_tensor(out=g2[:, 0:N], in0=xg[:, 0:N], in1=xg[:, 0:N], op=ALU.mult),
                         reads=[xg], writes=[g2])
                    P.op("pool", lambda g: g.tensor_scalar(out=g2[:, 0:N], in0=g2[:, 0:N], scalar1=0.044715, scalar2=1.0,
                         op0=ALU.mult, op1=ALU.add), reads=[g2], writes=[g2])
                    P.op("pool", lambda g, xg=xg: g.tensor_tensor(out=g2[:, 0:N], in0=g2[:, 0:N], in1=xg[:, 0:N], op=ALU.mult),
                         reads=[g2, xg], writes=[g2])
                    P.op("act", lambda a: a.activation(out=gl[:, 0:N], in_=g2[:, 0:N], func=AF.Sigmoid, scale=1.5957691216057308),
                         reads=[g2], writes=[gl])
                    P.op("dve", lambda v, xg=xg: v.tensor_tensor(out=gl[:, 0:N], in0=gl[:, 0:N], in1=xg[:, 0:N], op=ALU.mult),
                         reads=[gl, xg], writes=[gl])
                    P.op("dve", lambda v, j=j, xv=xv: v.tensor_tensor(out=gT[:, j, 0:N], in0=gl[:, 0:N], in1=xv[:, 0:N], op=ALU.mult),
                         reads=[gl, xv], writes=[gT])
                ends = [(c0 + 62, s) for (c0, L, s) in tl["segs"]] if smp else ([(254, None)] if tl["last"] else [])
                for (t0, s) in ends:
                    for cb in range(11):
                        pb = PS[4 + cb % 2]
                        fb = fst[cb % 2]
                        proj_tm(Wup, cb * 512, hT, t0, pb, M=2)
                        P.op("dve", lambda v, fb=fb, pb=pb: v.tensor_copy(out=fb[:, :], in_=pb[0:2, :]), reads=[pb], writes=[fb])
                        dst = fs[s, :, cb * 512:(cb + 1) * 512] if smp else fp[:, cb * 512:(cb + 1) * 512]
                        P.dma("pool", dst, fb[:, :], fb, reads=[fb])
                for j in range(nsub):
                    dj = dn[j % 2]
                    for hf in range(2):
                        pb = PS[6 + hf]
                        proj_tm(Wdn, hf * 512, gT, j * 128, pb, K=NFC)
                        if hf == 0:
                            P.op("act", lambda a, dj=dj, pb=pb: a.copy(out=dj[:, 0:512], in_=pb[:]), reads=[pb], writes=[dj])
                        else:
                            P.op("dve", lambda v, dj=dj, pb=pb: v.tensor_copy(out=dj[:, 512:1024], in_=pb[:]), reads=[pb], writes=[dj])
                    P.op("dve", lambda v, j=j: v.memset(ss3[:, j:j + 1], 0.0), writes=[ss3])
                    P.op("act", lambda a, dj=dj, j=j: a.activation(out=cm["junk"][:], in_=dj[:], func=AF.Square, accum_out=ss3[:, j:j + 1]),
                         reads=[dj], writes=[cm["junk"], ss3])
                    rstd_from(ss3[:, j:j + 1], rs3[:, j:j + 1], ss3, rs3, D)
                    P.op("dve", lambda v, dj=dj, j=j: v.scalar_tensor_tensor(out=dj[:], in0=dj[:], scalar=rs3[:, j:j + 1],
                         in1=gbc["fpost"][:], op0=ALU.mult, op1=ALU.mult), reads=[dj, rs3, gbc["fpost"]], writes=[dj])
                    P.op("pool", lambda g, dj=dj, j=j: g.tensor_tensor(out=dj[:], in0=dj[:], in1=xt[j][:], op=ALU.add),
                         reads=[dj, xt[j]], writes=[dj])
                    dst = ys[r0 - T + j * 128:r0 - T + (j + 1) * 128, :] if smp else yp[r0 + j * 128:r0 + (j + 1) * 128, :]
                    P.dma("pool", dst, dj[:, :], dj, reads=[dj])
            P.barrier()
        P.finish()
    return nc


_CACHE = {}


def run(T, NS, PAST, per_core):
    key = (T, NS, PAST)
    if key not in _CACHE:
        _CACHE[key] = build(T, NS, PAST)
    nc = _CACHE[key]
    res = run_bass_kernel_spmd(nc, per_core, core_ids=list(range(len(per_core))))
    return res.results


WNAMES = ["g_mem", "w_mem_kv", "g_mix_pre", "g_mix_post", "w_in", "w_sb_o", "conv_dw_w", "conv_dw_b", "conv_ln_g",
          "conv_ln_b", "w_conv_o", "w_mem_o", "w_out", "g_ffn_pre", "g_ffn_post", "w_ffn_up", "ffn_dw_w", "w_ffn_down"]


def make_maps(inp, ncores, NS):
    f = lambda a: np.ascontiguousarray(np.asarray(a, dtype=np.float32))
    B = inp["x_prompt"].shape[0]
    maps = []
    for c in range(ncores):
        b = c % B
        sl = slice(c * NS, (c + 1) * NS)
        m = {"xp": f(inp["x_prompt"][b]), "xs": f(inp["x_sample"][sl]).reshape(NS * 64, D),
             "memp": f(inp["mem_prompt"][b]), "ck": f(inp["cache_sb_k"][0, sl]), "cv": f(inp["cache_sb_v"][0, sl]),
             "sconv": f(inp["state_conv"][0, sl]), "sffn": f(inp["state_ffn_conv"][0, sl]),
             "cmk": f(inp["cache_mem_k"][0, sl]), "cmv": f(inp["cache_mem_v"][0, sl])}
        for n in WNAMES:
            w = f(inp[n][0])
            m[n] = w.reshape(1, -1) if w.ndim == 1 else w
        maps.append(m)
    return maps


def assemble(res, B, ncores):
    cat = lambda n, rng: np.stack([res[c][n] for c in rng])
    pc = range(B)
    sc = range(ncores)
    yp = cat("yp", pc); ys = np.concatenate([res[c]["ys"].reshape(-1, 64, D) for c in sc])
    kp = cat("kp", pc)[None]; vp = cat("vp", pc)[None]
    ks = np.concatenate([res[c]["ks"] for c in sc])[None]; vs = np.concatenate([res[c]["vs"] for c in sc])[None]
    cp = cat("cp", pc)[None]; cs = np.concatenate([res[c]["cs"] for c in sc])[None]
    fp = cat("fp", pc)[None]; fs = np.concatenate([res[c]["fs"] for c in sc])[None]
    mkp = cat("mkp", pc)[None]; mvp = cat("mvp", pc)[None]
    return (yp, ys, kp, vp, ks, vs, cp, cs, fp, fs, mkp, mvp)


def kernel(**inputs):
    T = inputs["x_prompt"].shape[1]
    PAST = inputs["cache_sb_k"].shape[3]
    ncores = 8
    NS = inputs["x_sample"].shape[0] // ncores
    maps = make_maps(inputs, ncores, NS)
    res = run(T, NS, PAST, maps)
    return assemble(res, inputs["x_prompt"].shape[0], ncores)
```

```python
import contextlib
import numpy as np
import concourse.bass as bass
import concourse.mybir as mybir
from concourse.bass_utils import run_bass_kernel_spmd

F32 = mybir.dt.float32
BF16 = mybir.dt.bfloat16
ALU = mybir.AluOpType
AF = mybir.ActivationFunctionType

D = 1024
NCH = 8
FF = 2816
FF2 = 5632
NFC = 22
EPS = 1e-6


class _SkipPhase(Exception):
    pass


def _phase_gate(k):
    import os
    en = os.environ.get("KPH")
    if en is not None and str(k) not in en.split(","):
        raise _SkipPhase()


class Buf:
    def __init__(self, t, name):
        self.t = t
        self.name = name
        self.w = None
        self.r = {}
        self.dsem = None
        self.dcnt = 0
        self.wl = {} if t is None else None

    def __getitem__(self, k):
        return self.t[k]


class Prog:
    def __init__(self, nc, es):
        self.nc = nc
        self.es = es
        self.E = {"pe": nc.tensor, "act": nc.scalar, "dve": nc.vector, "pool": nc.gpsimd, "sp": nc.sync}
        self.sem = {e: es.enter_context(nc.semaphore("s_" + e)) for e in ("pe", "act", "dve", "pool")}
        self.cnt = {e: 0 for e in self.sem}
        self.seen = {e: {} for e in self.E}
        self.dbufs = []

    def _wait(self, e, dep, same_ok):
        if dep is None:
            return
        key, sem, val, src = dep
        if src is not None:
            val = 16 * src.dcnt
        elif key == e and not same_ok:
            return
        if self.seen[e].get(key, 0) >= val:
            return
        self.E[e].wait_ge(sem, val)
        self.seen[e][key] = val

    def _deps(self, e, reads, writes):
        for b in reads:
            self._wait(e, b.w, True)
            if b.wl:
                for d in list(b.wl.values()):
                    self._wait(e, d, True)
        for b in writes:
            self._wait(e, b.w, False)
            for d in list(b.r.values()):
                self._wait(e, d, False)

    def op(self, e, fn, reads=(), writes=(), inc=True):
        self._deps(e, reads, writes)
        ins = fn(self.E[e])
        if inc:
            self.cnt[e] += 1
            ins.then_inc(self.sem[e], 1)
            t = self.cnt[e]
        else:
            t = self.cnt[e] + 1
        dep = (e, self.sem[e], t, None)
        for b in reads:
            b.r[e] = dep
        for b in writes:
            b.w = dep
            b.r = {}
        return ins

    def dma(self, q, out_ap, in_ap, sbuf, reads=(), writes=()):
        self._deps(q, reads, writes)
        if sbuf.dsem is None:
            sbuf.dsem = self.es.enter_context(self.nc.semaphore("d_" + sbuf.name))
            self.dbufs.append(sbuf)
        sbuf.dcnt += 1
        self.E[q].dma_start(out=out_ap, in_=in_ap).then_inc(sbuf.dsem, 16)
        key = ("d", id(sbuf))
        dep = (key, sbuf.dsem, 16 * sbuf.dcnt, sbuf)
        for b in reads:
            b.r[key] = dep
        for b in writes:
            if b.wl is not None:
                b.wl[key] = dep
            else:
                b.w = dep
                b.r = {}

    def barrier(self):
        for e in self.E:
            for k in self.sem:
                if k != e and self.cnt[k] > self.seen[e].get(k, 0):
                    self.E[e].wait_ge(self.sem[k], self.cnt[k])
                    self.seen[e][k] = self.cnt[k]
            for b in self.dbufs:
                key = ("d", id(b))
                if 16 * b.dcnt > self.seen[e].get(key, 0):
                    self.E[e].wait_ge(b.dsem, 16 * b.dcnt)
                    self.seen[e][key] = 16 * b.dcnt

    def finish(self):
        for b in self.dbufs:
            self.E["sp"].wait_ge(b.dsem, 16 * b.dcnt)


def build(T, NS, PAST):
    nc = bass.Bass("TRN2", target_bir_lowering=False)
    TS = NS * 64
    NT = T // 512
    PB = PAST // 128

    def din(name, shape):
        return nc.dram_tensor(name, list(shape), F32, kind="ExternalInput").ap()

    def dout(name, shape):
        return nc.dram_tensor(name, list(shape), F32, kind="ExternalOutput").ap()

    xp = din("xp", [T, D]); xs = din("xs", [TS, D]); memp = din("memp", [256, D])
    ck = din("ck", [NS, 16, PAST, 64]); cv = din("cv", [NS, 16, PAST, 64])
    sconv = din("sconv", [NS, 30, D]); sffn = din("sffn", [NS, 2, FF2])
    cmk = din("cmk", [NS, 4, 256, 256]); cmv = din("cmv", [NS, 4, 256, 256])
    g_mem = din("g_mem", [1, D]); w_mem_kv = din("w_mem_kv", [D, 2048])
    g_mix_pre = din("g_mix_pre", [1, D]); g_mix_post = din("g_mix_post", [1, D])
    w_in = din("w_in", [D, 9216]); w_sb_o = din("w_sb_o", [D, D])
    conv_dw_w = din("conv_dw_w", [31, D]); conv_dw_b = din("conv_dw_b", [1, D])
    conv_ln_g = din("conv_ln_g", [1, D]); conv_ln_b = din("conv_ln_b", [1, D])
    w_conv_o = din("w_conv_o", [D, D]); w_mem_o = din("w_mem_o", [D, D]); w_out = din("w_out", [D, D])
    g_ffn_pre = din("g_ffn_pre", [1, D]); g_ffn_post = din("g_ffn_post", [1, D])
    w_ffn_up = din("w_ffn_up", [D, FF2]); ffn_dw_w = din("ffn_dw_w", [3, FF2]); w_ffn_down = din("w_ffn_down", [FF, D])

    yp = dout("yp", [T, D]); ys = dout("ys", [TS, D])
    kp = dout("kp", [16, T, 64]); vp = dout("vp", [16, T, 64])
    ks = dout("ks", [NS, 16, 64, 64]); vs = dout("vs", [NS, 16, 64, 64])
    cp = dout("cp", [30, D]); cs = dout("cs", [NS, 30, D])
    fp = dout("fp", [2, FF2]); fs = dout("fs", [NS, 2, FF2])
    mkp = dout("mkp", [4, 256, 256]); mvp = dout("mvp", [4, 256, 256])

    TT = T + TS
    KTs = nc.dram_tensor("KTs", [128, 8, TT], BF16).ap()
    VSs = nc.dram_tensor("VSs", [TT, D], BF16).ap()
    OSs = nc.dram_tensor("OSs", [128, 8, TT], BF16).ap()
    YCs = nc.dram_tensor("YCs", [128, 8, TT], F32).ap()
    YMs = nc.dram_tensor("YMs", [128, 8, TT], F32).ap()
    XMs = nc.dram_tensor("XMs", [TT, D], F32).ap()

    tiles = []
    for i in range(NT):
        tiles.append(dict(r0=i * 512, N=512, segs=[(0, 512, None)], idx=i, sample=False))
    tiles.append(dict(r0=T, N=TS, segs=[(s * 64, 64, s) for s in range(NS)], idx=NT, sample=True))
    ntile = len(tiles)
    trk = {n: [Buf(None, f"{n}{i}") for i in range(ntile)] for n in ("KT", "VS", "OS", "YC", "YM", "XM")}

    def xrows(tl, j):
        r = tl["r0"] + j * 128
        if tl["sample"]:
            return xs[r - T:r - T + 128, :]
        return xp[r:r + 128, :]

    es = contextlib.ExitStack()
    with es:
        P = Prog(nc, es)

        uid = [0]

        def sb(st, name, shape, dt):
            uid[0] += 1
            name = f"{name}_{uid[0]}"
            return Buf(st.enter_context(nc.sbuf_tensor(name, list(shape), dt)), name)

        PS = [Buf(es.enter_context(nc.psum_tensor(f"ps{i}", [128, 512], F32)), f"ps{i}") for i in range(8)]

        identb = sb(es, "identb", [128, 128], BF16)
        identf = sb(es, "identf", [128, 128], F32)
        negtri = sb(es, "negtri", [128, 128], BF16)
        negones = sb(es, "negones", [128, 128], BF16)
        onesb = sb(es, "onesb", [128, 128], BF16)
        onesf = sb(es, "onesf", [128, 128], F32)
        for bfr, val in ((identb, 1.0), (identf, 1.0), (negtri, -1.0), (negones, -1.0), (onesb, 1.0),
                         (onesf, 1.0 / D)):
            P.op("pool", lambda g, b=bfr, v=val: g.memset(b[:], v), writes=[bfr])
        for bfr in (identb, identf):
            P.op("pool", lambda g, b=bfr: g.affine_select(out=b[:], in_=b[:], pattern=[[-1, 128]],
                 compare_op=ALU.is_equal, fill=0.0, base=0, channel_multiplier=1), reads=[bfr], writes=[bfr])
        P.op("pool", lambda g: g.affine_select(out=negtri[:], in_=negtri[:], pattern=[[-1, 128]],
             compare_op=ALU.is_ge, fill=0.0, base=0, channel_multiplier=1), reads=[negtri], writes=[negtri])

        gbc = {}
        gsrc = {"pre": g_mix_pre, "post": g_mix_post, "fpre": g_ffn_pre, "fpost": g_ffn_post, "mem": g_mem}
        xt = [None] * 4
        xn = [None] * 2
        cm = {}

        class _HT:
            def __getitem__(self, k):
                return cm["hT"].t[k]
        hT = _HT()

        def common(ph, nsub, N, gs):
            for j in range(nsub):
                xt[j] = sb(ph, f"xt{j}", [128, D], F32)
            for j in range(2):
                xn[j] = sb(ph, f"xn{j}", [128, D], BF16)
            cm["hT"] = sb(ph, "hT", [128, 8, N], BF16)
            cm["junk"] = sb(ph, "junk", [128, D], BF16)
            cm["ssq"] = sb(ph, "ssq", [128, 4], F32)
            cm["rstd"] = sb(ph, "rstd", [128, 4], F32)
            for nm in gs:
                gbc[nm] = sb(ph, "g_" + nm, [128, D], F32)
                P.dma("sp", gbc[nm][:], gsrc[nm][0:1, :].partition_broadcast(128), gbc[nm], writes=[gbc[nm]])

        def rstd_from(ss_ap, out_ap, ssb, outb, n):
            P.op("act", lambda a: a.activation(out=out_ap, in_=ss_ap, func=AF.Ln, scale=1.0 / n, bias=EPS),
                 reads=[ssb], writes=[outb])
            P.op("act", lambda a: a.activation(out=out_ap, in_=out_ap, func=AF.Exp, scale=-0.5),
                 reads=[outb], writes=[outb])

        def load_x(tl, src_fn=None, trkb=None):
            nsub = tl["N"] // 128
            for j in range(nsub):
                src = src_fn(tl, j) if src_fn else xrows(tl, j)
                P.dma("sp", xt[j][:], src, xt[j], reads=[trkb] if trkb else [], writes=[xt[j]])

        def norm_T(tl, g):
            nsub = tl["N"] // 128
            junk, ssq, rstd, hTb = cm["junk"], cm["ssq"], cm["rstd"], cm["hT"]
            P.op("dve", lambda v: v.memset(ssq[:], 0.0), writes=[ssq])
            for j in range(nsub):
                P.op("act", lambda a, j=j: a.activation(out=junk[:], in_=xt[j][:], func=AF.Square,
                     accum_out=ssq[:, j:j + 1]), reads=[xt[j]], writes=[junk, ssq])
            rstd_from(ssq[:, 0:nsub], rstd[:, 0:nsub], ssq, rstd, D)
            for j in range(nsub):
                xb = xn[j % 2]
                P.op("dve", lambda v, j=j, xb=xb: v.scalar_tensor_tensor(out=xb[:], in0=xt[j][:],
                     scalar=rstd[:, j:j + 1], in1=g[:], op0=ALU.mult, op1=ALU.mult),
                     reads=[xt[j], rstd, g], writes=[xb])
                for half in range(2):
                    pb = PS[6 + half]
                    for cc in range(4):
                        c = half * 4 + cc
                        P.op("pe", lambda t, c=c, cc=cc, pb=pb, xb=xb: t.matmul(pb[:, cc * 128:(cc + 1) * 128],
                             lhsT=xb[:, c * 128:(c + 1) * 128], rhs=identb[:], start=True, stop=True),
                             reads=[xb, identb], writes=[pb], inc=(cc == 3))
                    eng = "act" if half == 0 else "dve"
                    if eng == "act":
                        P.op("act", lambda a, half=half, pb=pb, j=j: a.copy(
                             out=hT[:, half * 4:half * 4 + 4, j * 128:(j + 1) * 128],
                             in_=pb[:].rearrange("p (c t) -> p c t", c=4)), reads=[pb], writes=[hTb])
                    else:
                        P.op("dve", lambda v, half=half, pb=pb, j=j: v.tensor_copy(
                             out=hT[:, half * 4:half * 4 + 4, j * 128:(j + 1) * 128],
                             in_=pb[:].rearrange("p (c t) -> p c t", c=4)), reads=[pb], writes=[hTb])

        wst = [None, None]
        wcnt = [0]

        def load_w(dst, src2d, nrc, ncols, dcol0=0):
            for rc in range(nrc):
                for cb in range(0, ncols, 2048):
                    w = min(2048, ncols - cb)
                    k = wcnt[0] % 2
                    wcnt[0] += 1
                    st = wst[k]
                    P.dma("sp", st[:, 0:w], src2d[rc * 128:(rc + 1) * 128, cb:cb + w], st, writes=[st])
                    eng = "dve" if k == 0 else "pool"
                    P.op(eng, lambda v, st=st, rc=rc, cb=cb, w=w: v.tensor_copy(
                         out=dst[:, rc, dcol0 + cb:dcol0 + cb + w], in_=st[:, 0:w]), reads=[st], writes=[dst])

        def load_cols(dst, srcs, R, nchunk, stg):
            r0 = 0
            for ap, nr in srcs:
                P.dma("sp", stg[r0:r0 + nr, 0:nchunk * 128], ap, stg, writes=[stg])
                r0 += nr
            pb = PS[6]
            for c in range(nchunk):
                P.op("pe", lambda t, c=c: t.matmul(pb[:, c * R:(c + 1) * R], lhsT=stg[0:R, c * 128:(c + 1) * 128],
                     rhs=identf[0:R, 0:R], start=True, stop=True), reads=[stg, identf], writes=[pb],
                     inc=(c == nchunk - 1))
            P.op("dve", lambda v: v.tensor_copy(out=dst[:].rearrange("p c r -> p (c r)"),
                 in_=pb[:, 0:nchunk * R]), reads=[pb], writes=[dst])

        def proj_fm(W, col0, rhsT, N, pb, K=NCH):
            rb = cm["hT"] if rhsT is hT else rhsT
            for c in range(K):
                P.op("pe", lambda t, c=c: t.matmul(pb[:, 0:N], lhsT=W[:, c, col0:col0 + 128], rhs=rhsT[:, c, 0:N],
                     start=(c == 0), stop=(c == K - 1)), reads=[W, rb], writes=[pb], inc=(c == K - 1))

        def proj_tm(W, col0, lhs, t0, pb, K=NCH, M=128):
            lb = cm["hT"] if lhs is hT else lhs
            for c in range(K):
                P.op("pe", lambda t, c=c: t.matmul(pb[0:M, :], lhsT=lhs[:, c, t0:t0 + M], rhs=W[:, c, col0:col0 + 512],
                     start=(c == 0), stop=(c == K - 1)), reads=[W, lb], writes=[pb], inc=(c == K - 1))

        memst = contextlib.ExitStack()
        mkT = sb(memst, "mkT", [128, 8, 256], BF16)
        mvb = sb(memst, "mvb", [128, 2, D], BF16)
        with contextlib.suppress(_SkipPhase), contextlib.ExitStack() as ph:
            _phase_gate(0)
            Wm = sb(ph, "Wm", [128, 8, 2048], BF16)
            with contextlib.ExitStack() as ws:
                wst[0] = sb(ws, "wst0", [128, 2048], F32); wst[1] = sb(ws, "wst1", [128, 2048], F32)
                load_w(Wm, w_mem_kv, 8, 2048)
                P.barrier()
            mtok = sb(ph, "mtok", [128, 2048], F32)
            common(ph, 2, 256, ["mem"])
            mt = dict(r0=0, N=256, segs=[], sample=False)
            load_x(mt, src_fn=lambda tl, j: memp[j * 128:(j + 1) * 128, :])
            norm_T(mt, gbc["mem"])
            for j in range(2):
                for hf in range(4):
                    pb = PS[hf % 4]
                    proj_tm(Wm, hf * 512, hT, j * 128, pb)
                    P.op("act" if hf % 2 else "dve", (lambda a, hf=hf, pb=pb: a.copy(out=mtok[:, hf * 512:(hf + 1) * 512], in_=pb[:])) if hf % 2
                         else (lambda v, hf=hf, pb=pb: v.tensor_copy(out=mtok[:, hf * 512:(hf + 1) * 512], in_=pb[:])),
                         reads=[pb], writes=[mtok])
                P.op("pool", lambda g, j=j: g.tensor_copy(out=mvb[:, j, :], in_=mtok[:, 1024:2048]),
                     reads=[mtok], writes=[mvb])
                P.dma("pool", mkp[:, j * 128:(j + 1) * 128, :].rearrange("h m d -> m h d"),
                      mtok[:, 0:1024].rearrange("m (h d) -> m h d", h=4), mtok, reads=[mtok])
                P.dma("pool", mvp[:, j * 128:(j + 1) * 128, :].rearrange("h m d -> m h d"),
                      mtok[:, 1024:2048].rearrange("m (h d) -> m h d", h=4), mtok, reads=[mtok])
            for cc in range(8):
                pb = PS[cc % 4]
                proj_fm(Wm, cc * 128, hT, 256, pb)
                P.op("act", lambda a, cc=cc, pb=pb: a.copy(out=mkT[:, cc, :], in_=pb[:, 0:256]), reads=[pb], writes=[mkT])

            P.barrier()
        with contextlib.suppress(_SkipPhase), contextlib.ExitStack() as ph:
            _phase_gate(1)
            Wq = sb(ph, "Wqm", [128, 8, D], BF16)
            Wmo = sb(ph, "Wmo", [128, 8, D], BF16)
            with contextlib.ExitStack() as ws:
                wst[0] = sb(ws, "wst0", [128, 2048], F32); wst[1] = sb(ws, "wst1", [128, 2048], F32)
                load_w(Wq, w_in[:, 5120:6144], 8, D)
                load_w(Wmo, w_mem_o, 8, D)
                P.barrier()
            common(ph, 4, 512, ["pre"])
            qmT = sb(ph, "qmT", [128, 8, 512], BF16)
            pT = [sb(ph, f"pT{k}", [128, 2, 512], BF16) for k in range(2)]
            omT = sb(ph, "omT", [128, 8, 512], BF16)
            rden = [sb(ph, f"rden{k}", [128, 512], F32) for k in range(2)]
            yst = [sb(ph, f"ystm{k}", [128, 512], F32) for k in range(2)]
            smkT = sb(ph, "smkT", [128, 8, 256], BF16)
            smvb = sb(ph, "smvb", [128, 2, D], BF16)
            cmst = [sb(ph, f"cmst{k}", [128, 2, 256], F32) for k in range(2)]

            def mem_attn(kT, vB, c0, L):
                for hm in range(4):
                    pk = pT[hm % 2]
                    for mc in range(2):
                        pb = PS[mc]
                        for dc in range(2):
                            P.op("pe", lambda t, mc=mc, dc=dc, pb=pb: t.matmul(pb[:, 0:L],
                                 lhsT=kT[:, hm * 2 + dc, mc * 128:(mc + 1) * 128], rhs=qmT[:, hm * 2 + dc, c0:c0 + L],
                                 start=(dc == 0), stop=(dc == 1)), reads=[kT, qmT], writes=[pb], inc=(dc == 1))
                        P.op("act", lambda a, mc=mc, pb=pb, pk=pk: a.activation(out=pk[:, mc, 0:L], in_=pb[:, 0:L], func=AF.Exp),
                             reads=[pb], writes=[pk])
                    pd = PS[2]
                    for mc in range(2):
                        P.op("pe", lambda t, mc=mc, pk=pk: t.matmul(pd[:, 0:L], lhsT=onesb[:], rhs=pk[:, mc, 0:L],
                             start=(mc == 0), stop=(mc == 1)), reads=[onesb, pk], writes=[pd], inc=(mc == 1))
                    rd = rden[hm % 2]
                    P.op("dve", lambda v, rd=rd: v.reciprocal(out=rd[:, 0:L], in_=pd[:, 0:L]), reads=[pd], writes=[rd])
                    for dc in range(2):
                        po = PS[4 + dc]
                        for mc in range(2):
                            P.op("pe", lambda t, mc=mc, dc=dc, po=po, pk=pk: t.matmul(po[:, 0:L],
                                 lhsT=vB[:, mc, hm * 256 + dc * 128:hm * 256 + dc * 128 + 128], rhs=pk[:, mc, 0:L],
                                 start=(mc == 0), stop=(mc == 1)), reads=[vB, pk], writes=[po], inc=(mc == 1))
                        P.op("dve", lambda v, dc=dc, po=po, rd=rd: v.tensor_tensor(out=omT[:, hm * 2 + dc, c0:c0 + L],
                             in0=po[:, 0:L], in1=rd[:, 0:L], op=ALU.mult), reads=[po, rd], writes=[omT])

            for tl in tiles:
                N = tl["N"]; ti = tl["idx"]; smp = tl["sample"]
                load_x(tl)
                norm_T(tl, gbc["pre"])
                for c2 in range(8):
                    pb = PS[c2 % 4]
                    proj_fm(Wq, c2 * 128, hT, N, pb)
                    P.op("act", lambda a, c2=c2, pb=pb: a.activation(out=qmT[:, c2, 0:N], in_=pb[:, 0:N], func=AF.Copy,
                         scale=1.0 / 16.0), reads=[pb], writes=[qmT])
                if not smp:
                    mem_attn(mkT, mvb, 0, N)
                else:
                    for (c0, L, s) in tl["segs"]:
                        for hm in range(4):
                            st = cmst[hm % 2]
                            P.dma("sp", st[:, :, :], cmk[s, hm, :, :].rearrange("(j m) d -> m j d", j=2), st, writes=[st])
                            pb = PS[6 + hm % 2]
                            for dc in range(2):
                                for j in range(2):
                                    P.op("pe", lambda t, dc=dc, j=j, pb=pb, st=st: t.matmul(
                                         pb[:, dc * 256 + j * 128:dc * 256 + j * 128 + 128],
                                         lhsT=st[:, j, dc * 128:(dc + 1) * 128], rhs=identf[:], start=True, stop=True),
                                         reads=[st, identf], writes=[pb], inc=(dc == 1 and j == 1))
                            P.op("dve", lambda v, hm=hm, pb=pb: v.tensor_copy(out=smkT[:, hm * 2:hm * 2 + 2, :],
                                 in_=pb[:].rearrange("p (c m) -> p c m", c=2)), reads=[pb], writes=[smkT])
                            st2 = cmst[(hm + 1) % 2]
                            P.dma("sp", st2[:, :, :], cmv[s, hm, :, :].rearrange("(j m) d -> m j d", j=2), st2, writes=[st2])
                            P.op("pool", lambda g, hm=hm, st2=st2: g.tensor_copy(out=smvb[:, :, hm * 256:(hm + 1) * 256],
                                 in_=st2[:, :, :]), reads=[st2], writes=[smvb])
                        mem_attn(smkT, smvb, c0, L)
                for c2 in range(8):
                    pb = PS[c2 % 4]
                    proj_fm(Wmo, c2 * 128, omT, N, pb)
                    y = yst[c2 % 2]
                    P.op("act", lambda a, pb=pb, y=y: a.copy(out=y[:, 0:N], in_=pb[:, 0:N]), reads=[pb], writes=[y])
                    P.dma("pool", YMs[:, c2, tl["r0"]:tl["r0"] + N], y[:, 0:N], y, reads=[y], writes=[trk["YM"][ti]])
            P.barrier()
        P.barrier()
        memst.close()

        with contextlib.suppress(_SkipPhase), contextlib.ExitStack() as ph:
            _phase_gate(2)
            Wc = sb(ph, "Wc", [128, 8, 2048], BF16)
            Wco = sb(ph, "Wco", [128, 8, D], BF16)
            with contextlib.ExitStack() as ws:
                wst[0] = sb(ws, "wst0", [128, 2048], F32); wst[1] = sb(ws, "wst1", [128, 2048], F32)
                load_w(Wc, w_in[:, 3072:5120], 8, 2048)
                load_w(Wco, w_conv_o, 8, D)
                P.barrier()
            common(ph, 4, 512, ["pre"])
            cw = sb(ph, "cw", [128, 8, 31], F32)
            cvec = sb(ph, "cvec", [128, 8, 3], F32)
            with contextlib.ExitStack() as ws:
                stg = sb(ws, "stg", [31, D], F32)
                load_cols(cw, [(conv_dw_w[:, :], 31)], 31, 8, stg)
                load_cols(cvec, [(conv_dw_b[0:1, :], 1), (conv_ln_g[0:1, :], 1), (conv_ln_b[0:1, :], 1)], 3, 8, stg)
                P.barrier()
            uP = sb(ph, "uP", [128, 8, 542], F32)
            uS = sb(ph, "uS", [128, 8, NS, 94], F32)
            cc_ = sb(ph, "cc", [128, 8, 512], F32)
            csq = [sb(ph, f"csq{k}", [128, 512], F32) for k in range(2)]
            ccT = sb(ph, "ccT", [128, 8, 512], BF16)
            sg = [sb(ph, f"sg{k}", [128, 512], F32) for k in range(2)]
            mean = sb(ph, "mean", [128, 512], F32)
            rs = sb(ph, "rs", [128, 512], F32)
            t1 = [sb(ph, f"t1{k}", [128, 512], F32) for k in range(2)]
            yst = [sb(ph, f"yst{k}", [128, 512], F32) for k in range(2)]
            cst = sb(ph, "cst", [30, D], F32)
            sst = sb(ph, "sst", [30, D], F32)
            P.op("dve", lambda v: v.memset(uP[:, :, 0:30], 0.0), writes=[uP])
            for tl in tiles:
                N = tl["N"]; ti = tl["idx"]; smp = tl["sample"]
                load_x(tl)
                norm_T(tl, gbc["pre"])
                if smp:
                    for s in range(NS):
                        P.dma("sp", sst[:, :], sconv[s, :, :], sst, writes=[sst])
                        pb = PS[4 + s % 2]
                        for c in range(8):
                            P.op("pe", lambda t, c=c, pb=pb: t.matmul(pb[:, c * 30:(c + 1) * 30],
                                 lhsT=sst[0:30, c * 128:(c + 1) * 128], rhs=identf[0:30, 0:30], start=True, stop=True),
                                 reads=[sst, identf], writes=[pb], inc=(c == 7))
                        P.op("dve", lambda v, s=s, pb=pb: v.tensor_copy(out=uS[:, :, s, 0:30],
                             in_=pb[:, 0:240].rearrange("p (c r) -> p c r", c=8)), reads=[pb], writes=[uS])
                for c2 in range(8):
                    pa, pbb = PS[(2 * c2) % 4], PS[(2 * c2 + 1) % 4]
                    proj_fm(Wc, c2 * 128, hT, N, pa)
                    proj_fm(Wc, 1024 + c2 * 128, hT, N, pbb)
                    sgk = sg[c2 % 2]
                    P.op("act", lambda a, pbb=pbb, sgk=sgk: a.activation(out=sgk[:, 0:N], in_=pbb[:, 0:N], func=AF.Sigmoid),
                         reads=[pbb], writes=[sgk])
                    if smp:
                        P.op("dve", lambda v, c2=c2, pa=pa, sgk=sgk: v.tensor_tensor(out=uS[:, c2, :, 30:94],
                             in0=pa[:, 0:N].rearrange("p (s t) -> p s t", s=NS),
                             in1=sgk[:, 0:N].rearrange("p (s t) -> p s t", s=NS), op=ALU.mult),
                             reads=[pa, sgk], writes=[uS])
                    else:
                        P.op("dve", lambda v, c2=c2, pa=pa, sgk=sgk: v.tensor_tensor(out=uP[:, c2, 30:542],
                             in0=pa[:, 0:N], in1=sgk[:, 0:N], op=ALU.mult), reads=[pa, sgk], writes=[uP])
                for c2 in range(8):
                    eng = "dve"
                    for (c0, L, s) in tl["segs"]:
                        usrc = (lambda k, c2=c2, s=s, L=L: uS[:, c2, s, k:k + L]) if smp else (lambda k, c2=c2, L=L: uP[:, c2, k:k + L])
                        ub = uS if smp else uP
                        P.op(eng, lambda v, c2=c2, c0=c0, L=L, usrc=usrc: v.tensor_scalar(out=cc_[:, c2, c0:c0 + L], in0=usrc(0),
                             scalar1=cw[:, c2, 0:1], scalar2=cvec[:, c2, 0:1], op0=ALU.mult, op1=ALU.add),
                             reads=[ub, cw, cvec], writes=[cc_])
                        for k in range(1, 31):
                            P.op(eng, lambda v, c2=c2, c0=c0, L=L, k=k, usrc=usrc: v.scalar_tensor_tensor(
                                 out=cc_[:, c2, c0:c0 + L], in0=usrc(k), scalar=cw[:, c2, k:k + 1],
                                 in1=cc_[:, c2, c0:c0 + L], op0=ALU.mult, op1=ALU.add),
                                 reads=[ub, cw, cc_], writes=[cc_])
                pm, pq = PS[4], PS[5]
                for c2 in range(8):
                    q = csq[c2 % 2]
                    P.op("act", lambda a, c2=c2, q=q: a.activation(out=q[:, 0:N], in_=cc_[:, c2, 0:N], func=AF.Square),
                         reads=[cc_], writes=[q])
                    P.op("pe", lambda t, c2=c2: t.matmul(pm[:, 0:N], lhsT=onesf[:], rhs=cc_[:, c2, 0:N],
                         start=(c2 == 0), stop=(c2 == 7)), reads=[onesf, cc_], writes=[pm])
                    P.op("pe", lambda t, c2=c2, q=q: t.matmul(pq[:, 0:N], lhsT=onesf[:], rhs=q[:, 0:N],
                         start=(c2 == 0), stop=(c2 == 7)), reads=[onesf, q], writes=[pq])
                P.op("act", lambda a: a.copy(out=mean[:, 0:N], in_=pm[:, 0:N]), reads=[pm], writes=[mean])
                P.op("dve", lambda v: v.tensor_tensor(out=rs[:, 0:N], in0=mean[:, 0:N], in1=mean[:, 0:N], op=ALU.mult),
                     reads=[mean], writes=[rs])
                P.op("dve", lambda v: v.tensor_tensor(out=rs[:, 0:N], in0=pq[:, 0:N], in1=rs[:, 0:N], op=ALU.subtract),
                     reads=[pq, rs], writes=[rs])
                rstd_from(rs[:, 0:N], rs[:, 0:N], rs, rs, 1.0)
                for c2 in range(8):
                    tk = t1[c2 % 2]
                    P.op("dve", lambda v, c2=c2, tk=tk: v.tensor_tensor(out=tk[:, 0:N], in0=cc_[:, c2, 0:N], in1=mean[:, 0:N],
                         op=ALU.subtract), reads=[cc_, mean], writes=[tk])
                    P.op("dve", lambda v, tk=tk: v.tensor_tensor(out=tk[:, 0:N], in0=tk[:, 0:N], in1=rs[:, 0:N], op=ALU.mult),
                         reads=[tk, rs], writes=[tk])
                    P.op("act", lambda a, c2=c2, tk=tk: a.activation(out=ccT[:, c2, 0:N], in_=tk[:, 0:N], func=AF.Silu,
                         scale=cvec[:, c2, 1:2], bias=cvec[:, c2, 2:3]), reads=[tk, cvec], writes=[ccT])
                for c2 in range(8):
                    pb = PS[c2 % 4]
                    proj_fm(Wco, c2 * 128, ccT, N, pb)
                    y = yst[c2 % 2]
                    P.op("act", lambda a, pb=pb, y=y: a.copy(out=y[:, 0:N], in_=pb[:, 0:N]), reads=[pb], writes=[y])
                    P.dma("pool", YCs[:, c2, tl["r0"]:tl["r0"] + N], y[:, 0:N], y, reads=[y], writes=[trk["YC"][ti]])
                ends = [(s, 64) for (_, _, s) in tl["segs"]] if smp else ([(None, 512)] if ti == NT - 1 else [])
                for (s, L) in ends:
                    for half in range(2):
                        pb = PS[4 + half]
                        for cq in range(4):
                            c2 = half * 4 + cq
                            src = uS[:, c2, s, L:L + 30] if smp else uP[:, c2, L:L + 30]
                            P.op("pe", lambda t, cq=cq, pb=pb, src=src: t.matmul(pb[0:30, cq * 128:(cq + 1) * 128], lhsT=src,
                                 rhs=identf[:], start=True, stop=True), reads=[uS if smp else uP, identf], writes=[pb],
                                 inc=(cq == 3))
                        P.op("dve", lambda v, half=half, pb=pb: v.tensor_copy(out=cst[:, half * 512:(half + 1) * 512],
                             in_=pb[0:30, :]), reads=[pb], writes=[cst])
                    P.dma("pool", cs[s, :, :] if smp else cp[:, :], cst[:, :], cst, reads=[cst])
                if not smp:
                    P.op("dve", lambda v: v.tensor_copy(out=uP[:, :, 0:30], in_=uP[:, :, 512:542]), reads=[uP], writes=[uP])

            P.barrier()
        with contextlib.suppress(_SkipPhase), contextlib.ExitStack() as ph:
            _phase_gate(3)
            Wqkv = sb(ph, "Wqkv", [128, 8, 3072], BF16)
            with contextlib.ExitStack() as ws:
                wst[0] = sb(ws, "wst0", [128, 2048], F32); wst[1] = sb(ws, "wst1", [128, 2048], F32)
                load_w(Wqkv, w_in[:, 0:3072], 8, 3072)
                P.barrier()
            common(ph, 4, 512, ["pre"])
            masks = sb(ph, "masks", [128, 4, 512], BF16)
            P.op("pool", lambda g: g.memset(masks[:], 1.0), writes=[masks])
            for r in range(4):
                P.op("pool", lambda g, r=r: g.affine_select(out=masks[:, r, :], in_=masks[:, r, :], pattern=[[1, 512]],
                     compare_op=ALU.is_gt, fill=0.0, base=-128 * r, channel_multiplier=-1), reads=[masks], writes=[masks])
            qT = sb(ph, "qT", [128, 8, 512], BF16)
            kT = sb(ph, "kT", [128, 8, 512], BF16)
            tok = [sb(ph, f"tok{k}", [128, D], F32) for k in range(2)]
            vbf = [sb(ph, f"vbf{k}", [128, D], BF16) for k in range(2)]
            osb = sb(ph, "osb", [128, 8, 512], BF16)
            Kt = [sb(ph, f"Kt{k}", [128, 2, 512], BF16) for k in range(2)]
            Vt = [sb(ph, f"Vt{k}", [128, 4, 2, 128], BF16) for k in range(2)]
            Vs = [sb(ph, f"Vs{k}", [128, 4, 128], BF16) for k in range(2)]
            for k in range(2):
                P.op("pool", lambda g, k=k: g.memset(Kt[k][:], 0.0), writes=[Kt[k]])
                P.op("pool", lambda g, k=k: g.memset(Vt[k][:], 0.0), writes=[Vt[k]])
            Ee = [sb(ph, f"Ee{k}", [128, 512], F32) for k in range(2)]
            Sp = [sb(ph, f"Sp{k}", [128, 512], BF16) for k in range(2)]
            Ls = [[sb(ph, f"Ls{h}{k}", [128, 512], BF16) for k in range(2)] for h in range(2)]
            Aa = [sb(ph, f"Aa{k}", [128, 512], BF16) for k in range(2)]
            kc = [sb(ph, f"kc{k}", [128, 4, 128], F32) for k in range(2)]
            vc = [sb(ph, f"vc{k}", [128, 2, 4, 64], F32) for k in range(2)]
            PC = [[PS[0], PS[1]], [PS[2], PS[3]]]; PO = PS[4]

            def S1(u):
                hh, p, Ktb, kcols, qcols, N, mask_ap = u["hh"], u["p"], u["Ktb"], u["kcols"], u["qcols"], u["N"], u["mask"]
                z, e, s_ = PC[hh][u["lsi"] % 2], Ee[hh], Sp[hh]
                P.op("pe", lambda t: t.matmul(z[:, 0:N], lhsT=Ktb[:, hh, kcols], rhs=qT[:, p, qcols], start=True, stop=False),
                     reads=[Ktb, qT], writes=[z])
                P.op("act", lambda a: a.activation(out=e[:, 0:N], in_=z[:, 0:N], func=AF.Exp), reads=[z], writes=[e])
                P.op("act", lambda a: a.activation(out=s_[:, 0:N], in_=e[:, 0:N], func=AF.Ln, bias=1.0, scale=1.0),
                     reads=[e], writes=[s_])
                if mask_ap is not None:
                    P.op("dve", lambda v: v.tensor_tensor(out=s_[:, 0:N], in0=s_[:, 0:N], in1=mask_ap, op=ALU.mult),
                         reads=[s_, masks], writes=[s_])

            def S2a(u):
                hh, N = u["hh"], u["N"]
                first, last, lsi = u["first"], u["last"], u["lsi"]
                cb_, s_, a_ = PC[hh][lsi % 2], Sp[hh], Aa[hh]
                lo, ln = Ls[hh][lsi % 2], Ls[hh][(lsi + 1) % 2]
                P.op("pe", lambda t: t.matmul(cb_[:, 0:N], lhsT=negtri[:, :], rhs=s_[:, 0:N], start=False, stop=first),
                     reads=[negtri, s_], writes=[cb_], inc=first)
                if not first:
                    P.op("pe", lambda t: t.matmul(cb_[:, 0:N], lhsT=negones[:, :], rhs=lo[:, 0:N], start=False, stop=True),
                         reads=[negones, lo], writes=[cb_])
                if not last:
                    if first:
                        P.op("pool", lambda g: g.tensor_copy(out=ln[:, 0:N], in_=s_[:, 0:N]), reads=[s_], writes=[ln])
                    else:
                        P.op("dve", lambda v: v.tensor_tensor(out=ln[:, 0:N], in0=lo[:, 0:N], in1=s_[:, 0:N], op=ALU.add),
                             reads=[lo, s_], writes=[ln])
                P.op("act", lambda a: a.activation(out=a_[:, 0:N], in_=cb_[:, 0:N], func=AF.Exp), reads=[cb_], writes=[a_])

            def S2b(u):
                hh, Vb, vr, N, mask_ap, first, last = u["hh"], u["Vb"], u["vr"], u["N"], u["mask"], u["first"], u["last"]
                hp = slice(hh * 64, hh * 64 + 64)
                a_ = Aa[hh]
                if mask_ap is not None:
                    P.op("dve", lambda v: v.tensor_tensor(out=a_[:, 0:N], in0=a_[:, 0:N], in1=mask_ap, op=ALU.mult),
                         reads=[a_, masks], writes=[a_])
                P.op("pe", lambda t: t.matmul(PO[:, 0:N], lhsT=Vb[:, vr, hh, :], rhs=a_[:, 0:N], start=(first and hh == 0),
                     stop=(last and hh == 1)), reads=[Vb, a_], writes=[PO], inc=(last and hh == 1))

            def emit_units(units):
                n = len(units)
                for i in range(n + 2):
                    if i < n:
                        if units[i].get("pre"):
                            units[i]["pre"]()
                        S1(units[i])
                    if 0 <= i - 1 < n:
                        S2a(units[i - 1])
                    if 0 <= i - 2 < n:
                        S2b(units[i - 2])

            ldc = [0]

            def load_kv(p, tj, ti):
                k = ldc[0] % 2
                ldc[0] += 1
                r0 = tiles[tj]["r0"]
                for hh in range(2):
                    P.dma("sp", Kt[k][hh * 64:(hh + 1) * 64, hh, :], KTs[hh * 64:(hh + 1) * 64, p, r0:r0 + 512], Kt[k],
                          reads=[trk["KT"][tj]], writes=[Kt[k]])
                P.dma("sp", Vs[k][:, :, :], VSs[r0:r0 + 512, p * 128:(p + 1) * 128].rearrange("(r s) c -> s r c", r=4),
                      Vs[k], reads=[trk["VS"][tj]], writes=[Vs[k]])
                for hh in range(2):
                    P.op("pool", lambda g, hh=hh: g.tensor_copy(out=Vt[k][:, :, hh, hh * 64:(hh + 1) * 64],
                         in_=Vs[k][:, :, hh * 64:(hh + 1) * 64]), reads=[Vs[k]], writes=[Vt[k]])
                return k

            for tl in tiles:
                N = tl["N"]; ti = tl["idx"]; smp = tl["sample"]; r0 = tl["r0"]
                nsub = N // 128
                load_x(tl)
                norm_T(tl, gbc["pre"])
                for p in range(8):
                    pq_, pk_ = PS[5], PS[6]
                    proj_fm(Wqkv, p * 128, hT, N, pq_)
                    P.op("act", lambda a, p=p: a.activation(out=qT[:, p, 0:N], in_=pq_[:, 0:N], func=AF.Copy, scale=0.125),
                         reads=[pq_], writes=[qT])
                    proj_fm(Wqkv, 1024 + p * 128, hT, N, pk_)
                    P.op("dve", lambda v, p=p: v.tensor_copy(out=kT[:, p, 0:N], in_=pk_[:, 0:N]), reads=[pk_], writes=[kT])
                P.dma("pool", KTs[:, :, r0:r0 + N], kT[:, :, 0:N], kT, reads=[kT], writes=[trk["KT"][ti]])
                for j in range(nsub):
                    for which in range(2):
                        tk = tok[which]
                        for hf in range(2):
                            pb = PS[5 + hf]
                            proj_tm(Wqkv, 1024 * (1 + which) + hf * 512, hT, j * 128, pb)
                            if hf == 0:
                                P.op("act", lambda a, tk=tk, pb=pb: a.copy(out=tk[:, 0:512], in_=pb[:]), reads=[pb], writes=[tk])
                            else:
                                P.op("dve", lambda v, tk=tk, pb=pb: v.tensor_copy(out=tk[:, 512:1024], in_=pb[:]), reads=[pb], writes=[tk])
                        if not smp:
                            dst = (kp if which == 0 else vp)[:, r0 + j * 128:r0 + (j + 1) * 128, :].rearrange("h t d -> t h d")
                            P.dma("pool", dst, tk[:, :].rearrange("t (h d) -> t h d", h=16), tk, reads=[tk])
                        else:
                            for s2 in range(2):
                                s = j * 2 + s2
                                dst = (ks if which == 0 else vs)[s, :, :, :].rearrange("h t d -> t h d")
                                P.dma("pool", dst, tk[s2 * 64:(s2 + 1) * 64, :].rearrange("t (h d) -> t h d", h=16), tk, reads=[tk])
                        if which == 1:
                            vb = vbf[j % 2]
                            P.op("pool", lambda g, vb=vb, tk=tk: g.tensor_copy(out=vb[:], in_=tk[:]), reads=[tk], writes=[vb])
                            P.dma("pool", VSs[r0 + j * 128:r0 + (j + 1) * 128, :], vb[:, :], vb, reads=[vb], writes=[trk["VS"][ti]])
                import os as _os
                _ka = _os.environ.get("KA_SKIP", "")
                if not smp:
                    for p in range(8 if "p" not in _ka else 0):
                        nkt = ti + 1
                        nsteps = 4 * nkt
                        units = []
                        bufk = {0: load_kv(p, ti, ti)}
                        step = 0
                        for jj in range(nkt):
                            tj = ti - jj
                            for r in (3, 2, 1, 0):
                                for hh in range(2):
                                    u = dict(hh=hh, p=p, jj=jj, kcols=slice(r * 128, (r + 1) * 128), vr=r, qcols=slice(0, N), N=N,
                                             first=(step == 0), last=(step == nsteps - 1),
                                             mask=(masks[:, r, :] if jj == 0 else None), lsi=step)
                                    if r == 2 and hh == 0 and jj + 1 < nkt:
                                        u["pre"] = (lambda jj=jj, tj=tj: bufk.__setitem__(jj + 1, load_kv(p, tj - 1, ti)))
                                    units.append(u)
                                step += 1

                        class _Lazy(dict):
                            pass
                        def _res(u):
                            k = bufk[u["jj"]]
                            u["Ktb"], u["Vb"] = Kt[k], Vt[k]
                        n = len(units)
                        for i in range(n + 2):
                            if i < n:
                                if units[i].get("pre"):
                                    units[i]["pre"]()
                                _res(units[i])
                                S1(units[i])
                            if 0 <= i - 1 < n:
                                S2a(units[i - 1])
                            if 0 <= i - 2 < n:
                                S2b(units[i - 2])
                        P.op("act", lambda a, p=p: a.copy(out=osb[:, p, 0:N], in_=PO[:, 0:N]), reads=[PO], writes=[osb])
                else:
                    for (c0, L, s) in tl["segs"]:
                        for p in range(8 if "s" not in _ka else 0):
                            k = ldc[0] % 2
                            ldc[0] += 1
                            nsteps = 1 + PB
                            qc = slice(c0, c0 + L)

                            def pre_first(k=k, p=p, c0=c0):
                                P.op("pool", lambda g: g.memset(Kt[k][:, :, 0:128], 0.0), writes=[Kt[k]])
                                P.op("pool", lambda g: g.memset(Vt[k][:, 0, :, :], 0.0), writes=[Vt[k]])
                                for hh in range(2):
                                    hs = slice(hh * 64, (hh + 1) * 64)
                                    P.dma("sp", Kt[k][hs, hh, 0:64], KTs[hs, p, r0 + c0:r0 + c0 + 64], Kt[k],
                                          reads=[trk["KT"][ti]], writes=[Kt[k]])
                                    P.dma("sp", Vt[k][0:64, 0, hh, hs], VSs[r0 + c0:r0 + c0 + 64, p * 128 + hh * 64:p * 128 + (hh + 1) * 64],
                                          Vt[k], reads=[trk["VS"][ti]], writes=[Vt[k]])

                            def pre_group(kk, g4, p=p, s=s):
                                kcb, vcb = kc[kk], vc[kk]
                                for h2 in range(2):
                                    P.dma("sp", kcb[:, :, h2 * 64:(h2 + 1) * 64], ck[s, 2 * p + h2, g4 * 512:(g4 + 1) * 512, :].rearrange(
                                          "(b k) d -> k b d", b=4), kcb, writes=[kcb])
                                    P.dma("sp", vcb[:, h2, :, :], cv[s, 2 * p + h2, g4 * 512:(g4 + 1) * 512, :].rearrange(
                                          "(b k) d -> k b d", b=4), vcb, writes=[vcb])
                                pt = PS[7]
                                for b in range(4):
                                    P.op("pe", lambda t, b=b: t.matmul(pt[:, b * 128:(b + 1) * 128],
                                         lhsT=kcb[:, b, :], rhs=identf[:], start=True, stop=True),
                                         reads=[kcb, identf], writes=[pt], inc=(b == 3))
                                for h2 in range(2):
                                    hs = slice(h2 * 64, (h2 + 1) * 64)
                                    P.op("dve", lambda v, h2=h2, hs=hs: v.tensor_copy(out=Kt[kk][hs, h2, :], in_=pt[hs, :]),
                                         reads=[pt], writes=[Kt[kk]])
                                    P.op("pool", lambda g, h2=h2, hs=hs: g.tensor_copy(out=Vt[kk][:, :, h2, hs],
                                         in_=vcb[:, h2, :, :]), reads=[vcb], writes=[Vt[kk]])

                            units = []
                            for hh in range(2):
                                units.append(dict(hh=hh, p=p, Ktb=Kt[k], Vb=Vt[k], kcols=slice(0, 128), vr=0, qcols=qc, N=L,
                                                  first=True, last=False, mask=masks[:, 0, 0:64], lsi=0,
                                                  pre=(pre_first if hh == 0 else None)))
                            step = 1
                            glist = list(range(PB // 4 - 1, -1, -1))
                            gk = []
                            for gi, g4 in enumerate(glist):
                                kk = ldc[0] % 2
                                ldc[0] += 1
                                gk.append(kk)
                            for gi, g4 in enumerate(glist):
                                kk = gk[gi]
                                for bi, b in enumerate((3, 2, 1, 0)):
                                    for hh in range(2):
                                        u = dict(hh=hh, p=p, Ktb=Kt[kk], Vb=Vt[kk], kcols=slice(b * 128, (b + 1) * 128), vr=b, qcols=qc, N=L,
                                                 first=False, last=(step == nsteps - 1), mask=None, lsi=step)
                                        if gi == 0 and bi == 0 and hh == 0:
                                            u["pre"] = (lambda kk=kk, g4=g4: pre_group(kk, g4))
                                        if bi == 1 and hh == 0 and gi + 1 < len(glist):
                                            u["pre"] = (lambda kk=gk[gi + 1], g4=glist[gi + 1]: pre_group(kk, g4))
                                        units.append(u)
                                    step += 1
                            emit_units(units)
                            P.op("act", lambda a, p=p, c0=c0, L=L: a.copy(out=osb[:, p, c0:c0 + L], in_=PO[:, 0:L]), reads=[PO], writes=[osb])
                P.dma("pool", OSs[:, :, r0:r0 + N], osb[:, :, 0:N], osb, reads=[osb], writes=[trk["OS"][ti]])

            P.barrier()
        with contextlib.suppress(_SkipPhase), contextlib.ExitStack() as ph:
            _phase_gate(4)
            Wg = sb(ph, "Wg", [128, 8, 3072], BF16)
            Wso = sb(ph, "Wso", [128, 8, D], BF16)
            Wo = sb(ph, "Wo", [128, 8, D], BF16)
            with contextlib.ExitStack() as ws:
                wst[0] = sb(ws, "wst0", [128, 2048], F32); wst[1] = sb(ws, "wst1", [128, 2048], F32)
                load_w(Wg, w_in[:, 6144:9216], 8, 3072)
                load_w(Wso, w_sb_o, 8, D)
                load_w(Wo, w_out, 8, D)
                P.barrier()
            common(ph, 4, 512, ["pre", "post"])
            os_ = sb(ph, "os_", [128, 8, 512], BF16)
            ycb = [sb(ph, f"ycb{k}", [128, 512], F32) for k in range(2)]
            ymb = [sb(ph, f"ymb{k}", [128, 512], F32) for k in range(2)]
            sgg = [sb(ph, f"sgg{k}", [128, 512], F32) for k in range(3)]
            mrg = [sb(ph, f"mrg{k}", [128, 512], F32) for k in range(2)]
            mg = sb(ph, "mg", [128, 8, 512], BF16)
            mo = [sb(ph, f"mo{k}", [128, D], F32) for k in range(2)]
            ss2 = sb(ph, "ss2", [128, 4], F32)
            rs2 = sb(ph, "rs2", [128, 4], F32)
            for tl in tiles:
                N = tl["N"]; ti = tl["idx"]; r0 = tl["r0"]
                nsub = N // 128
                load_x(tl)
                norm_T(tl, gbc["pre"])
                P.dma("sp", os_[:, :, 0:N], OSs[:, :, r0:r0 + N], os_, reads=[trk["OS"][ti]], writes=[os_])
                for c2 in range(8):
                    yc, ym = ycb[c2 % 2], ymb[c2 % 2]
                    P.dma("sp", yc[:, 0:N], YCs[:, c2, r0:r0 + N], yc, reads=[trk["YC"][ti]], writes=[yc])
                    P.dma("sp", ym[:, 0:N], YMs[:, c2, r0:r0 + N], ym, reads=[trk["YM"][ti]], writes=[ym])
                    pys, pg = PS[0 + (c2 % 2) * 4], [PS[1 + (c2 % 2) * 4], PS[2 + (c2 % 2) * 4], PS[3 + (c2 % 2) * 4]]
                    proj_fm(Wso, c2 * 128, os_, N, pys)
                    for gi in range(3):
                        proj_fm(Wg, gi * 1024 + c2 * 128, hT, N, pg[gi])
                        P.op("act", lambda a, gi=gi, pg=pg: a.activation(out=sgg[gi][:, 0:N], in_=pg[gi][:, 0:N], func=AF.Sigmoid),
                             reads=[pg[gi]], writes=[sgg[gi]])
                    m = mrg[c2 % 2]
                    P.op("dve", lambda v, m=m, pys=pys: v.tensor_tensor(out=m[:, 0:N], in0=pys[:, 0:N], in1=sgg[0][:, 0:N], op=ALU.mult),
                         reads=[pys, sgg[0]], writes=[m])
                    P.op("pool", lambda g, yc=yc: g.tensor_tensor(out=sgg[1][:, 0:N], in0=sgg[1][:, 0:N], in1=yc[:, 0:N], op=ALU.mult),
                         reads=[sgg[1], yc], writes=[sgg[1]])
                    P.op("pool", lambda g, ym=ym: g.tensor_tensor(out=sgg[2][:, 0:N], in0=sgg[2][:, 0:N], in1=ym[:, 0:N], op=ALU.mult),
                         reads=[sgg[2], ym], writes=[sgg[2]])
                    P.op("dve", lambda v, m=m: v.tensor_tensor(out=m[:, 0:N], in0=m[:, 0:N], in1=sgg[1][:, 0:N], op=ALU.add),
                         reads=[m, sgg[1]], writes=[m])
                    P.op("dve", lambda v, m=m, c2=c2: v.tensor_tensor(out=mg[:, c2, 0:N], in0=m[:, 0:N], in1=sgg[2][:, 0:N], op=ALU.add),
                         reads=[m, sgg[2]], writes=[mg])
                for j in range(nsub):
                    mj = mo[j % 2]
                    for hf in range(2):
                        pb = PS[hf]
                        proj_tm(Wo, hf * 512, mg, j * 128, pb)
                        if hf == 0:
                            P.op("act", lambda a, mj=mj, pb=pb: a.copy(out=mj[:, 0:512], in_=pb[:]), reads=[pb], writes=[mj])
                        else:
                            P.op("dve", lambda v, mj=mj, pb=pb: v.tensor_copy(out=mj[:, 512:1024], in_=pb[:]), reads=[pb], writes=[mj])
                    P.op("dve", lambda v, j=j: v.memset(ss2[:, j:j + 1], 0.0), writes=[ss2])
                    P.op("act", lambda a, mj=mj, j=j: a.activation(out=cm["junk"][:], in_=mj[:], func=AF.Square, accum_out=ss2[:, j:j + 1]),
                         reads=[mj], writes=[cm["junk"], ss2])
                    rstd_from(ss2[:, j:j + 1], rs2[:, j:j + 1], ss2, rs2, D)
                    xo = mj
                    P.op("dve", lambda v, mj=mj, j=j: v.scalar_tensor_tensor(out=mj[:], in0=mj[:], scalar=rs2[:, j:j + 1],
                         in1=gbc["post"][:], op0=ALU.mult, op1=ALU.mult), reads=[mj, rs2, gbc["post"]], writes=[mj])
                    P.op("pool", lambda g, mj=mj, j=j, xo=xo: g.tensor_tensor(out=xo[:], in0=mj[:], in1=xt[j][:], op=ALU.add),
                         reads=[mj, xt[j]], writes=[xo])
                    P.dma("pool", XMs[r0 + j * 128:r0 + (j + 1) * 128, :], xo[:, :], xo, reads=[xo], writes=[trk["XM"][ti]])

            P.barrier()
        with contextlib.suppress(_SkipPhase), contextlib.ExitStack() as ph:
            _phase_gate(5)
            Wup = sb(ph, "Wup", [128, 8, FF2], BF16)
            Wdn = sb(ph, "Wdn", [128, NFC, D], BF16)
            with contextlib.ExitStack() as ws:
                wst[0] = sb(ws, "wst0", [128, 2048], F32); wst[1] = sb(ws, "wst1", [128, 2048], F32)
                load_w(Wup, w_ffn_up, 8, FF2)
                load_w(Wdn, w_ffn_down, NFC, D)
                P.barrier()
            common(ph, 2, 256, ["fpre", "fpost"])
            fw = sb(ph, "fw", [128, 44, 3], F32)
            halP = sb(ph, "halP", [128, 44, 2], F32)
            halS = sb(ph, "halS", [128, 44, NS, 2], F32)
            sfs = sb(ph, "sfs", [3, 512], F32)
            for g11 in range(11):
                P.dma("sp", sfs[0:3, :], ffn_dw_w[:, g11 * 512:(g11 + 1) * 512], sfs, writes=[sfs])
                pb = PS[6]
                for c in range(4):
                    P.op("pe", lambda t, c=c: t.matmul(pb[:, c * 3:(c + 1) * 3], lhsT=sfs[0:3, c * 128:(c + 1) * 128],
                         rhs=identf[0:3, 0:3], start=True, stop=True), reads=[sfs, identf], writes=[pb], inc=(c == 3))
                P.op("dve", lambda v, g11=g11: v.tensor_copy(out=fw[:, g11 * 4:(g11 + 1) * 4, :],
                     in_=pb[:, 0:12].rearrange("p (c r) -> p c r", c=4)), reads=[pb], writes=[fw])
            for s in range(NS):
                for g11 in range(11):
                    P.dma("sp", sfs[0:2, :], sffn[s, :, g11 * 512:(g11 + 1) * 512], sfs, writes=[sfs])
                    pb = PS[7]
                    for c in range(4):
                        P.op("pe", lambda t, c=c: t.matmul(pb[:, c * 2:(c + 1) * 2], lhsT=sfs[0:2, c * 128:(c + 1) * 128],
                             rhs=identf[0:2, 0:2], start=True, stop=True), reads=[sfs, identf], writes=[pb], inc=(c == 3))
                    P.op("dve", lambda v, s=s, g11=g11: v.tensor_copy(out=halS[:, g11 * 4:(g11 + 1) * 4, s, :],
                         in_=pb[:, 0:8].rearrange("p (c r) -> p c r", c=4)), reads=[pb], writes=[halS])
            upb = [sb(ph, f"upb{k}", [128, 4, 66], F32) for k in range(2)]
            cv_ = [sb(ph, f"cvv{k}", [128, 256], F32) for k in range(2)]
            gl = sb(ph, "gl", [128, 256], F32)
            g2 = sb(ph, "g2", [128, 256], F32)
            gT = sb(ph, "gT", [128, NFC, 256], BF16)
            dn = [sb(ph, f"dn{k}", [128, D], F32) for k in range(2)]
            fst = [sb(ph, f"fst{k}", [2, 512], F32) for k in range(2)]
            ss3 = sb(ph, "ss3", [128, 2], F32)
            rs3 = sb(ph, "rs3", [128, 2], F32)
            P.op("dve", lambda v: v.memset(halP[:], 0.0), writes=[halP])
            ftiles = []
            for i in range(T // 256):
                ftiles.append(dict(r0=i * 256, N=256, segs=[(0, 256, None)], idx=i // 2, sample=False, last=(i == T // 256 - 1)))
            ftiles.append(dict(r0=T, N=TS, segs=[(s * 64, 64, s) for s in range(NS)], idx=NT, sample=True, last=True))
            for tl in ftiles:
                N = tl["N"]; ti = tl["idx"]; r0 = tl["r0"]; smp = tl["sample"]
                nsub = N // 128
                load_x(tl, src_fn=lambda tl, j: XMs[tl["r0"] + j * 128:tl["r0"] + (j + 1) * 128, :], trkb=trk["XM"][ti])
                norm_T(tl, gbc["fpre"])
                for j in range(NFC):
                    outs = []
                    for which in range(2):
                        ch = which * NFC + j
                        pb = PS[(2 * j + which) % 4]
                        proj_fm(Wup, ch * 128, hT, N, pb)
                        ub = upb[which]
                        if smp:
                            P.op("act", lambda a, ch=ch, ub=ub: a.copy(out=ub[:, :, 0:2], in_=halS[:, ch, :, :]), reads=[halS], writes=[ub])
                            P.op("act", lambda a, pb=pb, ub=ub: a.copy(out=ub[:, :, 2:66], in_=pb[:, 0:N].rearrange("p (s t) -> p s t", s=NS)),
                                 reads=[pb], writes=[ub])
                            src = lambda k, ub=ub: ub[:, :, k:k + 64]
                            o3 = lambda t_: t_[:, 0:N].rearrange("p (s t) -> p s t", s=NS)
                        else:
                            uf = ub[:, :, :].rearrange("p a b -> p (a b)")
                            P.op("act", lambda a, ch=ch, uf=uf: a.copy(out=uf[:, 0:2], in_=halP[:, ch, :]), reads=[halP], writes=[ub])
                            P.op("act", lambda a, pb=pb, uf=uf: a.copy(out=uf[:, 2:2 + N], in_=pb[:, 0:N]), reads=[pb], writes=[ub])
                            P.op("pool", lambda g, ch=ch, uf=uf: g.tensor_copy(out=halP[:, ch, :], in_=uf[:, N:N + 2]),
                                 reads=[ub], writes=[halP])
                            src = lambda k, uf=uf: uf[:, k:k + N]
                            o3 = lambda t_: t_[:, 0:N]
                        co = cv_[which]
                        P.op("dve", lambda v, co=co, src=src, o3=o3, ch=ch: v.tensor_scalar(out=o3(co), in0=src(0), scalar1=fw[:, ch, 0:1],
                             scalar2=0.0, op0=ALU.mult, op1=ALU.add), reads=[ub, fw], writes=[co])
                        for k in (1, 2):
                            P.op("dve", lambda v, co=co, src=src, o3=o3, ch=ch, k=k: v.scalar_tensor_tensor(out=o3(co), in0=src(k),
                                 scalar=fw[:, ch, k:k + 1], in1=o3(co), op0=ALU.mult, op1=ALU.add), reads=[ub, fw, co], writes=[co])
                        outs.append(co)
                    xg, xv = outs
                    P.op("pool", lambda g, xg=xg: g.tensor_tensor(out=g2[:, 0:N], in0=xg[:, 0:N], in1=xg[:, 0:N], op=ALU.mult),
                         reads=[xg], writes=[g2])
                    P.op("pool", lambda g: g.tensor_scalar(out=g2[:, 0:N], in0=g2[:, 0:N], scalar1=0.044715, scalar2=1.0,
                         op0=ALU.mult, op1=ALU.add), reads=[g2], writes=[g2])
                    P.op("pool", lambda g, xg=xg: g.tensor_tensor(out=g2[:, 0:N], in0=g2[:, 0:N], in1=xg[:, 0:N], op=ALU.mult),
                         reads=[g2, xg], writes=[g2])
                    P.op("act", lambda a: a.activation(out=gl[:, 0:N], in_=g2[:, 0:N], func=AF.Sigmoid, scale=1.5957691216057308),
                         reads=[g2], writes=[gl])
                    P.op("dve", lambda v, xg=xg: v.tensor_tensor(out=gl[:, 0:N], in0=gl[:, 0:N], in1=xg[:, 0:N], op=ALU.mult),
                         reads=[gl, xg], writes=[gl])
                    P.op("dve", lambda v, j=j, xv=xv: v.tensor_tensor(out=gT[:, j, 0:N], in0=gl[:, 0:N], in1=xv[:, 0:N], op=ALU.mult),
                         reads=[gl, xv], writes=[gT])
                ends = [(c0 + 62, s) for (c0, L, s) in tl["segs"]] if smp else ([(254, None)] if tl["last"] else [])
                for (t0, s) in ends:
                    for cb in range(11):
                        pb = PS[4 + cb % 2]
                        fb = fst[cb % 2]
                        proj_tm(Wup, cb * 512, hT, t0, pb, M=2)
                        P.op("dve", lambda v, fb=fb, pb=pb: v.tensor_copy(out=fb[:, :], in_=pb[0:2, :]), reads=[pb], writes=[fb])
                        dst = fs[s, :, cb * 512:(cb + 1) * 512] if smp else fp[:, cb * 512:(cb + 1) * 512]
                        P.dma("pool", dst, fb[:, :], fb, reads=[fb])
                for j in range(nsub):
                    dj = dn[j % 2]
                    for hf in range(2):
                        pb = PS[6 + hf]
                        proj_tm(Wdn, hf * 512, gT, j * 128, pb, K=NFC)
                        if hf == 0:
                            P.op("act", lambda a, dj=dj, pb=pb: a.copy(out=dj[:, 0:512], in_=pb[:]), reads=[pb], writes=[dj])
                        else:
                            P.op("dve", lambda v, dj=dj, pb=pb: v.tensor_copy(out=dj[:, 512:1024], in_=pb[:]), reads=[pb], writes=[dj])
                    P.op("dve", lambda v, j=j: v.memset(ss3[:, j:j + 1], 0.0), writes=[ss3])
                    P.op("act", lambda a, dj=dj, j=j: a.activation(out=cm["junk"][:], in_=dj[:], func=AF.Square, accum_out=ss3[:, j:j + 1]),
                         reads=[dj], writes=[cm["junk"], ss3])
                    rstd_from(ss3[:, j:j + 1], rs3[:, j:j + 1], ss3, rs3, D)
                    P.op("dve", lambda v, dj=dj, j=j: v.scalar_tensor_tensor(out=dj[:], in0=dj[:], scalar=rs3[:, j:j + 1],
                         in1=gbc["fpost"][:], op0=ALU.mult, op1=ALU.mult), reads=[dj, rs3, gbc["fpost"]], writes=[dj])
                    P.op("pool", lambda g, dj=dj, j=j: g.tensor_tensor(out=dj[:], in0=dj[:], in1=xt[j][:], op=ALU.add),
                         reads=[dj, xt[j]], writes=[dj])
                    dst = ys[r0 - T + j * 128:r0 - T + (j + 1) * 128, :] if smp else yp[r0 + j * 128:r0 + (j + 1) * 128, :]
                    P.dma("pool", dst, dj[:, :], dj, reads=[dj])
            P.barrier()
        P.finish()
    return nc


_CACHE = {}


def run(T, NS, PAST, per_core):
    key = (T, NS, PAST)
    if key not in _CACHE:
        _CACHE[key] = build(T, NS, PAST)
    nc = _CACHE[key]
    res = run_bass_kernel_spmd(nc, per_core, core_ids=list(range(len(per_core))))
    return res.results


WNAMES = ["g_mem", "w_mem_kv", "g_mix_pre", "g_mix_post", "w_in", "w_sb_o", "conv_dw_w", "conv_dw_b", "conv_ln_g",
          "conv_ln_b", "w_conv_o", "w_mem_o", "w_out", "g_ffn_pre", "g_ffn_post", "w_ffn_up", "ffn_dw_w", "w_ffn_down"]


def make_maps(inp, ncores, NS):
    f = lambda a: np.ascontiguousarray(np.asarray(a, dtype=np.float32))
    B = inp["x_prompt"].shape[0]
    maps = []
    for c in range(ncores):
        b = c % B
        sl = slice(c * NS, (c + 1) * NS)
        m = {"xp": f(inp["x_prompt"][b]), "xs": f(inp["x_sample"][sl]).reshape(NS * 64, D),
             "memp": f(inp["mem_prompt"][b]), "ck": f(inp["cache_sb_k"][0, sl]), "cv": f(inp["cache_sb_v"][0, sl]),
             "sconv": f(inp["state_conv"][0, sl]), "sffn": f(inp["state_ffn_conv"][0, sl]),
             "cmk": f(inp["cache_mem_k"][0, sl]), "cmv": f(inp["cache_mem_v"][0, sl])}
        for n in WNAMES:
            w = f(inp[n][0])
            m[n] = w.reshape(1, -1) if w.ndim == 1 else w
        maps.append(m)
    return maps


def assemble(res, B, ncores):
    cat = lambda n, rng: np.stack([res[c][n] for c in rng])
    pc = range(B)
    sc = range(ncores)
    yp = cat("yp", pc); ys = np.concatenate([res[c]["ys"].reshape(-1, 64, D) for c in sc])
    kp = cat("kp", pc)[None]; vp = cat("vp", pc)[None]
    ks = np.concatenate([res[c]["ks"] for c in sc])[None]; vs = np.concatenate([res[c]["vs"] for c in sc])[None]
    cp = cat("cp", pc)[None]; cs = np.concatenate([res[c]["cs"] for c in sc])[None]
    fp = cat("fp", pc)[None]; fs = np.concatenate([res[c]["fs"] for c in sc])[None]
    mkp = cat("mkp", pc)[None]; mvp = cat("mvp", pc)[None]
    return (yp, ys, kp, vp, ks, vs, cp, cs, fp, fs, mkp, mvp)


def kernel(**inputs):
    T = inputs["x_prompt"].shape[1]
    PAST = inputs["cache_sb_k"].shape[3]
    ncores = 8
    NS = inputs["x_sample"].shape[0] // ncores
    maps = make_maps(inputs, ncores, NS)
    res = run(T, NS, PAST, maps)
    return assemble(res, inputs["x_prompt"].shape[0], ncores)
```

```python
import contextlib
import numpy as np
import concourse.bass as bass
import concourse.mybir as mybir
from concourse.bass_utils import run_bass_kernel_spmd

F32 = mybir.dt.float32
BF16 = mybir.dt.bfloat16
ALU = mybir.AluOpType
AF = mybir.ActivationFunctionType

D = 1024
NCH = 8
FF = 2816
FF2 = 5632
NFC = 22
EPS = 1e-6


class _SkipPhase(Exception):
    pass


def _phase_gate(k):
    import os
    en = os.environ.get("KPH")
    if en is not None and str(k) not in en.split(","):
        raise _SkipPhase()


class Buf:
    def __init__(self, t, name):
        self.t = t
        self.name = name
        self.w = None
        self.r = {}
        self.dsem = None
        self.dcnt = 0
        self.wl = {} if t is None else None

    def __getitem__(self, k):
        return self.t[k]


class Prog:
    def __init__(self, nc, es):
        self.nc = nc
        self.es = es
        self.E = {"pe": nc.tensor, "act": nc.scalar, "dve": nc.vector, "pool": nc.gpsimd, "sp": nc.sync}
        self.sem = {e: es.enter_context(nc.semaphore("s_" + e)) for e in ("pe", "act", "dve", "pool")}
        self.cnt = {e: 0 for e in self.sem}
        self.seen = {e: {} for e in self.E}
        self.dbufs = []

    def _wait(self, e, dep, same_ok):
        if dep is None:
            return
        key, sem, val, src = dep
        if src is not None:
            val = 16 * src.dcnt
        elif key == e and not same_ok:
            return
        if self.seen[e].get(key, 0) >= val:
            return
        self.E[e].wait_ge(sem, val)
        self.seen[e][key] = val

    def _deps(self, e, reads, writes):
        for b in reads:
            self._wait(e, b.w, True)
            if b.wl:
                for d in list(b.wl.values()):
                    self._wait(e, d, True)
        for b in writes:
            self._wait(e, b.w, False)
            for d in list(b.r.values()):
                self._wait(e, d, False)

    def op(self, e, fn, reads=(), writes=(), inc=True):
        self._deps(e, reads, writes)
        ins = fn(self.E[e])
        if inc:
            self.cnt[e] += 1
            ins.then_inc(self.sem[e], 1)
            t = self.cnt[e]
        else:
            t = self.cnt[e] + 1
        dep = (e, self.sem[e], t, None)
        for b in reads:
            b.r[e] = dep
        for b in writes:
            b.w = dep
            b.r = {}
        return ins

    def dma(self, q, out_ap, in_ap, sbuf, reads=(), writes=()):
        self._deps(q, reads, writes)
        if sbuf.dsem is None:
            sbuf.dsem = self.es.enter_context(self.nc.semaphore("d_" + sbuf.name))
            self.dbufs.append(sbuf)
        sbuf.dcnt += 1
        self.E[q].dma_start(out=out_ap, in_=in_ap).then_inc(sbuf.dsem, 16)
        key = ("d", id(sbuf))
        dep = (key, sbuf.dsem, 16 * sbuf.dcnt, sbuf)
        for b in reads:
            b.r[key] = dep
        for b in writes:
            if b.wl is not None:
                b.wl[key] = dep
            else:
                b.w = dep
                b.r = {}

    def barrier(self):
        for e in self.E:
            for k in self.sem:
                if k != e and self.cnt[k] > self.seen[e].get(k, 0):
                    self.E[e].wait_ge(self.sem[k], self.cnt[k])
                    self.seen[e][k] = self.cnt[k]
            for b in self.dbufs:
                key = ("d", id(b))
                if 16 * b.dcnt > self.seen[e].get(key, 0):
                    self.E[e].wait_ge(b.dsem, 16 * b.dcnt)
                    self.seen[e][key] = 16 * b.dcnt

    def finish(self):
        for b in self.dbufs:
            self.E["sp"].wait_ge(b.dsem, 16 * b.dcnt)


def build(T, NS, PAST):
    nc = bass.Bass("TRN2", target_bir_lowering=False)
    TS = NS * 64
    NT = T // 512
    PB = PAST // 128

    def din(name, shape):
        return nc.dram_tensor(name, list(shape), F32, kind="ExternalInput").ap()

    def dout(name, shape):
        return nc.dram_tensor(name, list(shape), F32, kind="ExternalOutput").ap()

    xp = din("xp", [T, D]); xs = din("xs", [TS, D]); memp = din("memp", [256, D])
    ck = din("ck", [NS, 16, PAST, 64]); cv = din("cv", [NS, 16, PAST, 64])
    sconv = din("sconv", [NS, 30, D]); sffn = din("sffn", [NS, 2, FF2])
    cmk = din("cmk", [NS, 4, 256, 256]); cmv = din("cmv", [NS, 4, 256, 256])
    g_mem = din("g_mem", [1, D]); w_mem_kv = din("w_mem_kv", [D, 2048])
    g_mix_pre = din("g_mix_pre", [1, D]); g_mix_post = din("g_mix_post", [1, D])
    w_in = din("w_in", [D, 9216]); w_sb_o = din("w_sb_o", [D, D])
    conv_dw_w = din("conv_dw_w", [31, D]); conv_dw_b = din("conv_dw_b", [1, D])
    conv_ln_g = din("conv_ln_g", [1, D]); conv_ln_b = din("conv_ln_b", [1, D])
    w_conv_o = din("w_conv_o", [D, D]); w_mem_o = din("w_mem_o", [D, D]); w_out = din("w_out", [D, D])
    g_ffn_pre = din("g_ffn_pre", [1, D]); g_ffn_post = din("g_ffn_post", [1, D])
    w_ffn_up = din("w_ffn_up", [D, FF2]); ffn_dw_w = din("ffn_dw_w", [3, FF2]); w_ffn_down = din("w_ffn_down", [FF, D])

    yp = dout("yp", [T, D]); ys = dout("ys", [TS, D])
    kp = dout("kp", [16, T, 64]); vp = dout("vp", [16, T, 64])
    ks = dout("ks", [NS, 16, 64, 64]); vs = dout("vs", [NS, 16, 64, 64])
    cp = dout("cp", [30, D]); cs = dout("cs", [NS, 30, D])
    fp = dout("fp", [2, FF2]); fs = dout("fs", [NS, 2, FF2])
    mkp = dout("mkp", [4, 256, 256]); mvp = dout("mvp", [4, 256, 256])

    TT = T + TS
    KTs = nc.dram_tensor("KTs", [128, 8, TT], BF16).ap()
    VSs = nc.dram_tensor("VSs", [TT, D], BF16).ap()
    OSs = nc.dram_tensor("OSs", [128, 8, TT], BF16).ap()
    YCs = nc.dram_tensor("YCs", [128, 8, TT], F32).ap()
    YMs = nc.dram_tensor("YMs", [128, 8, TT], F32).ap()
    XMs = nc.dram_tensor("XMs", [TT, D], F32).ap()

    tiles = []
    for i in range(NT):
        tiles.append(dict(r0=i * 512, N=512, segs=[(0, 512, None)], idx=i, sample=False))
    tiles.append(dict(r0=T, N=TS, segs=[(s * 64, 64, s) for s in range(NS)], idx=NT, sample=True))
    ntile = len(tiles)
    trk = {n: [Buf(None, f"{n}{i}") for i in range(ntile)] for n in ("KT", "VS", "OS", "YC", "YM", "XM")}

    def xrows(tl, j):
        r = tl["r0"] + j * 128
        if tl["sample"]:
            return xs[r - T:r - T + 128, :]
        return xp[r:r + 128, :]

    es = contextlib.ExitStack()
    with es:
        P = Prog(nc, es)

        uid = [0]

        def sb(st, name, shape, dt):
            uid[0] += 1
            name = f"{name}_{uid[0]}"
            return Buf(st.enter_context(nc.sbuf_tensor(name, list(shape), dt)), name)

        PS = [Buf(es.enter_context(nc.psum_tensor(f"ps{i}", [128, 512], F32)), f"ps{i}") for i in range(8)]

        identb = sb(es, "identb", [128, 128], BF16)
        identf = sb(es, "identf", [128, 128], F32)
        negtri = sb(es, "negtri", [128, 128], BF16)
        negones = sb(es, "negones", [128, 128], BF16)
        onesb = sb(es, "onesb", [128, 128], BF16)
        onesf = sb(es, "onesf", [128, 128], F32)
        for bfr, val in ((identb, 1.0), (identf, 1.0), (negtri, -1.0), (negones, -1.0), (onesb, 1.0),
                         (onesf, 1.0 / D)):
            P.op("pool", lambda g, b=bfr, v=val: g.memset(b[:], v), writes=[bfr])
        for bfr in (identb, identf):
            P.op("pool", lambda g, b=bfr: g.affine_select(out=b[:], in_=b[:], pattern=[[-1, 128]],
                 compare_op=ALU.is_equal, fill=0.0, base=0, channel_multiplier=1), reads=[bfr], writes=[bfr])
        P.op("pool", lambda g: g.affine_select(out=negtri[:], in_=negtri[:], pattern=[[-1, 128]],
             compare_op=ALU.is_ge, fill=0.0, base=0, channel_multiplier=1), reads=[negtri], writes=[negtri])

        gbc = {}
        gsrc = {"pre": g_mix_pre, "post": g_mix_post, "fpre": g_ffn_pre, "fpost": g_ffn_post, "mem": g_mem}
        xt = [None] * 4
        xn = [None] * 2
        cm = {}

        class _HT:
            def __getitem__(self, k):
                return cm["hT"].t[k]
        hT = _HT()

        def common(ph, nsub, N, gs):
            for j in range(nsub):
                xt[j] = sb(ph, f"xt{j}", [128, D], F32)
            for j in range(2):
                xn[j] = sb(ph, f"xn{j}", [128, D], BF16)
            cm["hT"] = sb(ph, "hT", [128, 8, N], BF16)
            cm["junk"] = sb(ph, "junk", [128, D], BF16)
            cm["ssq"] = sb(ph, "ssq", [128, 4], F32)
            cm["rstd"] = sb(ph, "rstd", [128, 4], F32)
            for nm in gs:
                gbc[nm] = sb(ph, "g_" + nm, [128, D], F32)
                P.dma("sp", gbc[nm][:], gsrc[nm][0:1, :].partition_broadcast(128), gbc[nm], writes=[gbc[nm]])

        def rstd_from(ss_ap, out_ap, ssb, outb, n):
            P.op("act", lambda a: a.activation(out=out_ap, in_=ss_ap, func=AF.Ln, scale=1.0 / n, bias=EPS),
                 reads=[ssb], writes=[outb])
            P.op("act", lambda a: a.activation(out=out_ap, in_=out_ap, func=AF.Exp, scale=-0.5),
                 reads=[outb], writes=[outb])

        def load_x(tl, src_fn=None, trkb=None):
            nsub = tl["N"] // 128
            for j in range(nsub):
                src = src_fn(tl, j) if src_fn else xrows(tl, j)
                P.dma("sp", xt[j][:], src, xt[j], reads=[trkb] if trkb else [], writes=[xt[j]])

        def norm_T(tl, g):
            nsub = tl["N"] // 128
            junk, ssq, rstd, hTb = cm["junk"], cm["ssq"], cm["rstd"], cm["hT"]
            P.op("dve", lambda v: v.memset(ssq[:], 0.0), writes=[ssq])
            for j in range(nsub):
                P.op("act", lambda a, j=j: a.activation(out=junk[:], in_=xt[j][:], func=AF.Square,
                     accum_out=ssq[:, j:j + 1]), reads=[xt[j]], writes=[junk, ssq])
            rstd_from(ssq[:, 0:nsub], rstd[:, 0:nsub], ssq, rstd, D)
            for j in range(nsub):
                xb = xn[j % 2]
                P.op("dve", lambda v, j=j, xb=xb: v.scalar_tensor_tensor(out=xb[:], in0=xt[j][:],
                     scalar=rstd[:, j:j + 1], in1=g[:], op0=ALU.mult, op1=ALU.mult),
                     reads=[xt[j], rstd, g], writes=[xb])
                for half in range(2):
                    pb = PS[6 + half]
                    for cc in range(4):
                        c = half * 4 + cc
                        P.op("pe", lambda t, c=c, cc=cc, pb=pb, xb=xb: t.matmul(pb[:, cc * 128:(cc + 1) * 128],
                             lhsT=xb[:, c * 128:(c + 1) * 128], rhs=identb[:], start=True, stop=True),
                             reads=[xb, identb], writes=[pb], inc=(cc == 3))
                    eng = "act" if half == 0 else "dve"
                    if eng == "act":
                        P.op("act", lambda a, half=half, pb=pb, j=j: a.copy(
                             out=hT[:, half * 4:half * 4 + 4, j * 128:(j + 1) * 128],
                             in_=pb[:].rearrange("p (c t) -> p c t", c=4)), reads=[pb], writes=[hTb])
                    else:
                        P.op("dve", lambda v, half=half, pb=pb, j=j: v.tensor_copy(
                             out=hT[:, half * 4:half * 4 + 4, j * 128:(j + 1) * 128],
                             in_=pb[:].rearrange("p (c t) -> p c t", c=4)), reads=[pb], writes=[hTb])

        wst = [None, None]
        wcnt = [0]

        def load_w(dst, src2d, nrc, ncols, dcol0=0):
            for rc in range(nrc):
                for cb in range(0, ncols, 2048):
                    w = min(2048, ncols - cb)
                    k = wcnt[0] % 2
                    wcnt[0] += 1
                    st = wst[k]
                    P.dma("sp", st[:, 0:w], src2d[rc * 128:(rc + 1) * 128, cb:cb + w], st, writes=[st])
                    eng = "dve" if k == 0 else "pool"
                    P.op(eng, lambda v, st=st, rc=rc, cb=cb, w=w: v.tensor_copy(
                         out=dst[:, rc, dcol0 + cb:dcol0 + cb + w], in_=st[:, 0:w]), reads=[st], writes=[dst])

        def load_cols(dst, srcs, R, nchunk, stg):
            r0 = 0
            for ap, nr in srcs:
                P.dma("sp", stg[r0:r0 + nr, 0:nchunk * 128], ap, stg, writes=[stg])
                r0 += nr
            pb = PS[6]
            for c in range(nchunk):
                P.op("pe", lambda t, c=c: t.matmul(pb[:, c * R:(c + 1) * R], lhsT=stg[0:R, c * 128:(c + 1) * 128],
                     rhs=identf[0:R, 0:R], start=True, stop=True), reads=[stg, identf], writes=[pb],
                     inc=(c == nchunk - 1))
            P.op("dve", lambda v: v.tensor_copy(out=dst[:].rearrange("p c r -> p (c r)"),
                 in_=pb[:, 0:nchunk * R]), reads=[pb], writes=[dst])

        def proj_fm(W, col0, rhsT, N, pb, K=NCH):
            rb = cm["hT"] if rhsT is hT else rhsT
            for c in range(K):
                P.op("pe", lambda t, c=c: t.matmul(pb[:, 0:N], lhsT=W[:, c, col0:col0 + 128], rhs=rhsT[:, c, 0:N],
                     start=(c == 0), stop=(c == K - 1)), reads=[W, rb], writes=[pb], inc=(c == K - 1))

        def proj_tm(W, col0, lhs, t0, pb, K=NCH, M=128):
            lb = cm["hT"] if lhs is hT else lhs
            for c in range(K):
                P.op("pe", lambda t, c=c: t.matmul(pb[0:M, :], lhsT=lhs[:, c, t0:t0 + M], rhs=W[:, c, col0:col0 + 512],
                     start=(c == 0), stop=(c == K - 1)), reads=[W, lb], writes=[pb], inc=(c == K - 1))

        memst = contextlib.ExitStack()
        mkT = sb(memst, "mkT", [128, 8, 256], BF16)
        mvb = sb(memst, "mvb", [128, 2, D], BF16)
        with contextlib.suppress(_SkipPhase), contextlib.ExitStack() as ph:
            _phase_gate(0)
            Wm = sb(ph, "Wm", [128, 8, 2048], BF16)
            with contextlib.ExitStack() as ws:
                wst[0] = sb(ws, "wst0", [128, 2048], F32); wst[1] = sb(ws, "wst1", [128, 2048], F32)
                load_w(Wm, w_mem_kv, 8, 2048)
                P.barrier()
            mtok = sb(ph, "mtok", [128, 2048], F32)
            common(ph, 2, 256, ["mem"])
            mt = dict(r0=0, N=256, segs=[], sample=False)
            load_x(mt, src_fn=lambda tl, j: memp[j * 128:(j + 1) * 128, :])
            norm_T(mt, gbc["mem"])
            for j in range(2):
                for hf in range(4):
                    pb = PS[hf % 4]
                    proj_tm(Wm, hf * 512, hT, j * 128, pb)
                    P.op("act" if hf % 2 else "dve", (lambda a, hf=hf, pb=pb: a.copy(out=mtok[:, hf * 512:(hf + 1) * 512], in_=pb[:])) if hf % 2
                         else (lambda v, hf=hf, pb=pb: v.tensor_copy(out=mtok[:, hf * 512:(hf + 1) * 512], in_=pb[:])),
                         reads=[pb], writes=[mtok])
                P.op("pool", lambda g, j=j: g.tensor_copy(out=mvb[:, j, :], in_=mtok[:, 1024:2048]),
                     reads=[mtok], writes=[mvb])
                P.dma("pool", mkp[:, j * 128:(j + 1) * 128, :].rearrange("h m d -> m h d"),
                      mtok[:, 0:1024].rearrange("m (h d) -> m h d", h=4), mtok, reads=[mtok])
                P.dma("pool", mvp[:, j * 128:(j + 1) * 128, :].rearrange("h m d -> m h d"),
                      mtok[:, 1024:2048].rearrange("m (h d) -> m h d", h=4), mtok, reads=[mtok])
            for cc in range(8):
                pb = PS[cc % 4]
                proj_fm(Wm, cc * 128, hT, 256, pb)
                P.op("act", lambda a, cc=cc, pb=pb: a.copy(out=mkT[:, cc, :], in_=pb[:, 0:256]), reads=[pb], writes=[mkT])

            P.barrier()
        with contextlib.suppress(_SkipPhase), contextlib.ExitStack() as ph:
            _phase_gate(1)
            Wq = sb(ph, "Wqm", [128, 8, D], BF16)
            Wmo = sb(ph, "Wmo", [128, 8, D], BF16)
            with contextlib.ExitStack() as ws:
                wst[0] = sb(ws, "wst0", [128, 2048], F32); wst[1] = sb(ws, "wst1", [128, 2048], F32)
                load_w(Wq, w_in[:, 5120:6144], 8, D)
                load_w(Wmo, w_mem_o, 8, D)
                P.barrier()
            common(ph, 4, 512, ["pre"])
            qmT = sb(ph, "qmT", [128, 8, 512], BF16)
            pT = [sb(ph, f"pT{k}", [128, 2, 512], BF16) for k in range(2)]
            omT = sb(ph, "omT", [128, 8, 512], BF16)
            rden = [sb(ph, f"rden{k}", [128, 512], F32) for k in range(2)]
            yst = [sb(ph, f"ystm{k}", [128, 512], F32) for k in range(2)]
            smkT = sb(ph, "smkT", [128, 8, 256], BF16)
            smvb = sb(ph, "smvb", [128, 2, D], BF16)
            cmst = [sb(ph, f"cmst{k}", [128, 2, 256], F32) for k in range(2)]

            def mem_attn(kT, vB, c0, L):
                for hm in range(4):
                    pk = pT[hm % 2]
                    for mc in range(2):
                        pb = PS[mc]
                        for dc in range(2):
                            P.op("pe", lambda t, mc=mc, dc=dc, pb=pb: t.matmul(pb[:, 0:L],
                                 lhsT=kT[:, hm * 2 + dc, mc * 128:(mc + 1) * 128], rhs=qmT[:, hm * 2 + dc, c0:c0 + L],
                                 start=(dc == 0), stop=(dc == 1)), reads=[kT, qmT], writes=[pb], inc=(dc == 1))
                        P.op("act", lambda a, mc=mc, pb=pb, pk=pk: a.activation(out=pk[:, mc, 0:L], in_=pb[:, 0:L], func=AF.Exp),
                             reads=[pb], writes=[pk])
                    pd = PS[2]
                    for mc in range(2):
                        P.op("pe", lambda t, mc=mc, pk=pk: t.matmul(pd[:, 0:L], lhsT=onesb[:], rhs=pk[:, mc, 0:L],
                             start=(mc == 0), stop=(mc == 1)), reads=[onesb, pk], writes=[pd], inc=(mc == 1))
                    rd = rden[hm % 2]
                    P.op("dve", lambda v, rd=rd: v.reciprocal(out=rd[:, 0:L], in_=pd[:, 0:L]), reads=[pd], writes=[rd])
                    for dc in range(2):
                        po = PS[4 + dc]
                        for mc in range(2):
                            P.op("pe", lambda t, mc=mc, dc=dc, po=po, pk=pk: t.matmul(po[:, 0:L],
                                 lhsT=vB[:, mc, hm * 256 + dc * 128:hm * 256 + dc * 128 + 128], rhs=pk[:, mc, 0:L],
                                 start=(mc == 0), stop=(mc == 1)), reads=[vB, pk], writes=[po], inc=(mc == 1))
                        P.op("dve", lambda v, dc=dc, po=po, rd=rd: v.tensor_tensor(out=omT[:, hm * 2 + dc, c0:c0 + L],
                             in0=po[:, 0:L], in1=rd[:, 0:L], op=ALU.mult), reads=[po, rd], writes=[omT])

            for tl in tiles:
                N = tl["N"]; ti = tl["idx"]; smp = tl["sample"]
                load_x(tl)
                norm_T(tl, gbc["pre"])
                for c2 in range(8):
                    pb = PS[c2 % 4]
                    proj_fm(Wq, c2 * 128, hT, N, pb)
                    P.op("act", lambda a, c2=c2, pb=pb: a.activation(out=qmT[:, c2, 0:N], in_=pb[:, 0:N], func=AF.Copy,
                         scale=1.0 / 16.0), reads=[pb], writes=[qmT])
                if not smp:
                    mem_attn(mkT, mvb, 0, N)
                else:
                    for (c0, L, s) in tl["segs"]:
                        for hm in range(4):
                            st = cmst[hm % 2]
                            P.dma("sp", st[:, :, :], cmk[s, hm, :, :].rearrange("(j m) d -> m j d", j=2), st, writes=[st])
                            pb = PS[6 + hm % 2]
                            for dc in range(2):
                                for j in range(2):
                                    P.op("pe", lambda t, dc=dc, j=j, pb=pb, st=st: t.matmul(
                                         pb[:, dc * 256 + j * 128:dc * 256 + j * 128 + 128],
                                         lhsT=st[:, j, dc * 128:(dc + 1) * 128], rhs=identf[:], start=True, stop=True),
                                         reads=[st, identf], writes=[pb], inc=(dc == 1 and j == 1))
                            P.op("dve", lambda v, hm=hm, pb=pb: v.tensor_copy(out=smkT[:, hm * 2:hm * 2 + 2, :],
                                 in_=pb[:].rearrange("p (c m) -> p c m", c=2)), reads=[pb], writes=[smkT])
                            st2 = cmst[(hm + 1) % 2]
                            P.dma("sp", st2[:, :, :], cmv[s, hm, :, :].rearrange("(j m) d -> m j d", j=2), st2, writes=[st2])
                            P.op("pool", lambda g, hm=hm, st2=st2: g.tensor_copy(out=smvb[:, :, hm * 256:(hm + 1) * 256],
                                 in_=st2[:, :, :]), reads=[st2], writes=[smvb])
                        mem_attn(smkT, smvb, c0, L)
                for c2 in range(8):
                    pb = PS[c2 % 4]
                    proj_fm(Wmo, c2 * 128, omT, N, pb)
                    y = yst[c2 % 2]
                    P.op("act", lambda a, pb=pb, y=y: a.copy(out=y[:, 0:N], in_=pb[:, 0:N]), reads=[pb], writes=[y])
                    P.dma("pool", YMs[:, c2, tl["r0"]:tl["r0"] + N], y[:, 0:N], y, reads=[y], writes=[trk["YM"][ti]])
            P.barrier()
        P.barrier()
        memst.close()

        with contextlib.suppress(_SkipPhase), contextlib.ExitStack() as ph:
            _phase_gate(2)
            Wc = sb(ph, "Wc", [128, 8, 2048], BF16)
            Wco = sb(ph, "Wco", [128, 8, D], BF16)
            with contextlib.ExitStack() as ws:
                wst[0] = sb(ws, "wst0", [128, 2048], F32); wst[1] = sb(ws, "wst1", [128, 2048], F32)
                load_w(Wc, w_in[:, 3072:5120], 8, 2048)
                load_w(Wco, w_conv_o, 8, D)
                P.barrier()
            common(ph, 4, 512, ["pre"])
            cw = sb(ph, "cw", [128, 8, 31], F32)
            cvec = sb(ph, "cvec", [128, 8, 3], F32)
            with contextlib.ExitStack() as ws:
                stg = sb(ws, "stg", [31, D], F32)
                load_cols(cw, [(conv_dw_w[:, :], 31)], 31, 8, stg)
                load_cols(cvec, [(conv_dw_b[0:1, :], 1), (conv_ln_g[0:1, :], 1), (conv_ln_b[0:1, :], 1)], 3, 8, stg)
                P.barrier()
            uP = sb(ph, "uP", [128, 8, 542], F32)
            uP2 = sb(ph, "uP2", [128, 8, 542], F32)
            uS = sb(ph, "uS", [128, 8, NS, 94], F32)
            ccs = [sb(ph, "cc", [128, 8, 512], F32), sb(ph, "ccb", [128, 8, 512], F32)]
            ccvs = [[Buf(None, f"ccv{a_}{c_}") for c_ in range(8)] for a_ in range(2)]
            for a_ in range(2):
                for c_ in range(8):
                    ccvs[a_][c_].wl = None
            cdum = sb(ph, "cdum", [128, 4], F32)
            P.op("dve", lambda v: v.memset(cdum[:], 0.0), writes=[cdum])
            csq = [sb(ph, f"csq{k}", [128, 512], F32) for k in range(2)]
            ccT = sb(ph, "ccT", [128, 8, 512], BF16)
            sg = [sb(ph, f"sg{k}", [128, 512], F32) for k in range(2)]
            mean = sb(ph, "mean", [128, 512], F32)
            rs = sb(ph, "rs", [128, 512], F32)
            t1 = [sb(ph, f"t1{k}", [128, 512], F32) for k in range(2)]
            yst = [sb(ph, f"yst{k}", [128, 512], F32) for k in range(2)]
            cst = sb(ph, "cst", [30, D], F32)
            sst = sb(ph, "sst", [30, D], F32)
            uPs = [uP, uP2]
            P.op("dve", lambda v: v.memset(uPs[0][:, :, 0:30], 0.0), writes=[uPs[0]])

            def stageA1(tl):
                N = tl["N"]; ti = tl["idx"]; smp = tl["sample"]
                uP = uPs[ti % 2]
                if (not smp) and ti > 0:
                    P.op("dve", lambda v: v.tensor_copy(out=uP[:, :, 0:30], in_=uPs[(ti - 1) % 2][:, :, 512:542]),
                         reads=[uPs[(ti - 1) % 2]], writes=[uP])
                load_x(tl)
                norm_T(tl, gbc["pre"])
                if smp:
                    for s in range(NS):
                        P.dma("sp", sst[:, :], sconv[s, :, :], sst, writes=[sst])
                        pb = PS[4 + s % 2]
                        for c in range(8):
                            P.op("pe", lambda t, c=c, pb=pb: t.matmul(pb[:, c * 30:(c + 1) * 30],
                                 lhsT=sst[0:30, c * 128:(c + 1) * 128], rhs=identf[0:30, 0:30], start=True, stop=True),
                                 reads=[sst, identf], writes=[pb], inc=(c == 7))
                        P.op("dve", lambda v, s=s, pb=pb: v.tensor_copy(out=uS[:, :, s, 0:30],
                             in_=pb[:, 0:240].rearrange("p (c r) -> p c r", c=8)), reads=[pb], writes=[uS])
            def stageA2(tl):
                N = tl["N"]; ti = tl["idx"]; smp = tl["sample"]
                uP = uPs[ti % 2]
                for c2 in range(8):
                    pa, pbb = PS[(2 * c2) % 4], PS[(2 * c2 + 1) % 4]
                    proj_fm(Wc, c2 * 128, hT, N, pa)
                    proj_fm(Wc, 1024 + c2 * 128, hT, N, pbb)
                    sgk = sg[c2 % 2]
                    P.op("act", lambda a, pbb=pbb, sgk=sgk: a.activation(out=sgk[:, 0:N], in_=pbb[:, 0:N], func=AF.Sigmoid),
                         reads=[pbb], writes=[sgk])
                    if smp:
                        P.op("dve", lambda v, c2=c2, pa=pa, sgk=sgk: v.tensor_tensor(out=uS[:, c2, :, 30:94],
                             in0=pa[:, 0:N].rearrange("p (s t) -> p s t", s=NS),
                             in1=sgk[:, 0:N].rearrange("p (s t) -> p s t", s=NS), op=ALU.mult),
                             reads=[pa, sgk], writes=[uS])
                    else:
                        P.op("dve", lambda v, c2=c2, pa=pa, sgk=sgk: v.tensor_tensor(out=uP[:, c2, 30:542],
                             in0=pa[:, 0:N], in1=sgk[:, 0:N], op=ALU.mult), reads=[pa, sgk], writes=[uP])

            def stageC(tl):
                N = tl["N"]; ti = tl["idx"]; smp = tl["sample"]
                uP = uPs[ti % 2]
                cc_ = ccs[ti % 2]
                ccv = ccvs[ti % 2]
                P.op("dve", lambda v: v.tensor_copy(out=cdum[:, 2:3], in_=cdum[:, 3:4]), reads=[cc_, cdum], writes=ccv + [cdum, cc_])
                eng = "dve"
                ub = uS if smp else uP
                for (c0, L, s) in tl["segs"]:
                    def usrc(k, c2, s=s, L=L):
                        return uS[:, c2, s, k:k + L] if smp else uP[:, c2, k:k + L]
                    for c2 in range(8):
                        P.op(eng, lambda v, c2=c2, c0=c0, L=L: v.tensor_scalar(out=cc_[:, c2, c0:c0 + L], in0=usrc(0, c2),
                             scalar1=cw[:, c2, 0:1], scalar2=cvec[:, c2, 0:1], op0=ALU.mult, op1=ALU.add),
                             reads=[ub, cw, cvec], writes=[ccv[c2]])
                    for k in range(1, 31):
                        for c2 in range(8):
                            P.op(eng, lambda v, c2=c2, c0=c0, L=L, k=k: v.scalar_tensor_tensor(
                                 out=cc_[:, c2, c0:c0 + L], in0=usrc(k, c2), scalar=cw[:, c2, k:k + 1],
                                 in1=cc_[:, c2, c0:c0 + L], op0=ALU.mult, op1=ALU.add),
                                 reads=[ub, cw, ccv[c2]], writes=[ccv[c2]])
                P.op(eng, lambda v: v.tensor_copy(out=cdum[:, 0:1], in_=cdum[:, 1:2]), reads=ccv + [cdum], writes=[cc_, cdum])

            def stageL(tl):
                N = tl["N"]; ti = tl["idx"]; smp = tl["sample"]
                uP = uPs[ti % 2]
                cc_ = ccs[ti % 2]
                pm, pq = PS[4], PS[5]
                for c2 in range(8):
                    q = csq[c2 % 2]
                    P.op("act", lambda a, c2=c2, q=q: a.activation(out=q[:, 0:N], in_=cc_[:, c2, 0:N], func=AF.Square),
                         reads=[cc_], writes=[q])
                    P.op("pe", lambda t, c2=c2: t.matmul(pm[:, 0:N], lhsT=onesf[:], rhs=cc_[:, c2, 0:N],
                         start=(c2 == 0), stop=(c2 == 7)), reads=[onesf, cc_], writes=[pm])
                    P.op("pe", lambda t, c2=c2, q=q: t.matmul(pq[:, 0:N], lhsT=onesf[:], rhs=q[:, 0:N],
                         start=(c2 == 0), stop=(c2 == 7)), reads=[onesf, q], writes=[pq])
                P.op("act", lambda a: a.copy(out=mean[:, 0:N], in_=pm[:, 0:N]), reads=[pm], writes=[mean])
                P.op("dve", lambda v: v.tensor_tensor(out=rs[:, 0:N], in0=mean[:, 0:N], in1=mean[:, 0:N], op=ALU.mult),
                     reads=[mean], writes=[rs])
                P.op("dve", lambda v: v.tensor_tensor(out=rs[:, 0:N], in0=pq[:, 0:N], in1=rs[:, 0:N], op=ALU.subtract),
                     reads=[pq, rs], writes=[rs])
                rstd_from(rs[:, 0:N], rs[:, 0:N], rs, rs, 1.0)
                for c2 in range(8):
                    tk = t1[c2 % 2]
                    P.op("dve", lambda v, c2=c2, tk=tk: v.tensor_tensor(out=tk[:, 0:N], in0=cc_[:, c2, 0:N], in1=mean[:, 0:N],
                         op=ALU.subtract), reads=[cc_, mean], writes=[tk])
                    P.op("dve", lambda v, tk=tk: v.tensor_tensor(out=tk[:, 0:N], in0=tk[:, 0:N], in1=rs[:, 0:N], op=ALU.mult),
                         reads=[tk, rs], writes=[tk])
                    P.op("act", lambda a, c2=c2, tk=tk: a.activation(out=ccT[:, c2, 0:N], in_=tk[:, 0:N], func=AF.Silu,
                         scale=cvec[:, c2, 1:2], bias=cvec[:, c2, 2:3]), reads=[tk, cvec], writes=[ccT])
                for c2 in range(8):
                    pb = PS[c2 % 4]
                    proj_fm(Wco, c2 * 128, ccT, N, pb)
                    y = yst[c2 % 2]
                    P.op("act", lambda a, pb=pb, y=y: a.copy(out=y[:, 0:N], in_=pb[:, 0:N]), reads=[pb], writes=[y])
                    P.dma("pool", YCs[:, c2, tl["r0"]:tl["r0"] + N], y[:, 0:N], y, reads=[y], writes=[trk["YC"][ti]])
                ends = [(s, 64) for (_, _, s) in tl["segs"]] if smp else ([(None, 512)] if ti == NT - 1 else [])
                for (s, L) in ends:
                    for half in range(2):
                        pb = PS[4 + half]
                        for cq in range(4):
                            c2 = half * 4 + cq
                            src = uS[:, c2, s, L:L + 30] if smp else uP[:, c2, L:L + 30]
                            P.op("pe", lambda t, cq=cq, pb=pb, src=src: t.matmul(pb[0:30, cq * 128:(cq + 1) * 128], lhsT=src,
                                 rhs=identf[:], start=True, stop=True), reads=[uS if smp else uP, identf], writes=[pb],
                                 inc=(cq == 3))
                        P.op("dve", lambda v, half=half, pb=pb: v.tensor_copy(out=cst[:, half * 512:(half + 1) * 512],
                             in_=pb[0:30, :]), reads=[pb], writes=[cst])
                    P.dma("pool", cs[s, :, :] if smp else cp[:, :], cst[:, :], cst, reads=[cst])


            nt_ = len(tiles)
            for n_ in range(-2, nt_):
                if 0 <= n_ + 2 < nt_:
                    stageA1(tiles[n_ + 2])
                if 0 <= n_ + 1 < nt_:
                    stageC(tiles[n_ + 1])
                if 0 <= n_ + 2 < nt_:
                    stageA2(tiles[n_ + 2])
                if 0 <= n_ < nt_:
                    stageL(tiles[n_])
            P.barrier()
        with contextlib.suppress(_SkipPhase), contextlib.ExitStack() as ph:
            _phase_gate(3)
            Wqkv = sb(ph, "Wqkv", [128, 8, 3072], BF16)
            with contextlib.ExitStack() as ws:
                wst[0] = sb(ws, "wst0", [128, 2048], F32); wst[1] = sb(ws, "wst1", [128, 2048], F32)
                load_w(Wqkv, w_in[:, 0:3072], 8, 3072)
                P.barrier()
            common(ph, 4, 512, ["pre"])
            masks = sb(ph, "masks", [128, 4, 512], BF16)
            P.op("pool", lambda g: g.memset(masks[:], 1.0), writes=[masks])
            for r in range(4):
                P.op("pool", lambda g, r=r: g.affine_select(out=masks[:, r, :], in_=masks[:, r, :], pattern=[[1, 512]],
                     compare_op=ALU.is_gt, fill=0.0, base=-128 * r, channel_multiplier=-1), reads=[masks], writes=[masks])
            qT = sb(ph, "qT", [128, 8, 512], BF16)
            kT = sb(ph, "kT", [128, 8, 512], BF16)
            tok = [sb(ph, f"tok{k}", [128, D], F32) for k in range(2)]
            vbf = [sb(ph, f"vbf{k}", [128, D], BF16) for k in range(2)]
            osb = sb(ph, "osb", [128, 8, 512], BF16)
            Kt = [sb(ph, f"Kt{k}", [128, 2, 512], BF16) for k in range(2)]
            Vt = [sb(ph, f"Vt{k}", [128, 4, 2, 128], BF16) for k in range(2)]
            Vs = [sb(ph, f"Vs{k}", [128, 4, 128], BF16) for k in range(2)]
            for k in range(2):
                P.op("pool", lambda g, k=k: g.memset(Kt[k][:], 0.0), writes=[Kt[k]])
                P.op("pool", lambda g, k=k: g.memset(Vt[k][:], 0.0), writes=[Vt[k]])
            Ee = [sb(ph, f"Ee{k}", [128, 512], F32) for k in range(2)]
            Sp = [sb(ph, f"Sp{k}", [128, 512], BF16) for k in range(2)]
            Ls = [[sb(ph, f"Ls{h}{k}", [128, 512], BF16) for k in range(2)] for h in range(2)]
            Aa = [sb(ph, f"Aa{k}", [128, 512], BF16) for k in range(2)]
            kc = [sb(ph, f"kc{k}", [128, 4, 128], F32) for k in range(2)]
            vc = [sb(ph, f"vc{k}", [128, 2, 4, 64], F32) for k in range(2)]
            PC = [[PS[0], PS[1]], [PS[2], PS[3]]]; PO = PS[4]

            def S1(u):
                hh, p, Ktb, kcols, qcols, N, mask_ap = u["hh"], u["p"], u["Ktb"], u["kcols"], u["qcols"], u["N"], u["mask"]
                z, e, s_ = PC[hh][u["lsi"] % 2], Ee[hh], Sp[hh]
                P.op("pe", lambda t: t.matmul(z[:, 0:N], lhsT=Ktb[:, hh, kcols], rhs=qT[:, p, qcols], start=True, stop=False),
                     reads=[Ktb, qT], writes=[z])
                P.op("act", lambda a: a.activation(out=e[:, 0:N], in_=z[:, 0:N], func=AF.Exp), reads=[z], writes=[e])
                P.op("act", lambda a: a.activation(out=s_[:, 0:N], in_=e[:, 0:N], func=AF.Ln, bias=1.0, scale=1.0),
                     reads=[e], writes=[s_])
                if mask_ap is not None:
                    P.op("dve", lambda v: v.tensor_tensor(out=s_[:, 0:N], in0=s_[:, 0:N], in1=mask_ap, op=ALU.mult),
                         reads=[s_, masks], writes=[s_])

            def S2a(u):
                hh, N = u["hh"], u["N"]
                first, last, lsi = u["first"], u["last"], u["lsi"]
                cb_, s_, a_ = PC[hh][lsi % 2], Sp[hh], Aa[hh]
                lo, ln = Ls[hh][lsi % 2], Ls[hh][(lsi + 1) % 2]
                P.op("pe", lambda t: t.matmul(cb_[:, 0:N], lhsT=negtri[:, :], rhs=s_[:, 0:N], start=False, stop=first),
                     reads=[negtri, s_], writes=[cb_], inc=first)
                if not first:
                    P.op("pe", lambda t: t.matmul(cb_[:, 0:N], lhsT=negones[:, :], rhs=lo[:, 0:N], start=False, stop=True),
                         reads=[negones, lo], writes=[cb_])
                if not last:
                    if first:
                        P.op("pool", lambda g: g.tensor_copy(out=ln[:, 0:N], in_=s_[:, 0:N]), reads=[s_], writes=[ln])
                    else:
                        P.op("dve", lambda v: v.tensor_tensor(out=ln[:, 0:N], in0=lo[:, 0:N], in1=s_[:, 0:N], op=ALU.add),
                             reads=[lo, s_], writes=[ln])
                P.op("act", lambda a: a.activation(out=a_[:, 0:N], in_=cb_[:, 0:N], func=AF.Exp), reads=[cb_], writes=[a_])

            def S2b(u):
                hh, Vb, vr, N, mask_ap, first, last = u["hh"], u["Vb"], u["vr"], u["N"], u["mask"], u["first"], u["last"]
                hp = slice(hh * 64, hh * 64 + 64)
                a_ = Aa[hh]
                if mask_ap is not None:
                    P.op("dve", lambda v: v.tensor_tensor(out=a_[:, 0:N], in0=a_[:, 0:N], in1=mask_ap, op=ALU.mult),
                         reads=[a_, masks], writes=[a_])
                P.op("pe", lambda t: t.matmul(PO[:, 0:N], lhsT=Vb[:, vr, hh, :], rhs=a_[:, 0:N], start=(first and hh == 0),
                     stop=(last and hh == 1)), reads=[Vb, a_], writes=[PO], inc=(last and hh == 1))

            def emit_units(units):
                n = len(units)
                for i in range(n + 2):
                    if i < n:
                        if units[i].get("pre"):
                            units[i]["pre"]()
                        S1(units[i])
                    if 0 <= i - 1 < n:
                        S2a(units[i - 1])
                    if 0 <= i - 2 < n:
                        S2b(units[i - 2])

            ldc = [0]

            def load_kv(p, tj, ti):
                k = ldc[0] % 2
                ldc[0] += 1
                r0 = tiles[tj]["r0"]
                for hh in range(2):
                    P.dma("sp", Kt[k][hh * 64:(hh + 1) * 64, hh, :], KTs[hh * 64:(hh + 1) * 64, p, r0:r0 + 512], Kt[k],
                          reads=[trk["KT"][tj]], writes=[Kt[k]])
                P.dma("sp", Vs[k][:, :, :], VSs[r0:r0 + 512, p * 128:(p + 1) * 128].rearrange("(r s) c -> s r c", r=4),
                      Vs[k], reads=[trk["VS"][tj]], writes=[Vs[k]])
                for hh in range(2):
                    P.op("pool", lambda g, hh=hh: g.tensor_copy(out=Vt[k][:, :, hh, hh * 64:(hh + 1) * 64],
                         in_=Vs[k][:, :, hh * 64:(hh + 1) * 64]), reads=[Vs[k]], writes=[Vt[k]])
                return k

            for tl in tiles:
                N = tl["N"]; ti = tl["idx"]; smp = tl["sample"]; r0 = tl["r0"]
                nsub = N // 128
                load_x(tl)
                norm_T(tl, gbc["pre"])
                for p in range(8):
                    pq_, pk_ = PS[5], PS[6]
                    proj_fm(Wqkv, p * 128, hT, N, pq_)
                    P.op("act", lambda a, p=p: a.activation(out=qT[:, p, 0:N], in_=pq_[:, 0:N], func=AF.Copy, scale=0.125),
                         reads=[pq_], writes=[qT])
                    proj_fm(Wqkv, 1024 + p * 128, hT, N, pk_)
                    P.op("dve", lambda v, p=p: v.tensor_copy(out=kT[:, p, 0:N], in_=pk_[:, 0:N]), reads=[pk_], writes=[kT])
                P.dma("pool", KTs[:, :, r0:r0 + N], kT[:, :, 0:N], kT, reads=[kT], writes=[trk["KT"][ti]])
                for j in range(nsub):
                    for which in range(2):
                        tk = tok[which]
                        for hf in range(2):
                            pb = PS[5 + hf]
                            proj_tm(Wqkv, 1024 * (1 + which) + hf * 512, hT, j * 128, pb)
                            if hf == 0:
                                P.op("act", lambda a, tk=tk, pb=pb: a.copy(out=tk[:, 0:512], in_=pb[:]), reads=[pb], writes=[tk])
                            else:
                                P.op("dve", lambda v, tk=tk, pb=pb: v.tensor_copy(out=tk[:, 512:1024], in_=pb[:]), reads=[pb], writes=[tk])
                        if not smp:
                            dst = (kp if which == 0 else vp)[:, r0 + j * 128:r0 + (j + 1) * 128, :].rearrange("h t d -> t h d")
                            P.dma("pool", dst, tk[:, :].rearrange("t (h d) -> t h d", h=16), tk, reads=[tk])
                        else:
                            for s2 in range(2):
                                s = j * 2 + s2
                                dst = (ks if which == 0 else vs)[s, :, :, :].rearrange("h t d -> t h d")
                                P.dma("pool", dst, tk[s2 * 64:(s2 + 1) * 64, :].rearrange("t (h d) -> t h d", h=16), tk, reads=[tk])
                        if which == 1:
                            vb = vbf[j % 2]
                            P.op("pool", lambda g, vb=vb, tk=tk: g.tensor_copy(out=vb[:], in_=tk[:]), reads=[tk], writes=[vb])
                            P.dma("pool", VSs[r0 + j * 128:r0 + (j + 1) * 128, :], vb[:, :], vb, reads=[vb], writes=[trk["VS"][ti]])
                import os as _os
                _ka = _os.environ.get("KA_SKIP", "")
                if not smp:
                    for p in range(8 if "p" not in _ka else 0):
                        nkt = ti + 1
                        nsteps = 4 * nkt
                        units = []
                        bufk = {0: load_kv(p, ti, ti)}
                        step = 0
                        for jj in range(nkt):
                            tj = ti - jj
                            for r in (3, 2, 1, 0):
                                for hh in range(2):
                                    u = dict(hh=hh, p=p, jj=jj, kcols=slice(r * 128, (r + 1) * 128), vr=r, qcols=slice(0, N), N=N,
                                             first=(step == 0), last=(step == nsteps - 1),
                                             mask=(masks[:, r, :] if jj == 0 else None), lsi=step)
                                    if r == 2 and hh == 0 and jj + 1 < nkt:
                                        u["pre"] = (lambda jj=jj, tj=tj: bufk.__setitem__(jj + 1, load_kv(p, tj - 1, ti)))
                                    units.append(u)
                                step += 1

                        class _Lazy(dict):
                            pass
                        def _res(u):
                            k = bufk[u["jj"]]
                            u["Ktb"], u["Vb"] = Kt[k], Vt[k]
                        n = len(units)
                        for i in range(n + 2):
                            if i < n:
                                if units[i].get("pre"):
                                    units[i]["pre"]()
                                _res(units[i])
                                S1(units[i])
                            if 0 <= i - 1 < n:
                                S2a(units[i - 1])
                            if 0 <= i - 2 < n:
                                S2b(units[i - 2])
                        P.op("act", lambda a, p=p: a.copy(out=osb[:, p, 0:N], in_=PO[:, 0:N]), reads=[PO], writes=[osb])
                else:
                    for (c0, L, s) in tl["segs"]:
                        for p in range(8 if "s" not in _ka else 0):
                            k = ldc[0] % 2
                            ldc[0] += 1
                            nsteps = 1 + PB
                            qc = slice(c0, c0 + L)

                            def pre_first(k=k, p=p, c0=c0):
                                P.op("pool", lambda g: g.memset(Kt[k][:, :, 0:128], 0.0), writes=[Kt[k]])
                                P.op("pool", lambda g: g.memset(Vt[k][:, 0, :, :], 0.0), writes=[Vt[k]])
                                for hh in range(2):
                                    hs = slice(hh * 64, (hh + 1) * 64)
                                    P.dma("sp", Kt[k][hs, hh, 0:64], KTs[hs, p, r0 + c0:r0 + c0 + 64], Kt[k],
                                          reads=[trk["KT"][ti]], writes=[Kt[k]])
                                    P.dma("sp", Vt[k][0:64, 0, hh, hs], VSs[r0 + c0:r0 + c0 + 64, p * 128 + hh * 64:p * 128 + (hh + 1) * 64],
                                          Vt[k], reads=[trk["VS"][ti]], writes=[Vt[k]])

                            def pre_group(kk, g4, p=p, s=s):
                                kcb, vcb = kc[kk], vc[kk]
                                for h2 in range(2):
                                    P.dma("sp", kcb[:, :, h2 * 64:(h2 + 1) * 64], ck[s, 2 * p + h2, g4 * 512:(g4 + 1) * 512, :].rearrange(
                                          "(b k) d -> k b d", b=4), kcb, writes=[kcb])
                                    P.dma("sp", vcb[:, h2, :, :], cv[s, 2 * p + h2, g4 * 512:(g4 + 1) * 512, :].rearrange(
                                          "(b k) d -> k b d", b=4), vcb, writes=[vcb])
                                pt = PS[7]
                                for b in range(4):
                                    P.op("pe", lambda t, b=b: t.matmul(pt[:, b * 128:(b + 1) * 128],
                                         lhsT=kcb[:, b, :], rhs=identf[:], start=True, stop=True),
                                         reads=[kcb, identf], writes=[pt], inc=(b == 3))
                                for h2 in range(2):
                                    hs = slice(h2 * 64, (h2 + 1) * 64)
                                    P.op("dve", lambda v, h2=h2, hs=hs: v.tensor_copy(out=Kt[kk][hs, h2, :], in_=pt[hs, :]),
                                         reads=[pt], writes=[Kt[kk]])
                                    P.op("pool", lambda g, h2=h2, hs=hs: g.tensor_copy(out=Vt[kk][:, :, h2, hs],
                                         in_=vcb[:, h2, :, :]), reads=[vcb], writes=[Vt[kk]])

                            units = []
                            for hh in range(2):
                                units.append(dict(hh=hh, p=p, Ktb=Kt[k], Vb=Vt[k], kcols=slice(0, 128), vr=0, qcols=qc, N=L,
                                                  first=True, last=False, mask=masks[:, 0, 0:64], lsi=0,
                                                  pre=(pre_first if hh == 0 else None)))
                            step = 1
                            glist = list(range(PB // 4 - 1, -1, -1))
                            gk = []
                            for gi, g4 in enumerate(glist):
                                kk = ldc[0] % 2
                                ldc[0] += 1
                                gk.append(kk)
                            for gi, g4 in enumerate(glist):
                                kk = gk[gi]
                                for bi, b in enumerate((3, 2, 1, 0)):
                                    for hh in range(2):
                                        u = dict(hh=hh, p=p, Ktb=Kt[kk], Vb=Vt[kk], kcols=slice(b * 128, (b + 1) * 128), vr=b, qcols=qc, N=L,
                                                 first=False, last=(step == nsteps - 1), mask=None, lsi=step)
                                        if gi == 0 and bi == 0 and hh == 0:
                                            u["pre"] = (lambda kk=kk, g4=g4: pre_group(kk, g4))
                                        if bi == 1 and hh == 0 and gi + 1 < len(glist):
                                            u["pre"] = (lambda kk=gk[gi + 1], g4=glist[gi + 1]: pre_group(kk, g4))
                                        units.append(u)
                                    step += 1
                            emit_units(units)
                            P.op("act", lambda a, p=p, c0=c0, L=L: a.copy(out=osb[:, p, c0:c0 + L], in_=PO[:, 0:L]), reads=[PO], writes=[osb])
                P.dma("pool", OSs[:, :, r0:r0 + N], osb[:, :, 0:N], osb, reads=[osb], writes=[trk["OS"][ti]])

            P.barrier()
        with contextlib.suppress(_SkipPhase), contextlib.ExitStack() as ph:
            _phase_gate(4)
            Wg = sb(ph, "Wg", [128, 8, 3072], BF16)
            Wso = sb(ph, "Wso", [128, 8, D], BF16)
            Wo = sb(ph, "Wo", [128, 8, D], BF16)
            with contextlib.ExitStack() as ws:
                wst[0] = sb(ws, "wst0", [128, 2048], F32); wst[1] = sb(ws, "wst1", [128, 2048], F32)
                load_w(Wg, w_in[:, 6144:9216], 8, 3072)
                load_w(Wso, w_sb_o, 8, D)
                load_w(Wo, w_out, 8, D)
                P.barrier()
            common(ph, 4, 512, ["pre", "post"])
            os_ = sb(ph, "os_", [128, 8, 512], BF16)
            ycb = [sb(ph, f"ycb{k}", [128, 512], F32) for k in range(2)]
            ymb = [sb(ph, f"ymb{k}", [128, 512], F32) for k in range(2)]
            sgg = [sb(ph, f"sgg{k}", [128, 512], F32) for k in range(3)]
            mrg = [sb(ph, f"mrg{k}", [128, 512], F32) for k in range(2)]
            mg = sb(ph, "mg", [128, 8, 512], BF16)
            mo = [sb(ph, f"mo{k}", [128, D], F32) for k in range(2)]
            ss2 = sb(ph, "ss2", [128, 4], F32)
            rs2 = sb(ph, "rs2", [128, 4], F32)
            for tl in tiles:
                N = tl["N"]; ti = tl["idx"]; r0 = tl["r0"]
                nsub = N // 128
                load_x(tl)
                norm_T(tl, gbc["pre"])
                P.dma("sp", os_[:, :, 0:N], OSs[:, :, r0:r0 + N], os_, reads=[trk["OS"][ti]], writes=[os_])
                for c2 in range(8):
                    yc, ym = ycb[c2 % 2], ymb[c2 % 2]
                    P.dma("sp", yc[:, 0:N], YCs[:, c2, r0:r0 + N], yc, reads=[trk["YC"][ti]], writes=[yc])
                    P.dma("sp", ym[:, 0:N], YMs[:, c2, r0:r0 + N], ym, reads=[trk["YM"][ti]], writes=[ym])
                    pys, pg = PS[0 + (c2 % 2) * 4], [PS[1 + (c2 % 2) * 4], PS[2 + (c2 % 2) * 4], PS[3 + (c2 % 2) * 4]]
                    proj_fm(Wso, c2 * 128, os_, N, pys)
                    for gi in range(3):
                        proj_fm(Wg, gi * 1024 + c2 * 128, hT, N, pg[gi])
                        P.op("act", lambda a, gi=gi, pg=pg: a.activation(out=sgg[gi][:, 0:N], in_=pg[gi][:, 0:N], func=AF.Sigmoid),
                             reads=[pg[gi]], writes=[sgg[gi]])
                    m = mrg[c2 % 2]
                    P.op("dve", lambda v, m=m, pys=pys: v.tensor_tensor(out=m[:, 0:N], in0=pys[:, 0:N], in1=sgg[0][:, 0:N], op=ALU.mult),
                         reads=[pys, sgg[0]], writes=[m])
                    P.op("pool", lambda g, yc=yc: g.tensor_tensor(out=sgg[1][:, 0:N], in0=sgg[1][:, 0:N], in1=yc[:, 0:N], op=ALU.mult),
                         reads=[sgg[1], yc], writes=[sgg[1]])
                    P.op("pool", lambda g, ym=ym: g.tensor_tensor(out=sgg[2][:, 0:N], in0=sgg[2][:, 0:N], in1=ym[:, 0:N], op=ALU.mult),
                         reads=[sgg[2], ym], writes=[sgg[2]])
                    P.op("dve", lambda v, m=m: v.tensor_tensor(out=m[:, 0:N], in0=m[:, 0:N], in1=sgg[1][:, 0:N], op=ALU.add),
                         reads=[m, sgg[1]], writes=[m])
                    P.op("dve", lambda v, m=m, c2=c2: v.tensor_tensor(out=mg[:, c2, 0:N], in0=m[:, 0:N], in1=sgg[2][:, 0:N], op=ALU.add),
                         reads=[m, sgg[2]], writes=[mg])
                for j in range(nsub):
                    mj = mo[j % 2]
                    for hf in range(2):
                        pb = PS[hf]
                        proj_tm(Wo, hf * 512, mg, j * 128, pb)
                        if hf == 0:
                            P.op("act", lambda a, mj=mj, pb=pb: a.copy(out=mj[:, 0:512], in_=pb[:]), reads=[pb], writes=[mj])
                        else:
                            P.op("dve", lambda v, mj=mj, pb=pb: v.tensor_copy(out=mj[:, 512:1024], in_=pb[:]), reads=[pb], writes=[mj])
                    P.op("dve", lambda v, j=j: v.memset(ss2[:, j:j + 1], 0.0), writes=[ss2])
                    P.op("act", lambda a, mj=mj, j=j: a.activation(out=cm["junk"][:], in_=mj[:], func=AF.Square, accum_out=ss2[:, j:j + 1]),
                         reads=[mj], writes=[cm["junk"], ss2])
                    rstd_from(ss2[:, j:j + 1], rs2[:, j:j + 1], ss2, rs2, D)
                    xo = mj
                    P.op("dve", lambda v, mj=mj, j=j: v.scalar_tensor_tensor(out=mj[:], in0=mj[:], scalar=rs2[:, j:j + 1],
                         in1=gbc["post"][:], op0=ALU.mult, op1=ALU.mult), reads=[mj, rs2, gbc["post"]], writes=[mj])
                    P.op("pool", lambda g, mj=mj, j=j, xo=xo: g.tensor_tensor(out=xo[:], in0=mj[:], in1=xt[j][:], op=ALU.add),
                         reads=[mj, xt[j]], writes=[xo])
                    P.dma("pool", XMs[r0 + j * 128:r0 + (j + 1) * 128, :], xo[:, :], xo, reads=[xo], writes=[trk["XM"][ti]])

            P.barrier()
        with contextlib.suppress(_SkipPhase), contextlib.ExitStack() as ph:
            _phase_gate(5)
            Wup = sb(ph, "Wup", [128, 8, FF2], BF16)
            Wdn = sb(ph, "Wdn", [128, NFC, D], BF16)
            with contextlib.ExitStack() as ws:
                wst[0] = sb(ws, "wst0", [128, 2048], F32); wst[1] = sb(ws, "wst1", [128, 2048], F32)
                load_w(Wup, w_ffn_up, 8, FF2)
                load_w(Wdn, w_ffn_down, NFC, D)
                P.barrier()
            common(ph, 2, 256, ["fpre", "fpost"])
            fw = sb(ph, "fw", [128, 44, 3], F32)
            halP = sb(ph, "halP", [128, 44, 2], F32)
            halS = sb(ph, "halS", [128, 44, NS, 2], F32)
            sfs = sb(ph, "sfs", [3, 512], F32)
            for g11 in range(11):
                P.dma("sp", sfs[0:3, :], ffn_dw_w[:, g11 * 512:(g11 + 1) * 512], sfs, writes=[sfs])
                pb = PS[6]
                for c in range(4):
                    P.op("pe", lambda t, c=c: t.matmul(pb[:, c * 3:(c + 1) * 3], lhsT=sfs[0:3, c * 128:(c + 1) * 128],
                         rhs=identf[0:3, 0:3], start=True, stop=True), reads=[sfs, identf], writes=[pb], inc=(c == 3))
                P.op("dve", lambda v, g11=g11: v.tensor_copy(out=fw[:, g11 * 4:(g11 + 1) * 4, :],
                     in_=pb[:, 0:12].rearrange("p (c r) -> p c r", c=4)), reads=[pb], writes=[fw])
            for s in range(NS):
                for g11 in range(11):
                    P.dma("sp", sfs[0:2, :], sffn[s, :, g11 * 512:(g11 + 1) * 512], sfs, writes=[sfs])
                    pb = PS[7]
                    for c in range(4):
                        P.op("pe", lambda t, c=c: t.matmul(pb[:, c * 2:(c + 1) * 2], lhsT=sfs[0:2, c * 128:(c + 1) * 128],
                             rhs=identf[0:2, 0:2], start=True, stop=True), reads=[sfs, identf], writes=[pb], inc=(c == 3))
                    P.op("dve", lambda v, s=s, g11=g11: v.tensor_copy(out=halS[:, g11 * 4:(g11 + 1) * 4, s, :],
                         in_=pb[:, 0:8].rearrange("p (c r) -> p c r", c=4)), reads=[pb], writes=[halS])
            upb = [sb(ph, f"upb{k}", [128, 4, 66], F32) for k in range(2)]
            cv_ = [sb(ph, f"cvv{k}", [128, 256], F32) for k in range(2)]
            gl = sb(ph, "gl", [128, 256], F32)
            g2 = sb(ph, "g2", [128, 256], F32)
            gT = sb(ph, "gT", [128, NFC, 256], BF16)
            dn = [sb(ph, f"dn{k}", [128, D], F32) for k in range(2)]
            fst = [sb(ph, f"fst{k}", [2, 512], F32) for k in range(2)]
            ss3 = sb(ph, "ss3", [128, 2], F32)
            rs3 = sb(ph, "rs3", [128, 2], F32)
            P.op("dve", lambda v: v.memset(halP[:], 0.0), writes=[halP])
            ftiles = []
            for i in range(T // 256):
                ftiles.append(dict(r0=i * 256, N=256, segs=[(0, 256, None)], idx=i // 2, sample=False, last=(i == T // 256 - 1)))
            ftiles.append(dict(r0=T, N=TS, segs=[(s * 64, 64, s) for s in range(NS)], idx=NT, sample=True, last=True))
            for tl in ftiles:
                N = tl["N"]; ti = tl["idx"]; r0 = tl["r0"]; smp = tl["sample"]
                nsub = N // 128
                load_x(tl, src_fn=lambda tl, j: XMs[tl["r0"] + j * 128:tl["r0"] + (j + 1) * 128, :], trkb=trk["XM"][ti])
                norm_T(tl, gbc["fpre"])
                for j in range(NFC):
                    outs = []
                    for which in range(2):
                        ch = which * NFC + j
                        pb = PS[(2 * j + which) % 4]
                        proj_fm(Wup, ch * 128, hT, N, pb)
                        ub = upb[which]
                        if smp:
                            P.op("act", lambda a, ch=ch, ub=ub: a.copy(out=ub[:, :, 0:2], in_=halS[:, ch, :, :]), reads=[halS], writes=[ub])
                            P.op("act", lambda a, pb=pb, ub=ub: a.copy(out=ub[:, :, 2:66], in_=pb[:, 0:N].rearrange("p (s t) -> p s t", s=NS)),
                                 reads=[pb], writes=[ub])
                            src = lambda k, ub=ub: ub[:, :, k:k + 64]
                            o3 = lambda t_: t_[:, 0:N].rearrange("p (s t) -> p s t", s=NS)
                        else:
                            uf = ub[:, :, :].rearrange("p a b -> p (a b)")
                            P.op("act", lambda a, ch=ch, uf=uf: a.copy(out=uf[:, 0:2], in_=halP[:, ch, :]), reads=[halP], writes=[ub])
                            P.op("act", lambda a, pb=pb, uf=uf: a.copy(out=uf[:, 2:2 + N], in_=pb[:, 0:N]), reads=[pb], writes=[ub])
                            P.op("pool", lambda g, ch=ch, uf=uf: g.tensor_copy(out=halP[:, ch, :], in_=uf[:, N:N + 2]),
                                 reads=[ub], writes=[halP])
                            src = lambda k, uf=uf: uf[:, k:k + N]
                            o3 = lambda t_: t_[:, 0:N]
                        co = cv_[which]
                        P.op("dve", lambda v, co=co, src=src, o3=o3, ch=ch: v.tensor_scalar(out=o3(co), in0=src(0), scalar1=fw[:, ch, 0:1],
                             scalar2=0.0, op0=ALU.mult, op1=ALU.add), reads=[ub, fw], writes=[co])
                        for k in (1, 2):
                            P.op("dve", lambda v, co=co, src=src, o3=o3, ch=ch, k=k: v.scalar_tensor_tensor(out=o3(co), in0=src(k),
                                 scalar=fw[:, ch, k:k + 1], in1=o3(co), op0=ALU.mult, op1=ALU.add), reads=[ub, fw, co], writes=[co])
                        outs.append(co)
                    xg, xv = outs
                    P.op("pool", lambda g, xg=xg: g.tensor_tensor(out=g2[:, 0:N], in0=xg[:, 0:N], in1=xg[:, 0:N], op=ALU.mult),
                         reads=[xg], writes=[g2])
                    P.op("pool", lambda g: g.tensor_scalar(out=g2[:, 0:N], in0=g2[:, 0:N], scalar1=0.044715, scalar2=1.0,
                         op0=ALU.mult, op1=ALU.add), reads=[g2], writes=[g2])
                    P.op("pool", lambda g, xg=xg: g.tensor_tensor(out=g2[:, 0:N], in0=g2[:, 0:N], in1=xg[:, 0:N], op=ALU.mult),
                         reads=[g2, xg], writes=[g2])
                    P.op("act", lambda a: a.activation(out=gl[:, 0:N], in_=g2[:, 0:N], func=AF.Sigmoid, scale=1.5957691216057308),
                         reads=[g2], writes=[gl])
                    P.op("dve", lambda v, xg=xg: v.tensor_tensor(out=gl[:, 0:N], in0=gl[:, 0:N], in1=xg[:, 0:N], op=ALU.mult),
                         reads=[gl, xg], writes=[gl])
                    P.op("dve", lambda v, j=j, xv=xv: v.tensor_tensor(out=gT[:, j, 0:N], in0=gl[:, 0:N], in1=xv[:, 0:N], op=ALU.mult),
                         reads=[gl, xv], writes=[gT])
                ends = [(c0 + 62, s) for (c0, L, s) in tl["segs"]] if smp else ([(254, None)] if tl["last"] else [])
                for (t0, s) in ends:
                    for cb in range(11):
                        pb = PS[4 + cb % 2]
                        fb = fst[cb % 2]
                        proj_tm(Wup, cb * 512, hT, t0, pb, M=2)
                        P.op("dve", lambda v, fb=fb, pb=pb: v.tensor_copy(out=fb[:, :], in_=pb[0:2, :]), reads=[pb], writes=[fb])
                        dst = fs[s, :, cb * 512:(cb + 1) * 512] if smp else fp[:, cb * 512:(cb + 1) * 512]
                        P.dma("pool", dst, fb[:, :], fb, reads=[fb])
                for j in range(nsub):
                    dj = dn[j % 2]
                    for hf in range(2):
                        pb = PS[6 + hf]
                        proj_tm(Wdn, hf * 512, gT, j * 128, pb, K=NFC)
                        if hf == 0:
                            P.op("act", lambda a, dj=dj, pb=pb: a.copy(out=dj[:, 0:512], in_=pb[:]), reads=[pb], writes=[dj])
                        else:
                            P.op("dve", lambda v, dj=dj, pb=pb: v.tensor_copy(out=dj[:, 512:1024], in_=pb[:]), reads=[pb], writes=[dj])
                    P.op("dve", lambda v, j=j: v.memset(ss3[:, j:j + 1], 0.0), writes=[ss3])
                    P.op("act", lambda a, dj=dj, j=j: a.activation(out=cm["junk"][:], in_=dj[:], func=AF.Square, accum_out=ss3[:, j:j + 1]),
                         reads=[dj], writes=[cm["junk"], ss3])
                    rstd_from(ss3[:, j:j + 1], rs3[:, j:j + 1], ss3, rs3, D)
                    P.op("dve", lambda v, dj=dj, j=j: v.scalar_tensor_tensor(out=dj[:], in0=dj[:], scalar=rs3[:, j:j + 1],
                         in1=gbc["fpost"][:], op0=ALU.mult, op1=ALU.mult), reads=[dj, rs3, gbc["fpost"]], writes=[dj])
                    P.op("pool", lambda g, dj=dj, j=j: g.tensor_tensor(out=dj[:], in0=dj[:], in1=xt[j][:], op=ALU.add),
                         reads=[dj, xt[j]], writes=[dj])
                    dst = ys[r0 - T + j * 128:r0 - T + (j + 1) * 128, :] if smp else yp[r0 + j * 128:r0 + (j + 1) * 128, :]
                    P.dma("pool", dst, dj[:, :], dj, reads=[dj])
            P.barrier()
        P.finish()
    return nc


_CACHE = {}


def run(T, NS, PAST, per_core):
    key = (T, NS, PAST)
    if key not in _CACHE:
        _CACHE[key] = build(T, NS, PAST)
    nc = _CACHE[key]
    res = run_bass_kernel_spmd(nc, per_core, core_ids=list(range(len(per_core))))
    return res.results


WNAMES = ["g_mem", "w_mem_kv", "g_mix_pre", "g_mix_post", "w_in", "w_sb_o", "conv_dw_w", "conv_dw_b", "conv_ln_g",
          "conv_ln_b", "w_conv_o", "w_mem_o", "w_out", "g_ffn_pre", "g_ffn_post", "w_ffn_up", "ffn_dw_w", "w_ffn_down"]


def make_maps(inp, ncores, NS):
    f = lambda a: np.ascontiguousarray(np.asarray(a, dtype=np.float32))
    B = inp["x_prompt"].shape[0]
    maps = []
    for c in range(ncores):
        b = c % B
        sl = slice(c * NS, (c + 1) * NS)
        m = {"xp": f(inp["x_prompt"][b]), "xs": f(inp["x_sample"][sl]).reshape(NS * 64, D),
             "memp": f(inp["mem_prompt"][b]), "ck": f(inp["cache_sb_k"][0, sl]), "cv": f(inp["cache_sb_v"][0, sl]),
             "sconv": f(inp["state_conv"][0, sl]), "sffn": f(inp["state_ffn_conv"][0, sl]),
             "cmk": f(inp["cache_mem_k"][0, sl]), "cmv": f(inp["cache_mem_v"][0, sl])}
        for n in WNAMES:
            w = f(inp[n][0])
            m[n] = w.reshape(1, -1) if w.ndim == 1 else w
        maps.append(m)
    return maps


def assemble(res, B, ncores):
    cat = lambda n, rng: np.stack([res[c][n] for c in rng])
    pc = range(B)
    sc = range(ncores)
    yp = cat("yp", pc); ys = np.concatenate([res[c]["ys"].reshape(-1, 64, D) for c in sc])
    kp = cat("kp", pc)[None]; vp = cat("vp", pc)[None]
    ks = np.concatenate([res[c]["ks"] for c in sc])[None]; vs = np.concatenate([res[c]["vs"] for c in sc])[None]
    cp = cat("cp", pc)[None]; cs = np.concatenate([res[c]["cs"] for c in sc])[None]
    fp = cat("fp", pc)[None]; fs = np.concatenate([res[c]["fs"] for c in sc])[None]
    mkp = cat("mkp", pc)[None]; mvp = cat("mvp", pc)[None]
    return (yp, ys, kp, vp, ks, vs, cp, cs, fp, fs, mkp, mvp)


def kernel(**inputs):
    T = inputs["x_prompt"].shape[1]
    PAST = inputs["cache_sb_k"].shape[3]
    ncores = 8
    NS = inputs["x_sample"].shape[0] // ncores
    maps = make_maps(inputs, ncores, NS)
    res = run(T, NS, PAST, maps)
    return assemble(res, inputs["x_prompt"].shape[0], ncores)
```

```python
import contextlib
import numpy as np
import concourse.bass as bass
import concourse.mybir as mybir
from concourse.bass_utils import run_bass_kernel_spmd

F32 = mybir.dt.float32
BF16 = mybir.dt.bfloat16
ALU = mybir.AluOpType
AF = mybir.ActivationFunctionType

D = 1024
NCH = 8
FF = 2816
FF2 = 5632
NFC = 22
EPS = 1e-6


class _SkipPhase(Exception):
    pass


def _phase_gate(k):
    import os
    en = os.environ.get("KPH")
    if en is not None and str(k) not in en.split(","):
        raise _SkipPhase()


class Buf:
    def __init__(self, t, name):
        self.t = t
        self.name = name
        self.w = None
        self.r = {}
        self.dsem = None
        self.dcnt = 0
        self.wl = {} if t is None else None

    def __getitem__(self, k):
        return self.t[k]


class Prog:
    def __init__(self, nc, es):
        self.nc = nc
        self.es = es
        self.E = {"pe": nc.tensor, "act": nc.scalar, "dve": nc.vector, "pool": nc.gpsimd, "sp": nc.sync}
        self.sem = {e: es.enter_context(nc.semaphore("s_" + e)) for e in ("pe", "act", "dve", "pool")}
        self.cnt = {e: 0 for e in self.sem}
        self.seen = {e: {} for e in self.E}
        self.dbufs = []

    def _wait(self, e, dep, same_ok):
        if dep is None:
            return
        key, sem, val, src = dep
        if src is not None:
            val = 16 * src.dcnt
        elif key == e and not same_ok:
            return
        if self.seen[e].get(key, 0) >= val:
            return
        self.E[e].wait_ge(sem, val)
        self.seen[e][key] = val

    def _deps(self, e, reads, writes):
        for b in reads:
            self._wait(e, b.w, True)
            if b.wl:
                for d in list(b.wl.values()):
                    self._wait(e, d, True)
        for b in writes:
            self._wait(e, b.w, False)
            for d in list(b.r.values()):
                self._wait(e, d, False)

    def op(self, e, fn, reads=(), writes=(), inc=True):
        self._deps(e, reads, writes)
        ins = fn(self.E[e])
        if inc:
            self.cnt[e] += 1
            ins.then_inc(self.sem[e], 1)
            t = self.cnt[e]
        else:
            t = self.cnt[e] + 1
        dep = (e, self.sem[e], t, None)
        for b in reads:
            b.r[e] = dep
        for b in writes:
            b.w = dep
            b.r = {}
        return ins

    def dma(self, q, out_ap, in_ap, sbuf, reads=(), writes=()):
        self._deps(q, reads, writes)
        if sbuf.dsem is None:
            sbuf.dsem = self.es.enter_context(self.nc.semaphore("d_" + sbuf.name))
            self.dbufs.append(sbuf)
        sbuf.dcnt += 1
        self.E[q].dma_start(out=out_ap, in_=in_ap).then_inc(sbuf.dsem, 16)
        key = ("d", id(sbuf))
        dep = (key, sbuf.dsem, 16 * sbuf.dcnt, sbuf)
        for b in reads:
            b.r[key] = dep
        for b in writes:
            if b.wl is not None:
                b.wl[key] = dep
            else:
                b.w = dep
                b.r = {}

    def barrier(self):
        for e in self.E:
            for k in self.sem:
                if k != e and self.cnt[k] > self.seen[e].get(k, 0):
                    self.E[e].wait_ge(self.sem[k], self.cnt[k])
                    self.seen[e][k] = self.cnt[k]
            for b in self.dbufs:
                key = ("d", id(b))
                if 16 * b.dcnt > self.seen[e].get(key, 0):
                    self.E[e].wait_ge(b.dsem, 16 * b.dcnt)
                    self.seen[e][key] = 16 * b.dcnt

    def finish(self):
        for b in self.dbufs:
            self.E["sp"].wait_ge(b.dsem, 16 * b.dcnt)


def build(T, NS, PAST):
    nc = bass.Bass("TRN2", target_bir_lowering=False)
    TS = NS * 64
    NT = T // 512
    PB = PAST // 128

    def din(name, shape):
        return nc.dram_tensor(name, list(shape), F32, kind="ExternalInput").ap()

    def dout(name, shape):
        return nc.dram_tensor(name, list(shape), F32, kind="ExternalOutput").ap()

    xp = din("xp", [T, D]); xs = din("xs", [TS, D]); memp = din("memp", [256, D])
    ck = din("ck", [NS, 16, PAST, 64]); cv = din("cv", [NS, 16, PAST, 64])
    sconv = din("sconv", [NS, 30, D]); sffn = din("sffn", [NS, 2, FF2])
    cmk = din("cmk", [NS, 4, 256, 256]); cmv = din("cmv", [NS, 4, 256, 256])
    g_mem = din("g_mem", [1, D]); w_mem_kv = din("w_mem_kv", [D, 2048])
    g_mix_pre = din("g_mix_pre", [1, D]); g_mix_post = din("g_mix_post", [1, D])
    w_in = din("w_in", [D, 9216]); w_sb_o = din("w_sb_o", [D, D])
    conv_dw_w = din("conv_dw_w", [31, D]); conv_dw_b = din("conv_dw_b", [1, D])
    conv_ln_g = din("conv_ln_g", [1, D]); conv_ln_b = din("conv_ln_b", [1, D])
    w_conv_o = din("w_conv_o", [D, D]); w_mem_o = din("w_mem_o", [D, D]); w_out = din("w_out", [D, D])
    g_ffn_pre = din("g_ffn_pre", [1, D]); g_ffn_post = din("g_ffn_post", [1, D])
    w_ffn_up = din("w_ffn_up", [D, FF2]); ffn_dw_w = din("ffn_dw_w", [3, FF2]); w_ffn_down = din("w_ffn_down", [FF, D])

    yp = dout("yp", [T, D]); ys = dout("ys", [TS, D])
    kp = dout("kp", [16, T, 64]); vp = dout("vp", [16, T, 64])
    ks = dout("ks", [NS, 16, 64, 64]); vs = dout("vs", [NS, 16, 64, 64])
    cp = dout("cp", [30, D]); cs = dout("cs", [NS, 30, D])
    fp = dout("fp", [2, FF2]); fs = dout("fs", [NS, 2, FF2])
    mkp = dout("mkp", [4, 256, 256]); mvp = dout("mvp", [4, 256, 256])

    TT = T + TS
    KTs = nc.dram_tensor("KTs", [128, 8, TT], BF16).ap()
    VSs = nc.dram_tensor("VSs", [TT, D], BF16).ap()
    OSs = nc.dram_tensor("OSs", [128, 8, TT], BF16).ap()
    YCs = nc.dram_tensor("YCs", [128, 8, TT], F32).ap()
    YMs = nc.dram_tensor("YMs", [128, 8, TT], F32).ap()
    XMs = nc.dram_tensor("XMs", [TT, D], F32).ap()

    tiles = []
    for i in range(NT):
        tiles.append(dict(r0=i * 512, N=512, segs=[(0, 512, None)], idx=i, sample=False))
    tiles.append(dict(r0=T, N=TS, segs=[(s * 64, 64, s) for s in range(NS)], idx=NT, sample=True))
    ntile = len(tiles)
    trk = {n: [Buf(None, f"{n}{i}") for i in range(ntile)] for n in ("KT", "VS", "OS", "YC", "YM", "XM")}

    def xrows(tl, j):
        r = tl["r0"] + j * 128
        if tl["sample"]:
            return xs[r - T:r - T + 128, :]
        return xp[r:r + 128, :]

    es = contextlib.ExitStack()
    with es:
        P = Prog(nc, es)

        uid = [0]

        def sb(st, name, shape, dt):
            uid[0] += 1
            name = f"{name}_{uid[0]}"
            return Buf(st.enter_context(nc.sbuf_tensor(name, list(shape), dt)), name)

        PS = [Buf(es.enter_context(nc.psum_tensor(f"ps{i}", [128, 512], F32)), f"ps{i}") for i in range(8)]

        identb = sb(es, "identb", [128, 128], BF16)
        identf = sb(es, "identf", [128, 128], F32)
        negtri = sb(es, "negtri", [128, 128], BF16)
        negones = sb(es, "negones", [128, 128], BF16)
        onesb = sb(es, "onesb", [128, 128], BF16)
        onesf = sb(es, "onesf", [128, 128], F32)
        for bfr, val in ((identb, 1.0), (identf, 1.0), (negtri, -1.0), (negones, -1.0), (onesb, 1.0),
                         (onesf, 1.0 / D)):
            P.op("pool", lambda g, b=bfr, v=val: g.memset(b[:], v), writes=[bfr])
        for bfr in (identb, identf):
            P.op("pool", lambda g, b=bfr: g.affine_select(out=b[:], in_=b[:], pattern=[[-1, 128]],
                 compare_op=ALU.is_equal, fill=0.0, base=0, channel_multiplier=1), reads=[bfr], writes=[bfr])
        P.op("pool", lambda g: g.affine_select(out=negtri[:], in_=negtri[:], pattern=[[-1, 128]],
             compare_op=ALU.is_ge, fill=0.0, base=0, channel_multiplier=1), reads=[negtri], writes=[negtri])

        gbc = {}
        gsrc = {"pre": g_mix_pre, "post": g_mix_post, "fpre": g_ffn_pre, "fpost": g_ffn_post, "mem": g_mem}
        xt = [None] * 4
        xn = [None] * 2
        cm = {}

        class _HT:
            def __getitem__(self, k):
                return cm["hT"].t[k]
        hT = _HT()

        def common(ph, nsub, N, gs):
            for j in range(nsub):
                xt[j] = sb(ph, f"xt{j}", [128, D], F32)
            for j in range(2):
                xn[j] = sb(ph, f"xn{j}", [128, D], BF16)
            cm["hT"] = sb(ph, "hT", [128, 8, N], BF16)
            cm["junk"] = sb(ph, "junk", [128, D], BF16)
            cm["ssq"] = sb(ph, "ssq", [128, 4], F32)
            cm["rstd"] = sb(ph, "rstd", [128, 4], F32)
            for nm in gs:
                gbc[nm] = sb(ph, "g_" + nm, [128, D], F32)
                P.dma("sp", gbc[nm][:], gsrc[nm][0:1, :].partition_broadcast(128), gbc[nm], writes=[gbc[nm]])

        def rstd_from(ss_ap, out_ap, ssb, outb, n):
            P.op("act", lambda a: a.activation(out=out_ap, in_=ss_ap, func=AF.Ln, scale=1.0 / n, bias=EPS),
                 reads=[ssb], writes=[outb])
            P.op("act", lambda a: a.activation(out=out_ap, in_=out_ap, func=AF.Exp, scale=-0.5),
                 reads=[outb], writes=[outb])

        def load_x(tl, src_fn=None, trkb=None):
            nsub = tl["N"] // 128
            for j in range(nsub):
                src = src_fn(tl, j) if src_fn else xrows(tl, j)
                P.dma("sp", xt[j][:], src, xt[j], reads=[trkb] if trkb else [], writes=[xt[j]])

        def norm_T(tl, g):
            nsub = tl["N"] // 128
            junk, ssq, rstd, hTb = cm["junk"], cm["ssq"], cm["rstd"], cm["hT"]
            P.op("dve", lambda v: v.memset(ssq[:], 0.0), writes=[ssq])
            for j in range(nsub):
                P.op("act", lambda a, j=j: a.activation(out=junk[:], in_=xt[j][:], func=AF.Square,
                     accum_out=ssq[:, j:j + 1]), reads=[xt[j]], writes=[junk, ssq])
            rstd_from(ssq[:, 0:nsub], rstd[:, 0:nsub], ssq, rstd, D)
            for j in range(nsub):
                xb = xn[j % 2]
                P.op("dve", lambda v, j=j, xb=xb: v.scalar_tensor_tensor(out=xb[:], in0=xt[j][:],
                     scalar=rstd[:, j:j + 1], in1=g[:], op0=ALU.mult, op1=ALU.mult),
                     reads=[xt[j], rstd, g], writes=[xb])
                for half in range(2):
                    pb = PS[6 + half]
                    for cc in range(4):
                        c = half * 4 + cc
                        P.op("pe", lambda t, c=c, cc=cc, pb=pb, xb=xb: t.matmul(pb[:, cc * 128:(cc + 1) * 128],
                             lhsT=xb[:, c * 128:(c + 1) * 128], rhs=identb[:], start=True, stop=True),
                             reads=[xb, identb], writes=[pb], inc=(cc == 3))
                    eng = "act" if half == 0 else "dve"
                    if eng == "act":
                        P.op("act", lambda a, half=half, pb=pb, j=j: a.copy(
                             out=hT[:, half * 4:half * 4 + 4, j * 128:(j + 1) * 128],
                             in_=pb[:].rearrange("p (c t) -> p c t", c=4)), reads=[pb], writes=[hTb])
                    else:
                        P.op("dve", lambda v, half=half, pb=pb, j=j: v.tensor_copy(
                             out=hT[:, half * 4:half * 4 + 4, j * 128:(j + 1) * 128],
                             in_=pb[:].rearrange("p (c t) -> p c t", c=4)), reads=[pb], writes=[hTb])

        wst = [None, None]
        wcnt = [0]

        def load_w(dst, src2d, nrc, ncols, dcol0=0):
            for rc in range(nrc):
                for cb in range(0, ncols, 2048):
                    w = min(2048, ncols - cb)
                    k = wcnt[0] % 2
                    wcnt[0] += 1
                    st = wst[k]
                    P.dma("sp", st[:, 0:w], src2d[rc * 128:(rc + 1) * 128, cb:cb + w], st, writes=[st])
                    eng = "dve" if k == 0 else "pool"
                    P.op(eng, lambda v, st=st, rc=rc, cb=cb, w=w: v.tensor_copy(
                         out=dst[:, rc, dcol0 + cb:dcol0 + cb + w], in_=st[:, 0:w]), reads=[st], writes=[dst])

        def load_cols(dst, srcs, R, nchunk, stg):
            r0 = 0
            for ap, nr in srcs:
                P.dma("sp", stg[r0:r0 + nr, 0:nchunk * 128], ap, stg, writes=[stg])
                r0 += nr
            pb = PS[6]
            for c in range(nchunk):
                P.op("pe", lambda t, c=c: t.matmul(pb[:, c * R:(c + 1) * R], lhsT=stg[0:R, c * 128:(c + 1) * 128],
                     rhs=identf[0:R, 0:R], start=True, stop=True), reads=[stg, identf], writes=[pb],
                     inc=(c == nchunk - 1))
            P.op("dve", lambda v: v.tensor_copy(out=dst[:].rearrange("p c r -> p (c r)"),
                 in_=pb[:, 0:nchunk * R]), reads=[pb], writes=[dst])

        def proj_fm(W, col0, rhsT, N, pb, K=NCH):
            rb = cm["hT"] if rhsT is hT else rhsT
            for c in range(K):
                P.op("pe", lambda t, c=c: t.matmul(pb[:, 0:N], lhsT=W[:, c, col0:col0 + 128], rhs=rhsT[:, c, 0:N],
                     start=(c == 0), stop=(c == K - 1)), reads=[W, rb], writes=[pb], inc=(c == K - 1))

        def proj_tm(W, col0, lhs, t0, pb, K=NCH, M=128):
            lb = cm["hT"] if lhs is hT else lhs
            for c in range(K):
                P.op("pe", lambda t, c=c: t.matmul(pb[0:M, :], lhsT=lhs[:, c, t0:t0 + M], rhs=W[:, c, col0:col0 + 512],
                     start=(c == 0), stop=(c == K - 1)), reads=[W, lb], writes=[pb], inc=(c == K - 1))

        memst = contextlib.ExitStack()
        mkT = sb(memst, "mkT", [128, 8, 256], BF16)
        mvb = sb(memst, "mvb", [128, 2, D], BF16)
        with contextlib.suppress(_SkipPhase), contextlib.ExitStack() as ph:
            _phase_gate(0)
            Wm = sb(ph, "Wm", [128, 8, 2048], BF16)
            with contextlib.ExitStack() as ws:
                wst[0] = sb(ws, "wst0", [128, 2048], F32); wst[1] = sb(ws, "wst1", [128, 2048], F32)
                load_w(Wm, w_mem_kv, 8, 2048)
                P.barrier()
            mtok = sb(ph, "mtok", [128, 2048], F32)
            common(ph, 2, 256, ["mem"])
            mt = dict(r0=0, N=256, segs=[], sample=False)
            load_x(mt, src_fn=lambda tl, j: memp[j * 128:(j + 1) * 128, :])
            norm_T(mt, gbc["mem"])
            for j in range(2):
                for hf in range(4):
                    pb = PS[hf % 4]
                    proj_tm(Wm, hf * 512, hT, j * 128, pb)
                    P.op("act" if hf % 2 else "dve", (lambda a, hf=hf, pb=pb: a.copy(out=mtok[:, hf * 512:(hf + 1) * 512], in_=pb[:])) if hf % 2
                         else (lambda v, hf=hf, pb=pb: v.tensor_copy(out=mtok[:, hf * 512:(hf + 1) * 512], in_=pb[:])),
                         reads=[pb], writes=[mtok])
                P.op("pool", lambda g, j=j: g.tensor_copy(out=mvb[:, j, :], in_=mtok[:, 1024:2048]),
                     reads=[mtok], writes=[mvb])
                P.dma("pool", mkp[:, j * 128:(j + 1) * 128, :].rearrange("h m d -> m h d"),
                      mtok[:, 0:1024].rearrange("m (h d) -> m h d", h=4), mtok, reads=[mtok])
                P.dma("pool", mvp[:, j * 128:(j + 1) * 128, :].rearrange("h m d -> m h d"),
                      mtok[:, 1024:2048].rearrange("m (h d) -> m h d", h=4), mtok, reads=[mtok])
            for cc in range(8):
                pb = PS[cc % 4]
                proj_fm(Wm, cc * 128, hT, 256, pb)
                P.op("act", lambda a, cc=cc, pb=pb: a.copy(out=mkT[:, cc, :], in_=pb[:, 0:256]), reads=[pb], writes=[mkT])

            P.barrier()
        with contextlib.suppress(_SkipPhase), contextlib.ExitStack() as ph:
            _phase_gate(1)
            Wq = sb(ph, "Wqm", [128, 8, D], BF16)
            Wmo = sb(ph, "Wmo", [128, 8, D], BF16)
            with contextlib.ExitStack() as ws:
                wst[0] = sb(ws, "wst0", [128, 2048], F32); wst[1] = sb(ws, "wst1", [128, 2048], F32)
                load_w(Wq, w_in[:, 5120:6144], 8, D)
                load_w(Wmo, w_mem_o, 8, D)
                P.barrier()
            common(ph, 4, 512, ["pre"])
            qmT = sb(ph, "qmT", [128, 8, 512], BF16)
            pT = [sb(ph, f"pT{k}", [128, 2, 512], BF16) for k in range(2)]
            omT = sb(ph, "omT", [128, 8, 512], BF16)
            rden = [sb(ph, f"rden{k}", [128, 512], F32) for k in range(2)]
            yst = [sb(ph, f"ystm{k}", [128, 512], F32) for k in range(2)]
            smkT = sb(ph, "smkT", [128, 8, 256], BF16)
            smvb = sb(ph, "smvb", [128, 2, D], BF16)
            cmst = [sb(ph, f"cmst{k}", [128, 2, 256], F32) for k in range(2)]

            def mem_attn(kT, vB, c0, L):
                for hm in range(4):
                    pk = pT[hm % 2]
                    for mc in range(2):
                        pb = PS[mc]
                        for dc in range(2):
                            P.op("pe", lambda t, mc=mc, dc=dc, pb=pb: t.matmul(pb[:, 0:L],
                                 lhsT=kT[:, hm * 2 + dc, mc * 128:(mc + 1) * 128], rhs=qmT[:, hm * 2 + dc, c0:c0 + L],
                                 start=(dc == 0), stop=(dc == 1)), reads=[kT, qmT], writes=[pb], inc=(dc == 1))
                        P.op("act", lambda a, mc=mc, pb=pb, pk=pk: a.activation(out=pk[:, mc, 0:L], in_=pb[:, 0:L], func=AF.Exp),
                             reads=[pb], writes=[pk])
                    pd = PS[2]
                    for mc in range(2):
                        P.op("pe", lambda t, mc=mc, pk=pk: t.matmul(pd[:, 0:L], lhsT=onesb[:], rhs=pk[:, mc, 0:L],
                             start=(mc == 0), stop=(mc == 1)), reads=[onesb, pk], writes=[pd], inc=(mc == 1))
                    rd = rden[hm % 2]
                    P.op("dve", lambda v, rd=rd: v.reciprocal(out=rd[:, 0:L], in_=pd[:, 0:L]), reads=[pd], writes=[rd])
                    for dc in range(2):
                        po = PS[4 + dc]
                        for mc in range(2):
                            P.op("pe", lambda t, mc=mc, dc=dc, po=po, pk=pk: t.matmul(po[:, 0:L],
                                 lhsT=vB[:, mc, hm * 256 + dc * 128:hm * 256 + dc * 128 + 128], rhs=pk[:, mc, 0:L],
                                 start=(mc == 0), stop=(mc == 1)), reads=[vB, pk], writes=[po], inc=(mc == 1))
                        P.op("dve", lambda v, dc=dc, po=po, rd=rd: v.tensor_tensor(out=omT[:, hm * 2 + dc, c0:c0 + L],
                             in0=po[:, 0:L], in1=rd[:, 0:L], op=ALU.mult), reads=[po, rd], writes=[omT])

            for tl in tiles:
                N = tl["N"]; ti = tl["idx"]; smp = tl["sample"]
                load_x(tl)
                norm_T(tl, gbc["pre"])
                for c2 in range(8):
                    pb = PS[c2 % 4]
                    proj_fm(Wq, c2 * 128, hT, N, pb)
                    P.op("act", lambda a, c2=c2, pb=pb: a.activation(out=qmT[:, c2, 0:N], in_=pb[:, 0:N], func=AF.Copy,
                         scale=1.0 / 16.0), reads=[pb], writes=[qmT])
                if not smp:
                    mem_attn(mkT, mvb, 0, N)
                else:
                    for (c0, L, s) in tl["segs"]:
                        for hm in range(4):
                            st = cmst[hm % 2]
                            P.dma("sp", st[:, :, :], cmk[s, hm, :, :].rearrange("(j m) d -> m j d", j=2), st, writes=[st])
                            pb = PS[6 + hm % 2]
                            for dc in range(2):
                                for j in range(2):
                                    P.op("pe", lambda t, dc=dc, j=j, pb=pb, st=st: t.matmul(
                                         pb[:, dc * 256 + j * 128:dc * 256 + j * 128 + 128],
                                         lhsT=st[:, j, dc * 128:(dc + 1) * 128], rhs=identf[:], start=True, stop=True),
                                         reads=[st, identf], writes=[pb], inc=(dc == 1 and j == 1))
                            P.op("dve", lambda v, hm=hm, pb=pb: v.tensor_copy(out=smkT[:, hm * 2:hm * 2 + 2, :],
                                 in_=pb[:].rearrange("p (c m) -> p c m", c=2)), reads=[pb], writes=[smkT])
                            st2 = cmst[(hm + 1) % 2]
                            P.dma("sp", st2[:, :, :], cmv[s, hm, :, :].rearrange("(j m) d -> m j d", j=2), st2, writes=[st2])
                            P.op("pool", lambda g, hm=hm, st2=st2: g.tensor_copy(out=smvb[:, :, hm * 256:(hm + 1) * 256],
                                 in_=st2[:, :, :]), reads=[st2], writes=[smvb])
                        mem_attn(smkT, smvb, c0, L)
                for c2 in range(8):
                    pb = PS[c2 % 4]
                    proj_fm(Wmo, c2 * 128, omT, N, pb)
                    y = yst[c2 % 2]
                    P.op("act", lambda a, pb=pb, y=y: a.copy(out=y[:, 0:N], in_=pb[:, 0:N]), reads=[pb], writes=[y])
                    P.dma("pool", YMs[:, c2, tl["r0"]:tl["r0"] + N], y[:, 0:N], y, reads=[y], writes=[trk["YM"][ti]])
            P.barrier()
        P.barrier()
        memst.close()

        with contextlib.suppress(_SkipPhase), contextlib.ExitStack() as ph:
            _phase_gate(2)
            Wc = sb(ph, "Wc", [128, 8, 2048], BF16)
            Wco = sb(ph, "Wco", [128, 8, D], BF16)
            with contextlib.ExitStack() as ws:
                wst[0] = sb(ws, "wst0", [128, 2048], F32); wst[1] = sb(ws, "wst1", [128, 2048], F32)
                load_w(Wc, w_in[:, 3072:5120], 8, 2048)
                load_w(Wco, w_conv_o, 8, D)
                P.barrier()
            common(ph, 4, 512, ["pre"])
            cw = sb(ph, "cw", [128, 8, 31], F32)
            cvec = sb(ph, "cvec", [128, 8, 3], F32)
            with contextlib.ExitStack() as ws:
                stg = sb(ws, "stg", [31, D], F32)
                load_cols(cw, [(conv_dw_w[:, :], 31)], 31, 8, stg)
                load_cols(cvec, [(conv_dw_b[0:1, :], 1), (conv_ln_g[0:1, :], 1), (conv_ln_b[0:1, :], 1)], 3, 8, stg)
                P.barrier()
            uP = sb(ph, "uP", [128, 8, 542], F32)
            uP2 = sb(ph, "uP2", [128, 8, 542], F32)
            uS = sb(ph, "uS", [128, 8, NS, 94], F32)
            ccs = [sb(ph, "cc", [128, 8, 512], F32), sb(ph, "ccb", [128, 8, 512], F32)]
            ccvs = [[Buf(None, f"ccv{a_}{c_}") for c_ in range(8)] for a_ in range(2)]
            for a_ in range(2):
                for c_ in range(8):
                    ccvs[a_][c_].wl = None
            cdum = sb(ph, "cdum", [128, 4], F32)
            P.op("dve", lambda v: v.memset(cdum[:], 0.0), writes=[cdum])
            csq = [sb(ph, f"csq{k}", [128, 512], F32) for k in range(2)]
            ccT = sb(ph, "ccT", [128, 8, 512], BF16)
            sg = [sb(ph, f"sg{k}", [128, 512], F32) for k in range(2)]
            mean = sb(ph, "mean", [128, 512], F32)
            rs = sb(ph, "rs", [128, 512], F32)
            t1 = [sb(ph, f"t1{k}", [128, 512], F32) for k in range(2)]
            yst = [sb(ph, f"yst{k}", [128, 512], F32) for k in range(2)]
            cst = sb(ph, "cst", [30, D], F32)
            sst = sb(ph, "sst", [30, D], F32)
            uPs = [uP, uP2]
            P.op("dve", lambda v: v.memset(uPs[0][:, :, 0:30], 0.0), writes=[uPs[0]])

            def stageA1(tl):
                N = tl["N"]; ti = tl["idx"]; smp = tl["sample"]
                uP = uPs[ti % 2]
                if (not smp) and ti > 0:
                    P.op("dve", lambda v: v.tensor_copy(out=uP[:, :, 0:30], in_=uPs[(ti - 1) % 2][:, :, 512:542]),
                         reads=[uPs[(ti - 1) % 2]], writes=[uP])
                load_x(tl)
                norm_T(tl, gbc["pre"])
                if smp:
                    for s in range(NS):
                        P.dma("sp", sst[:, :], sconv[s, :, :], sst, writes=[sst])
                        pb = PS[4 + s % 2]
                        for c in range(8):
                            P.op("pe", lambda t, c=c, pb=pb: t.matmul(pb[:, c * 30:(c + 1) * 30],
                                 lhsT=sst[0:30, c * 128:(c + 1) * 128], rhs=identf[0:30, 0:30], start=True, stop=True),
                                 reads=[sst, identf], writes=[pb], inc=(c == 7))
                        P.op("dve", lambda v, s=s, pb=pb: v.tensor_copy(out=uS[:, :, s, 0:30],
                             in_=pb[:, 0:240].rearrange("p (c r) -> p c r", c=8)), reads=[pb], writes=[uS])
            def stageA2(tl):
                N = tl["N"]; ti = tl["idx"]; smp = tl["sample"]
                uP = uPs[ti % 2]
                for c2 in range(8):
                    pa, pbb = PS[(2 * c2) % 4], PS[(2 * c2 + 1) % 4]
                    proj_fm(Wc, c2 * 128, hT, N, pa)
                    proj_fm(Wc, 1024 + c2 * 128, hT, N, pbb)
                    sgk = sg[c2 % 2]
                    P.op("act", lambda a, pbb=pbb, sgk=sgk: a.activation(out=sgk[:, 0:N], in_=pbb[:, 0:N], func=AF.Sigmoid),
                         reads=[pbb], writes=[sgk])
                    if smp:
                        P.op("dve", lambda v, c2=c2, pa=pa, sgk=sgk: v.tensor_tensor(out=uS[:, c2, :, 30:94],
                             in0=pa[:, 0:N].rearrange("p (s t) -> p s t", s=NS),
                             in1=sgk[:, 0:N].rearrange("p (s t) -> p s t", s=NS), op=ALU.mult),
                             reads=[pa, sgk], writes=[uS])
                    else:
                        P.op("dve", lambda v, c2=c2, pa=pa, sgk=sgk: v.tensor_tensor(out=uP[:, c2, 30:542],
                             in0=pa[:, 0:N], in1=sgk[:, 0:N], op=ALU.mult), reads=[pa, sgk], writes=[uP])

            def stageC(tl):
                N = tl["N"]; ti = tl["idx"]; smp = tl["sample"]
                uP = uPs[ti % 2]
                cc_ = ccs[ti % 2]
                ccv = ccvs[ti % 2]
                P.op("dve", lambda v: v.tensor_copy(out=cdum[:, 2:3], in_=cdum[:, 3:4]), reads=[cc_, cdum], writes=ccv + [cdum, cc_])
                eng = "dve"
                ub = uS if smp else uP
                for (c0, L, s) in tl["segs"]:
                    def usrc(k, c2, s=s, L=L):
                        return uS[:, c2, s, k:k + L] if smp else uP[:, c2, k:k + L]
                    for c2 in range(8):
                        P.op(eng, lambda v, c2=c2, c0=c0, L=L: v.tensor_scalar(out=cc_[:, c2, c0:c0 + L], in0=usrc(0, c2),
                             scalar1=cw[:, c2, 0:1], scalar2=cvec[:, c2, 0:1], op0=ALU.mult, op1=ALU.add),
                             reads=[ub, cw, cvec], writes=[ccv[c2]])
                    for k in range(1, 31):
                        for c2 in range(8):
                            P.op(eng, lambda v, c2=c2, c0=c0, L=L, k=k: v.scalar_tensor_tensor(
                                 out=cc_[:, c2, c0:c0 + L], in0=usrc(k, c2), scalar=cw[:, c2, k:k + 1],
                                 in1=cc_[:, c2, c0:c0 + L], op0=ALU.mult, op1=ALU.add),
                                 reads=[ub, cw, ccv[c2]], writes=[ccv[c2]])
                P.op(eng, lambda v: v.tensor_copy(out=cdum[:, 0:1], in_=cdum[:, 1:2]), reads=ccv + [cdum], writes=[cc_, cdum])

            def stageL(tl):
                N = tl["N"]; ti = tl["idx"]; smp = tl["sample"]
                uP = uPs[ti % 2]
                cc_ = ccs[ti % 2]
                pm, pq = PS[4], PS[5]
                for c2 in range(8):
                    q = csq[c2 % 2]
                    P.op("act", lambda a, c2=c2, q=q: a.activation(out=q[:, 0:N], in_=cc_[:, c2, 0:N], func=AF.Square),
                         reads=[cc_], writes=[q])
                    P.op("pe", lambda t, c2=c2: t.matmul(pm[:, 0:N], lhsT=onesf[:], rhs=cc_[:, c2, 0:N],
                         start=(c2 == 0), stop=(c2 == 7)), reads=[onesf, cc_], writes=[pm])
                    P.op("pe", lambda t, c2=c2, q=q: t.matmul(pq[:, 0:N], lhsT=onesf[:], rhs=q[:, 0:N],
                         start=(c2 == 0), stop=(c2 == 7)), reads=[onesf, q], writes=[pq])
                P.op("act", lambda a: a.copy(out=mean[:, 0:N], in_=pm[:, 0:N]), reads=[pm], writes=[mean])
                P.op("dve", lambda v: v.tensor_tensor(out=rs[:, 0:N], in0=mean[:, 0:N], in1=mean[:, 0:N], op=ALU.mult),
                     reads=[mean], writes=[rs])
                P.op("dve", lambda v: v.tensor_tensor(out=rs[:, 0:N], in0=pq[:, 0:N], in1=rs[:, 0:N], op=ALU.subtract),
                     reads=[pq, rs], writes=[rs])
                rstd_from(rs[:, 0:N], rs[:, 0:N], rs, rs, 1.0)
                for c2 in range(8):
                    tk = t1[c2 % 2]
                    P.op("dve", lambda v, c2=c2, tk=tk: v.tensor_tensor(out=tk[:, 0:N], in0=cc_[:, c2, 0:N], in1=mean[:, 0:N],
                         op=ALU.subtract), reads=[cc_, mean], writes=[tk])
                    P.op("dve", lambda v, tk=tk: v.tensor_tensor(out=tk[:, 0:N], in0=tk[:, 0:N], in1=rs[:, 0:N], op=ALU.mult),
                         reads=[tk, rs], writes=[tk])
                    P.op("act", lambda a, c2=c2, tk=tk: a.activation(out=ccT[:, c2, 0:N], in_=tk[:, 0:N], func=AF.Silu,
                         scale=cvec[:, c2, 1:2], bias=cvec[:, c2, 2:3]), reads=[tk, cvec], writes=[ccT])
                for c2 in range(8):
                    pb = PS[c2 % 4]
                    proj_fm(Wco, c2 * 128, ccT, N, pb)
                    y = yst[c2 % 2]
                    P.op("act", lambda a, pb=pb, y=y: a.copy(out=y[:, 0:N], in_=pb[:, 0:N]), reads=[pb], writes=[y])
                    P.dma("pool", YCs[:, c2, tl["r0"]:tl["r0"] + N], y[:, 0:N], y, reads=[y], writes=[trk["YC"][ti]])
                ends = [(s, 64) for (_, _, s) in tl["segs"]] if smp else ([(None, 512)] if ti == NT - 1 else [])
                for (s, L) in ends:
                    for half in range(2):
                        pb = PS[4 + half]
                        for cq in range(4):
                            c2 = half * 4 + cq
                            src = uS[:, c2, s, L:L + 30] if smp else uP[:, c2, L:L + 30]
                            P.op("pe", lambda t, cq=cq, pb=pb, src=src: t.matmul(pb[0:30, cq * 128:(cq + 1) * 128], lhsT=src,
                                 rhs=identf[:], start=True, stop=True), reads=[uS if smp else uP, identf], writes=[pb],
                                 inc=(cq == 3))
                        P.op("dve", lambda v, half=half, pb=pb: v.tensor_copy(out=cst[:, half * 512:(half + 1) * 512],
                             in_=pb[0:30, :]), reads=[pb], writes=[cst])
                    P.dma("pool", cs[s, :, :] if smp else cp[:, :], cst[:, :], cst, reads=[cst])


            nt_ = len(tiles)
            for n_ in range(-2, nt_):
                if 0 <= n_ + 2 < nt_:
                    stageA1(tiles[n_ + 2])
                if 0 <= n_ + 1 < nt_:
                    stageC(tiles[n_ + 1])
                if 0 <= n_ + 2 < nt_:
                    stageA2(tiles[n_ + 2])
                if 0 <= n_ < nt_:
                    stageL(tiles[n_])
            P.barrier()
        with contextlib.suppress(_SkipPhase), contextlib.ExitStack() as ph:
            _phase_gate(3)
            Wqkv = sb(ph, "Wqkv", [128, 8, 3072], BF16)
            with contextlib.ExitStack() as ws:
                wst[0] = sb(ws, "wst0", [128, 2048], F32); wst[1] = sb(ws, "wst1", [128, 2048], F32)
                load_w(Wqkv, w_in[:, 0:3072], 8, 3072)
                P.barrier()
            common(ph, 4, 512, ["pre"])
            masks = sb(ph, "masks", [128, 4, 512], BF16)
            P.op("pool", lambda g: g.memset(masks[:], 1.0), writes=[masks])
            for r in range(4):
                P.op("pool", lambda g, r=r: g.affine_select(out=masks[:, r, :], in_=masks[:, r, :], pattern=[[1, 512]],
                     compare_op=ALU.is_gt, fill=0.0, base=-128 * r, channel_multiplier=-1), reads=[masks], writes=[masks])
            qT = sb(ph, "qT", [128, 8, 512], BF16)
            kT = sb(ph, "kT", [128, 8, 512], BF16)
            tok = [sb(ph, f"tok{k}", [128, D], F32) for k in range(2)]
            vbf = [sb(ph, f"vbf{k}", [128, D], BF16) for k in range(2)]
            osb = sb(ph, "osb", [128, 8, 512], BF16)
            Kt = [sb(ph, f"Kt{k}", [128, 2, 512], BF16) for k in range(2)]
            Vt = [sb(ph, f"Vt{k}", [128, 4, 2, 128], BF16) for k in range(2)]
            Vs = [sb(ph, f"Vs{k}", [128, 4, 128], BF16) for k in range(2)]
            for k in range(2):
                P.op("pool", lambda g, k=k: g.memset(Kt[k][:], 0.0), writes=[Kt[k]])
                P.op("pool", lambda g, k=k: g.memset(Vt[k][:], 0.0), writes=[Vt[k]])
            Ee = [PS[5], PS[6]]
            Sp = [sb(ph, f"Sp{k}", [128, 512], BF16) for k in range(2)]
            Ls = [[sb(ph, f"Ls{h}{k}", [128, 512], BF16) for k in range(2)] for h in range(2)]
            Aa = [sb(ph, f"Aa{k}", [128, 512], BF16) for k in range(2)]
            kc = [sb(ph, f"kc{k}", [128, 4, 128], F32) for k in range(2)]
            vc = [sb(ph, f"vc{k}", [128, 2, 4, 64], F32) for k in range(2)]
            PC = [[PS[0], PS[1]], [PS[2], PS[3]]]; PO = PS[4]

            def S1a(u):
                hh, p, Ktb, kcols, qcols, N = u["hh"], u["p"], u["Ktb"], u["kcols"], u["qcols"], u["N"]
                z, e = PC[hh][u["lsi"] % 2], Ee[hh]
                P.op("pe", lambda t: t.matmul(z[:, 0:N], lhsT=Ktb[:, hh, kcols], rhs=qT[:, p, qcols], start=True, stop=False),
                     reads=[Ktb, qT], writes=[z])
                P.op("act", lambda a: a.activation(out=e[:, 0:N], in_=z[:, 0:N], func=AF.Exp), reads=[z], writes=[e])

            def S1b(u):
                hh, N, mask_ap = u["hh"], u["N"], u["mask"]
                e, s_ = Ee[hh], Sp[hh]
                P.op("act", lambda a: a.activation(out=s_[:, 0:N], in_=e[:, 0:N], func=AF.Ln, bias=1.0, scale=1.0),
                     reads=[e], writes=[s_])
                if mask_ap is not None:
                    P.op("dve", lambda v: v.tensor_tensor(out=s_[:, 0:N], in0=s_[:, 0:N], in1=mask_ap, op=ALU.mult),
                         reads=[s_, masks], writes=[s_])

            def S2a(u):
                hh, N = u["hh"], u["N"]
                first, last, lsi = u["first"], u["last"], u["lsi"]
                cb_, s_, a_ = PC[hh][lsi % 2], Sp[hh], Aa[hh]
                lo, ln = Ls[hh][lsi % 2], Ls[hh][(lsi + 1) % 2]
                P.op("pe", lambda t: t.matmul(cb_[:, 0:N], lhsT=negtri[:, :], rhs=s_[:, 0:N], start=False, stop=first),
                     reads=[negtri, s_], writes=[cb_], inc=first)
                if not first:
                    P.op("pe", lambda t: t.matmul(cb_[:, 0:N], lhsT=negones[:, :], rhs=lo[:, 0:N], start=False, stop=True),
                         reads=[negones, lo], writes=[cb_])
                if not last:
                    if first:
                        P.op("pool", lambda g: g.tensor_copy(out=ln[:, 0:N], in_=s_[:, 0:N]), reads=[s_], writes=[ln])
                    else:
                        P.op("dve", lambda v: v.tensor_tensor(out=ln[:, 0:N], in0=lo[:, 0:N], in1=s_[:, 0:N], op=ALU.add),
                             reads=[lo, s_], writes=[ln])
                P.op("act", lambda a: a.activation(out=a_[:, 0:N], in_=cb_[:, 0:N], func=AF.Exp), reads=[cb_], writes=[a_])

            def S2b(u):
                hh, Vb, vr, N, mask_ap, first, last = u["hh"], u["Vb"], u["vr"], u["N"], u["mask"], u["first"], u["last"]
                hp = slice(hh * 64, hh * 64 + 64)
                a_ = Aa[hh]
                if mask_ap is not None:
                    P.op("dve", lambda v: v.tensor_tensor(out=a_[:, 0:N], in0=a_[:, 0:N], in1=mask_ap, op=ALU.mult),
                         reads=[a_, masks], writes=[a_])
                P.op("pe", lambda t: t.matmul(PO[:, 0:N], lhsT=Vb[:, vr, hh, :], rhs=a_[:, 0:N], start=(first and hh == 0),
                     stop=(last and hh == 1)), reads=[Vb, a_], writes=[PO], inc=(last and hh == 1))

            def emit_units(units, res=None):
                n = len(units)
                for i in range(n + 3):
                    if i < n:
                        if units[i].get("pre"):
                            units[i]["pre"]()
                        if res is not None:
                            res(units[i])
                        S1a(units[i])
                    if 0 <= i - 2 < n:
                        S2a(units[i - 2])
                    if i < n:
                        S1b(units[i])
                    if 0 <= i - 3 < n:
                        S2b(units[i - 3])

            ldc = [0]

            def load_kv(p, tj, ti):
                k = ldc[0] % 2
                ldc[0] += 1
                r0 = tiles[tj]["r0"]
                for hh in range(2):
                    P.dma("sp", Kt[k][hh * 64:(hh + 1) * 64, hh, :], KTs[hh * 64:(hh + 1) * 64, p, r0:r0 + 512], Kt[k],
                          reads=[trk["KT"][tj]], writes=[Kt[k]])
                P.dma("sp", Vs[k][:, :, :], VSs[r0:r0 + 512, p * 128:(p + 1) * 128].rearrange("(r s) c -> s r c", r=4),
                      Vs[k], reads=[trk["VS"][tj]], writes=[Vs[k]])
                for hh in range(2):
                    P.op("pool", lambda g, hh=hh: g.tensor_copy(out=Vt[k][:, :, hh, hh * 64:(hh + 1) * 64],
                         in_=Vs[k][:, :, hh * 64:(hh + 1) * 64]), reads=[Vs[k]], writes=[Vt[k]])
                return k

            for tl in tiles:
                N = tl["N"]; ti = tl["idx"]; smp = tl["sample"]; r0 = tl["r0"]
                nsub = N // 128
                load_x(tl)
                norm_T(tl, gbc["pre"])
                for p in range(8):
                    pq_, pk_ = PS[5], PS[6]
                    proj_fm(Wqkv, p * 128, hT, N, pq_)
                    P.op("act", lambda a, p=p: a.activation(out=qT[:, p, 0:N], in_=pq_[:, 0:N], func=AF.Copy, scale=0.125),
                         reads=[pq_], writes=[qT])
                    proj_fm(Wqkv, 1024 + p * 128, hT, N, pk_)
                    P.op("dve", lambda v, p=p: v.tensor_copy(out=kT[:, p, 0:N], in_=pk_[:, 0:N]), reads=[pk_], writes=[kT])
                P.dma("pool", KTs[:, :, r0:r0 + N], kT[:, :, 0:N], kT, reads=[kT], writes=[trk["KT"][ti]])
                for j in range(nsub):
                    for which in range(2):
                        tk = tok[which]
                        for hf in range(2):
                            pb = PS[5 + hf]
                            proj_tm(Wqkv, 1024 * (1 + which) + hf * 512, hT, j * 128, pb)
                            if hf == 0:
                                P.op("act", lambda a, tk=tk, pb=pb: a.copy(out=tk[:, 0:512], in_=pb[:]), reads=[pb], writes=[tk])
                            else:
                                P.op("dve", lambda v, tk=tk, pb=pb: v.tensor_copy(out=tk[:, 512:1024], in_=pb[:]), reads=[pb], writes=[tk])
                        if not smp:
                            dst = (kp if which == 0 else vp)[:, r0 + j * 128:r0 + (j + 1) * 128, :].rearrange("h t d -> t h d")
                            P.dma("pool", dst, tk[:, :].rearrange("t (h d) -> t h d", h=16), tk, reads=[tk])
                        else:
                            for s2 in range(2):
                                s = j * 2 + s2
                                dst = (ks if which == 0 else vs)[s, :, :, :].rearrange("h t d -> t h d")
                                P.dma("pool", dst, tk[s2 * 64:(s2 + 1) * 64, :].rearrange("t (h d) -> t h d", h=16), tk, reads=[tk])
                        if which == 1:
                            vb = vbf[j % 2]
                            P.op("pool", lambda g, vb=vb, tk=tk: g.tensor_copy(out=vb[:], in_=tk[:]), reads=[tk], writes=[vb])
                            P.dma("pool", VSs[r0 + j * 128:r0 + (j + 1) * 128, :], vb[:, :], vb, reads=[vb], writes=[trk["VS"][ti]])
                import os as _os
                _ka = _os.environ.get("KA_SKIP", "")
                if not smp:
                    for p in range(8 if "p" not in _ka else 0):
                        nkt = ti + 1
                        nsteps = 4 * nkt
                        units = []
                        bufk = {0: load_kv(p, ti, ti)}
                        step = 0
                        for jj in range(nkt):
                            tj = ti - jj
                            for r in (3, 2, 1, 0):
                                for hh in range(2):
                                    u = dict(hh=hh, p=p, jj=jj, kcols=slice(r * 128, (r + 1) * 128), vr=r, qcols=slice(0, N), N=N,
                                             first=(step == 0), last=(step == nsteps - 1),
                                             mask=(masks[:, r, :] if jj == 0 else None), lsi=step)
                                    if r == 1 and hh == 0 and jj + 1 < nkt:
                                        u["pre"] = (lambda jj=jj, tj=tj: bufk.__setitem__(jj + 1, load_kv(p, tj - 1, ti)))
                                    units.append(u)
                                step += 1

                        class _Lazy(dict):
                            pass
                        def _res(u):
                            k = bufk[u["jj"]]
                            u["Ktb"], u["Vb"] = Kt[k], Vt[k]
                        emit_units(units, _res)
                        P.op("act", lambda a, p=p: a.copy(out=osb[:, p, 0:N], in_=PO[:, 0:N]), reads=[PO], writes=[osb])
                else:
                    for (c0, L, s) in tl["segs"]:
                        for p in range(8 if "s" not in _ka else 0):
                            k = ldc[0] % 2
                            ldc[0] += 1
                            nsteps = 1 + PB
                            qc = slice(c0, c0 + L)

                            def pre_first(k=k, p=p, c0=c0):
                                P.op("pool", lambda g: g.memset(Kt[k][:, :, 0:128], 0.0), writes=[Kt[k]])
                                P.op("pool", lambda g: g.memset(Vt[k][:, 0, :, :], 0.0), writes=[Vt[k]])
                                for hh in range(2):
                                    hs = slice(hh * 64, (hh + 1) * 64)
                                    P.dma("sp", Kt[k][hs, hh, 0:64], KTs[hs, p, r0 + c0:r0 + c0 + 64], Kt[k],
                                          reads=[trk["KT"][ti]], writes=[Kt[k]])
                                    P.dma("sp", Vt[k][0:64, 0, hh, hs], VSs[r0 + c0:r0 + c0 + 64, p * 128 + hh * 64:p * 128 + (hh + 1) * 64],
                                          Vt[k], reads=[trk["VS"][ti]], writes=[Vt[k]])

                            def pre_group(kk, g4, p=p, s=s):
                                kcb, vcb = kc[kk], vc[kk]
                                for h2 in range(2):
                                    P.dma("sp", kcb[:, :, h2 * 64:(h2 + 1) * 64], ck[s, 2 * p + h2, g4 * 512:(g4 + 1) * 512, :].rearrange(
                                          "(b k) d -> k b d", b=4), kcb, writes=[kcb])
                                    P.dma("sp", vcb[:, h2, :, :], cv[s, 2 * p + h2, g4 * 512:(g4 + 1) * 512, :].rearrange(
                                          "(b k) d -> k b d", b=4), vcb, writes=[vcb])
                                pt = PS[7]
                                for b in range(4):
                                    P.op("pe", lambda t, b=b: t.matmul(pt[:, b * 128:(b + 1) * 128],
                                         lhsT=kcb[:, b, :], rhs=identf[:], start=True, stop=True),
                                         reads=[kcb, identf], writes=[pt], inc=(b == 3))
                                for h2 in range(2):
                                    hs = slice(h2 * 64, (h2 + 1) * 64)
                                    P.op("dve", lambda v, h2=h2, hs=hs: v.tensor_copy(out=Kt[kk][hs, h2, :], in_=pt[hs, :]),
                                         reads=[pt], writes=[Kt[kk]])
                                    P.op("pool", lambda g, h2=h2, hs=hs: g.tensor_copy(out=Vt[kk][:, :, h2, hs],
                                         in_=vcb[:, h2, :, :]), reads=[vcb], writes=[Vt[kk]])

                            units = []
                            for hh in range(2):
                                units.append(dict(hh=hh, p=p, Ktb=Kt[k], Vb=Vt[k], kcols=slice(0, 128), vr=0, qcols=qc, N=L,
                                                  first=True, last=False, mask=masks[:, 0, 0:64], lsi=0,
                                                  pre=(pre_first if hh == 0 else None)))
                            step = 1
                            glist = list(range(PB // 4 - 1, -1, -1))
                            gk = []
                            for gi, g4 in enumerate(glist):
                                kk = ldc[0] % 2
                                ldc[0] += 1
                                gk.append(kk)
                            for gi, g4 in enumerate(glist):
                                kk = gk[gi]
                                for bi, b in enumerate((3, 2, 1, 0)):
                                    for hh in range(2):
                                        u = dict(hh=hh, p=p, Ktb=Kt[kk], Vb=Vt[kk], kcols=slice(b * 128, (b + 1) * 128), vr=b, qcols=qc, N=L,
                                                 first=False, last=(step == nsteps - 1), mask=None, lsi=step)
                                        if gi == 0 and bi == 0 and hh == 0:
                                            u["pre"] = (lambda kk=kk, g4=g4: pre_group(kk, g4))
                                        if bi == 2 and hh == 0 and gi + 1 < len(glist):
                                            u["pre"] = (lambda kk=gk[gi + 1], g4=glist[gi + 1]: pre_group(kk, g4))
                                        units.append(u)
                                    step += 1
                            emit_units(units)
                            P.op("act", lambda a, p=p, c0=c0, L=L: a.copy(out=osb[:, p, c0:c0 + L], in_=PO[:, 0:L]), reads=[PO], writes=[osb])
                P.dma("pool", OSs[:, :, r0:r0 + N], osb[:, :, 0:N], osb, reads=[osb], writes=[trk["OS"][ti]])

            P.barrier()
        with contextlib.suppress(_SkipPhase), contextlib.ExitStack() as ph:
            _phase_gate(4)
            Wg = sb(ph, "Wg", [128, 8, 3072], BF16)
            Wso = sb(ph, "Wso", [128, 8, D], BF16)
            Wo = sb(ph, "Wo", [128, 8, D], BF16)
            with contextlib.ExitStack() as ws:
                wst[0] = sb(ws, "wst0", [128, 2048], F32); wst[1] = sb(ws, "wst1", [128, 2048], F32)
                load_w(Wg, w_in[:, 6144:9216], 8, 3072)
                load_w(Wso, w_sb_o, 8, D)
                load_w(Wo, w_out, 8, D)
                P.barrier()
            common(ph, 4, 512, ["pre", "post"])
            os_ = sb(ph, "os_", [128, 8, 512], BF16)
            ycb = [sb(ph, f"ycb{k}", [128, 512], F32) for k in range(2)]
            ymb = [sb(ph, f"ymb{k}", [128, 512], F32) for k in range(2)]
            sgg = [sb(ph, f"sgg{k}", [128, 512], F32) for k in range(3)]
            mrg = [sb(ph, f"mrg{k}", [128, 512], F32) for k in range(2)]
            mg = sb(ph, "mg", [128, 8, 512], BF16)
            mo = [sb(ph, f"mo{k}", [128, D], F32) for k in range(2)]
            ss2 = sb(ph, "ss2", [128, 4], F32)
            rs2 = sb(ph, "rs2", [128, 4], F32)
            for tl in tiles:
                N = tl["N"]; ti = tl["idx"]; r0 = tl["r0"]
                nsub = N // 128
                load_x(tl)
                norm_T(tl, gbc["pre"])
                P.dma("sp", os_[:, :, 0:N], OSs[:, :, r0:r0 + N], os_, reads=[trk["OS"][ti]], writes=[os_])
                for c2 in range(8):
                    yc, ym = ycb[c2 % 2], ymb[c2 % 2]
                    P.dma("sp", yc[:, 0:N], YCs[:, c2, r0:r0 + N], yc, reads=[trk["YC"][ti]], writes=[yc])
                    P.dma("sp", ym[:, 0:N], YMs[:, c2, r0:r0 + N], ym, reads=[trk["YM"][ti]], writes=[ym])
                    pys, pg = PS[0 + (c2 % 2) * 4], [PS[1 + (c2 % 2) * 4], PS[2 + (c2 % 2) * 4], PS[3 + (c2 % 2) * 4]]
                    proj_fm(Wso, c2 * 128, os_, N, pys)
                    for gi in range(3):
                        proj_fm(Wg, gi * 1024 + c2 * 128, hT, N, pg[gi])
                        P.op("act", lambda a, gi=gi, pg=pg: a.activation(out=sgg[gi][:, 0:N], in_=pg[gi][:, 0:N], func=AF.Sigmoid),
                             reads=[pg[gi]], writes=[sgg[gi]])
                    m = mrg[c2 % 2]
                    P.op("dve", lambda v, m=m, pys=pys: v.tensor_tensor(out=m[:, 0:N], in0=pys[:, 0:N], in1=sgg[0][:, 0:N], op=ALU.mult),
                         reads=[pys, sgg[0]], writes=[m])
                    P.op("pool", lambda g, yc=yc: g.tensor_tensor(out=sgg[1][:, 0:N], in0=sgg[1][:, 0:N], in1=yc[:, 0:N], op=ALU.mult),
                         reads=[sgg[1], yc], writes=[sgg[1]])
                    P.op("pool", lambda g, ym=ym: g.tensor_tensor(out=sgg[2][:, 0:N], in0=sgg[2][:, 0:N], in1=ym[:, 0:N], op=ALU.mult),
                         reads=[sgg[2], ym], writes=[sgg[2]])
                    P.op("dve", lambda v, m=m: v.tensor_tensor(out=m[:, 0:N], in0=m[:, 0:N], in1=sgg[1][:, 0:N], op=ALU.add),
                         reads=[m, sgg[1]], writes=[m])
                    P.op("dve", lambda v, m=m, c2=c2: v.tensor_tensor(out=mg[:, c2, 0:N], in0=m[:, 0:N], in1=sgg[2][:, 0:N], op=ALU.add),
                         reads=[m, sgg[2]], writes=[mg])
                for j in range(nsub):
                    mj = mo[j % 2]
                    for hf in range(2):
                        pb = PS[hf]
                        proj_tm(Wo, hf * 512, mg, j * 128, pb)
                        if hf == 0:
                            P.op("act", lambda a, mj=mj, pb=pb: a.copy(out=mj[:, 0:512], in_=pb[:]), reads=[pb], writes=[mj])
                        else:
                            P.op("dve", lambda v, mj=mj, pb=pb: v.tensor_copy(out=mj[:, 512:1024], in_=pb[:]), reads=[pb], writes=[mj])
                    P.op("dve", lambda v, j=j: v.memset(ss2[:, j:j + 1], 0.0), writes=[ss2])
                    P.op("act", lambda a, mj=mj, j=j: a.activation(out=cm["junk"][:], in_=mj[:], func=AF.Square, accum_out=ss2[:, j:j + 1]),
                         reads=[mj], writes=[cm["junk"], ss2])
                    rstd_from(ss2[:, j:j + 1], rs2[:, j:j + 1], ss2, rs2, D)
                    xo = mj
                    P.op("dve", lambda v, mj=mj, j=j: v.scalar_tensor_tensor(out=mj[:], in0=mj[:], scalar=rs2[:, j:j + 1],
                         in1=gbc["post"][:], op0=ALU.mult, op1=ALU.mult), reads=[mj, rs2, gbc["post"]], writes=[mj])
                    P.op("pool", lambda g, mj=mj, j=j, xo=xo: g.tensor_tensor(out=xo[:], in0=mj[:], in1=xt[j][:], op=ALU.add),
                         reads=[mj, xt[j]], writes=[xo])
                    P.dma("pool", XMs[r0 + j * 128:r0 + (j + 1) * 128, :], xo[:, :], xo, reads=[xo], writes=[trk["XM"][ti]])

            P.barrier()
        with contextlib.suppress(_SkipPhase), contextlib.ExitStack() as ph:
            _phase_gate(5)
            Wup = sb(ph, "Wup", [128, 8, FF2], BF16)
            Wdn = sb(ph, "Wdn", [128, NFC, D], BF16)
            with contextlib.ExitStack() as ws:
                wst[0] = sb(ws, "wst0", [128, 2048], F32); wst[1] = sb(ws, "wst1", [128, 2048], F32)
                load_w(Wup, w_ffn_up, 8, FF2)
                load_w(Wdn, w_ffn_down, NFC, D)
                P.barrier()
            common(ph, 2, 256, ["fpre", "fpost"])
            fw = sb(ph, "fw", [128, 44, 3], F32)
            halP = sb(ph, "halP", [128, 44, 2], F32)
            halS = sb(ph, "halS", [128, 44, NS, 2], F32)
            sfs = sb(ph, "sfs", [3, 512], F32)
            for g11 in range(11):
                P.dma("sp", sfs[0:3, :], ffn_dw_w[:, g11 * 512:(g11 + 1) * 512], sfs, writes=[sfs])
                pb = PS[6]
                for c in range(4):
                    P.op("pe", lambda t, c=c: t.matmul(pb[:, c * 3:(c + 1) * 3], lhsT=sfs[0:3, c * 128:(c + 1) * 128],
                         rhs=identf[0:3, 0:3], start=True, stop=True), reads=[sfs, identf], writes=[pb], inc=(c == 3))
                P.op("dve", lambda v, g11=g11: v.tensor_copy(out=fw[:, g11 * 4:(g11 + 1) * 4, :],
                     in_=pb[:, 0:12].rearrange("p (c r) -> p c r", c=4)), reads=[pb], writes=[fw])
            for s in range(NS):
                for g11 in range(11):
                    P.dma("sp", sfs[0:2, :], sffn[s, :, g11 * 512:(g11 + 1) * 512], sfs, writes=[sfs])
                    pb = PS[7]
                    for c in range(4):
                        P.op("pe", lambda t, c=c: t.matmul(pb[:, c * 2:(c + 1) * 2], lhsT=sfs[0:2, c * 128:(c + 1) * 128],
                             rhs=identf[0:2, 0:2], start=True, stop=True), reads=[sfs, identf], writes=[pb], inc=(c == 3))
                    P.op("dve", lambda v, s=s, g11=g11: v.tensor_copy(out=halS[:, g11 * 4:(g11 + 1) * 4, s, :],
                         in_=pb[:, 0:8].rearrange("p (c r) -> p c r", c=4)), reads=[pb], writes=[halS])
            upb = [sb(ph, f"upb{k}", [128, 4, 66], F32) for k in range(2)]
            cv_ = [sb(ph, f"cvv{k}", [128, 256], F32) for k in range(2)]
            gl = sb(ph, "gl", [128, 256], F32)
            g2 = sb(ph, "g2", [128, 256], F32)
            gT = sb(ph, "gT", [128, NFC, 256], BF16)
            dn = [sb(ph, f"dn{k}", [128, D], F32) for k in range(2)]
            fst = [sb(ph, f"fst{k}", [2, 512], F32) for k in range(2)]
            ss3 = sb(ph, "ss3", [128, 2], F32)
            rs3 = sb(ph, "rs3", [128, 2], F32)
            P.op("dve", lambda v: v.memset(halP[:], 0.0), writes=[halP])
            ftiles = []
            for i in range(T // 256):
                ftiles.append(dict(r0=i * 256, N=256, segs=[(0, 256, None)], idx=i // 2, sample=False, last=(i == T // 256 - 1)))
            ftiles.append(dict(r0=T, N=TS, segs=[(s * 64, 64, s) for s in range(NS)], idx=NT, sample=True, last=True))
            for tl in ftiles:
                N = tl["N"]; ti = tl["idx"]; r0 = tl["r0"]; smp = tl["sample"]
                nsub = N // 128
                load_x(tl, src_fn=lambda tl, j: XMs[tl["r0"] + j * 128:tl["r0"] + (j + 1) * 128, :], trkb=trk["XM"][ti])
                norm_T(tl, gbc["fpre"])
                for j in range(NFC):
                    outs = []
                    for which in range(2):
                        ch = which * NFC + j
                        pb = PS[(2 * j + which) % 4]
                        proj_fm(Wup, ch * 128, hT, N, pb)
                        ub = upb[which]
                        if smp:
                            P.op("act", lambda a, ch=ch, ub=ub: a.copy(out=ub[:, :, 0:2], in_=halS[:, ch, :, :]), reads=[halS], writes=[ub])
                            P.op("act", lambda a, pb=pb, ub=ub: a.copy(out=ub[:, :, 2:66], in_=pb[:, 0:N].rearrange("p (s t) -> p s t", s=NS)),
                                 reads=[pb], writes=[ub])
                            src = lambda k, ub=ub: ub[:, :, k:k + 64]
                            o3 = lambda t_: t_[:, 0:N].rearrange("p (s t) -> p s t", s=NS)
                        else:
                            uf = ub[:, :, :].rearrange("p a b -> p (a b)")
                            P.op("act", lambda a, ch=ch, uf=uf: a.copy(out=uf[:, 0:2], in_=halP[:, ch, :]), reads=[halP], writes=[ub])
                            P.op("act", lambda a, pb=pb, uf=uf: a.copy(out=uf[:, 2:2 + N], in_=pb[:, 0:N]), reads=[pb], writes=[ub])
                            P.op("pool", lambda g, ch=ch, uf=uf: g.tensor_copy(out=halP[:, ch, :], in_=uf[:, N:N + 2]),
                                 reads=[ub], writes=[halP])
                            src = lambda k, uf=uf: uf[:, k:k + N]
                            o3 = lambda t_: t_[:, 0:N]
                        co = cv_[which]
                        P.op("dve", lambda v, co=co, src=src, o3=o3, ch=ch: v.tensor_scalar(out=o3(co), in0=src(0), scalar1=fw[:, ch, 0:1],
                             scalar2=0.0, op0=ALU.mult, op1=ALU.add), reads=[ub, fw], writes=[co])
                        for k in (1, 2):
                            P.op("dve", lambda v, co=co, src=src, o3=o3, ch=ch, k=k: v.scalar_tensor_tensor(out=o3(co), in0=src(k),
                                 scalar=fw[:, ch, k:k + 1], in1=o3(co), op0=ALU.mult, op1=ALU.add), reads=[ub, fw, co], writes=[co])
                        outs.append(co)
                    xg, xv = outs
                    P.op("pool", lambda g, xg=xg: g.tensor_tensor(out=g2[:, 0:N], in0=xg[:, 0:N], in1=xg[:, 0:N], op=ALU.mult),
                         reads=[xg], writes=[g2])
                    P.op("pool", lambda g: g.tensor_scalar(out=g2[:, 0:N], in0=g2[:, 0:N], scalar1=0.044715, scalar2=1.0,
                         op0=ALU.mult, op1=ALU.add), reads=[g2], writes=[g2])
                    P.op("pool", lambda g, xg=xg: g.tensor_tensor(out=g2[:, 0:N], in0=g2[:, 0:N], in1=xg[:, 0:N], op=ALU.mult),
                         reads=[g2, xg], writes=[g2])
                    P.op("act", lambda a: a.activation(out=gl[:, 0:N], in_=g2[:, 0:N], func=AF.Sigmoid, scale=1.5957691216057308),
                         reads=[g2], writes=[gl])
                    P.op("dve", lambda v, xg=xg: v.tensor_tensor(out=gl[:, 0:N], in0=gl[:, 0:N], in1=xg[:, 0:N], op=ALU.mult),
                         reads=[gl, xg], writes=[gl])
                    P.op("dve", lambda v, j=j, xv=xv: v.tensor_tensor(out=gT[:, j, 0:N], in0=gl[:, 0:N], in1=xv[:, 0:N], op=ALU.mult),
                         reads=[gl, xv], writes=[gT])
                ends = [(c0 + 62, s) for (c0, L, s) in tl["segs"]] if smp else ([(254, None)] if tl["last"] else [])
                for (t0, s) in ends:
                    for cb in range(11):
                        pb = PS[4 + cb % 2]
                        fb = fst[cb % 2]
                        proj_tm(Wup, cb * 512, hT, t0, pb, M=2)
                        P.op("dve", lambda v, fb=fb, pb=pb: v.tensor_copy(out=fb[:, :], in_=pb[0:2, :]), reads=[pb], writes=[fb])
                        dst = fs[s, :, cb * 512:(cb + 1) * 512] if smp else fp[:, cb * 512:(cb + 1) * 512]
                        P.dma("pool", dst, fb[:, :], fb, reads=[fb])
                for j in range(nsub):
                    dj = dn[j % 2]
                    for hf in range(2):
                        pb = PS[6 + hf]
                        proj_tm(Wdn, hf * 512, gT, j * 128, pb, K=NFC)
                        if hf == 0:
                            P.op("act", lambda a, dj=dj, pb=pb: a.copy(out=dj[:, 0:512], in_=pb[:]), reads=[pb], writes=[dj])
                        else:
                            P.op("dve", lambda v, dj=dj, pb=pb: v.tensor_copy(out=dj[:, 512:1024], in_=pb[:]), reads=[pb], writes=[dj])
                    P.op("dve", lambda v, j=j: v.memset(ss3[:, j:j + 1], 0.0), writes=[ss3])
                    P.op("act", lambda a, dj=dj, j=j: a.activation(out=cm["junk"][:], in_=dj[:], func=AF.Square, accum_out=ss3[:, j:j + 1]),
                         reads=[dj], writes=[cm["junk"], ss3])
                    rstd_from(ss3[:, j:j + 1], rs3[:, j:j + 1], ss3, rs3, D)
                    P.op("dve", lambda v, dj=dj, j=j: v.scalar_tensor_tensor(out=dj[:], in0=dj[:], scalar=rs3[:, j:j + 1],
                         in1=gbc["fpost"][:], op0=ALU.mult, op1=ALU.mult), reads=[dj, rs3, gbc["fpost"]], writes=[dj])
                    P.op("pool", lambda g, dj=dj, j=j: g.tensor_tensor(out=dj[:], in0=dj[:], in1=xt[j][:], op=ALU.add),
                         reads=[dj, xt[j]], writes=[dj])
                    dst = ys[r0 - T + j * 128:r0 - T + (j + 1) * 128, :] if smp else yp[r0 + j * 128:r0 + (j + 1) * 128, :]
                    P.dma("pool", dst, dj[:, :], dj, reads=[dj])
            P.barrier()
        P.finish()
    return nc


_CACHE = {}


def run(T, NS, PAST, per_core):
    key = (T, NS, PAST)
    if key not in _CACHE:
        _CACHE[key] = build(T, NS, PAST)
    nc = _CACHE[key]
    res = run_bass_kernel_spmd(nc, per_core, core_ids=list(range(len(per_core))))
    return res.results


WNAMES = ["g_mem", "w_mem_kv", "g_mix_pre", "g_mix_post", "w_in", "w_sb_o", "conv_dw_w", "conv_dw_b", "conv_ln_g",
          "conv_ln_b", "w_conv_o", "w_mem_o", "w_out", "g_ffn_pre", "g_ffn_post", "w_ffn_up", "ffn_dw_w", "w_ffn_down"]


def make_maps(inp, ncores, NS):
    f = lambda a: np.ascontiguousarray(np.asarray(a, dtype=np.float32))
    B = inp["x_prompt"].shape[0]
    maps = []
    for c in range(ncores):
        b = c % B
        sl = slice(c * NS, (c + 1) * NS)
        m = {"xp": f(inp["x_prompt"][b]), "xs": f(inp["x_sample"][sl]).reshape(NS * 64, D),
             "memp": f(inp["mem_prompt"][b]), "ck": f(inp["cache_sb_k"][0, sl]), "cv": f(inp["cache_sb_v"][0, sl]),
             "sconv": f(inp["state_conv"][0, sl]), "sffn": f(inp["state_ffn_conv"][0, sl]),
             "cmk": f(inp["cache_mem_k"][0, sl]), "cmv": f(inp["cache_mem_v"][0, sl])}
        for n in WNAMES:
            w = f(inp[n][0])
            m[n] = w.reshape(1, -1) if w.ndim == 1 else w
        maps.append(m)
    return maps


def assemble(res, B, ncores):
    cat = lambda n, rng: np.stack([res[c][n] for c in rng])
    pc = range(B)
    sc = range(ncores)
    yp = cat("yp", pc); ys = np.concatenate([res[c]["ys"].reshape(-1, 64, D) for c in sc])
    kp = cat("kp", pc)[None]; vp = cat("vp", pc)[None]
    ks = np.concatenate([res[c]["ks"] for c in sc])[None]; vs = np.concatenate([res[c]["vs"] for c in sc])[None]
    cp = cat("cp", pc)[None]; cs = np.concatenate([res[c]["cs"] for c in sc])[None]
    fp = cat("fp", pc)[None]; fs = np.concatenate([res[c]["fs"] for c in sc])[None]
    mkp = cat("mkp", pc)[None]; mvp = cat("mvp", pc)[None]
    return (yp, ys, kp, vp, ks, vs, cp, cs, fp, fs, mkp, mvp)


def kernel(**inputs):
    T = inputs["x_prompt"].shape[1]
    PAST = inputs["cache_sb_k"].shape[3]
    ncores = 8
    NS = inputs["x_sample"].shape[0] // ncores
    maps = make_maps(inputs, ncores, NS)
    res = run(T, NS, PAST, maps)
    return assemble(res, inputs["x_prompt"].shape[0], ncores)
```

```python
import contextlib
import numpy as np
import concourse.bass as bass
import concourse.mybir as mybir
from concourse.bass_utils import run_bass_kernel_spmd

F32 = mybir.dt.float32
BF16 = mybir.dt.bfloat16
ALU = mybir.AluOpType
AF = mybir.ActivationFunctionType

D = 1024
NCH = 8
FF = 2816
FF2 = 5632
NFC = 22
EPS = 1e-6


class _SkipPhase(Exception):
    pass


def _phase_gate(k):
    import os
    en = os.environ.get("KPH")
    if en is not None and str(k) not in en.split(","):
        raise _SkipPhase()


class Buf:
    def __init__(self, t, name):
        self.t = t
        self.name = name
        self.w = None
        self.r = {}
        self.dsem = None
        self.dcnt = 0
        self.wl = {} if t is None else None

    def __getitem__(self, k):
        return self.t[k]


class _Alias:
    def __init__(self, base, ap):
        object.__setattr__(self, "base", base)
        object.__setattr__(self, "ap", ap)

    def __getitem__(self, k):
        return self.ap[k]

    def __getattr__(self, n):
        return getattr(object.__getattribute__(self, "base"), n)

    def __setattr__(self, n, v):
        setattr(object.__getattribute__(self, "base"), n, v)


class Prog:
    def __init__(self, nc, es):
        self.nc = nc
        self.es = es
        self.E = {"pe": nc.tensor, "act": nc.scalar, "dve": nc.vector, "pool": nc.gpsimd, "sp": nc.sync}
        self.sem = {e: es.enter_context(nc.semaphore("s_" + e)) for e in ("pe", "act", "dve", "pool")}
        self.cnt = {e: 0 for e in self.sem}
        self.seen = {e: {} for e in self.E}
        self.dbufs = []

    def _wait(self, e, dep, same_ok):
        if dep is None:
            return
        key, sem, val, src = dep
        if src is not None:
            val = 16 * src.dcnt
        elif key == e and not same_ok:
            return
        if self.seen[e].get(key, 0) >= val:
            return
        self.E[e].wait_ge(sem, val)
        self.seen[e][key] = val

    def _deps(self, e, reads, writes):
        for b in reads:
            self._wait(e, b.w, True)
            if b.wl:
                for d in list(b.wl.values()):
                    self._wait(e, d, True)
        for b in writes:
            self._wait(e, b.w, False)
            for d in list(b.r.values()):
                self._wait(e, d, False)

    def op(self, e, fn, reads=(), writes=(), inc=True):
        self._deps(e, reads, writes)
        ins = fn(self.E[e])
        if inc:
            self.cnt[e] += 1
            ins.then_inc(self.sem[e], 1)
            t = self.cnt[e]
        else:
            t = self.cnt[e] + 1
        dep = (e, self.sem[e], t, None)
        for b in reads:
            b.r[e] = dep
        for b in writes:
            b.w = dep
            b.r = {}
        return ins

    def dma(self, q, out_ap, in_ap, sbuf, reads=(), writes=()):
        self._deps(q, reads, writes)
        if sbuf.dsem is None:
            sbuf.dsem = self.es.enter_context(self.nc.semaphore("d_" + sbuf.name))
            self.dbufs.append(sbuf)
        sbuf.dcnt += 1
        self.E[q].dma_start(out=out_ap, in_=in_ap).then_inc(sbuf.dsem, 16)
        key = ("d", id(sbuf))
        dep = (key, sbuf.dsem, 16 * sbuf.dcnt, sbuf)
        for b in reads:
            b.r[key] = dep
        for b in writes:
            if b.wl is not None:
                b.wl[key] = dep
            else:
                b.w = dep
                b.r = {}

    def barrier(self):
        for e in self.E:
            for k in self.sem:
                if k != e and self.cnt[k] > self.seen[e].get(k, 0):
                    self.E[e].wait_ge(self.sem[k], self.cnt[k])
                    self.seen[e][k] = self.cnt[k]
            for b in self.dbufs:
                key = ("d", id(b))
                if 16 * b.dcnt > self.seen[e].get(key, 0):
                    self.E[e].wait_ge(b.dsem, 16 * b.dcnt)
                    self.seen[e][key] = 16 * b.dcnt

    def finish(self):
        for b in self.dbufs:
            self.E["sp"].wait_ge(b.dsem, 16 * b.dcnt)


def build(T, NS, PAST):
    nc = bass.Bass("TRN2", target_bir_lowering=False)
    TS = NS * 64
    NT = T // 512
    PB = PAST // 128

    def din(name, shape):
        return nc.dram_tensor(name, list(shape), F32, kind="ExternalInput").ap()

    def dout(name, shape):
        return nc.dram_tensor(name, list(shape), F32, kind="ExternalOutput").ap()

    xp = din("xp", [T, D]); xs = din("xs", [TS, D]); memp = din("memp", [256, D])
    ck = din("ck", [NS, 16, PAST, 64]); cv = din("cv", [NS, 16, PAST, 64])
    sconv = din("sconv", [NS, 30, D]); sffn = din("sffn", [NS, 2, FF2])
    cmk = din("cmk", [NS, 4, 256, 256]); cmv = din("cmv", [NS, 4, 256, 256])
    g_mem = din("g_mem", [1, D]); w_mem_kv = din("w_mem_kv", [D, 2048])
    g_mix_pre = din("g_mix_pre", [1, D]); g_mix_post = din("g_mix_post", [1, D])
    w_in = din("w_in", [D, 9216]); w_sb_o = din("w_sb_o", [D, D])
    conv_dw_w = din("conv_dw_w", [31, D]); conv_dw_b = din("conv_dw_b", [1, D])
    conv_ln_g = din("conv_ln_g", [1, D]); conv_ln_b = din("conv_ln_b", [1, D])
    w_conv_o = din("w_conv_o", [D, D]); w_mem_o = din("w_mem_o", [D, D]); w_out = din("w_out", [D, D])
    g_ffn_pre = din("g_ffn_pre", [1, D]); g_ffn_post = din("g_ffn_post", [1, D])
    w_ffn_up = din("w_ffn_up", [D, FF2]); ffn_dw_w = din("ffn_dw_w", [3, FF2]); w_ffn_down = din("w_ffn_down", [FF, D])

    yp = dout("yp", [T, D]); ys = dout("ys", [TS, D])
    kp = dout("kp", [16, T, 64]); vp = dout("vp", [16, T, 64])
    ks = dout("ks", [NS, 16, 64, 64]); vs = dout("vs", [NS, 16, 64, 64])
    cp = dout("cp", [30, D]); cs = dout("cs", [NS, 30, D])
    fp = dout("fp", [2, FF2]); fs = dout("fs", [NS, 2, FF2])
    mkp = dout("mkp", [4, 256, 256]); mvp = dout("mvp", [4, 256, 256])

    TT = T + TS
    KTs = nc.dram_tensor("KTs", [128, 8, TT], BF16).ap()
    VSs = nc.dram_tensor("VSs", [TT, D], BF16).ap()
    OSs = nc.dram_tensor("OSs", [128, 8, TT], BF16).ap()
    YCs = nc.dram_tensor("YCs", [128, 8, TT], F32).ap()
    YMs = nc.dram_tensor("YMs", [128, 8, TT], F32).ap()
    XMs = nc.dram_tensor("XMs", [TT, D], F32).ap()

    tiles = []
    for i in range(NT):
        tiles.append(dict(r0=i * 512, N=512, segs=[(0, 512, None)], idx=i, sample=False))
    tiles.append(dict(r0=T, N=TS, segs=[(s * 64, 64, s) for s in range(NS)], idx=NT, sample=True))
    ntile = len(tiles)
    trk = {n: [Buf(None, f"{n}{i}") for i in range(ntile)] for n in ("KT", "VS", "OS", "YC", "YM", "XM")}

    def xrows(tl, j):
        r = tl["r0"] + j * 128
        if tl["sample"]:
            return xs[r - T:r - T + 128, :]
        return xp[r:r + 128, :]

    es = contextlib.ExitStack()
    with es:
        P = Prog(nc, es)

        uid = [0]

        def sb(st, name, shape, dt):
            uid[0] += 1
            name = f"{name}_{uid[0]}"
            return Buf(st.enter_context(nc.sbuf_tensor(name, list(shape), dt)), name)

        PS = [Buf(es.enter_context(nc.psum_tensor(f"ps{i}", [128, 512], F32)), f"ps{i}") for i in range(8)]

        identb = sb(es, "identb", [128, 128], BF16)
        identf = sb(es, "identf", [128, 128], F32)
        negtri = sb(es, "negtri", [128, 128], BF16)
        negones = sb(es, "negones", [128, 128], BF16)
        onesb = sb(es, "onesb", [128, 128], BF16)
        onesf = sb(es, "onesf", [128, 128], F32)
        for bfr, val in ((identb, 1.0), (identf, 1.0), (negtri, -1.0), (negones, -1.0), (onesb, 1.0),
                         (onesf, 1.0 / D)):
            P.op("pool", lambda g, b=bfr, v=val: g.memset(b[:], v), writes=[bfr])
        for bfr in (identb, identf):
            P.op("pool", lambda g, b=bfr: g.affine_select(out=b[:], in_=b[:], pattern=[[-1, 128]],
                 compare_op=ALU.is_equal, fill=0.0, base=0, channel_multiplier=1), reads=[bfr], writes=[bfr])
        P.op("pool", lambda g: g.affine_select(out=negtri[:], in_=negtri[:], pattern=[[-1, 128]],
             compare_op=ALU.is_ge, fill=0.0, base=0, channel_multiplier=1), reads=[negtri], writes=[negtri])

        gbc = {}
        gsrc = {"pre": g_mix_pre, "post": g_mix_post, "fpre": g_ffn_pre, "fpost": g_ffn_post, "mem": g_mem}
        xt = [None] * 4
        xn = [None] * 2
        cm = {}

        class _HT:
            def __getitem__(self, k):
                return cm["hT"].t[k]
        hT = _HT()

        def common(ph, nsub, N, gs, one_xn=False, no_junk=False):
            for j in range(nsub):
                xt[j] = sb(ph, f"xt{j}", [128, D], F32)
            for j in range(2):
                xn[j] = xn[0] if (one_xn and j == 1) else sb(ph, f"xn{j}", [128, D], BF16)
            cm["hT"] = sb(ph, "hT", [128, 8, N], BF16)
            if not no_junk:
                cm["junk"] = sb(ph, "junk", [128, D], BF16)
            cm["ssq"] = sb(ph, "ssq", [128, 4], F32)
            cm["rstd"] = sb(ph, "rstd", [128, 4], F32)
            for nm in gs:
                gbc[nm] = sb(ph, "g_" + nm, [128, D], F32)
                P.dma("sp", gbc[nm][:], gsrc[nm][0:1, :].partition_broadcast(128), gbc[nm], writes=[gbc[nm]])

        def rstd_from(ss_ap, out_ap, ssb, outb, n):
            P.op("act", lambda a: a.activation(out=out_ap, in_=ss_ap, func=AF.Ln, scale=1.0 / n, bias=EPS),
                 reads=[ssb], writes=[outb])
            P.op("act", lambda a: a.activation(out=out_ap, in_=out_ap, func=AF.Exp, scale=-0.5),
                 reads=[outb], writes=[outb])

        def load_x(tl, src_fn=None, trkb=None):
            nsub = tl["N"] // 128
            for j in range(nsub):
                src = src_fn(tl, j) if src_fn else xrows(tl, j)
                P.dma("sp", xt[j][:], src, xt[j], reads=[trkb] if trkb else [], writes=[xt[j]])

        def norm_T(tl, g):
            nsub = tl["N"] // 128
            junk, ssq, rstd, hTb = cm["junk"], cm["ssq"], cm["rstd"], cm["hT"]
            P.op("dve", lambda v: v.memset(ssq[:], 0.0), writes=[ssq])
            for j in range(nsub):
                P.op("act", lambda a, j=j: a.activation(out=junk[:], in_=xt[j][:], func=AF.Square,
                     accum_out=ssq[:, j:j + 1]), reads=[xt[j]], writes=[junk, ssq])
            rstd_from(ssq[:, 0:nsub], rstd[:, 0:nsub], ssq, rstd, D)
            for j in range(nsub):
                xb = xn[j % 2]
                P.op("dve", lambda v, j=j, xb=xb: v.scalar_tensor_tensor(out=xb[:], in0=xt[j][:],
                     scalar=rstd[:, j:j + 1], in1=g[:], op0=ALU.mult, op1=ALU.mult),
                     reads=[xt[j], rstd, g], writes=[xb])
                for half in range(2):
                    pb = PS[6 + half]
                    for cc in range(4):
                        c = half * 4 + cc
                        P.op("pe", lambda t, c=c, cc=cc, pb=pb, xb=xb: t.matmul(pb[:, cc * 128:(cc + 1) * 128],
                             lhsT=xb[:, c * 128:(c + 1) * 128], rhs=identb[:], start=True, stop=True),
                             reads=[xb, identb], writes=[pb], inc=(cc == 3))
                    eng = "act" if half == 0 else "dve"
                    if eng == "act":
                        P.op("act", lambda a, half=half, pb=pb, j=j: a.copy(
                             out=hT[:, half * 4:half * 4 + 4, j * 128:(j + 1) * 128],
                             in_=pb[:].rearrange("p (c t) -> p c t", c=4)), reads=[pb], writes=[hTb])
                    else:
                        P.op("dve", lambda v, half=half, pb=pb, j=j: v.tensor_copy(
                             out=hT[:, half * 4:half * 4 + 4, j * 128:(j + 1) * 128],
                             in_=pb[:].rearrange("p (c t) -> p c t", c=4)), reads=[pb], writes=[hTb])

        wst = [None, None]
        wcnt = [0]

        def load_w(dst, src2d, nrc, ncols, dcol0=0):
            for rc in range(nrc):
                for cb in range(0, ncols, 2048):
                    w = min(2048, ncols - cb)
                    k = wcnt[0] % 2
                    wcnt[0] += 1
                    st = wst[k]
                    P.dma("sp", st[:, 0:w], src2d[rc * 128:(rc + 1) * 128, cb:cb + w], st, writes=[st])
                    eng = "dve" if k == 0 else "pool"
                    P.op(eng, lambda v, st=st, rc=rc, cb=cb, w=w: v.tensor_copy(
                         out=dst[:, rc, dcol0 + cb:dcol0 + cb + w], in_=st[:, 0:w]), reads=[st], writes=[dst])

        def load_cols(dst, srcs, R, nchunk, stg):
            r0 = 0
            for ap, nr in srcs:
                P.dma("sp", stg[r0:r0 + nr, 0:nchunk * 128], ap, stg, writes=[stg])
                r0 += nr
            pb = PS[6]
            for c in range(nchunk):
                P.op("pe", lambda t, c=c: t.matmul(pb[:, c * R:(c + 1) * R], lhsT=stg[0:R, c * 128:(c + 1) * 128],
                     rhs=identf[0:R, 0:R], start=True, stop=True), reads=[stg, identf], writes=[pb],
                     inc=(c == nchunk - 1))
            P.op("dve", lambda v: v.tensor_copy(out=dst[:].rearrange("p c r -> p (c r)"),
                 in_=pb[:, 0:nchunk * R]), reads=[pb], writes=[dst])

        def proj_fm(W, col0, rhsT, N, pb, K=NCH):
            rb = cm["hT"] if rhsT is hT else rhsT
            for c in range(K):
                P.op("pe", lambda t, c=c: t.matmul(pb[:, 0:N], lhsT=W[:, c, col0:col0 + 128], rhs=rhsT[:, c, 0:N],
                     start=(c == 0), stop=(c == K - 1)), reads=[W, rb], writes=[pb], inc=(c == K - 1))

        def proj_tm(W, col0, lhs, t0, pb, K=NCH, M=128):
            lb = cm["hT"] if lhs is hT else lhs
            for c in range(K):
                P.op("pe", lambda t, c=c: t.matmul(pb[0:M, :], lhsT=lhs[:, c, t0:t0 + M], rhs=W[:, c, col0:col0 + 512],
                     start=(c == 0), stop=(c == K - 1)), reads=[W, lb], writes=[pb], inc=(c == K - 1))

        memst = contextlib.ExitStack()
        mkT = sb(memst, "mkT", [128, 8, 256], BF16)
        mvb = sb(memst, "mvb", [128, 2, D], BF16)
        with contextlib.suppress(_SkipPhase), contextlib.ExitStack() as ph:
            _phase_gate(0)
            Wm = sb(ph, "Wm", [128, 8, 2048], BF16)
            with contextlib.ExitStack() as ws:
                wst[0] = sb(ws, "wst0", [128, 2048], F32); wst[1] = sb(ws, "wst1", [128, 2048], F32)
                load_w(Wm, w_mem_kv, 8, 2048)
                P.barrier()
            mtok = sb(ph, "mtok", [128, 2048], F32)
            common(ph, 2, 256, ["mem"])
            mt = dict(r0=0, N=256, segs=[], sample=False)
            load_x(mt, src_fn=lambda tl, j: memp[j * 128:(j + 1) * 128, :])
            norm_T(mt, gbc["mem"])
            for j in range(2):
                for hf in range(4):
                    pb = PS[hf % 4]
                    proj_tm(Wm, hf * 512, hT, j * 128, pb)
                    P.op("act" if hf % 2 else "dve", (lambda a, hf=hf, pb=pb: a.copy(out=mtok[:, hf * 512:(hf + 1) * 512], in_=pb[:])) if hf % 2
                         else (lambda v, hf=hf, pb=pb: v.tensor_copy(out=mtok[:, hf * 512:(hf + 1) * 512], in_=pb[:])),
                         reads=[pb], writes=[mtok])
                P.op("pool", lambda g, j=j: g.tensor_copy(out=mvb[:, j, :], in_=mtok[:, 1024:2048]),
                     reads=[mtok], writes=[mvb])
                P.dma("pool", mkp[:, j * 128:(j + 1) * 128, :].rearrange("h m d -> m h d"),
                      mtok[:, 0:1024].rearrange("m (h d) -> m h d", h=4), mtok, reads=[mtok])
                P.dma("pool", mvp[:, j * 128:(j + 1) * 128, :].rearrange("h m d -> m h d"),
                      mtok[:, 1024:2048].rearrange("m (h d) -> m h d", h=4), mtok, reads=[mtok])
            for cc in range(8):
                pb = PS[cc % 4]
                proj_fm(Wm, cc * 128, hT, 256, pb)
                P.op("act", lambda a, cc=cc, pb=pb: a.copy(out=mkT[:, cc, :], in_=pb[:, 0:256]), reads=[pb], writes=[mkT])

            P.barrier()
        with contextlib.suppress(_SkipPhase), contextlib.ExitStack() as ph:
            _phase_gate(1)
            Wq = sb(ph, "Wqm", [128, 8, D], BF16)
            Wmo = sb(ph, "Wmo", [128, 8, D], BF16)
            with contextlib.ExitStack() as ws:
                wst[0] = sb(ws, "wst0", [128, 2048], F32); wst[1] = sb(ws, "wst1", [128, 2048], F32)
                load_w(Wq, w_in[:, 5120:6144], 8, D)
                load_w(Wmo, w_mem_o, 8, D)
                P.barrier()
            common(ph, 4, 512, ["pre"])
            qmT = sb(ph, "qmT", [128, 8, 512], BF16)
            pT = [sb(ph, f"pT{k}", [128, 2, 512], BF16) for k in range(2)]
            omT = sb(ph, "omT", [128, 8, 512], BF16)
            rden = [sb(ph, f"rden{k}", [128, 512], F32) for k in range(2)]
            yst = [sb(ph, f"ystm{k}", [128, 512], F32) for k in range(2)]
            smkT = sb(ph, "smkT", [128, 8, 256], BF16)
            smvb = sb(ph, "smvb", [128, 2, D], BF16)
            cmst = [sb(ph, f"cmst{k}", [128, 2, 256], F32) for k in range(2)]

            def mem_attn(kT, vB, c0, L):
                for hm in range(4):
                    pk = pT[hm % 2]
                    for mc in range(2):
                        pb = PS[mc]
                        for dc in range(2):
                            P.op("pe", lambda t, mc=mc, dc=dc, pb=pb: t.matmul(pb[:, 0:L],
                                 lhsT=kT[:, hm * 2 + dc, mc * 128:(mc + 1) * 128], rhs=qmT[:, hm * 2 + dc, c0:c0 + L],
                                 start=(dc == 0), stop=(dc == 1)), reads=[kT, qmT], writes=[pb], inc=(dc == 1))
                        P.op("act", lambda a, mc=mc, pb=pb, pk=pk: a.activation(out=pk[:, mc, 0:L], in_=pb[:, 0:L], func=AF.Exp),
                             reads=[pb], writes=[pk])
                    pd = PS[2]
                    for mc in range(2):
                        P.op("pe", lambda t, mc=mc, pk=pk: t.matmul(pd[:, 0:L], lhsT=onesb[:], rhs=pk[:, mc, 0:L],
                             start=(mc == 0), stop=(mc == 1)), reads=[onesb, pk], writes=[pd], inc=(mc == 1))
                    rd = rden[hm % 2]
                    P.op("dve", lambda v, rd=rd: v.reciprocal(out=rd[:, 0:L], in_=pd[:, 0:L]), reads=[pd], writes=[rd])
                    for dc in range(2):
                        po = PS[4 + dc]
                        for mc in range(2):
                            P.op("pe", lambda t, mc=mc, dc=dc, po=po, pk=pk: t.matmul(po[:, 0:L],
                                 lhsT=vB[:, mc, hm * 256 + dc * 128:hm * 256 + dc * 128 + 128], rhs=pk[:, mc, 0:L],
                                 start=(mc == 0), stop=(mc == 1)), reads=[vB, pk], writes=[po], inc=(mc == 1))
                        P.op("dve", lambda v, dc=dc, po=po, rd=rd: v.tensor_tensor(out=omT[:, hm * 2 + dc, c0:c0 + L],
                             in0=po[:, 0:L], in1=rd[:, 0:L], op=ALU.mult), reads=[po, rd], writes=[omT])

            for tl in tiles:
                N = tl["N"]; ti = tl["idx"]; smp = tl["sample"]
                load_x(tl)
                norm_T(tl, gbc["pre"])
                for c2 in range(8):
                    pb = PS[c2 % 4]
                    proj_fm(Wq, c2 * 128, hT, N, pb)
                    P.op("act", lambda a, c2=c2, pb=pb: a.activation(out=qmT[:, c2, 0:N], in_=pb[:, 0:N], func=AF.Copy,
                         scale=1.0 / 16.0), reads=[pb], writes=[qmT])
                if not smp:
                    mem_attn(mkT, mvb, 0, N)
                else:
                    for (c0, L, s) in tl["segs"]:
                        for hm in range(4):
                            st = cmst[hm % 2]
                            P.dma("sp", st[:, :, :], cmk[s, hm, :, :].rearrange("(j m) d -> m j d", j=2), st, writes=[st])
                            pb = PS[6 + hm % 2]
                            for dc in range(2):
                                for j in range(2):
                                    P.op("pe", lambda t, dc=dc, j=j, pb=pb, st=st: t.matmul(
                                         pb[:, dc * 256 + j * 128:dc * 256 + j * 128 + 128],
                                         lhsT=st[:, j, dc * 128:(dc + 1) * 128], rhs=identf[:], start=True, stop=True),
                                         reads=[st, identf], writes=[pb], inc=(dc == 1 and j == 1))
                            P.op("dve", lambda v, hm=hm, pb=pb: v.tensor_copy(out=smkT[:, hm * 2:hm * 2 + 2, :],
                                 in_=pb[:].rearrange("p (c m) -> p c m", c=2)), reads=[pb], writes=[smkT])
                            st2 = cmst[(hm + 1) % 2]
                            P.dma("sp", st2[:, :, :], cmv[s, hm, :, :].rearrange("(j m) d -> m j d", j=2), st2, writes=[st2])
                            P.op("pool", lambda g, hm=hm, st2=st2: g.tensor_copy(out=smvb[:, :, hm * 256:(hm + 1) * 256],
                                 in_=st2[:, :, :]), reads=[st2], writes=[smvb])
                        mem_attn(smkT, smvb, c0, L)
                for c2 in range(8):
                    pb = PS[c2 % 4]
                    proj_fm(Wmo, c2 * 128, omT, N, pb)
                    y = yst[c2 % 2]
                    P.op("act", lambda a, pb=pb, y=y: a.copy(out=y[:, 0:N], in_=pb[:, 0:N]), reads=[pb], writes=[y])
                    P.dma("pool", YMs[:, c2, tl["r0"]:tl["r0"] + N], y[:, 0:N], y, reads=[y], writes=[trk["YM"][ti]])
            P.barrier()
        P.barrier()
        memst.close()

        with contextlib.suppress(_SkipPhase), contextlib.ExitStack() as ph:
            _phase_gate(2)
            Wc = sb(ph, "Wc", [128, 8, 2048], BF16)
            Wco = sb(ph, "Wco", [128, 8, D], BF16)
            with contextlib.ExitStack() as ws:
                wst[0] = sb(ws, "wst0", [128, 2048], F32); wst[1] = sb(ws, "wst1", [128, 2048], F32)
                load_w(Wc, w_in[:, 3072:5120], 8, 2048)
                load_w(Wco, w_conv_o, 8, D)
                P.barrier()
            common(ph, 4, 512, ["pre"])
            cw = sb(ph, "cw", [128, 8, 31], F32)
            cvec = sb(ph, "cvec", [128, 8, 3], F32)
            with contextlib.ExitStack() as ws:
                stg = sb(ws, "stg", [31, D], F32)
                load_cols(cw, [(conv_dw_w[:, :], 31)], 31, 8, stg)
                load_cols(cvec, [(conv_dw_b[0:1, :], 1), (conv_ln_g[0:1, :], 1), (conv_ln_b[0:1, :], 1)], 3, 8, stg)
                P.barrier()
            uP = sb(ph, "uP", [128, 8, 542], F32)
            uP2 = sb(ph, "uP2", [128, 8, 542], F32)
            uS = sb(ph, "uS", [128, 8, NS, 94], F32)
            ccs = [sb(ph, "cc", [128, 8, 512], F32), sb(ph, "ccb", [128, 8, 512], F32)]
            ccvs = [[Buf(None, f"ccv{a_}{c_}") for c_ in range(8)] for a_ in range(2)]
            for a_ in range(2):
                for c_ in range(8):
                    ccvs[a_][c_].wl = None
            cdum = sb(ph, "cdum", [128, 4], F32)
            P.op("dve", lambda v: v.memset(cdum[:], 0.0), writes=[cdum])
            csq = [sb(ph, f"csq{k}", [128, 512], F32) for k in range(2)]
            ccT = sb(ph, "ccT", [128, 8, 512], BF16)
            sg = [sb(ph, f"sg{k}", [128, 512], F32) for k in range(2)]
            mean = sb(ph, "mean", [128, 512], F32)
            rs = sb(ph, "rs", [128, 512], F32)
            t1 = [sb(ph, f"t1{k}", [128, 512], F32) for k in range(2)]
            yst = [sb(ph, f"yst{k}", [128, 512], F32) for k in range(2)]
            cst = sb(ph, "cst", [30, D], F32)
            sst = sb(ph, "sst", [30, D], F32)
            uPs = [uP, uP2]
            P.op("dve", lambda v: v.memset(uPs[0][:, :, 0:30], 0.0), writes=[uPs[0]])

            def stageA1(tl):
                N = tl["N"]; ti = tl["idx"]; smp = tl["sample"]
                uP = uPs[ti % 2]
                if (not smp) and ti > 0:
                    P.op("dve", lambda v: v.tensor_copy(out=uP[:, :, 0:30], in_=uPs[(ti - 1) % 2][:, :, 512:542]),
                         reads=[uPs[(ti - 1) % 2]], writes=[uP])
                load_x(tl)
                norm_T(tl, gbc["pre"])
                if smp:
                    for s in range(NS):
                        P.dma("sp", sst[:, :], sconv[s, :, :], sst, writes=[sst])
                        pb = PS[4 + s % 2]
                        for c in range(8):
                            P.op("pe", lambda t, c=c, pb=pb: t.matmul(pb[:, c * 30:(c + 1) * 30],
                                 lhsT=sst[0:30, c * 128:(c + 1) * 128], rhs=identf[0:30, 0:30], start=True, stop=True),
                                 reads=[sst, identf], writes=[pb], inc=(c == 7))
                        P.op("dve", lambda v, s=s, pb=pb: v.tensor_copy(out=uS[:, :, s, 0:30],
                             in_=pb[:, 0:240].rearrange("p (c r) -> p c r", c=8)), reads=[pb], writes=[uS])
            def stageA2(tl):
                N = tl["N"]; ti = tl["idx"]; smp = tl["sample"]
                uP = uPs[ti % 2]
                for c2 in range(8):
                    pa, pbb = PS[(2 * c2) % 4], PS[(2 * c2 + 1) % 4]
                    proj_fm(Wc, c2 * 128, hT, N, pa)
                    proj_fm(Wc, 1024 + c2 * 128, hT, N, pbb)
                    sgk = sg[c2 % 2]
                    P.op("act", lambda a, pbb=pbb, sgk=sgk: a.activation(out=sgk[:, 0:N], in_=pbb[:, 0:N], func=AF.Sigmoid),
                         reads=[pbb], writes=[sgk])
                    if smp:
                        P.op("dve", lambda v, c2=c2, pa=pa, sgk=sgk: v.tensor_tensor(out=uS[:, c2, :, 30:94],
                             in0=pa[:, 0:N].rearrange("p (s t) -> p s t", s=NS),
                             in1=sgk[:, 0:N].rearrange("p (s t) -> p s t", s=NS), op=ALU.mult),
                             reads=[pa, sgk], writes=[uS])
                    else:
                        P.op("dve", lambda v, c2=c2, pa=pa, sgk=sgk: v.tensor_tensor(out=uP[:, c2, 30:542],
                             in0=pa[:, 0:N], in1=sgk[:, 0:N], op=ALU.mult), reads=[pa, sgk], writes=[uP])

            def stageC(tl):
                N = tl["N"]; ti = tl["idx"]; smp = tl["sample"]
                uP = uPs[ti % 2]
                cc_ = ccs[ti % 2]
                ccv = ccvs[ti % 2]
                P.op("dve", lambda v: v.tensor_copy(out=cdum[:, 2:3], in_=cdum[:, 3:4]), reads=[cc_, cdum], writes=ccv + [cdum, cc_])
                eng = "dve"
                ub = uS if smp else uP
                for (c0, L, s) in tl["segs"]:
                    def usrc(k, c2, s=s, L=L):
                        return uS[:, c2, s, k:k + L] if smp else uP[:, c2, k:k + L]
                    for c2 in range(8):
                        P.op(eng, lambda v, c2=c2, c0=c0, L=L: v.tensor_scalar(out=cc_[:, c2, c0:c0 + L], in0=usrc(0, c2),
                             scalar1=cw[:, c2, 0:1], scalar2=cvec[:, c2, 0:1], op0=ALU.mult, op1=ALU.add),
                             reads=[ub, cw, cvec], writes=[ccv[c2]])
                    for k in range(1, 31):
                        for c2 in range(8):
                            P.op(eng, lambda v, c2=c2, c0=c0, L=L, k=k: v.scalar_tensor_tensor(
                                 out=cc_[:, c2, c0:c0 + L], in0=usrc(k, c2), scalar=cw[:, c2, k:k + 1],
                                 in1=cc_[:, c2, c0:c0 + L], op0=ALU.mult, op1=ALU.add),
                                 reads=[ub, cw, ccv[c2]], writes=[ccv[c2]])
                P.op(eng, lambda v: v.tensor_copy(out=cdum[:, 0:1], in_=cdum[:, 1:2]), reads=ccv + [cdum], writes=[cc_, cdum])

            def stageL(tl):
                N = tl["N"]; ti = tl["idx"]; smp = tl["sample"]
                uP = uPs[ti % 2]
                cc_ = ccs[ti % 2]
                pm, pq = PS[4], PS[5]
                for c2 in range(8):
                    q = csq[c2 % 2]
                    P.op("act", lambda a, c2=c2, q=q: a.activation(out=q[:, 0:N], in_=cc_[:, c2, 0:N], func=AF.Square),
                         reads=[cc_], writes=[q])
                    P.op("pe", lambda t, c2=c2: t.matmul(pm[:, 0:N], lhsT=onesf[:], rhs=cc_[:, c2, 0:N],
                         start=(c2 == 0), stop=(c2 == 7)), reads=[onesf, cc_], writes=[pm])
                    P.op("pe", lambda t, c2=c2, q=q: t.matmul(pq[:, 0:N], lhsT=onesf[:], rhs=q[:, 0:N],
                         start=(c2 == 0), stop=(c2 == 7)), reads=[onesf, q], writes=[pq])
                P.op("act", lambda a: a.copy(out=mean[:, 0:N], in_=pm[:, 0:N]), reads=[pm], writes=[mean])
                P.op("dve", lambda v: v.tensor_tensor(out=rs[:, 0:N], in0=mean[:, 0:N], in1=mean[:, 0:N], op=ALU.mult),
                     reads=[mean], writes=[rs])
                P.op("dve", lambda v: v.tensor_tensor(out=rs[:, 0:N], in0=pq[:, 0:N], in1=rs[:, 0:N], op=ALU.subtract),
                     reads=[pq, rs], writes=[rs])
                rstd_from(rs[:, 0:N], rs[:, 0:N], rs, rs, 1.0)
                for c2 in range(8):
                    tk = t1[c2 % 2]
                    P.op("dve", lambda v, c2=c2, tk=tk: v.tensor_tensor(out=tk[:, 0:N], in0=cc_[:, c2, 0:N], in1=mean[:, 0:N],
                         op=ALU.subtract), reads=[cc_, mean], writes=[tk])
                    P.op("dve", lambda v, tk=tk: v.tensor_tensor(out=tk[:, 0:N], in0=tk[:, 0:N], in1=rs[:, 0:N], op=ALU.mult),
                         reads=[tk, rs], writes=[tk])
                    P.op("act", lambda a, c2=c2, tk=tk: a.activation(out=ccT[:, c2, 0:N], in_=tk[:, 0:N], func=AF.Silu,
                         scale=cvec[:, c2, 1:2], bias=cvec[:, c2, 2:3]), reads=[tk, cvec], writes=[ccT])
                for c2 in range(8):
                    pb = PS[c2 % 4]
                    proj_fm(Wco, c2 * 128, ccT, N, pb)
                    y = yst[c2 % 2]
                    P.op("act", lambda a, pb=pb, y=y: a.copy(out=y[:, 0:N], in_=pb[:, 0:N]), reads=[pb], writes=[y])
                    P.dma("pool", YCs[:, c2, tl["r0"]:tl["r0"] + N], y[:, 0:N], y, reads=[y], writes=[trk["YC"][ti]])
                ends = [(s, 64) for (_, _, s) in tl["segs"]] if smp else ([(None, 512)] if ti == NT - 1 else [])
                for (s, L) in ends:
                    for half in range(2):
                        pb = PS[4 + half]
                        for cq in range(4):
                            c2 = half * 4 + cq
                            src = uS[:, c2, s, L:L + 30] if smp else uP[:, c2, L:L + 30]
                            P.op("pe", lambda t, cq=cq, pb=pb, src=src: t.matmul(pb[0:30, cq * 128:(cq + 1) * 128], lhsT=src,
                                 rhs=identf[:], start=True, stop=True), reads=[uS if smp else uP, identf], writes=[pb],
                                 inc=(cq == 3))
                        P.op("dve", lambda v, half=half, pb=pb: v.tensor_copy(out=cst[:, half * 512:(half + 1) * 512],
                             in_=pb[0:30, :]), reads=[pb], writes=[cst])
                    P.dma("pool", cs[s, :, :] if smp else cp[:, :], cst[:, :], cst, reads=[cst])


            nt_ = len(tiles)
            for n_ in range(-2, nt_):
                if 0 <= n_ + 2 < nt_:
                    stageA1(tiles[n_ + 2])
                if 0 <= n_ + 1 < nt_:
                    stageC(tiles[n_ + 1])
                if 0 <= n_ + 2 < nt_:
                    stageA2(tiles[n_ + 2])
                if 0 <= n_ < nt_:
                    stageL(tiles[n_])
            P.barrier()
        with contextlib.suppress(_SkipPhase), contextlib.ExitStack() as ph:
            _phase_gate(3)
            Wqkv = sb(ph, "Wqkv", [128, 8, 3072], BF16)
            with contextlib.ExitStack() as ws:
                wst[0] = sb(ws, "wst0", [128, 2048], F32); wst[1] = sb(ws, "wst1", [128, 2048], F32)
                load_w(Wqkv, w_in[:, 0:3072], 8, 3072)
                P.barrier()
            common(ph, 4, 512, ["pre"])
            masks = sb(ph, "masks", [128, 4, 512], BF16)
            P.op("pool", lambda g: g.memset(masks[:], 1.0), writes=[masks])
            for r in range(4):
                P.op("pool", lambda g, r=r: g.affine_select(out=masks[:, r, :], in_=masks[:, r, :], pattern=[[1, 512]],
                     compare_op=ALU.is_gt, fill=0.0, base=-128 * r, channel_multiplier=-1), reads=[masks], writes=[masks])
            qT = sb(ph, "qT", [128, 8, 512], BF16)
            kT = sb(ph, "kT", [128, 8, 512], BF16)
            tok = [sb(ph, f"tok{k}", [128, D], F32) for k in range(2)]
            vbf = [sb(ph, f"vbf{k}", [128, D], BF16) for k in range(2)]
            osb = sb(ph, "osb", [128, 8, 512], BF16)
            Kt = [sb(ph, f"Kt{k}", [128, 2, 512], BF16) for k in range(2)]
            Vt = [sb(ph, f"Vt{k}", [128, 4, 2, 128], BF16) for k in range(2)]
            Vs = [sb(ph, f"Vs{k}", [128, 4, 128], BF16) for k in range(2)]
            for k in range(2):
                P.op("pool", lambda g, k=k: g.memset(Kt[k][:], 0.0), writes=[Kt[k]])
                P.op("pool", lambda g, k=k: g.memset(Vt[k][:], 0.0), writes=[Vt[k]])
            Ee = [PS[5], PS[6]]
            Sp = [sb(ph, f"Sp{k}", [128, 512], BF16) for k in range(2)]
            Ls = [[sb(ph, f"Ls{h}{k}", [128, 512], BF16) for k in range(2)] for h in range(2)]
            Aa = [sb(ph, f"Aa{k}", [128, 512], BF16) for k in range(2)]
            kc = [sb(ph, f"kc{k}", [128, 4, 128], F32) for k in range(2)]
            vc = [sb(ph, f"vc{k}", [128, 2, 4, 64], F32) for k in range(2)]
            PC = [[PS[0], PS[1]], [PS[2], PS[3]]]; PO = PS[4]

            def S1a(u):
                hh, p, Ktb, kcols, qcols, N = u["hh"], u["p"], u["Ktb"], u["kcols"], u["qcols"], u["N"]
                z, e = PC[hh][u["lsi"] % 2], Ee[hh]
                P.op("pe", lambda t: t.matmul(z[:, 0:N], lhsT=Ktb[:, hh, kcols], rhs=qT[:, p, qcols], start=True, stop=False),
                     reads=[Ktb, qT], writes=[z])
                P.op("act", lambda a: a.activation(out=e[:, 0:N], in_=z[:, 0:N], func=AF.Exp), reads=[z], writes=[e])

            def S1b(u):
                hh, N, mask_ap = u["hh"], u["N"], u["mask"]
                e, s_ = Ee[hh], Sp[hh]
                P.op("act", lambda a: a.activation(out=s_[:, 0:N], in_=e[:, 0:N], func=AF.Ln, bias=1.0, scale=1.0),
                     reads=[e], writes=[s_])
                if mask_ap is not None:
                    P.op("dve", lambda v: v.tensor_tensor(out=s_[:, 0:N], in0=s_[:, 0:N], in1=mask_ap, op=ALU.mult),
                         reads=[s_, masks], writes=[s_])

            def S2a(u):
                hh, N = u["hh"], u["N"]
                first, last, lsi = u["first"], u["last"], u["lsi"]
                cb_, s_, a_ = PC[hh][lsi % 2], Sp[hh], Aa[hh]
                lo, ln = Ls[hh][lsi % 2], Ls[hh][(lsi + 1) % 2]
                P.op("pe", lambda t: t.matmul(cb_[:, 0:N], lhsT=negtri[:, :], rhs=s_[:, 0:N], start=False, stop=first),
                     reads=[negtri, s_], writes=[cb_], inc=first)
                if not first:
                    P.op("pe", lambda t: t.matmul(cb_[:, 0:N], lhsT=negones[:, :], rhs=lo[:, 0:N], start=False, stop=True),
                         reads=[negones, lo], writes=[cb_])
                if not last:
                    if first:
                        P.op("pool", lambda g: g.tensor_copy(out=ln[:, 0:N], in_=s_[:, 0:N]), reads=[s_], writes=[ln])
                    else:
                        P.op("dve", lambda v: v.tensor_tensor(out=ln[:, 0:N], in0=lo[:, 0:N], in1=s_[:, 0:N], op=ALU.add),
                             reads=[lo, s_], writes=[ln])
                P.op("act", lambda a: a.activation(out=a_[:, 0:N], in_=cb_[:, 0:N], func=AF.Exp), reads=[cb_], writes=[a_])

            def S2b(u):
                hh, Vb, vr, N, mask_ap, first, last = u["hh"], u["Vb"], u["vr"], u["N"], u["mask"], u["first"], u["last"]
                hp = slice(hh * 64, hh * 64 + 64)
                a_ = Aa[hh]
                if mask_ap is not None:
                    P.op("dve", lambda v: v.tensor_tensor(out=a_[:, 0:N], in0=a_[:, 0:N], in1=mask_ap, op=ALU.mult),
                         reads=[a_, masks], writes=[a_])
                P.op("pe", lambda t: t.matmul(PO[:, 0:N], lhsT=Vb[:, vr, hh, :], rhs=a_[:, 0:N], start=(first and hh == 0),
                     stop=(last and hh == 1)), reads=[Vb, a_], writes=[PO], inc=(last and hh == 1))

            def emit_units(units, res=None):
                n = len(units)
                for i in range(n + 3):
                    if i < n:
                        if units[i].get("pre"):
                            units[i]["pre"]()
                        if res is not None:
                            res(units[i])
                        S1a(units[i])
                    if 0 <= i - 2 < n:
                        S2a(units[i - 2])
                    if i < n:
                        S1b(units[i])
                    if 0 <= i - 3 < n:
                        S2b(units[i - 3])

            ldc = [0]

            def load_kv(p, tj, ti):
                k = ldc[0] % 2
                ldc[0] += 1
                r0 = tiles[tj]["r0"]
                for hh in range(2):
                    P.dma("sp", Kt[k][hh * 64:(hh + 1) * 64, hh, :], KTs[hh * 64:(hh + 1) * 64, p, r0:r0 + 512], Kt[k],
                          reads=[trk["KT"][tj]], writes=[Kt[k]])
                P.dma("sp", Vs[k][:, :, :], VSs[r0:r0 + 512, p * 128:(p + 1) * 128].rearrange("(r s) c -> s r c", r=4),
                      Vs[k], reads=[trk["VS"][tj]], writes=[Vs[k]])
                for hh in range(2):
                    P.op("pool", lambda g, hh=hh: g.tensor_copy(out=Vt[k][:, :, hh, hh * 64:(hh + 1) * 64],
                         in_=Vs[k][:, :, hh * 64:(hh + 1) * 64]), reads=[Vs[k]], writes=[Vt[k]])
                return k

            for tl in tiles:
                N = tl["N"]; ti = tl["idx"]; smp = tl["sample"]; r0 = tl["r0"]
                nsub = N // 128
                load_x(tl)
                norm_T(tl, gbc["pre"])
                for p in range(8):
                    pq_, pk_ = PS[5], PS[6]
                    proj_fm(Wqkv, p * 128, hT, N, pq_)
                    P.op("act", lambda a, p=p: a.activation(out=qT[:, p, 0:N], in_=pq_[:, 0:N], func=AF.Copy, scale=0.125),
                         reads=[pq_], writes=[qT])
                    proj_fm(Wqkv, 1024 + p * 128, hT, N, pk_)
                    P.op("dve", lambda v, p=p: v.tensor_copy(out=kT[:, p, 0:N], in_=pk_[:, 0:N]), reads=[pk_], writes=[kT])
                P.dma("pool", KTs[:, :, r0:r0 + N], kT[:, :, 0:N], kT, reads=[kT], writes=[trk["KT"][ti]])
                for j in range(nsub):
                    for which in range(2):
                        tk = tok[which]
                        for hf in range(2):
                            pb = PS[5 + hf]
                            proj_tm(Wqkv, 1024 * (1 + which) + hf * 512, hT, j * 128, pb)
                            if hf == 0:
                                P.op("act", lambda a, tk=tk, pb=pb: a.copy(out=tk[:, 0:512], in_=pb[:]), reads=[pb], writes=[tk])
                            else:
                                P.op("dve", lambda v, tk=tk, pb=pb: v.tensor_copy(out=tk[:, 512:1024], in_=pb[:]), reads=[pb], writes=[tk])
                        if not smp:
                            dst = (kp if which == 0 else vp)[:, r0 + j * 128:r0 + (j + 1) * 128, :].rearrange("h t d -> t h d")
                            P.dma("pool", dst, tk[:, :].rearrange("t (h d) -> t h d", h=16), tk, reads=[tk])
                        else:
                            for s2 in range(2):
                                s = j * 2 + s2
                                dst = (ks if which == 0 else vs)[s, :, :, :].rearrange("h t d -> t h d")
                                P.dma("pool", dst, tk[s2 * 64:(s2 + 1) * 64, :].rearrange("t (h d) -> t h d", h=16), tk, reads=[tk])
                        if which == 1:
                            vb = vbf[j % 2]
                            P.op("pool", lambda g, vb=vb, tk=tk: g.tensor_copy(out=vb[:], in_=tk[:]), reads=[tk], writes=[vb])
                            P.dma("pool", VSs[r0 + j * 128:r0 + (j + 1) * 128, :], vb[:, :], vb, reads=[vb], writes=[trk["VS"][ti]])
                import os as _os
                _ka = _os.environ.get("KA_SKIP", "")
                if not smp:
                    for p in range(8 if "p" not in _ka else 0):
                        nkt = ti + 1
                        nsteps = 4 * nkt
                        units = []
                        bufk = {0: load_kv(p, ti, ti)}
                        step = 0
                        for jj in range(nkt):
                            tj = ti - jj
                            for r in (3, 2, 1, 0):
                                for hh in range(2):
                                    u = dict(hh=hh, p=p, jj=jj, kcols=slice(r * 128, (r + 1) * 128), vr=r, qcols=slice(0, N), N=N,
                                             first=(step == 0), last=(step == nsteps - 1),
                                             mask=(masks[:, r, :] if jj == 0 else None), lsi=step)
                                    if r == 1 and hh == 0 and jj + 1 < nkt:
                                        u["pre"] = (lambda jj=jj, tj=tj: bufk.__setitem__(jj + 1, load_kv(p, tj - 1, ti)))
                                    units.append(u)
                                step += 1

                        class _Lazy(dict):
                            pass
                        def _res(u):
                            k = bufk[u["jj"]]
                            u["Ktb"], u["Vb"] = Kt[k], Vt[k]
                        emit_units(units, _res)
                        P.op("act", lambda a, p=p: a.copy(out=osb[:, p, 0:N], in_=PO[:, 0:N]), reads=[PO], writes=[osb])
                else:
                    for (c0, L, s) in tl["segs"]:
                        for p in range(8 if "s" not in _ka else 0):
                            k = ldc[0] % 2
                            ldc[0] += 1
                            nsteps = 1 + PB
                            qc = slice(c0, c0 + L)

                            def pre_first(k=k, p=p, c0=c0):
                                P.op("pool", lambda g: g.memset(Kt[k][:, :, 0:128], 0.0), writes=[Kt[k]])
                                P.op("pool", lambda g: g.memset(Vt[k][:, 0, :, :], 0.0), writes=[Vt[k]])
                                for hh in range(2):
                                    hs = slice(hh * 64, (hh + 1) * 64)
                                    P.dma("sp", Kt[k][hs, hh, 0:64], KTs[hs, p, r0 + c0:r0 + c0 + 64], Kt[k],
                                          reads=[trk["KT"][ti]], writes=[Kt[k]])
                                    P.dma("sp", Vt[k][0:64, 0, hh, hs], VSs[r0 + c0:r0 + c0 + 64, p * 128 + hh * 64:p * 128 + (hh + 1) * 64],
                                          Vt[k], reads=[trk["VS"][ti]], writes=[Vt[k]])

                            def pre_group(kk, g4, p=p, s=s):
                                kcb, vcb = kc[kk], vc[kk]
                                for h2 in range(2):
                                    P.dma("sp", kcb[:, :, h2 * 64:(h2 + 1) * 64], ck[s, 2 * p + h2, g4 * 512:(g4 + 1) * 512, :].rearrange(
                                          "(b k) d -> k b d", b=4), kcb, writes=[kcb])
                                    P.dma("sp", vcb[:, h2, :, :], cv[s, 2 * p + h2, g4 * 512:(g4 + 1) * 512, :].rearrange(
                                          "(b k) d -> k b d", b=4), vcb, writes=[vcb])
                                pt = PS[7]
                                for b in range(4):
                                    P.op("pe", lambda t, b=b: t.matmul(pt[:, b * 128:(b + 1) * 128],
                                         lhsT=kcb[:, b, :], rhs=identf[:], start=True, stop=True),
                                         reads=[kcb, identf], writes=[pt], inc=(b == 3))
                                for h2 in range(2):
                                    hs = slice(h2 * 64, (h2 + 1) * 64)
                                    P.op("dve", lambda v, h2=h2, hs=hs: v.tensor_copy(out=Kt[kk][hs, h2, :], in_=pt[hs, :]),
                                         reads=[pt], writes=[Kt[kk]])
                                    P.op("pool", lambda g, h2=h2, hs=hs: g.tensor_copy(out=Vt[kk][:, :, h2, hs],
                                         in_=vcb[:, h2, :, :]), reads=[vcb], writes=[Vt[kk]])

                            units = []
                            for hh in range(2):
                                units.append(dict(hh=hh, p=p, Ktb=Kt[k], Vb=Vt[k], kcols=slice(0, 128), vr=0, qcols=qc, N=L,
                                                  first=True, last=False, mask=masks[:, 0, 0:64], lsi=0,
                                                  pre=(pre_first if hh == 0 else None)))
                            step = 1
                            glist = list(range(PB // 4 - 1, -1, -1))
                            gk = []
                            for gi, g4 in enumerate(glist):
                                kk = ldc[0] % 2
                                ldc[0] += 1
                                gk.append(kk)
                            for gi, g4 in enumerate(glist):
                                kk = gk[gi]
                                for bi, b in enumerate((3, 2, 1, 0)):
                                    for hh in range(2):
                                        u = dict(hh=hh, p=p, Ktb=Kt[kk], Vb=Vt[kk], kcols=slice(b * 128, (b + 1) * 128), vr=b, qcols=qc, N=L,
                                                 first=False, last=(step == nsteps - 1), mask=None, lsi=step)
                                        if gi == 0 and bi == 0 and hh == 0:
                                            u["pre"] = (lambda kk=kk, g4=g4: pre_group(kk, g4))
                                        if bi == 2 and hh == 0 and gi + 1 < len(glist):
                                            u["pre"] = (lambda kk=gk[gi + 1], g4=glist[gi + 1]: pre_group(kk, g4))
                                        units.append(u)
                                    step += 1
                            emit_units(units)
                            P.op("act", lambda a, p=p, c0=c0, L=L: a.copy(out=osb[:, p, c0:c0 + L], in_=PO[:, 0:L]), reads=[PO], writes=[osb])
                P.dma("pool", OSs[:, :, r0:r0 + N], osb[:, :, 0:N], osb, reads=[osb], writes=[trk["OS"][ti]])

            P.barrier()
        with contextlib.suppress(_SkipPhase), contextlib.ExitStack() as ph:
            _phase_gate(4)
            Wg = sb(ph, "Wg", [128, 8, 3072], BF16)
            Wso = sb(ph, "Wso", [128, 8, D], BF16)
            Wo = sb(ph, "Wo", [128, 8, D], BF16)
            with contextlib.ExitStack() as ws:
                wst[0] = sb(ws, "wst0", [128, 2048], F32); wst[1] = sb(ws, "wst1", [128, 2048], F32)
                load_w(Wg, w_in[:, 6144:9216], 8, 3072)
                load_w(Wso, w_sb_o, 8, D)
                load_w(Wo, w_out, 8, D)
                P.barrier()
            common(ph, 4, 512, ["pre", "post"])
            os_ = sb(ph, "os_", [128, 8, 512], BF16)
            ycb = [sb(ph, f"ycb{k}", [128, 512], F32) for k in range(2)]
            ymb = [sb(ph, f"ymb{k}", [128, 512], F32) for k in range(2)]
            sgg = [sb(ph, f"sgg{k}", [128, 512], F32) for k in range(3)]
            mrg = [sb(ph, f"mrg{k}", [128, 512], F32) for k in range(2)]
            mg = sb(ph, "mg", [128, 8, 512], BF16)
            mo = [sb(ph, f"mo{k}", [128, D], F32) for k in range(2)]
            ss2 = sb(ph, "ss2", [128, 4], F32)
            rs2 = sb(ph, "rs2", [128, 4], F32)
            for tl in tiles:
                N = tl["N"]; ti = tl["idx"]; r0 = tl["r0"]
                nsub = N // 128
                load_x(tl)
                norm_T(tl, gbc["pre"])
                P.dma("sp", os_[:, :, 0:N], OSs[:, :, r0:r0 + N], os_, reads=[trk["OS"][ti]], writes=[os_])
                for c2 in range(8):
                    yc, ym = ycb[c2 % 2], ymb[c2 % 2]
                    P.dma("sp", yc[:, 0:N], YCs[:, c2, r0:r0 + N], yc, reads=[trk["YC"][ti]], writes=[yc])
                    P.dma("sp", ym[:, 0:N], YMs[:, c2, r0:r0 + N], ym, reads=[trk["YM"][ti]], writes=[ym])
                    pys, pg = PS[0 + (c2 % 2) * 4], [PS[1 + (c2 % 2) * 4], PS[2 + (c2 % 2) * 4], PS[3 + (c2 % 2) * 4]]
                    proj_fm(Wso, c2 * 128, os_, N, pys)
                    for gi in range(3):
                        proj_fm(Wg, gi * 1024 + c2 * 128, hT, N, pg[gi])
                        P.op("act", lambda a, gi=gi, pg=pg: a.activation(out=sgg[gi][:, 0:N], in_=pg[gi][:, 0:N], func=AF.Sigmoid),
                             reads=[pg[gi]], writes=[sgg[gi]])
                    m = mrg[c2 % 2]
                    P.op("dve", lambda v, m=m, pys=pys: v.tensor_tensor(out=m[:, 0:N], in0=pys[:, 0:N], in1=sgg[0][:, 0:N], op=ALU.mult),
                         reads=[pys, sgg[0]], writes=[m])
                    P.op("pool", lambda g, yc=yc: g.tensor_tensor(out=sgg[1][:, 0:N], in0=sgg[1][:, 0:N], in1=yc[:, 0:N], op=ALU.mult),
                         reads=[sgg[1], yc], writes=[sgg[1]])
                    P.op("pool", lambda g, ym=ym: g.tensor_tensor(out=sgg[2][:, 0:N], in0=sgg[2][:, 0:N], in1=ym[:, 0:N], op=ALU.mult),
                         reads=[sgg[2], ym], writes=[sgg[2]])
                    P.op("dve", lambda v, m=m: v.tensor_tensor(out=m[:, 0:N], in0=m[:, 0:N], in1=sgg[1][:, 0:N], op=ALU.add),
                         reads=[m, sgg[1]], writes=[m])
                    P.op("dve", lambda v, m=m, c2=c2: v.tensor_tensor(out=mg[:, c2, 0:N], in0=m[:, 0:N], in1=sgg[2][:, 0:N], op=ALU.add),
                         reads=[m, sgg[2]], writes=[mg])
                for j in range(nsub):
                    mj = mo[j % 2]
                    for hf in range(2):
                        pb = PS[hf]
                        proj_tm(Wo, hf * 512, mg, j * 128, pb)
                        if hf == 0:
                            P.op("act", lambda a, mj=mj, pb=pb: a.copy(out=mj[:, 0:512], in_=pb[:]), reads=[pb], writes=[mj])
                        else:
                            P.op("dve", lambda v, mj=mj, pb=pb: v.tensor_copy(out=mj[:, 512:1024], in_=pb[:]), reads=[pb], writes=[mj])
                    P.op("dve", lambda v, j=j: v.memset(ss2[:, j:j + 1], 0.0), writes=[ss2])
                    P.op("act", lambda a, mj=mj, j=j: a.activation(out=cm["junk"][:], in_=mj[:], func=AF.Square, accum_out=ss2[:, j:j + 1]),
                         reads=[mj], writes=[cm["junk"], ss2])
                    rstd_from(ss2[:, j:j + 1], rs2[:, j:j + 1], ss2, rs2, D)
                    xo = mj
                    P.op("dve", lambda v, mj=mj, j=j: v.scalar_tensor_tensor(out=mj[:], in0=mj[:], scalar=rs2[:, j:j + 1],
                         in1=gbc["post"][:], op0=ALU.mult, op1=ALU.mult), reads=[mj, rs2, gbc["post"]], writes=[mj])
                    P.op("pool", lambda g, mj=mj, j=j, xo=xo: g.tensor_tensor(out=xo[:], in0=mj[:], in1=xt[j][:], op=ALU.add),
                         reads=[mj, xt[j]], writes=[xo])
                    P.dma("pool", XMs[r0 + j * 128:r0 + (j + 1) * 128, :], xo[:, :], xo, reads=[xo], writes=[trk["XM"][ti]])

            P.barrier()
        with contextlib.suppress(_SkipPhase), contextlib.ExitStack() as ph:
            _phase_gate(5)
            Wup = sb(ph, "Wup", [128, 8, FF2], BF16)
            Wdn = sb(ph, "Wdn", [128, NFC, D], BF16)
            with contextlib.ExitStack() as ws:
                wst[0] = sb(ws, "wst0", [128, 2048], F32); wst[1] = sb(ws, "wst1", [128, 2048], F32)
                load_w(Wup, w_ffn_up, 8, FF2)
                load_w(Wdn, w_ffn_down, NFC, D)
                P.barrier()
            common(ph, 4, 512, ["fpre", "fpost"], one_xn=True, no_junk=True)
            fw = sb(ph, "fw", [128, 44, 3], F32)
            halP = sb(ph, "halP", [128, 44, 2], F32)
            halS = sb(ph, "halS", [128, 44, NS, 2], F32)
            sfs_stack = contextlib.ExitStack()
            sfs = sb(sfs_stack, "sfs", [3, 512], F32)
            for g11 in range(11):
                P.dma("sp", sfs[0:3, :], ffn_dw_w[:, g11 * 512:(g11 + 1) * 512], sfs, writes=[sfs])
                pb = PS[6]
                for c in range(4):
                    P.op("pe", lambda t, c=c: t.matmul(pb[:, c * 3:(c + 1) * 3], lhsT=sfs[0:3, c * 128:(c + 1) * 128],
                         rhs=identf[0:3, 0:3], start=True, stop=True), reads=[sfs, identf], writes=[pb], inc=(c == 3))
                P.op("dve", lambda v, g11=g11: v.tensor_copy(out=fw[:, g11 * 4:(g11 + 1) * 4, :],
                     in_=pb[:, 0:12].rearrange("p (c r) -> p c r", c=4)), reads=[pb], writes=[fw])
            for s in range(NS):
                for g11 in range(11):
                    P.dma("sp", sfs[0:2, :], sffn[s, :, g11 * 512:(g11 + 1) * 512], sfs, writes=[sfs])
                    pb = PS[7]
                    for c in range(4):
                        P.op("pe", lambda t, c=c: t.matmul(pb[:, c * 2:(c + 1) * 2], lhsT=sfs[0:2, c * 128:(c + 1) * 128],
                             rhs=identf[0:2, 0:2], start=True, stop=True), reads=[sfs, identf], writes=[pb], inc=(c == 3))
                    P.op("dve", lambda v, s=s, g11=g11: v.tensor_copy(out=halS[:, g11 * 4:(g11 + 1) * 4, s, :],
                         in_=pb[:, 0:8].rearrange("p (c r) -> p c r", c=4)), reads=[pb], writes=[halS])
            P.barrier()
            sfs_stack.close()
            upb = [sb(ph, f"upb{k}", [128, 4, 130], F32) for k in range(2)]
            cv_ = [sb(ph, f"cvv{k}", [128, 512], F32) for k in range(2)]
            gl = sb(ph, "gl", [128, 512], F32)
            cm["junk"] = _Alias(gl, gl[:].bitcast(BF16))
            g2 = gl
            gT = sb(ph, "gT", [128, NFC, 512], BF16)
            dn = [sb(ph, "dn0", [128, D], F32)] * 2
            ss3 = sb(ph, "ss3", [128, 4], F32)
            rs3 = sb(ph, "rs3", [128, 4], F32)
            P.op("dve", lambda v: v.memset(halP[:], 0.0), writes=[halP])
            ftiles = []
            for i in range(T // 512):
                ftiles.append(dict(r0=i * 512, N=512, segs=[(0, 512, None)], idx=i, sample=False, last=(i == T // 512 - 1)))
            ftiles.append(dict(r0=T, N=TS, segs=[(s * 64, 64, s) for s in range(NS)], idx=NT, sample=True, last=True))
            for tl in ftiles:
                N = tl["N"]; ti = tl["idx"]; r0 = tl["r0"]; smp = tl["sample"]
                nsub = N // 128
                load_x(tl, src_fn=lambda tl, j: XMs[tl["r0"] + j * 128:tl["r0"] + (j + 1) * 128, :], trkb=trk["XM"][ti])
                norm_T(tl, gbc["fpre"])
                for j in range(NFC):
                    outs = []
                    for which in range(2):
                        ch = which * NFC + j
                        pb = PS[(2 * j + which) % 4]
                        proj_fm(Wup, ch * 128, hT, N, pb)
                        ub = upb[which]
                        if smp:
                            P.op("act", lambda a, ch=ch, ub=ub: a.copy(out=ub[:, :, 0:2], in_=halS[:, ch, :, :]), reads=[halS], writes=[ub])
                            P.op("act", lambda a, pb=pb, ub=ub: a.copy(out=ub[:, :, 2:66], in_=pb[:, 0:N].rearrange("p (s t) -> p s t", s=NS)),
                                 reads=[pb], writes=[ub])
                            src = lambda k, ub=ub: ub[:, :, k:k + 64]
                            o3 = lambda t_: t_[:, 0:N].rearrange("p (s t) -> p s t", s=NS)
                        else:
                            uf = ub[:, :, :].rearrange("p a b -> p (a b)")
                            P.op("act", lambda a, ch=ch, uf=uf: a.copy(out=uf[:, 0:2], in_=halP[:, ch, :]), reads=[halP], writes=[ub])
                            P.op("act", lambda a, pb=pb, uf=uf: a.copy(out=uf[:, 2:2 + N], in_=pb[:, 0:N]), reads=[pb], writes=[ub])
                            P.op("pool", lambda g, ch=ch, uf=uf: g.tensor_copy(out=halP[:, ch, :], in_=uf[:, N:N + 2]),
                                 reads=[ub], writes=[halP])
                            src = lambda k, uf=uf: uf[:, k:k + N]
                            o3 = lambda t_: t_[:, 0:N]
                        outs.append((cv_[which], src, o3, ch, ub))
                    for (co, src, o3, ch, ub) in outs:
                        P.op("dve", lambda v, co=co, src=src, o3=o3, ch=ch: v.tensor_scalar(out=o3(co), in0=src(0), scalar1=fw[:, ch, 0:1],
                             scalar2=0.0, op0=ALU.mult, op1=ALU.add), reads=[ub, fw], writes=[co])
                    for k in (1, 2):
                        for (co, src, o3, ch, ub) in outs:
                            P.op("dve", lambda v, co=co, src=src, o3=o3, ch=ch, k=k: v.scalar_tensor_tensor(out=o3(co), in0=src(k),
                                 scalar=fw[:, ch, k:k + 1], in1=o3(co), op0=ALU.mult, op1=ALU.add), reads=[ub, fw, co], writes=[co])
                    outs = [o[0] for o in outs]
                    xg, xv = outs
                    P.op("pool", lambda g, xg=xg: g.tensor_tensor(out=g2[:, 0:N], in0=xg[:, 0:N], in1=xg[:, 0:N], op=ALU.mult),
                         reads=[xg], writes=[g2])
                    P.op("pool", lambda g: g.tensor_scalar(out=g2[:, 0:N], in0=g2[:, 0:N], scalar1=0.044715, scalar2=1.0,
                         op0=ALU.mult, op1=ALU.add), reads=[g2], writes=[g2])
                    P.op("pool", lambda g, xg=xg: g.tensor_tensor(out=g2[:, 0:N], in0=g2[:, 0:N], in1=xg[:, 0:N], op=ALU.mult),
                         reads=[g2, xg], writes=[g2])
                    P.op("act", lambda a: a.activation(out=gl[:, 0:N], in_=g2[:, 0:N], func=AF.Sigmoid, scale=1.5957691216057308),
                         reads=[g2], writes=[gl])
                    P.op("dve", lambda v, xg=xg: v.tensor_tensor(out=gl[:, 0:N], in0=gl[:, 0:N], in1=xg[:, 0:N], op=ALU.mult),
                         reads=[gl, xg], writes=[gl])
                    P.op("dve", lambda v, j=j, xv=xv: v.tensor_tensor(out=gT[:, j, 0:N], in0=gl[:, 0:N], in1=xv[:, 0:N], op=ALU.mult),
                         reads=[gl, xv], writes=[gT])
                ends = [(c0 + 62, s) for (c0, L, s) in tl["segs"]] if smp else ([(510, None)] if tl["last"] else [])
                for (t0, s) in ends:
                    for cb in range(11):
                        pb = PS[4 + cb % 2]
                        fb = gl
                        proj_tm(Wup, cb * 512, hT, t0, pb, M=2)
                        P.op("dve", lambda v, fb=fb, pb=pb: v.tensor_copy(out=fb[0:2, :], in_=pb[0:2, :]), reads=[pb], writes=[fb])
                        dst = fs[s, :, cb * 512:(cb + 1) * 512] if smp else fp[:, cb * 512:(cb + 1) * 512]
                        P.dma("pool", dst, fb[0:2, :], fb, reads=[fb])
                for j in range(nsub):
                    dj = dn[j % 2]
                    for hf in range(2):
                        pb = PS[6 + hf]
                        proj_tm(Wdn, hf * 512, gT, j * 128, pb, K=NFC)
                        if hf == 0:
                            P.op("act", lambda a, dj=dj, pb=pb: a.copy(out=dj[:, 0:512], in_=pb[:]), reads=[pb], writes=[dj])
                        else:
                            P.op("dve", lambda v, dj=dj, pb=pb: v.tensor_copy(out=dj[:, 512:1024], in_=pb[:]), reads=[pb], writes=[dj])
                    P.op("dve", lambda v, j=j: v.memset(ss3[:, j:j + 1], 0.0), writes=[ss3])
                    P.op("act", lambda a, dj=dj, j=j: a.activation(out=cm["junk"][:], in_=dj[:], func=AF.Square, accum_out=ss3[:, j:j + 1]),
                         reads=[dj], writes=[cm["junk"], ss3])
                    rstd_from(ss3[:, j:j + 1], rs3[:, j:j + 1], ss3, rs3, D)
                    P.op("dve", lambda v, dj=dj, j=j: v.scalar_tensor_tensor(out=dj[:], in0=dj[:], scalar=rs3[:, j:j + 1],
                         in1=gbc["fpost"][:], op0=ALU.mult, op1=ALU.mult), reads=[dj, rs3, gbc["fpost"]], writes=[dj])
                    P.op("pool", lambda g, dj=dj, j=j: g.tensor_tensor(out=dj[:], in0=dj[:], in1=xt[j][:], op=ALU.add),
                         reads=[dj, xt[j]], writes=[dj])
                    dst = ys[r0 - T + j * 128:r0 - T + (j + 1) * 128, :] if smp else yp[r0 + j * 128:r0 + (j + 1) * 128, :]
                    P.dma("pool", dst, dj[:, :], dj, reads=[dj])
            P.barrier()
        P.finish()
    return nc


_CACHE = {}


def run(T, NS, PAST, per_core):
    key = (T, NS, PAST)
    if key not in _CACHE:
        _CACHE[key] = build(T, NS, PAST)
    nc = _CACHE[key]
    res = run_bass_kernel_spmd(nc, per_core, core_ids=list(range(len(per_core))))
    return res.results


WNAMES = ["g_mem", "w_mem_kv", "g_mix_pre", "g_mix_post", "w_in", "w_sb_o", "conv_dw_w", "conv_dw_b", "conv_ln_g",
          "conv_ln_b", "w_conv_o", "w_mem_o", "w_out", "g_ffn_pre", "g_ffn_post", "w_ffn_up", "ffn_dw_w", "w_ffn_down"]


def make_maps(inp, ncores, NS):
    f = lambda a: np.ascontiguousarray(np.asarray(a, dtype=np.float32))
    B = inp["x_prompt"].shape[0]
    maps = []
    for c in range(ncores):
        b = c % B
        sl = slice(c * NS, (c + 1) * NS)
        m = {"xp": f(inp["x_prompt"][b]), "xs": f(inp["x_sample"][sl]).reshape(NS * 64, D),
             "memp": f(inp["mem_prompt"][b]), "ck": f(inp["cache_sb_k"][0, sl]), "cv": f(inp["cache_sb_v"][0, sl]),
             "sconv": f(inp["state_conv"][0, sl]), "sffn": f(inp["state_ffn_conv"][0, sl]),
             "cmk": f(inp["cache_mem_k"][0, sl]), "cmv": f(inp["cache_mem_v"][0, sl])}
        for n in WNAMES:
            w = f(inp[n][0])
            m[n] = w.reshape(1, -1) if w.ndim == 1 else w
        maps.append(m)
    return maps


def assemble(res, B, ncores):
    cat = lambda n, rng: np.stack([res[c][n] for c in rng])
    pc = range(B)
    sc = range(ncores)
    yp = cat("yp", pc); ys = np.concatenate([res[c]["ys"].reshape(-1, 64, D) for c in sc])
    kp = cat("kp", pc)[None]; vp = cat("vp", pc)[None]
    ks = np.concatenate([res[c]["ks"] for c in sc])[None]; vs = np.concatenate([res[c]["vs"] for c in sc])[None]
    cp = cat("cp", pc)[None]; cs = np.concatenate([res[c]["cs"] for c in sc])[None]
    fp = cat("fp", pc)[None]; fs = np.concatenate([res[c]["fs"] for c in sc])[None]
    mkp = cat("mkp", pc)[None]; mvp = cat("mvp", pc)[None]
    return (yp, ys, kp, vp, ks, vs, cp, cs, fp, fs, mkp, mvp)


def kernel(**inputs):
    T = inputs["x_prompt"].shape[1]
    PAST = inputs["cache_sb_k"].shape[3]
    ncores = 8
    NS = inputs["x_sample"].shape[0] // ncores
    maps = make_maps(inputs, ncores, NS)
    res = run(T, NS, PAST, maps)
    return assemble(res, inputs["x_prompt"].shape[0], ncores)
```

```python
import contextlib
import numpy as np
import concourse.bass as bass
import concourse.mybir as mybir
from concourse.bass_utils import run_bass_kernel_spmd

F32 = mybir.dt.float32
BF16 = mybir.dt.bfloat16
ALU = mybir.AluOpType
AF = mybir.ActivationFunctionType

D = 1024
NCH = 8
FF = 2816
FF2 = 5632
NFC = 22
EPS = 1e-6


class _SkipPhase(Exception):
    pass


def _phase_gate(k):
    import os
    en = os.environ.get("KPH")
    if en is not None and str(k) not in en.split(","):
        raise _SkipPhase()


class Buf:
    def __init__(self, t, name):
        self.t = t
        self.name = name
        self.w = None
        self.r = {}
        self.dsem = None
        self.dcnt = 0
        self.wl = {} if t is None else None

    def __getitem__(self, k):
        return self.t[k]


class _Alias:
    def __init__(self, base, ap):
        object.__setattr__(self, "base", base)
        object.__setattr__(self, "ap", ap)

    def __getitem__(self, k):
        return self.ap[k]

    def __getattr__(self, n):
        return getattr(object.__getattribute__(self, "base"), n)

    def __setattr__(self, n, v):
        setattr(object.__getattribute__(self, "base"), n, v)


class Prog:
    def __init__(self, nc, es):
        self.nc = nc
        self.es = es
        self.E = {"pe": nc.tensor, "act": nc.scalar, "dve": nc.vector, "pool": nc.gpsimd, "sp": nc.sync}
        self.sem = {e: es.enter_context(nc.semaphore("s_" + e)) for e in ("pe", "act", "dve", "pool")}
        self.cnt = {e: 0 for e in self.sem}
        self.seen = {e: {} for e in self.E}
        self.dbufs = []

    def _wait(self, e, dep, same_ok):
        if dep is None:
            return
        key, sem, val, src = dep
        if src is not None:
            val = 16 * src.dcnt
        elif key == e and not same_ok:
            return
        if self.seen[e].get(key, 0) >= val:
            return
        self.E[e].wait_ge(sem, val)
        self.seen[e][key] = val

    def _deps(self, e, reads, writes):
        for b in reads:
            self._wait(e, b.w, True)
            if b.wl:
                for d in list(b.wl.values()):
                    self._wait(e, d, True)
        for b in writes:
            self._wait(e, b.w, False)
            for d in list(b.r.values()):
                self._wait(e, d, False)

    def op(self, e, fn, reads=(), writes=(), inc=True):
        self._deps(e, reads, writes)
        ins = fn(self.E[e])
        if inc:
            self.cnt[e] += 1
            ins.then_inc(self.sem[e], 1)
            t = self.cnt[e]
        else:
            t = self.cnt[e] + 1
        dep = (e, self.sem[e], t, None)
        for b in reads:
            b.r[e] = dep
        for b in writes:
            b.w = dep
            b.r = {}
        return ins

    def dma(self, q, out_ap, in_ap, sbuf, reads=(), writes=()):
        self._deps(q, reads, writes)
        if sbuf.dsem is None:
            sbuf.dsem = self.es.enter_context(self.nc.semaphore("d_" + sbuf.name))
            self.dbufs.append(sbuf)
        sbuf.dcnt += 1
        self.E[q].dma_start(out=out_ap, in_=in_ap).then_inc(sbuf.dsem, 16)
        key = ("d", id(sbuf))
        dep = (key, sbuf.dsem, 16 * sbuf.dcnt, sbuf)
        for b in reads:
            b.r[key] = dep
        for b in writes:
            if b.wl is not None:
                b.wl[key] = dep
            else:
                b.w = dep
                b.r = {}

    def barrier(self):
        for e in self.E:
            for k in self.sem:
                if k != e and self.cnt[k] > self.seen[e].get(k, 0):
                    self.E[e].wait_ge(self.sem[k], self.cnt[k])
                    self.seen[e][k] = self.cnt[k]
            for b in self.dbufs:
                key = ("d", id(b))
                if 16 * b.dcnt > self.seen[e].get(key, 0):
                    self.E[e].wait_ge(b.dsem, 16 * b.dcnt)
                    self.seen[e][key] = 16 * b.dcnt

    def finish(self):
        for b in self.dbufs:
            self.E["sp"].wait_ge(b.dsem, 16 * b.dcnt)


def build(T, NS, PAST):
    nc = bass.Bass("TRN2", target_bir_lowering=False)
    TS = NS * 64
    NT = T // 512
    PB = PAST // 128

    def din(name, shape):
        return nc.dram_tensor(name, list(shape), F32, kind="ExternalInput").ap()

    def dout(name, shape):
        return nc.dram_tensor(name, list(shape), F32, kind="ExternalOutput").ap()

    xp = din("xp", [T, D]); xs = din("xs", [TS, D]); memp = din("memp", [256, D])
    ck = din("ck", [NS, 16, PAST, 64]); cv = din("cv", [NS, 16, PAST, 64])
    sconv = din("sconv", [NS, 30, D]); sffn = din("sffn", [NS, 2, FF2])
    cmk = din("cmk", [NS, 4, 256, 256]); cmv = din("cmv", [NS, 4, 256, 256])
    g_mem = din("g_mem", [1, D]); w_mem_kv = din("w_mem_kv", [D, 2048])
    g_mix_pre = din("g_mix_pre", [1, D]); g_mix_post = din("g_mix_post", [1, D])
    w_in = din("w_in", [D, 9216]); w_sb_o = din("w_sb_o", [D, D])
    conv_dw_w = din("conv_dw_w", [31, D]); conv_dw_b = din("conv_dw_b", [1, D])
    conv_ln_g = din("conv_ln_g", [1, D]); conv_ln_b = din("conv_ln_b", [1, D])
    w_conv_o = din("w_conv_o", [D, D]); w_mem_o = din("w_mem_o", [D, D]); w_out = din("w_out", [D, D])
    g_ffn_pre = din("g_ffn_pre", [1, D]); g_ffn_post = din("g_ffn_post", [1, D])
    w_ffn_up = din("w_ffn_up", [D, FF2]); ffn_dw_w = din("ffn_dw_w", [3, FF2]); w_ffn_down = din("w_ffn_down", [FF, D])

    yp = dout("yp", [T, D]); ys = dout("ys", [TS, D])
    kp = dout("kp", [16, T, 64]); vp = dout("vp", [16, T, 64])
    ks = dout("ks", [NS, 16, 64, 64]); vs = dout("vs", [NS, 16, 64, 64])
    cp = dout("cp", [30, D]); cs = dout("cs", [NS, 30, D])
    fp = dout("fp", [2, FF2]); fs = dout("fs", [NS, 2, FF2])
    mkp = dout("mkp", [4, 256, 256]); mvp = dout("mvp", [4, 256, 256])

    TT = T + TS
    KTs = nc.dram_tensor("KTs", [128, 8, TT], BF16).ap()
    VSs = nc.dram_tensor("VSs", [TT, D], BF16).ap()
    OSs = nc.dram_tensor("OSs", [128, 8, TT], BF16).ap()
    YCs = nc.dram_tensor("YCs", [128, 8, TT], F32).ap()
    YMs = nc.dram_tensor("YMs", [128, 8, TT], F32).ap()
    XMs = nc.dram_tensor("XMs", [TT, D], F32).ap()

    tiles = []
    for i in range(NT):
        tiles.append(dict(r0=i * 512, N=512, segs=[(0, 512, None)], idx=i, sample=False))
    tiles.append(dict(r0=T, N=TS, segs=[(s * 64, 64, s) for s in range(NS)], idx=NT, sample=True))
    ntile = len(tiles)
    trk = {n: [Buf(None, f"{n}{i}") for i in range(ntile)] for n in ("KT", "VS", "OS", "YC", "YM", "XM")}

    def xrows(tl, j):
        r = tl["r0"] + j * 128
        if tl["sample"]:
            return xs[r - T:r - T + 128, :]
        return xp[r:r + 128, :]

    es = contextlib.ExitStack()
    with es:
        P = Prog(nc, es)

        uid = [0]

        def sb(st, name, shape, dt):
            uid[0] += 1
            name = f"{name}_{uid[0]}"
            return Buf(st.enter_context(nc.sbuf_tensor(name, list(shape), dt)), name)

        PS = [Buf(es.enter_context(nc.psum_tensor(f"ps{i}", [128, 512], F32)), f"ps{i}") for i in range(8)]

        identb = sb(es, "identb", [128, 128], BF16)
        identf = sb(es, "identf", [128, 128], F32)
        negtri = sb(es, "negtri", [128, 128], BF16)
        negones = sb(es, "negones", [128, 128], BF16)
        onesb = sb(es, "onesb", [128, 128], BF16)
        onesf = sb(es, "onesf", [128, 128], F32)
        for bfr, val in ((identb, 1.0), (identf, 1.0), (negtri, -1.0), (negones, -1.0), (onesb, 1.0),
                         (onesf, 1.0 / D)):
            P.op("pool", lambda g, b=bfr, v=val: g.memset(b[:], v), writes=[bfr])
        for bfr in (identb, identf):
            P.op("pool", lambda g, b=bfr: g.affine_select(out=b[:], in_=b[:], pattern=[[-1, 128]],
                 compare_op=ALU.is_equal, fill=0.0, base=0, channel_multiplier=1), reads=[bfr], writes=[bfr])
        P.op("pool", lambda g: g.affine_select(out=negtri[:], in_=negtri[:], pattern=[[-1, 128]],
             compare_op=ALU.is_ge, fill=0.0, base=0, channel_multiplier=1), reads=[negtri], writes=[negtri])

        gbc = {}
        gsrc = {"pre": g_mix_pre, "post": g_mix_post, "fpre": g_ffn_pre, "fpost": g_ffn_post, "mem": g_mem}
        xt = [None] * 4
        xn = [None] * 2
        cm = {}

        class _HT:
            def __getitem__(self, k):
                return cm["hT"].t[k]
        hT = _HT()

        def common(ph, nsub, N, gs, one_xn=False, no_junk=False):
            for j in range(nsub):
                xt[j] = sb(ph, f"xt{j}", [128, D], F32)
            for j in range(2):
                xn[j] = xn[0] if (one_xn and j == 1) else sb(ph, f"xn{j}", [128, D], BF16)
            cm["hT"] = sb(ph, "hT", [128, 8, N], BF16)
            if not no_junk:
                cm["junk"] = sb(ph, "junk", [128, D], BF16)
            cm["ssq"] = sb(ph, "ssq", [128, 4], F32)
            cm["rstd"] = sb(ph, "rstd", [128, 4], F32)
            for nm in gs:
                gbc[nm] = sb(ph, "g_" + nm, [128, D], F32)
                P.dma("sp", gbc[nm][:], gsrc[nm][0:1, :].partition_broadcast(128), gbc[nm], writes=[gbc[nm]])

        def rstd_from(ss_ap, out_ap, ssb, outb, n):
            P.op("act", lambda a: a.activation(out=out_ap, in_=ss_ap, func=AF.Ln, scale=1.0 / n, bias=EPS),
                 reads=[ssb], writes=[outb])
            P.op("act", lambda a: a.activation(out=out_ap, in_=out_ap, func=AF.Exp, scale=-0.5),
                 reads=[outb], writes=[outb])

        def load_x(tl, src_fn=None, trkb=None):
            nsub = tl["N"] // 128
            for j in range(nsub):
                src = src_fn(tl, j) if src_fn else xrows(tl, j)
                P.dma("sp", xt[j][:], src, xt[j], reads=[trkb] if trkb else [], writes=[xt[j]])

        def norm_T(tl, g):
            nsub = tl["N"] // 128
            junk, ssq, rstd, hTb = cm["junk"], cm["ssq"], cm["rstd"], cm["hT"]
            P.op("dve", lambda v: v.memset(ssq[:], 0.0), writes=[ssq])
            for j in range(nsub):
                P.op("act", lambda a, j=j: a.activation(out=junk[:], in_=xt[j][:], func=AF.Square,
                     accum_out=ssq[:, j:j + 1]), reads=[xt[j]], writes=[junk, ssq])
            rstd_from(ssq[:, 0:nsub], rstd[:, 0:nsub], ssq, rstd, D)
            for j in range(nsub):
                xb = xn[j % 2]
                P.op("dve", lambda v, j=j, xb=xb: v.scalar_tensor_tensor(out=xb[:], in0=xt[j][:],
                     scalar=rstd[:, j:j + 1], in1=g[:], op0=ALU.mult, op1=ALU.mult),
                     reads=[xt[j], rstd, g], writes=[xb])
                for half in range(2):
                    pb = PS[6 + half]
                    for cc in range(4):
                        c = half * 4 + cc
                        P.op("pe", lambda t, c=c, cc=cc, pb=pb, xb=xb: t.matmul(pb[:, cc * 128:(cc + 1) * 128],
                             lhsT=xb[:, c * 128:(c + 1) * 128], rhs=identb[:], start=True, stop=True),
                             reads=[xb, identb], writes=[pb], inc=(cc == 3))
                    eng = "act" if half == 0 else "dve"
                    if eng == "act":
                        P.op("act", lambda a, half=half, pb=pb, j=j: a.copy(
                             out=hT[:, half * 4:half * 4 + 4, j * 128:(j + 1) * 128],
                             in_=pb[:].rearrange("p (c t) -> p c t", c=4)), reads=[pb], writes=[hTb])
                    else:
                        P.op("dve", lambda v, half=half, pb=pb, j=j: v.tensor_copy(
                             out=hT[:, half * 4:half * 4 + 4, j * 128:(j + 1) * 128],
                             in_=pb[:].rearrange("p (c t) -> p c t", c=4)), reads=[pb], writes=[hTb])

        wst = [None, None]
        wcnt = [0]

        def load_w(dst, src2d, nrc, ncols, dcol0=0):
            for rc in range(nrc):
                for cb in range(0, ncols, 2048):
                    w = min(2048, ncols - cb)
                    k = wcnt[0] % 2
                    wcnt[0] += 1
                    st = wst[k]
                    P.dma("sp", st[:, 0:w], src2d[rc * 128:(rc + 1) * 128, cb:cb + w], st, writes=[st])
                    eng = "dve" if k == 0 else "pool"
                    P.op(eng, lambda v, st=st, rc=rc, cb=cb, w=w: v.tensor_copy(
                         out=dst[:, rc, dcol0 + cb:dcol0 + cb + w], in_=st[:, 0:w]), reads=[st], writes=[dst])

        def load_cols(dst, srcs, R, nchunk, stg):
            r0 = 0
            for ap, nr in srcs:
                P.dma("sp", stg[r0:r0 + nr, 0:nchunk * 128], ap, stg, writes=[stg])
                r0 += nr
            pb = PS[6]
            for c in range(nchunk):
                P.op("pe", lambda t, c=c: t.matmul(pb[:, c * R:(c + 1) * R], lhsT=stg[0:R, c * 128:(c + 1) * 128],
                     rhs=identf[0:R, 0:R], start=True, stop=True), reads=[stg, identf], writes=[pb],
                     inc=(c == nchunk - 1))
            P.op("dve", lambda v: v.tensor_copy(out=dst[:].rearrange("p c r -> p (c r)"),
                 in_=pb[:, 0:nchunk * R]), reads=[pb], writes=[dst])

        def proj_fm(W, col0, rhsT, N, pb, K=NCH):
            rb = cm["hT"] if rhsT is hT else rhsT
            for c in range(K):
                P.op("pe", lambda t, c=c: t.matmul(pb[:, 0:N], lhsT=W[:, c, col0:col0 + 128], rhs=rhsT[:, c, 0:N],
                     start=(c == 0), stop=(c == K - 1)), reads=[W, rb], writes=[pb], inc=(c == K - 1))

        def proj_tm(W, col0, lhs, t0, pb, K=NCH, M=128):
            lb = cm["hT"] if lhs is hT else lhs
            for c in range(K):
                P.op("pe", lambda t, c=c: t.matmul(pb[0:M, :], lhsT=lhs[:, c, t0:t0 + M], rhs=W[:, c, col0:col0 + 512],
                     start=(c == 0), stop=(c == K - 1)), reads=[W, lb], writes=[pb], inc=(c == K - 1))

        memst = contextlib.ExitStack()
        mkT = sb(memst, "mkT", [128, 8, 256], BF16)
        mvb = sb(memst, "mvb", [128, 2, D], BF16)
        with contextlib.suppress(_SkipPhase), contextlib.ExitStack() as ph:
            _phase_gate(0)
            Wm = sb(ph, "Wm", [128, 8, 2048], BF16)
            with contextlib.ExitStack() as ws:
                wst[0] = sb(ws, "wst0", [128, 2048], F32); wst[1] = sb(ws, "wst1", [128, 2048], F32)
                load_w(Wm, w_mem_kv, 8, 2048)
                P.barrier()
            mtok = sb(ph, "mtok", [128, 2048], F32)
            common(ph, 2, 256, ["mem"])
            mt = dict(r0=0, N=256, segs=[], sample=False)
            load_x(mt, src_fn=lambda tl, j: memp[j * 128:(j + 1) * 128, :])
            norm_T(mt, gbc["mem"])
            for j in range(2):
                for hf in range(4):
                    pb = PS[hf % 4]
                    proj_tm(Wm, hf * 512, hT, j * 128, pb)
                    P.op("act" if hf % 2 else "dve", (lambda a, hf=hf, pb=pb: a.copy(out=mtok[:, hf * 512:(hf + 1) * 512], in_=pb[:])) if hf % 2
                         else (lambda v, hf=hf, pb=pb: v.tensor_copy(out=mtok[:, hf * 512:(hf + 1) * 512], in_=pb[:])),
                         reads=[pb], writes=[mtok])
                P.op("pool", lambda g, j=j: g.tensor_copy(out=mvb[:, j, :], in_=mtok[:, 1024:2048]),
                     reads=[mtok], writes=[mvb])
                P.dma("pool", mkp[:, j * 128:(j + 1) * 128, :].rearrange("h m d -> m h d"),
                      mtok[:, 0:1024].rearrange("m (h d) -> m h d", h=4), mtok, reads=[mtok])
                P.dma("pool", mvp[:, j * 128:(j + 1) * 128, :].rearrange("h m d -> m h d"),
                      mtok[:, 1024:2048].rearrange("m (h d) -> m h d", h=4), mtok, reads=[mtok])
            for cc in range(8):
                pb = PS[cc % 4]
                proj_fm(Wm, cc * 128, hT, 256, pb)
                P.op("act", lambda a, cc=cc, pb=pb: a.copy(out=mkT[:, cc, :], in_=pb[:, 0:256]), reads=[pb], writes=[mkT])

            P.barrier()
        with contextlib.suppress(_SkipPhase), contextlib.ExitStack() as ph:
            _phase_gate(1)
            Wq = sb(ph, "Wqm", [128, 8, D], BF16)
            Wmo = sb(ph, "Wmo", [128, 8, D], BF16)
            with contextlib.ExitStack() as ws:
                wst[0] = sb(ws, "wst0", [128, 2048], F32); wst[1] = sb(ws, "wst1", [128, 2048], F32)
                load_w(Wq, w_in[:, 5120:6144], 8, D)
                load_w(Wmo, w_mem_o, 8, D)
                P.barrier()
            common(ph, 4, 512, ["pre"])
            qmT = sb(ph, "qmT", [128, 8, 512], BF16)
            pT = [sb(ph, f"pT{k}", [128, 2, 512], BF16) for k in range(2)]
            omT = sb(ph, "omT", [128, 8, 512], BF16)
            rden = [sb(ph, f"rden{k}", [128, 512], F32) for k in range(2)]
            yst = [sb(ph, f"ystm{k}", [128, 512], F32) for k in range(2)]
            smkT = sb(ph, "smkT", [128, 8, 256], BF16)
            smvb = sb(ph, "smvb", [128, 2, D], BF16)
            cmst = [sb(ph, f"cmst{k}", [128, 2, 256], F32) for k in range(2)]

            def mem_attn(kT, vB, c0, L):
                for hm in range(4):
                    pk = pT[hm % 2]
                    for mc in range(2):
                        pb = PS[mc]
                        for dc in range(2):
                            P.op("pe", lambda t, mc=mc, dc=dc, pb=pb: t.matmul(pb[:, 0:L],
                                 lhsT=kT[:, hm * 2 + dc, mc * 128:(mc + 1) * 128], rhs=qmT[:, hm * 2 + dc, c0:c0 + L],
                                 start=(dc == 0), stop=(dc == 1)), reads=[kT, qmT], writes=[pb], inc=(dc == 1))
                        P.op("act", lambda a, mc=mc, pb=pb, pk=pk: a.activation(out=pk[:, mc, 0:L], in_=pb[:, 0:L], func=AF.Exp),
                             reads=[pb], writes=[pk])
                    pd = PS[2]
                    for mc in range(2):
                        P.op("pe", lambda t, mc=mc, pk=pk: t.matmul(pd[:, 0:L], lhsT=onesb[:], rhs=pk[:, mc, 0:L],
                             start=(mc == 0), stop=(mc == 1)), reads=[onesb, pk], writes=[pd], inc=(mc == 1))
                    rd = rden[hm % 2]
                    P.op("dve", lambda v, rd=rd: v.reciprocal(out=rd[:, 0:L], in_=pd[:, 0:L]), reads=[pd], writes=[rd])
                    for dc in range(2):
                        po = PS[4 + dc]
                        for mc in range(2):
                            P.op("pe", lambda t, mc=mc, dc=dc, po=po, pk=pk: t.matmul(po[:, 0:L],
                                 lhsT=vB[:, mc, hm * 256 + dc * 128:hm * 256 + dc * 128 + 128], rhs=pk[:, mc, 0:L],
                                 start=(mc == 0), stop=(mc == 1)), reads=[vB, pk], writes=[po], inc=(mc == 1))
                        P.op("dve", lambda v, dc=dc, po=po, rd=rd: v.tensor_tensor(out=omT[:, hm * 2 + dc, c0:c0 + L],
                             in0=po[:, 0:L], in1=rd[:, 0:L], op=ALU.mult), reads=[po, rd], writes=[omT])

            for tl in tiles:
                N = tl["N"]; ti = tl["idx"]; smp = tl["sample"]
                load_x(tl)
                norm_T(tl, gbc["pre"])
                for c2 in range(8):
                    pb = PS[c2 % 4]
                    proj_fm(Wq, c2 * 128, hT, N, pb)
                    P.op("act", lambda a, c2=c2, pb=pb: a.activation(out=qmT[:, c2, 0:N], in_=pb[:, 0:N], func=AF.Copy,
                         scale=1.0 / 16.0), reads=[pb], writes=[qmT])
                if not smp:
                    mem_attn(mkT, mvb, 0, N)
                else:
                    for (c0, L, s) in tl["segs"]:
                        for hm in range(4):
                            st = cmst[hm % 2]
                            P.dma("sp", st[:, :, :], cmk[s, hm, :, :].rearrange("(j m) d -> m j d", j=2), st, writes=[st])
                            pb = PS[6 + hm % 2]
                            for dc in range(2):
                                for j in range(2):
                                    P.op("pe", lambda t, dc=dc, j=j, pb=pb, st=st: t.matmul(
                                         pb[:, dc * 256 + j * 128:dc * 256 + j * 128 + 128],
                                         lhsT=st[:, j, dc * 128:(dc + 1) * 128], rhs=identf[:], start=True, stop=True),
                                         reads=[st, identf], writes=[pb], inc=(dc == 1 and j == 1))
                            P.op("dve", lambda v, hm=hm, pb=pb: v.tensor_copy(out=smkT[:, hm * 2:hm * 2 + 2, :],
                                 in_=pb[:].rearrange("p (c m) -> p c m", c=2)), reads=[pb], writes=[smkT])
                            st2 = cmst[(hm + 1) % 2]
                            P.dma("sp", st2[:, :, :], cmv[s, hm, :, :].rearrange("(j m) d -> m j d", j=2), st2, writes=[st2])
                            P.op("pool", lambda g, hm=hm, st2=st2: g.tensor_copy(out=smvb[:, :, hm * 256:(hm + 1) * 256],
                                 in_=st2[:, :, :]), reads=[st2], writes=[smvb])
                        mem_attn(smkT, smvb, c0, L)
                for c2 in range(8):
                    pb = PS[c2 % 4]
                    proj_fm(Wmo, c2 * 128, omT, N, pb)
                    y = yst[c2 % 2]
                    P.op("act", lambda a, pb=pb, y=y: a.copy(out=y[:, 0:N], in_=pb[:, 0:N]), reads=[pb], writes=[y])
                    P.dma("pool", YMs[:, c2, tl["r0"]:tl["r0"] + N], y[:, 0:N], y, reads=[y], writes=[trk["YM"][ti]])
            P.barrier()
        P.barrier()
        memst.close()

        with contextlib.suppress(_SkipPhase), contextlib.ExitStack() as ph:
            _phase_gate(2)
            Wc = sb(ph, "Wc", [128, 8, 2048], BF16)
            Wco = sb(ph, "Wco", [128, 8, D], BF16)
            with contextlib.ExitStack() as ws:
                wst[0] = sb(ws, "wst0", [128, 2048], F32); wst[1] = sb(ws, "wst1", [128, 2048], F32)
                load_w(Wc, w_in[:, 3072:5120], 8, 2048)
                load_w(Wco, w_conv_o, 8, D)
                P.barrier()
            common(ph, 4, 512, ["pre"])
            cw = sb(ph, "cw", [128, 8, 31], F32)
            cvec = sb(ph, "cvec", [128, 8, 3], F32)
            with contextlib.ExitStack() as ws:
                stg = sb(ws, "stg", [31, D], F32)
                load_cols(cw, [(conv_dw_w[:, :], 31)], 31, 8, stg)
                load_cols(cvec, [(conv_dw_b[0:1, :], 1), (conv_ln_g[0:1, :], 1), (conv_ln_b[0:1, :], 1)], 3, 8, stg)
                P.barrier()
            uP = sb(ph, "uP", [128, 8, 542], F32)
            uP2 = sb(ph, "uP2", [128, 8, 542], F32)
            uS = sb(ph, "uS", [128, 8, NS, 94], F32)
            ccs = [sb(ph, "cc", [128, 8, 512], F32), sb(ph, "ccb", [128, 8, 512], F32)]
            ccvs = [[Buf(None, f"ccv{a_}{c_}") for c_ in range(8)] for a_ in range(2)]
            for a_ in range(2):
                for c_ in range(8):
                    ccvs[a_][c_].wl = None
            cdum = sb(ph, "cdum", [128, 4], F32)
            P.op("dve", lambda v: v.memset(cdum[:], 0.0), writes=[cdum])
            csq = [sb(ph, f"csq{k}", [128, 512], F32) for k in range(2)]
            ccT = sb(ph, "ccT", [128, 8, 512], BF16)
            sg = [sb(ph, f"sg{k}", [128, 512], F32) for k in range(2)]
            mean = sb(ph, "mean", [128, 512], F32)
            rs = sb(ph, "rs", [128, 512], F32)
            t1 = [sb(ph, f"t1{k}", [128, 512], F32) for k in range(2)]
            yst = [sb(ph, f"yst{k}", [128, 512], F32) for k in range(2)]
            cst = sb(ph, "cst", [30, D], F32)
            sst = sb(ph, "sst", [30, D], F32)
            uPs = [uP, uP2]
            P.op("dve", lambda v: v.memset(uPs[0][:, :, 0:30], 0.0), writes=[uPs[0]])

            def stageA1(tl):
                N = tl["N"]; ti = tl["idx"]; smp = tl["sample"]
                uP = uPs[ti % 2]
                if (not smp) and ti > 0:
                    P.op("dve", lambda v: v.tensor_copy(out=uP[:, :, 0:30], in_=uPs[(ti - 1) % 2][:, :, 512:542]),
                         reads=[uPs[(ti - 1) % 2]], writes=[uP])
                load_x(tl)
                norm_T(tl, gbc["pre"])
                if smp:
                    for s in range(NS):
                        P.dma("sp", sst[:, :], sconv[s, :, :], sst, writes=[sst])
                        pb = PS[4 + s % 2]
                        for c in range(8):
                            P.op("pe", lambda t, c=c, pb=pb: t.matmul(pb[:, c * 30:(c + 1) * 30],
                                 lhsT=sst[0:30, c * 128:(c + 1) * 128], rhs=identf[0:30, 0:30], start=True, stop=True),
                                 reads=[sst, identf], writes=[pb], inc=(c == 7))
                        P.op("dve", lambda v, s=s, pb=pb: v.tensor_copy(out=uS[:, :, s, 0:30],
                             in_=pb[:, 0:240].rearrange("p (c r) -> p c r", c=8)), reads=[pb], writes=[uS])
            def stageA2(tl):
                N = tl["N"]; ti = tl["idx"]; smp = tl["sample"]
                uP = uPs[ti % 2]
                for c2 in range(8):
                    pa, pbb = PS[(2 * c2) % 4], PS[(2 * c2 + 1) % 4]
                    proj_fm(Wc, c2 * 128, hT, N, pa)
                    proj_fm(Wc, 1024 + c2 * 128, hT, N, pbb)
                    sgk = sg[c2 % 2]
                    P.op("act", lambda a, pbb=pbb, sgk=sgk: a.activation(out=sgk[:, 0:N], in_=pbb[:, 0:N], func=AF.Sigmoid),
                         reads=[pbb], writes=[sgk])
                    if smp:
                        P.op("dve", lambda v, c2=c2, pa=pa, sgk=sgk: v.tensor_tensor(out=uS[:, c2, :, 30:94],
                             in0=pa[:, 0:N].rearrange("p (s t) -> p s t", s=NS),
                             in1=sgk[:, 0:N].rearrange("p (s t) -> p s t", s=NS), op=ALU.mult),
                             reads=[pa, sgk], writes=[uS])
                    else:
                        P.op("dve", lambda v, c2=c2, pa=pa, sgk=sgk: v.tensor_tensor(out=uP[:, c2, 30:542],
                             in0=pa[:, 0:N], in1=sgk[:, 0:N], op=ALU.mult), reads=[pa, sgk], writes=[uP])

            def stageC(tl):
                N = tl["N"]; ti = tl["idx"]; smp = tl["sample"]
                uP = uPs[ti % 2]
                cc_ = ccs[ti % 2]
                ccv = ccvs[ti % 2]
                P.op("dve", lambda v: v.tensor_copy(out=cdum[:, 2:3], in_=cdum[:, 3:4]), reads=[cc_, cdum], writes=ccv + [cdum, cc_])
                eng = "dve"
                ub = uS if smp else uP
                for (c0, L, s) in tl["segs"]:
                    def usrc(k, c2, s=s, L=L):
                        return uS[:, c2, s, k:k + L] if smp else uP[:, c2, k:k + L]
                    for c2 in range(8):
                        P.op(eng, lambda v, c2=c2, c0=c0, L=L: v.tensor_scalar(out=cc_[:, c2, c0:c0 + L], in0=usrc(0, c2),
                             scalar1=cw[:, c2, 0:1], scalar2=cvec[:, c2, 0:1], op0=ALU.mult, op1=ALU.add),
                             reads=[ub, cw, cvec], writes=[ccv[c2]])
                    for k in range(1, 31):
                        for c2 in range(8):
                            P.op(eng, lambda v, c2=c2, c0=c0, L=L, k=k: v.scalar_tensor_tensor(
                                 out=cc_[:, c2, c0:c0 + L], in0=usrc(k, c2), scalar=cw[:, c2, k:k + 1],
                                 in1=cc_[:, c2, c0:c0 + L], op0=ALU.mult, op1=ALU.add),
                                 reads=[ub, cw, ccv[c2]], writes=[ccv[c2]])
                P.op(eng, lambda v: v.tensor_copy(out=cdum[:, 0:1], in_=cdum[:, 1:2]), reads=ccv + [cdum], writes=[cc_, cdum])

            def stageL(tl):
                N = tl["N"]; ti = tl["idx"]; smp = tl["sample"]
                uP = uPs[ti % 2]
                cc_ = ccs[ti % 2]
                pm, pq = PS[4], PS[5]
                for c2 in range(8):
                    q = csq[c2 % 2]
                    P.op("act", lambda a, c2=c2, q=q: a.activation(out=q[:, 0:N], in_=cc_[:, c2, 0:N], func=AF.Square),
                         reads=[cc_], writes=[q])
                    P.op("pe", lambda t, c2=c2: t.matmul(pm[:, 0:N], lhsT=onesf[:], rhs=cc_[:, c2, 0:N],
                         start=(c2 == 0), stop=(c2 == 7)), reads=[onesf, cc_], writes=[pm])
                    P.op("pe", lambda t, c2=c2, q=q: t.matmul(pq[:, 0:N], lhsT=onesf[:], rhs=q[:, 0:N],
                         start=(c2 == 0), stop=(c2 == 7)), reads=[onesf, q], writes=[pq])
                P.op("act", lambda a: a.copy(out=mean[:, 0:N], in_=pm[:, 0:N]), reads=[pm], writes=[mean])
                P.op("dve", lambda v: v.tensor_tensor(out=rs[:, 0:N], in0=mean[:, 0:N], in1=mean[:, 0:N], op=ALU.mult),
                     reads=[mean], writes=[rs])
                P.op("dve", lambda v: v.tensor_tensor(out=rs[:, 0:N], in0=pq[:, 0:N], in1=rs[:, 0:N], op=ALU.subtract),
                     reads=[pq, rs], writes=[rs])
                rstd_from(rs[:, 0:N], rs[:, 0:N], rs, rs, 1.0)
                for c2 in range(8):
                    tk = t1[c2 % 2]
                    P.op("dve", lambda v, c2=c2, tk=tk: v.tensor_tensor(out=tk[:, 0:N], in0=cc_[:, c2, 0:N], in1=mean[:, 0:N],
                         op=ALU.subtract), reads=[cc_, mean], writes=[tk])
                    P.op("dve", lambda v, tk=tk: v.tensor_tensor(out=tk[:, 0:N], in0=tk[:, 0:N], in1=rs[:, 0:N], op=ALU.mult),
                         reads=[tk, rs], writes=[tk])
                    P.op("act", lambda a, c2=c2, tk=tk: a.activation(out=ccT[:, c2, 0:N], in_=tk[:, 0:N], func=AF.Silu,
                         scale=cvec[:, c2, 1:2], bias=cvec[:, c2, 2:3]), reads=[tk, cvec], writes=[ccT])
                for c2 in range(8):
                    pb = PS[c2 % 4]
                    proj_fm(Wco, c2 * 128, ccT, N, pb)
                    y = yst[c2 % 2]
                    P.op("act", lambda a, pb=pb, y=y: a.copy(out=y[:, 0:N], in_=pb[:, 0:N]), reads=[pb], writes=[y])
                    P.dma("pool", YCs[:, c2, tl["r0"]:tl["r0"] + N], y[:, 0:N], y, reads=[y], writes=[trk["YC"][ti]])
                ends = [(s, 64) for (_, _, s) in tl["segs"]] if smp else ([(None, 512)] if ti == NT - 1 else [])
                for (s, L) in ends:
                    for half in range(2):
                        pb = PS[4 + half]
                        for cq in range(4):
                            c2 = half * 4 + cq
                            src = uS[:, c2, s, L:L + 30] if smp else uP[:, c2, L:L + 30]
                            P.op("pe", lambda t, cq=cq, pb=pb, src=src: t.matmul(pb[0:30, cq * 128:(cq + 1) * 128], lhsT=src,
                                 rhs=identf[:], start=True, stop=True), reads=[uS if smp else uP, identf], writes=[pb],
                                 inc=(cq == 3))
                        P.op("dve", lambda v, half=half, pb=pb: v.tensor_copy(out=cst[:, half * 512:(half + 1) * 512],
                             in_=pb[0:30, :]), reads=[pb], writes=[cst])
                    P.dma("pool", cs[s, :, :] if smp else cp[:, :], cst[:, :], cst, reads=[cst])


            nt_ = len(tiles)
            for n_ in range(-2, nt_):
                if 0 <= n_ + 2 < nt_:
                    stageA1(tiles[n_ + 2])
                if 0 <= n_ + 1 < nt_:
                    stageC(tiles[n_ + 1])
                if 0 <= n_ + 2 < nt_:
                    stageA2(tiles[n_ + 2])
                if 0 <= n_ < nt_:
                    stageL(tiles[n_])
            P.barrier()
        with contextlib.suppress(_SkipPhase), contextlib.ExitStack() as ph:
            _phase_gate(3)
            Wqkv = sb(ph, "Wqkv", [128, 8, 3072], BF16)
            with contextlib.ExitStack() as ws:
                wst[0] = sb(ws, "wst0", [128, 2048], F32); wst[1] = sb(ws, "wst1", [128, 2048], F32)
                load_w(Wqkv, w_in[:, 0:3072], 8, 3072)
                P.barrier()
            common(ph, 4, 512, ["pre"])
            masks = sb(ph, "masks", [128, 4, 512], BF16)
            P.op("pool", lambda g: g.memset(masks[:], 1.0), writes=[masks])
            for r in range(4):
                P.op("pool", lambda g, r=r: g.affine_select(out=masks[:, r, :], in_=masks[:, r, :], pattern=[[1, 512]],
                     compare_op=ALU.is_gt, fill=0.0, base=-128 * r, channel_multiplier=-1), reads=[masks], writes=[masks])
            qT = sb(ph, "qT", [128, 8, 512], BF16)
            kT = sb(ph, "kT", [128, 8, 512], BF16)
            tok = [sb(ph, f"tok{k}", [128, D], F32) for k in range(2)]
            vbf = [sb(ph, f"vbf{k}", [128, D], BF16) for k in range(2)]
            osb = sb(ph, "osb", [128, 8, 512], BF16)
            Kt = [sb(ph, f"Kt{k}", [128, 2, 512], BF16) for k in range(2)]
            Vt = [sb(ph, f"Vt{k}", [128, 4, 2, 128], BF16) for k in range(2)]
            Vs = [sb(ph, f"Vs{k}", [128, 4, 128], BF16) for k in range(2)]
            for k in range(2):
                P.op("pool", lambda g, k=k: g.memset(Kt[k][:], 0.0), writes=[Kt[k]])
                P.op("pool", lambda g, k=k: g.memset(Vt[k][:], 0.0), writes=[Vt[k]])
            Ee = [PS[5], PS[6]]
            Sp = [sb(ph, f"Sp{k}", [128, 512], BF16) for k in range(2)]
            Ls = [[sb(ph, f"Ls{h}{k}", [128, 512], BF16) for k in range(2)] for h in range(2)]
            Aa = [sb(ph, f"Aa{k}", [128, 512], BF16) for k in range(2)]
            kc = [sb(ph, f"kc{k}", [128, 4, 128], F32) for k in range(2)]
            vc = [sb(ph, f"vc{k}", [128, 2, 4, 64], F32) for k in range(2)]
            PC = [[PS[0], PS[1]], [PS[2], PS[3]]]; PO = PS[4]

            def S1a(u):
                hh, p, Ktb, kcols, qcols, N = u["hh"], u["p"], u["Ktb"], u["kcols"], u["qcols"], u["N"]
                z, e = PC[hh][u["lsi"] % 2], Ee[hh]
                P.op("pe", lambda t: t.matmul(z[:, 0:N], lhsT=Ktb[:, hh, kcols], rhs=qT[:, p, qcols], start=True, stop=False),
                     reads=[Ktb, qT], writes=[z])
                P.op("act", lambda a: a.activation(out=e[:, 0:N], in_=z[:, 0:N], func=AF.Exp), reads=[z], writes=[e])

            def S1b(u):
                hh, N, mask_ap = u["hh"], u["N"], u["mask"]
                e, s_ = Ee[hh], Sp[hh]
                P.op("act", lambda a: a.activation(out=s_[:, 0:N], in_=e[:, 0:N], func=AF.Ln, bias=1.0, scale=1.0),
                     reads=[e], writes=[s_])
                if mask_ap is not None:
                    P.op("dve", lambda v: v.tensor_tensor(out=s_[:, 0:N], in0=s_[:, 0:N], in1=mask_ap, op=ALU.mult),
                         reads=[s_, masks], writes=[s_])

            def S2a(u):
                hh, N = u["hh"], u["N"]
                first, last, lsi = u["first"], u["last"], u["lsi"]
                cb_, s_, a_ = PC[hh][lsi % 2], Sp[hh], Aa[hh]
                lo, ln = Ls[hh][lsi % 2], Ls[hh][(lsi + 1) % 2]
                P.op("pe", lambda t: t.matmul(cb_[:, 0:N], lhsT=negtri[:, :], rhs=s_[:, 0:N], start=False, stop=first),
                     reads=[negtri, s_], writes=[cb_], inc=first)
                if not first:
                    P.op("pe", lambda t: t.matmul(cb_[:, 0:N], lhsT=negones[:, :], rhs=lo[:, 0:N], start=False, stop=True),
                         reads=[negones, lo], writes=[cb_])
                if not last:
                    if first:
                        P.op("pool", lambda g: g.tensor_copy(out=ln[:, 0:N], in_=s_[:, 0:N]), reads=[s_], writes=[ln])
                    else:
                        P.op("dve", lambda v: v.tensor_tensor(out=ln[:, 0:N], in0=lo[:, 0:N], in1=s_[:, 0:N], op=ALU.add),
                             reads=[lo, s_], writes=[ln])
                P.op("act", lambda a: a.activation(out=a_[:, 0:N], in_=cb_[:, 0:N], func=AF.Exp), reads=[cb_], writes=[a_])

            def S2b(u):
                hh, Vb, vr, N, mask_ap, first, last = u["hh"], u["Vb"], u["vr"], u["N"], u["mask"], u["first"], u["last"]
                hp = slice(hh * 64, hh * 64 + 64)
                a_ = Aa[hh]
                if mask_ap is not None:
                    P.op("dve", lambda v: v.tensor_tensor(out=a_[:, 0:N], in0=a_[:, 0:N], in1=mask_ap, op=ALU.mult),
                         reads=[a_, masks], writes=[a_])
                P.op("pe", lambda t: t.matmul(PO[:, 0:N], lhsT=Vb[:, vr, hh, :], rhs=a_[:, 0:N], start=(first and hh == 0),
                     stop=(last and hh == 1)), reads=[Vb, a_], writes=[PO], inc=(last and hh == 1))

            def emit_units(units, res=None):
                n = len(units)
                for i in range(n + 3):
                    if i < n:
                        if units[i].get("pre"):
                            units[i]["pre"]()
                        if res is not None:
                            res(units[i])
                        S1a(units[i])
                    if 0 <= i - 2 < n:
                        S2a(units[i - 2])
                    if i < n:
                        S1b(units[i])
                    if 0 <= i - 3 < n:
                        S2b(units[i - 3])
                        if units[i - 3].get("post"):
                            units[i - 3]["post"]()

            ldc = [0]

            def load_kv(p, tj, ti):
                k = ldc[0] % 2
                ldc[0] += 1
                r0 = tiles[tj]["r0"]
                for hh in range(2):
                    P.dma("sp", Kt[k][hh * 64:(hh + 1) * 64, hh, :], KTs[hh * 64:(hh + 1) * 64, p, r0:r0 + 512], Kt[k],
                          reads=[trk["KT"][tj]], writes=[Kt[k]])
                P.dma("sp", Vs[k][:, :, :], VSs[r0:r0 + 512, p * 128:(p + 1) * 128].rearrange("(r s) c -> s r c", r=4),
                      Vs[k], reads=[trk["VS"][tj]], writes=[Vs[k]])
                for hh in range(2):
                    P.op("pool", lambda g, hh=hh: g.tensor_copy(out=Vt[k][:, :, hh, hh * 64:(hh + 1) * 64],
                         in_=Vs[k][:, :, hh * 64:(hh + 1) * 64]), reads=[Vs[k]], writes=[Vt[k]])
                return k

            for tl in tiles:
                N = tl["N"]; ti = tl["idx"]; smp = tl["sample"]; r0 = tl["r0"]
                nsub = N // 128
                load_x(tl)
                norm_T(tl, gbc["pre"])
                for p in range(8):
                    pq_, pk_ = PS[5], PS[6]
                    proj_fm(Wqkv, p * 128, hT, N, pq_)
                    P.op("act", lambda a, p=p: a.activation(out=qT[:, p, 0:N], in_=pq_[:, 0:N], func=AF.Copy, scale=0.125),
                         reads=[pq_], writes=[qT])
                    proj_fm(Wqkv, 1024 + p * 128, hT, N, pk_)
                    P.op("dve", lambda v, p=p: v.tensor_copy(out=kT[:, p, 0:N], in_=pk_[:, 0:N]), reads=[pk_], writes=[kT])
                P.dma("pool", KTs[:, :, r0:r0 + N], kT[:, :, 0:N], kT, reads=[kT], writes=[trk["KT"][ti]])
                for j in range(nsub):
                    for which in range(2):
                        tk = tok[which]
                        for hf in range(2):
                            pb = PS[5 + hf]
                            proj_tm(Wqkv, 1024 * (1 + which) + hf * 512, hT, j * 128, pb)
                            if hf == 0:
                                P.op("act", lambda a, tk=tk, pb=pb: a.copy(out=tk[:, 0:512], in_=pb[:]), reads=[pb], writes=[tk])
                            else:
                                P.op("dve", lambda v, tk=tk, pb=pb: v.tensor_copy(out=tk[:, 512:1024], in_=pb[:]), reads=[pb], writes=[tk])
                        if not smp:
                            dst = (kp if which == 0 else vp)[:, r0 + j * 128:r0 + (j + 1) * 128, :].rearrange("h t d -> t h d")
                            P.dma("pool", dst, tk[:, :].rearrange("t (h d) -> t h d", h=16), tk, reads=[tk])
                        else:
                            for s2 in range(2):
                                s = j * 2 + s2
                                dst = (ks if which == 0 else vs)[s, :, :, :].rearrange("h t d -> t h d")
                                P.dma("pool", dst, tk[s2 * 64:(s2 + 1) * 64, :].rearrange("t (h d) -> t h d", h=16), tk, reads=[tk])
                        if which == 1:
                            vb = vbf[j % 2]
                            P.op("pool", lambda g, vb=vb, tk=tk: g.tensor_copy(out=vb[:], in_=tk[:]), reads=[tk], writes=[vb])
                            P.dma("pool", VSs[r0 + j * 128:r0 + (j + 1) * 128, :], vb[:, :], vb, reads=[vb], writes=[trk["VS"][ti]])
                import os as _os
                _ka = _os.environ.get("KA_SKIP", "")
                if not smp:
                    NPp = 8 if "p" not in _ka else 0
                    nkt = ti + 1
                    nsteps = 4 * nkt
                    bufks = [dict() for _ in range(8)]
                    allu = []
                    for p in range(NPp):
                        bk = bufks[p]
                        step = 0
                        for jj in range(nkt):
                            tj = ti - jj
                            for r in (3, 2, 1, 0):
                                for hh in range(2):
                                    u = dict(hh=hh, p=p, jj=jj, bk=bk, kcols=slice(r * 128, (r + 1) * 128), vr=r, qcols=slice(0, N), N=N,
                                             first=(step == 0), last=(step == nsteps - 1),
                                             mask=(masks[:, r, :] if jj == 0 else None), lsi=step)
                                    if p == 0 and step == 0 and hh == 0:
                                        u["pre"] = (lambda bk=bk: bk.__setitem__(0, load_kv(0, ti, ti)))
                                    if r == 1 and hh == 0:
                                        if jj + 1 < nkt:
                                            u["pre"] = (lambda bk=bk, p=p, jj=jj, tj=tj: bk.__setitem__(jj + 1, load_kv(p, tj - 1, ti)))
                                        elif p + 1 < NPp:
                                            u["pre"] = (lambda nb=bufks[p + 1], p=p: nb.__setitem__(0, load_kv(p + 1, ti, ti)))
                                    if step == nsteps - 1 and hh == 1:
                                        u["post"] = (lambda p=p: P.op("act", lambda a: a.copy(out=osb[:, p, 0:N], in_=PO[:, 0:N]),
                                                                      reads=[PO], writes=[osb]))
                                    allu.append(u)
                                step += 1

                    def _res(u):
                        k = u["bk"][u["jj"]]
                        u["Ktb"], u["Vb"] = Kt[k], Vt[k]
                    emit_units(allu, _res)
                else:
                    for (c0, L, s) in tl["segs"]:
                        for p in range(8 if "s" not in _ka else 0):
                            k = ldc[0] % 2
                            ldc[0] += 1
                            nsteps = 1 + PB
                            qc = slice(c0, c0 + L)

                            def pre_first(k=k, p=p, c0=c0):
                                P.op("pool", lambda g: g.memset(Kt[k][:, :, 0:128], 0.0), writes=[Kt[k]])
                                P.op("pool", lambda g: g.memset(Vt[k][:, 0, :, :], 0.0), writes=[Vt[k]])
                                for hh in range(2):
                                    hs = slice(hh * 64, (hh + 1) * 64)
                                    P.dma("sp", Kt[k][hs, hh, 0:64], KTs[hs, p, r0 + c0:r0 + c0 + 64], Kt[k],
                                          reads=[trk["KT"][ti]], writes=[Kt[k]])
                                    P.dma("sp", Vt[k][0:64, 0, hh, hs], VSs[r0 + c0:r0 + c0 + 64, p * 128 + hh * 64:p * 128 + (hh + 1) * 64],
                                          Vt[k], reads=[trk["VS"][ti]], writes=[Vt[k]])

                            def pre_group(kk, g4, p=p, s=s):
                                kcb, vcb = kc[kk], vc[kk]
                                for h2 in range(2):
                                    P.dma("sp", kcb[:, :, h2 * 64:(h2 + 1) * 64], ck[s, 2 * p + h2, g4 * 512:(g4 + 1) * 512, :].rearrange(
                                          "(b k) d -> k b d", b=4), kcb, writes=[kcb])
                                    P.dma("sp", vcb[:, h2, :, :], cv[s, 2 * p + h2, g4 * 512:(g4 + 1) * 512, :].rearrange(
                                          "(b k) d -> k b d", b=4), vcb, writes=[vcb])
                                pt = PS[7]
                                for b in range(4):
                                    P.op("pe", lambda t, b=b: t.matmul(pt[:, b * 128:(b + 1) * 128],
                                         lhsT=kcb[:, b, :], rhs=identf[:], start=True, stop=True),
                                         reads=[kcb, identf], writes=[pt], inc=(b == 3))
                                for h2 in range(2):
                                    hs = slice(h2 * 64, (h2 + 1) * 64)
                                    P.op("dve", lambda v, h2=h2, hs=hs: v.tensor_copy(out=Kt[kk][hs, h2, :], in_=pt[hs, :]),
                                         reads=[pt], writes=[Kt[kk]])
                                    P.op("pool", lambda g, h2=h2, hs=hs: g.tensor_copy(out=Vt[kk][:, :, h2, hs],
                                         in_=vcb[:, h2, :, :]), reads=[vcb], writes=[Vt[kk]])

                            units = []
                            for hh in range(2):
                                units.append(dict(hh=hh, p=p, Ktb=Kt[k], Vb=Vt[k], kcols=slice(0, 128), vr=0, qcols=qc, N=L,
                                                  first=True, last=False, mask=masks[:, 0, 0:64], lsi=0,
                                                  pre=(pre_first if hh == 0 else None)))
                            step = 1
                            glist = list(range(PB // 4 - 1, -1, -1))
                            gk = []
                            for gi, g4 in enumerate(glist):
                                kk = ldc[0] % 2
                                ldc[0] += 1
                                gk.append(kk)
                            for gi, g4 in enumerate(glist):
                                kk = gk[gi]
                                for bi, b in enumerate((3, 2, 1, 0)):
                                    for hh in range(2):
                                        u = dict(hh=hh, p=p, Ktb=Kt[kk], Vb=Vt[kk], kcols=slice(b * 128, (b + 1) * 128), vr=b, qcols=qc, N=L,
                                                 first=False, last=(step == nsteps - 1), mask=None, lsi=step)
                                        if gi == 0 and bi == 0 and hh == 0:
                                            u["pre"] = (lambda kk=kk, g4=g4: pre_group(kk, g4))
                                        if bi == 2 and hh == 0 and gi + 1 < len(glist):
                                            u["pre"] = (lambda kk=gk[gi + 1], g4=glist[gi + 1]: pre_group(kk, g4))
                                        units.append(u)
                                    step += 1
                            emit_units(units)
                            P.op("act", lambda a, p=p, c0=c0, L=L: a.copy(out=osb[:, p, c0:c0 + L], in_=PO[:, 0:L]), reads=[PO], writes=[osb])
                P.dma("pool", OSs[:, :, r0:r0 + N], osb[:, :, 0:N], osb, reads=[osb], writes=[trk["OS"][ti]])

            P.barrier()
        with contextlib.suppress(_SkipPhase), contextlib.ExitStack() as ph:
            _phase_gate(4)
            Wg = sb(ph, "Wg", [128, 8, 3072], BF16)
            Wso = sb(ph, "Wso", [128, 8, D], BF16)
            Wo = sb(ph, "Wo", [128, 8, D], BF16)
            with contextlib.ExitStack() as ws:
                wst[0] = sb(ws, "wst0", [128, 2048], F32); wst[1] = sb(ws, "wst1", [128, 2048], F32)
                load_w(Wg, w_in[:, 6144:9216], 8, 3072)
                load_w(Wso, w_sb_o, 8, D)
                load_w(Wo, w_out, 8, D)
                P.barrier()
            common(ph, 4, 512, ["pre", "post"])
            os_ = sb(ph, "os_", [128, 8, 512], BF16)
            ycb = [sb(ph, f"ycb{k}", [128, 512], F32) for k in range(2)]
            ymb = [sb(ph, f"ymb{k}", [128, 512], F32) for k in range(2)]
            sgg = [sb(ph, f"sgg{k}", [128, 512], F32) for k in range(3)]
            mrg = [sb(ph, f"mrg{k}", [128, 512], F32) for k in range(2)]
            mg = sb(ph, "mg", [128, 8, 512], BF16)
            mo = [sb(ph, f"mo{k}", [128, D], F32) for k in range(2)]
            ss2 = sb(ph, "ss2", [128, 4], F32)
            rs2 = sb(ph, "rs2", [128, 4], F32)
            for tl in tiles:
                N = tl["N"]; ti = tl["idx"]; r0 = tl["r0"]
                nsub = N // 128
                load_x(tl)
                norm_T(tl, gbc["pre"])
                P.dma("sp", os_[:, :, 0:N], OSs[:, :, r0:r0 + N], os_, reads=[trk["OS"][ti]], writes=[os_])
                for c2 in range(8):
                    yc, ym = ycb[c2 % 2], ymb[c2 % 2]
                    P.dma("sp", yc[:, 0:N], YCs[:, c2, r0:r0 + N], yc, reads=[trk["YC"][ti]], writes=[yc])
                    P.dma("sp", ym[:, 0:N], YMs[:, c2, r0:r0 + N], ym, reads=[trk["YM"][ti]], writes=[ym])
                    pys, pg = PS[0 + (c2 % 2) * 4], [PS[1 + (c2 % 2) * 4], PS[2 + (c2 % 2) * 4], PS[3 + (c2 % 2) * 4]]
                    proj_fm(Wso, c2 * 128, os_, N, pys)
                    for gi in range(3):
                        proj_fm(Wg, gi * 1024 + c2 * 128, hT, N, pg[gi])
                        P.op("act", lambda a, gi=gi, pg=pg: a.activation(out=sgg[gi][:, 0:N], in_=pg[gi][:, 0:N], func=AF.Sigmoid),
                             reads=[pg[gi]], writes=[sgg[gi]])
                    m = mrg[c2 % 2]
                    P.op("dve", lambda v, m=m, pys=pys: v.tensor_tensor(out=m[:, 0:N], in0=pys[:, 0:N], in1=sgg[0][:, 0:N], op=ALU.mult),
                         reads=[pys, sgg[0]], writes=[m])
                    P.op("pool", lambda g, yc=yc: g.tensor_tensor(out=sgg[1][:, 0:N], in0=sgg[1][:, 0:N], in1=yc[:, 0:N], op=ALU.mult),
                         reads=[sgg[1], yc], writes=[sgg[1]])
                    P.op("pool", lambda g, ym=ym: g.tensor_tensor(out=sgg[2][:, 0:N], in0=sgg[2][:, 0:N], in1=ym[:, 0:N], op=ALU.mult),
                         reads=[sgg[2], ym], writes=[sgg[2]])
                    P.op("dve", lambda v, m=m: v.tensor_tensor(out=m[:, 0:N], in0=m[:, 0:N], in1=sgg[1][:, 0:N], op=ALU.add),
                         reads=[m, sgg[1]], writes=[m])
                    P.op("dve", lambda v, m=m, c2=c2: v.tensor_tensor(out=mg[:, c2, 0:N], in0=m[:, 0:N], in1=sgg[2][:, 0:N], op=ALU.add),
                         reads=[m, sgg[2]], writes=[mg])
                for j in range(nsub):
                    mj = mo[j % 2]
                    for hf in range(2):
                        pb = PS[hf]
                        proj_tm(Wo, hf * 512, mg, j * 128, pb)
                        if hf == 0:
                            P.op("act", lambda a, mj=mj, pb=pb: a.copy(out=mj[:, 0:512], in_=pb[:]), reads=[pb], writes=[mj])
                        else:
                            P.op("dve", lambda v, mj=mj, pb=pb: v.tensor_copy(out=mj[:, 512:1024], in_=pb[:]), reads=[pb], writes=[mj])
                    P.op("dve", lambda v, j=j: v.memset(ss2[:, j:j + 1], 0.0), writes=[ss2])
                    P.op("act", lambda a, mj=mj, j=j: a.activation(out=cm["junk"][:], in_=mj[:], func=AF.Square, accum_out=ss2[:, j:j + 1]),
                         reads=[mj], writes=[cm["junk"], ss2])
                    rstd_from(ss2[:, j:j + 1], rs2[:, j:j + 1], ss2, rs2, D)
                    xo = mj
                    P.op("dve", lambda v, mj=mj, j=j: v.scalar_tensor_tensor(out=mj[:], in0=mj[:], scalar=rs2[:, j:j + 1],
                         in1=gbc["post"][:], op0=ALU.mult, op1=ALU.mult), reads=[mj, rs2, gbc["post"]], writes=[mj])
                    P.op("pool", lambda g, mj=mj, j=j, xo=xo: g.tensor_tensor(out=xo[:], in0=mj[:], in1=xt[j][:], op=ALU.add),
                         reads=[mj, xt[j]], writes=[xo])
                    P.dma("pool", XMs[r0 + j * 128:r0 + (j + 1) * 128, :], xo[:, :], xo, reads=[xo], writes=[trk["XM"][ti]])

            P.barrier()
        with contextlib.suppress(_SkipPhase), contextlib.ExitStack() as ph:
            _phase_gate(5)
            Wup = sb(ph, "Wup", [128, 8, FF2], BF16)
            Wdn = sb(ph, "Wdn", [128, NFC, D], BF16)
            with contextlib.ExitStack() as ws:
                wst[0] = sb(ws, "wst0", [128, 2048], F32); wst[1] = sb(ws, "wst1", [128, 2048], F32)
                load_w(Wup, w_ffn_up, 8, FF2)
                load_w(Wdn, w_ffn_down, NFC, D)
                P.barrier()
            common(ph, 4, 512, ["fpre", "fpost"], one_xn=True, no_junk=True)
            fw = sb(ph, "fw", [128, 44, 3], F32)
            halP = sb(ph, "halP", [128, 44, 2], F32)
            halS = sb(ph, "halS", [128, 44, NS, 2], F32)
            sfs_stack = contextlib.ExitStack()
            sfs = sb(sfs_stack, "sfs", [3, 512], F32)
            for g11 in range(11):
                P.dma("sp", sfs[0:3, :], ffn_dw_w[:, g11 * 512:(g11 + 1) * 512], sfs, writes=[sfs])
                pb = PS[6]
                for c in range(4):
                    P.op("pe", lambda t, c=c: t.matmul(pb[:, c * 3:(c + 1) * 3], lhsT=sfs[0:3, c * 128:(c + 1) * 128],
                         rhs=identf[0:3, 0:3], start=True, stop=True), reads=[sfs, identf], writes=[pb], inc=(c == 3))
                P.op("dve", lambda v, g11=g11: v.tensor_copy(out=fw[:, g11 * 4:(g11 + 1) * 4, :],
                     in_=pb[:, 0:12].rearrange("p (c r) -> p c r", c=4)), reads=[pb], writes=[fw])
            for s in range(NS):
                for g11 in range(11):
                    P.dma("sp", sfs[0:2, :], sffn[s, :, g11 * 512:(g11 + 1) * 512], sfs, writes=[sfs])
                    pb = PS[7]
                    for c in range(4):
                        P.op("pe", lambda t, c=c: t.matmul(pb[:, c * 2:(c + 1) * 2], lhsT=sfs[0:2, c * 128:(c + 1) * 128],
                             rhs=identf[0:2, 0:2], start=True, stop=True), reads=[sfs, identf], writes=[pb], inc=(c == 3))
                    P.op("dve", lambda v, s=s, g11=g11: v.tensor_copy(out=halS[:, g11 * 4:(g11 + 1) * 4, s, :],
                         in_=pb[:, 0:8].rearrange("p (c r) -> p c r", c=4)), reads=[pb], writes=[halS])
            P.barrier()
            sfs_stack.close()
            upb = [sb(ph, f"upb{k}", [128, 4, 130], F32) for k in range(2)]
            cv_ = [sb(ph, f"cvv{k}", [128, 512], F32) for k in range(2)]
            gl = sb(ph, "gl", [128, 512], F32)
            cm["junk"] = _Alias(gl, gl[:].bitcast(BF16))
            g2 = gl
            gT = sb(ph, "gT", [128, NFC, 512], BF16)
            dn = [sb(ph, "dn0", [128, D], F32)] * 2
            ss3 = sb(ph, "ss3", [128, 4], F32)
            rs3 = sb(ph, "rs3", [128, 4], F32)
            P.op("dve", lambda v: v.memset(halP[:], 0.0), writes=[halP])
            ftiles = []
            for i in range(T // 512):
                ftiles.append(dict(r0=i * 512, N=512, segs=[(0, 512, None)], idx=i, sample=False, last=(i == T // 512 - 1)))
            ftiles.append(dict(r0=T, N=TS, segs=[(s * 64, 64, s) for s in range(NS)], idx=NT, sample=True, last=True))
            for tl in ftiles:
                N = tl["N"]; ti = tl["idx"]; r0 = tl["r0"]; smp = tl["sample"]
                nsub = N // 128
                load_x(tl, src_fn=lambda tl, j: XMs[tl["r0"] + j * 128:tl["r0"] + (j + 1) * 128, :], trkb=trk["XM"][ti])
                norm_T(tl, gbc["fpre"])
                for j in range(NFC):
                    outs = []
                    for which in range(2):
                        ch = which * NFC + j
                        pb = PS[(2 * j + which) % 4]
                        proj_fm(Wup, ch * 128, hT, N, pb)
                        ub = upb[which]
                        if smp:
                            P.op("act", lambda a, ch=ch, ub=ub: a.copy(out=ub[:, :, 0:2], in_=halS[:, ch, :, :]), reads=[halS], writes=[ub])
                            P.op("act", lambda a, pb=pb, ub=ub: a.copy(out=ub[:, :, 2:66], in_=pb[:, 0:N].rearrange("p (s t) -> p s t", s=NS)),
                                 reads=[pb], writes=[ub])
                            src = lambda k, ub=ub: ub[:, :, k:k + 64]
                            o3 = lambda t_: t_[:, 0:N].rearrange("p (s t) -> p s t", s=NS)
                        else:
                            uf = ub[:, :, :].rearrange("p a b -> p (a b)")
                            P.op("act", lambda a, ch=ch, uf=uf: a.copy(out=uf[:, 0:2], in_=halP[:, ch, :]), reads=[halP], writes=[ub])
                            P.op("act", lambda a, pb=pb, uf=uf: a.copy(out=uf[:, 2:2 + N], in_=pb[:, 0:N]), reads=[pb], writes=[ub])
                            P.op("pool", lambda g, ch=ch, uf=uf: g.tensor_copy(out=halP[:, ch, :], in_=uf[:, N:N + 2]),
                                 reads=[ub], writes=[halP])
                            src = lambda k, uf=uf: uf[:, k:k + N]
                            o3 = lambda t_: t_[:, 0:N]
                        outs.append((cv_[which], src, o3, ch, ub))
                    for (co, src, o3, ch, ub) in outs:
                        P.op("dve", lambda v, co=co, src=src, o3=o3, ch=ch: v.tensor_scalar(out=o3(co), in0=src(0), scalar1=fw[:, ch, 0:1],
                             scalar2=0.0, op0=ALU.mult, op1=ALU.add), reads=[ub, fw], writes=[co])
                    for k in (1, 2):
                        for (co, src, o3, ch, ub) in outs:
                            P.op("dve", lambda v, co=co, src=src, o3=o3, ch=ch, k=k: v.scalar_tensor_tensor(out=o3(co), in0=src(k),
                                 scalar=fw[:, ch, k:k + 1], in1=o3(co), op0=ALU.mult, op1=ALU.add), reads=[ub, fw, co], writes=[co])
                    outs = [o[0] for o in outs]
                    xg, xv = outs
                    P.op("pool", lambda g, xg=xg: g.tensor_tensor(out=g2[:, 0:N], in0=xg[:, 0:N], in1=xg[:, 0:N], op=ALU.mult),
                         reads=[xg], writes=[g2])
                    P.op("pool", lambda g: g.tensor_scalar(out=g2[:, 0:N], in0=g2[:, 0:N], scalar1=0.044715, scalar2=1.0,
                         op0=ALU.mult, op1=ALU.add), reads=[g2], writes=[g2])
                    P.op("pool", lambda g, xg=xg: g.tensor_tensor(out=g2[:, 0:N], in0=g2[:, 0:N], in1=xg[:, 0:N], op=ALU.mult),
                         reads=[g2, xg], writes=[g2])
                    P.op("act", lambda a: a.activation(out=gl[:, 0:N], in_=g2[:, 0:N], func=AF.Sigmoid, scale=1.5957691216057308),
                         reads=[g2], writes=[gl])
                    P.op("dve", lambda v, xg=xg: v.tensor_tensor(out=gl[:, 0:N], in0=gl[:, 0:N], in1=xg[:, 0:N], op=ALU.mult),
                         reads=[gl, xg], writes=[gl])
                    P.op("dve", lambda v, j=j, xv=xv: v.tensor_tensor(out=gT[:, j, 0:N], in0=gl[:, 0:N], in1=xv[:, 0:N], op=ALU.mult),
                         reads=[gl, xv], writes=[gT])
                ends = [(c0 + 62, s) for (c0, L, s) in tl["segs"]] if smp else ([(510, None)] if tl["last"] else [])
                for (t0, s) in ends:
                    for cb in range(11):
                        pb = PS[4 + cb % 2]
                        fb = gl
                        proj_tm(Wup, cb * 512, hT, t0, pb, M=2)
                        P.op("dve", lambda v, fb=fb, pb=pb: v.tensor_copy(out=fb[0:2, :], in_=pb[0:2, :]), reads=[pb], writes=[fb])
                        dst = fs[s, :, cb * 512:(cb + 1) * 512] if smp else fp[:, cb * 512:(cb + 1) * 512]
                        P.dma("pool", dst, fb[0:2, :], fb, reads=[fb])
                for j in range(nsub):
                    dj = dn[j % 2]
                    for hf in range(2):
                        pb = PS[6 + hf]
                        proj_tm(Wdn, hf * 512, gT, j * 128, pb, K=NFC)
                        if hf == 0:
                            P.op("act", lambda a, dj=dj, pb=pb: a.copy(out=dj[:, 0:512], in_=pb[:]), reads=[pb], writes=[dj])
                        else:
                            P.op("dve", lambda v, dj=dj, pb=pb: v.tensor_copy(out=dj[:, 512:1024], in_=pb[:]), reads=[pb], writes=[dj])
                    P.op("dve", lambda v, j=j: v.memset(ss3[:, j:j + 1], 0.0), writes=[ss3])
                    P.op("act", lambda a, dj=dj, j=j: a.activation(out=cm["junk"][:], in_=dj[:], func=AF.Square, accum_out=ss3[:, j:j + 1]),
                         reads=[dj], writes=[cm["junk"], ss3])
                    rstd_from(ss3[:, j:j + 1], rs3[:, j:j + 1], ss3, rs3, D)
                    P.op("dve", lambda v, dj=dj, j=j: v.scalar_tensor_tensor(out=dj[:], in0=dj[:], scalar=rs3[:, j:j + 1],
                         in1=gbc["fpost"][:], op0=ALU.mult, op1=ALU.mult), reads=[dj, rs3, gbc["fpost"]], writes=[dj])
                    P.op("pool", lambda g, dj=dj, j=j: g.tensor_tensor(out=dj[:], in0=dj[:], in1=xt[j][:], op=ALU.add),
                         reads=[dj, xt[j]], writes=[dj])
                    dst = ys[r0 - T + j * 128:r0 - T + (j + 1) * 128, :] if smp else yp[r0 + j * 128:r0 + (j + 1) * 128, :]
                    P.dma("pool", dst, dj[:, :], dj, reads=[dj])
            P.barrier()
        P.finish()
    return nc


_CACHE = {}


def run(T, NS, PAST, per_core):
    key = (T, NS, PAST)
    if key not in _CACHE:
        _CACHE[key] = build(T, NS, PAST)
    nc = _CACHE[key]
    res = run_bass_kernel_spmd(nc, per_core, core_ids=list(range(len(per_core))))
    return res.results


WNAMES = ["g_mem", "w_mem_kv", "g_mix_pre", "g_mix_post", "w_in", "w_sb_o", "conv_dw_w", "conv_dw_b", "conv_ln_g",
          "conv_ln_b", "w_conv_o", "w_mem_o", "w_out", "g_ffn_pre", "g_ffn_post", "w_ffn_up", "ffn_dw_w", "w_ffn_down"]


def make_maps(inp, ncores, NS):
    f = lambda a: np.ascontiguousarray(np.asarray(a, dtype=np.float32))
    B = inp["x_prompt"].shape[0]
    maps = []
    for c in range(ncores):
        b = c % B
        sl = slice(c * NS, (c + 1) * NS)
        m = {"xp": f(inp["x_prompt"][b]), "xs": f(inp["x_sample"][sl]).reshape(NS * 64, D),
             "memp": f(inp["mem_prompt"][b]), "ck": f(inp["cache_sb_k"][0, sl]), "cv": f(inp["cache_sb_v"][0, sl]),
             "sconv": f(inp["state_conv"][0, sl]), "sffn": f(inp["state_ffn_conv"][0, sl]),
             "cmk": f(inp["cache_mem_k"][0, sl]), "cmv": f(inp["cache_mem_v"][0, sl])}
        for n in WNAMES:
            w = f(inp[n][0])
            m[n] = w.reshape(1, -1) if w.ndim == 1 else w
        maps.append(m)
    return maps


def assemble(res, B, ncores):
    cat = lambda n, rng: np.stack([res[c][n] for c in rng])
    pc = range(B)
    sc = range(ncores)
    yp = cat("yp", pc); ys = np.concatenate([res[c]["ys"].reshape(-1, 64, D) for c in sc])
    kp = cat("kp", pc)[None]; vp = cat("vp", pc)[None]
    ks = np.concatenate([res[c]["ks"] for c in sc])[None]; vs = np.concatenate([res[c]["vs"] for c in sc])[None]
    cp = cat("cp", pc)[None]; cs = np.concatenate([res[c]["cs"] for c in sc])[None]
    fp = cat("fp", pc)[None]; fs = np.concatenate([res[c]["fs"] for c in sc])[None]
    mkp = cat("mkp", pc)[None]; mvp = cat("mvp", pc)[None]
    return (yp, ys, kp, vp, ks, vs, cp, cs, fp, fs, mkp, mvp)


def kernel(**inputs):
    T = inputs["x_prompt"].shape[1]
    PAST = inputs["cache_sb_k"].shape[3]
    ncores = 8
    NS = inputs["x_sample"].shape[0] // ncores
    maps = make_maps(inputs, ncores, NS)
    res = run(T, NS, PAST, maps)
    return assemble(res, inputs["x_prompt"].shape[0], ncores)
```
